# Optimizing a Trainium2 kernel written in Bass

```python
import math
import jax
import jax.numpy as jnp
from jax import lax
import numpy as np

D_MODEL = 2048
BATCH = 2
SEQ = 4096
DEPTH = 4

CTX_LEN = 256
GRID_W = 64
HEAD_DIM = 128
N_GROUPS = 4
GROUP_W = D_MODEL // N_GROUPS
MIX_W = N_GROUPS * GROUP_W

NA_HEADS = GROUP_W // HEAD_DIM
NA_WIN_ROWS = 8
NA_WIN_COLS = 16
NA_QBLOCK_COLS = 16
DN_HEADS = GROUP_W // HEAD_DIM
DN_CONV_W = 5
DN_CHUNK = 64
ML_HEADS = GROUP_W // HEAD_DIM
ML_QK = HEAD_DIM // 2
ML_V = GROUP_W // ML_HEADS
ML_CHUNK = 64
WA_HEADS = GROUP_W // HEAD_DIM
WA_KV_HEADS = WA_HEADS // 2
WA_WINDOW = 128
WA_BLOCK = 128
ROPE_THETA = 10000.0
FFN_HIDDEN = ((8 * D_MODEL // 3 + 255) // 256) * 256
EPS = 1e-6

NA_IN = 3 * GROUP_W
DN_IN = 4 * GROUP_W + 4 * DN_HEADS
ML_IN = 2 * ML_HEADS * ML_QK + 2 * GROUP_W + 4 * ML_HEADS
WA_IN = GROUP_W + 2 * WA_KV_HEADS * HEAD_DIM
IN_COLS = NA_IN + DN_IN + ML_IN + WA_IN

kernel_name = "hybrid_parallel_group_flow_backbone"


def rmsnorm(x, gain):
    xf = x.astype(jnp.float32)
    xf = xf * lax.rsqrt(jnp.mean(xf * xf, axis=-1, keepdims=True) + EPS)
    return (xf * gain.astype(jnp.float32)).astype(x.dtype)


def l2norm(x):
    return x * lax.rsqrt(jnp.sum(x * x, axis=-1, keepdims=True) + EPS)


def modulate(h, shift, scale):
    return h * (1 + scale) + shift


def split_heads(t, n_heads):
    b, l, w = t.shape
    return t.reshape(b, l, n_heads, w // n_heads).transpose(0, 2, 1, 3)


def merge_heads(t):
    b, h, l, d = t.shape
    return t.transpose(0, 2, 1, 3).reshape(b, l, h * d)


def softmax32(s):
    return jax.nn.softmax(s.astype(jnp.float32), axis=-1)


def flip_seq(t, direction):
    return jnp.flip(t, axis=2) if direction == 1 else t


def swiglu(h, w_in, w_out):
    gate, up = jnp.split(h @ w_in, 2, axis=-1)
    return (jax.nn.silu(gate) * up) @ w_out


def axial_rope_tables(n):
    t = jnp.arange(n)
    n_freq = HEAD_DIM // 4
    inv_freq = ROPE_THETA ** (-jnp.arange(n_freq, dtype=jnp.float32) / n_freq)
    pos = jnp.stack([t // GRID_W, t % GRID_W], axis=-1).astype(jnp.float32)
    ang = pos[:, :, None] * inv_freq
    return jnp.cos(ang), jnp.sin(ang)


def apply_axial_rope(x, cos, sin):
    b, h, n, d = x.shape
    xr = x.reshape(b, h, n, 2, 2, d // 4)
    x1, x2 = xr[..., 0, :], xr[..., 1, :]
    cos, sin = cos.astype(x.dtype), sin.astype(x.dtype)
    return jnp.stack([x1 * cos - x2 * sin, x2 * cos + x1 * sin], axis=-2).reshape(b, h, n, d)


def centred_conv_silu(t, w):
    pad = w.shape[0] // 2
    out = lax.conv_general_dilated(t, w[:, None, :], window_strides=(1,), padding=[(pad, pad)],
                                   dimension_numbers=('NWC', 'WIO', 'NWC'), feature_group_count=t.shape[-1])
    return jax.nn.silu(out)


def neighbourhood_mixer(p, pc, qk_gain, rpb, need_ctx):
    q, k, v = (split_heads(t, NA_HEADS) for t in jnp.split(p, 3, axis=-1))
    qc, kc, vc = (split_heads(t, NA_HEADS) for t in jnp.split(pc, 3, axis=-1))
    q, k, kc = rmsnorm(q, qk_gain[0]), rmsnorm(k, qk_gain[1]), rmsnorm(kc, qk_gain[1])
    bsz, h, n, d = q.shape
    scale = d ** -0.5
    rows = n // GRID_W
    kh = min(NA_WIN_ROWS, rows)
    kw = NA_WIN_COLS
    qbw = NA_QBLOCK_COLS
    kbw = qbw + kw
    ncb = GRID_W // qbw
    r = jnp.arange(rows)
    key_rows = jnp.clip(r - kh // 2, 0, rows - kh)[:, None] + jnp.arange(kh)
    qcol = jnp.arange(GRID_W).reshape(ncb, qbw)
    key_cols = jnp.clip(qcol[:, 0] - kw // 2, 0, GRID_W - kbw)[:, None] + jnp.arange(kbw)
    win_start = jnp.clip(qcol - kw // 2, 0, GRID_W - kw)
    idx = (key_rows[:, None, :, None] * GRID_W + key_cols[None, :, None, :]).reshape(rows, ncb, kh * kbw)
    kg = jnp.take(k, idx, axis=2)
    vg = jnp.take(v, idx, axis=2)
    qb = q.reshape(bsz, h, rows, ncb, qbw, d)
    drow = key_rows - r[:, None]
    dcol = key_cols[:, None, :] - qcol[:, :, None]
    in_win = (key_cols[:, None, :] >= win_start[:, :, None]) & (key_cols[:, None, :] < win_start[:, :, None] + kw)
    bias = rpb[:, drow[:, None, None, :, None] + NA_WIN_ROWS - 1,
               jnp.clip(dcol, 1 - kw, kw - 1)[None, :, :, None, :] + NA_WIN_COLS - 1]
    bias = jnp.where(in_win[None, None, :, :, None, :], bias.astype(jnp.float32), -jnp.inf)
    bias = bias.reshape(h, rows, ncb, qbw, kh * kbw)
    s_loc = jnp.einsum('bhrjqd,bhrjkd->bhrjqk', qb, kg).astype(jnp.float32) * scale + bias
    s_ctx = jnp.einsum('bhrjqd,bhkd->bhrjqk', qb, kc).astype(jnp.float32) * scale
    prob = softmax32(jnp.concatenate([s_loc, s_ctx], axis=-1)).astype(v.dtype)
    n_loc = kh * kbw
    o = (jnp.einsum('bhrjqk,bhrjkd->bhrjqd', prob[..., :n_loc], vg)
         + jnp.einsum('bhrjqk,bhkd->bhrjqd', prob[..., n_loc:], vc))
    y = merge_heads(o.reshape(bsz, h, n, d))
    yc = None
    if need_ctx:
        qc = rmsnorm(qc, qk_gain[0])
        pcx = softmax32(jnp.einsum('bhqd,bhkd->bhqk', qc, kc).astype(jnp.float32) * scale).astype(vc.dtype)
        yc = merge_heads(jnp.einsum('bhqk,bhkd->bhqd', pcx, vc))
    return y, yc


def gated_delta_chunked(q, k, v, g, beta, s0, with_output):
    bsz, h, n, dk = q.shape
    dv = v.shape[-1]
    cs = DN_CHUNK
    nc = n // cs
    q = q.reshape(bsz, h, nc, cs, dk)
    k = k.reshape(bsz, h, nc, cs, dk)
    v = v.reshape(bsz, h, nc, cs, dv)
    g = jnp.cumsum(g.reshape(bsz, h, nc, cs), axis=-1)
    beta = beta.reshape(bsz, h, nc, cs)
    lower = jnp.tril(jnp.ones((cs, cs), bool))
    strict = jnp.tril(jnp.ones((cs, cs), bool), -1)
    decay = jnp.exp(jnp.where(lower, g[..., :, None] - g[..., None, :], -jnp.inf))
    kb = k * beta[..., None]
    a = jnp.where(strict, jnp.einsum('bhncd,bhnsd->bhncs', kb, k) * decay, 0.0) + jnp.eye(cs, dtype=q.dtype)
    u = lax.linalg.triangular_solve(a, v * beta[..., None], left_side=True, lower=True, unit_diagonal=True)
    w = lax.linalg.triangular_solve(a, kb * jnp.exp(g)[..., None], left_side=True, lower=True, unit_diagonal=True)
    g_last = g[..., -1]
    k_end = k * jnp.exp(g_last[..., None] - g)[..., None]
    xs = [u, w, k_end, g_last]
    if with_output:
        xs += [q * jnp.exp(g)[..., None], jnp.einsum('bhncd,bhnsd->bhncs', q, k) * decay]
    xs = tuple(jnp.moveaxis(t, 2, 0) for t in xs)

    def step(s, inp):
        u_i, w_i, ke_i, gl_i = inp[:4]
        v_new = u_i - jnp.einsum('bhcd,bhde->bhce', w_i, s)
        s_new = s * jnp.exp(gl_i)[..., None, None] + jnp.einsum('bhcd,bhce->bhde', ke_i, v_new)
        if with_output:
            qd_i, qk_i = inp[4:]
            o_i = jnp.einsum('bhcd,bhde->bhce', qd_i, s) + jnp.einsum('bhcs,bhse->bhce', qk_i, v_new)
            return s_new, o_i
        return s_new, None

    s_fin, o = lax.scan(step, s0, xs)
    if not with_output:
        return None, s_fin
    return jnp.moveaxis(o, 0, 2).reshape(bsz, h, n, dv), s_fin


def deltanet_mixer(p, pc, conv_w, a_log, dt_bias, norm_g, need_ctx):
    a = jnp.exp(a_log.astype(jnp.float32))[:, None, :, None]
    dtb = dt_bias.astype(jnp.float32)[:, None, :, None]

    def prep(t):
        b, l, _ = t.shape
        qkv = centred_conv_silu(t[..., :3 * GROUP_W], conv_w).astype(jnp.float32)
        q, k, v = (split_heads(u, DN_HEADS) for u in jnp.split(qkv, 3, axis=-1))
        q = l2norm(q) * HEAD_DIM ** -0.5
        k = l2norm(k)
        gb = t[..., 4 * GROUP_W:].astype(jnp.float32).reshape(b, l, 2, 2, DN_HEADS).transpose(2, 3, 0, 4, 1)
        beta = jax.nn.sigmoid(gb[0])
        g = -a * jax.nn.softplus(gb[1] + dtb)
        return q, k, v, beta, g

    q, k, v, beta, g = prep(p)
    qc, kc, vc, beta_c, g_c = prep(pc)
    s0 = jnp.zeros(kc.shape[:2] + (HEAD_DIM, HEAD_DIM), jnp.float32)
    y, yc = 0.0, 0.0
    for d in range(2):
        oc, s_ctx = gated_delta_chunked(flip_seq(qc, d), flip_seq(kc, d), flip_seq(vc, d),
                                        flip_seq(g_c[d], d), flip_seq(beta_c[d], d), s0, need_ctx)
        o, _ = gated_delta_chunked(flip_seq(q, d), flip_seq(k, d), flip_seq(v, d),
                                   flip_seq(g[d], d), flip_seq(beta[d], d), s_ctx, True)
        y = y + flip_seq(o, d)
        if need_ctx:
            yc = yc + flip_seq(oc, d)

    def finish(o, t):
        gate = split_heads(t[..., 3 * GROUP_W:4 * GROUP_W].astype(jnp.float32), DN_HEADS)
        return merge_heads(rmsnorm(o, norm_g) * jax.nn.silu(gate)).astype(t.dtype)

    return finish(y, p), (finish(yc, pc) if need_ctx else None)


def mlstm_chunked(q, k, v, ig, fg, state, with_output):
    bsz, h, n, dqk = q.shape
    dv = v.shape[-1]
    cs = ML_CHUNK
    nc = n // cs
    q = q.reshape(bsz, h, nc, cs, dqk)
    k = k.reshape(bsz, h, nc, cs, dqk)
    v = v.reshape(bsz, h, nc, cs, dv)
    ig = ig.reshape(bsz, h, nc, cs)
    b = jnp.cumsum(jax.nn.log_sigmoid(fg).reshape(bsz, h, nc, cs), axis=-1)
    b_last = b[..., -1]
    g_end = b_last[..., None] - b + ig
    m_loc = jnp.max(g_end, axis=-1)
    wgt = jnp.exp(g_end - m_loc[..., None])
    c_loc = jnp.einsum('bhnld,bhnle->bhnde', k * wgt[..., None], v)
    n_loc = jnp.einsum('bhnl,bhnld->bhnd', wgt, k)

    def step(carry, inp):
        c_prev, n_prev, m_prev = carry
        bl, ml, cl, nl = inp
        m_new = jnp.maximum(bl + m_prev, ml)
        a = jnp.exp(bl + m_prev - m_new)
        e = jnp.exp(ml - m_new)
        new = (a[..., None, None] * c_prev + e[..., None, None] * cl, a[..., None] * n_prev + e[..., None] * nl, m_new)
        return new, (carry if with_output else None)

    xs = tuple(jnp.moveaxis(t, 2, 0) for t in (b_last, m_loc, c_loc, n_loc))
    final, starts = lax.scan(step, state, xs)
    if not with_output:
        return None, final
    c_st, n_st, m_st = (jnp.moveaxis(t, 0, 2) for t in starts)
    lower = jnp.tril(jnp.ones((cs, cs), bool))
    log_d = jnp.where(lower, b[..., :, None] - b[..., None, :] + ig[..., None, :], -jnp.inf)
    m_inter = b + m_st[..., None]
    m_t = jnp.maximum(jnp.max(log_d, axis=-1), m_inter)
    s = jnp.einsum('bhnld,bhnsd->bhnls', q, k) * jnp.exp(log_d - m_t[..., None])
    inter = jnp.exp(m_inter - m_t)
    num = jnp.einsum('bhnls,bhnse->bhnle', s, v) + inter[..., None] * jnp.einsum('bhnld,bhnde->bhnle', q, c_st)
    den = jnp.sum(s, axis=-1) + inter * jnp.einsum('bhnld,bhnd->bhnl', q, n_st)
    hh = num / jnp.maximum(jnp.abs(den), jnp.exp(-m_t))[..., None]
    return hh.reshape(bsz, h, n, dv), final


def mlstm_mixer(p, pc, i_bias, f_bias, norm_g, need_ctx):
    qkw = ML_HEADS * ML_QK
    ib = i_bias.astype(jnp.float32)[:, None, :, None]
    fb = f_bias.astype(jnp.float32)[:, None, :, None]

    def prep(t):
        b, l, _ = t.shape
        tf = t.astype(jnp.float32)
        q = split_heads(tf[..., :qkw], ML_HEADS)
        k = split_heads(tf[..., qkw:2 * qkw], ML_HEADS) * ML_QK ** -0.5
        v = split_heads(tf[..., 2 * qkw:2 * qkw + GROUP_W], ML_HEADS)
        gates = tf[..., 2 * qkw + 2 * GROUP_W:].reshape(b, l, 2, 2, ML_HEADS).transpose(2, 3, 0, 4, 1)
        return q, k, v, gates[0] + ib, gates[1] + fb

    q, k, v, ig, fg = prep(p)
    qc, kc, vc, ig_c, fg_c = prep(pc)
    bsz = kc.shape[0]
    state0 = (jnp.zeros((bsz, ML_HEADS, ML_QK, ML_V), jnp.float32),
              jnp.zeros((bsz, ML_HEADS, ML_QK), jnp.float32),
              jnp.zeros((bsz, ML_HEADS), jnp.float32))
    y, yc = 0.0, 0.0
    for d in range(2):
        hc, st = mlstm_chunked(flip_seq(qc, d), flip_seq(kc, d), flip_seq(vc, d),
                               flip_seq(ig_c[d], d), flip_seq(fg_c[d], d), state0, need_ctx)
        hl, _ = mlstm_chunked(flip_seq(q, d), flip_seq(k, d), flip_seq(v, d),
                              flip_seq(ig[d], d), flip_seq(fg[d], d), st, True)
        y = y + flip_seq(hl, d)
        if need_ctx:
            yc = yc + flip_seq(hc, d)

    def finish(hh, t):
        o_gate = jax.nn.sigmoid(t[..., 2 * qkw + GROUP_W:2 * qkw + 2 * GROUP_W].astype(jnp.float32))
        return (o_gate * merge_heads(rmsnorm(hh, norm_g[:, None, :]))).astype(t.dtype)

    return finish(y, p), (finish(yc, pc) if need_ctx else None)


def window_mixer(p, pc, qk_gain, sink, cos, sin, need_ctx):
    kvw = WA_KV_HEADS * HEAD_DIM
    q = split_heads(p[..., :GROUP_W], WA_HEADS)
    k = split_heads(p[..., GROUP_W:GROUP_W + kvw], WA_KV_HEADS)
    v = split_heads(p[..., GROUP_W + kvw:], WA_KV_HEADS)
    kc = rmsnorm(split_heads(pc[..., GROUP_W:GROUP_W + kvw], WA_KV_HEADS), qk_gain[1])
    vc = split_heads(pc[..., GROUP_W + kvw:], WA_KV_HEADS)
    q = apply_axial_rope(rmsnorm(q, qk_gain[0]), cos, sin)
    k = apply_axial_rope(rmsnorm(k, qk_gain[1]), cos, sin)
    bsz, _, n, d = q.shape
    nb = n // WA_BLOCK
    rep = WA_HEADS // WA_KV_HEADS
    lc = kc.shape[2]
    scale = d ** -0.5
    sink32 = sink.astype(jnp.float32).reshape(1, WA_KV_HEADS, rep, 1, 1)
    qg = q.reshape(bsz, WA_KV_HEADS, rep, nb, WA_BLOCK, d)

    def band(t):
        tp = jnp.pad(t, ((0, 0), (0, 0), (WA_BLOCK, WA_BLOCK), (0, 0))).reshape(bsz, WA_KV_HEADS, nb + 2, WA_BLOCK, d)
        return jnp.concatenate([tp[:, :, :nb], tp[:, :, 1:nb + 1], tp[:, :, 2:]], axis=3)

    kb, vb = band(k), band(v)
    qpos = jnp.arange(n).reshape(nb, WA_BLOCK)
    kpos = (jnp.arange(nb)[:, None] - 1) * WA_BLOCK + jnp.arange(3 * WA_BLOCK)
    ok = ((kpos[:, None, :] >= 0) & (kpos[:, None, :] < n)
          & (jnp.abs(qpos[:, :, None] - kpos[:, None, :]) <= WA_WINDOW))
    s_loc = jnp.where(ok, jnp.einsum('bgrnqd,bgnkd->bgrnqk', qg, kb).astype(jnp.float32) * scale, -jnp.inf)
    s_ctx = jnp.einsum('bgrnqd,bgkd->bgrnqk', qg, kc).astype(jnp.float32) * scale
    sink_col = jnp.broadcast_to(sink32[..., None], s_ctx.shape[:-1] + (1,))
    prob = softmax32(jnp.concatenate([s_loc, s_ctx, sink_col], axis=-1)).astype(v.dtype)
    n_loc = 3 * WA_BLOCK
    o = (jnp.einsum('bgrnqk,bgnkd->bgrnqd', prob[..., :n_loc], vb)
         + jnp.einsum('bgrnqk,bgkd->bgrnqd', prob[..., n_loc:n_loc + lc], vc))
    y = merge_heads(o.reshape(bsz, WA_HEADS, n, d))
    yc = None
    if need_ctx:
        qc = rmsnorm(split_heads(pc[..., :GROUP_W], WA_HEADS), qk_gain[0]).reshape(bsz, WA_KV_HEADS, rep, lc, d)
        sc = jnp.einsum('bgrqd,bgkd->bgrqk', qc, kc).astype(jnp.float32) * scale
        sc = jnp.concatenate([sc, jnp.broadcast_to(sink32, sc.shape[:-1] + (1,))], axis=-1)
        pcx = softmax32(sc).astype(vc.dtype)
        oc = jnp.einsum('bgrqk,bgkd->bgrqd', pcx[..., :lc], vc)
        yc = merge_heads(oc.reshape(bsz, WA_HEADS, lc, d))
    return y, yc


def setup_inputs(seed: int = 0) -> dict:
    key = jax.random.key(seed)
    ks = jax.random.split(key, 24)
    f32 = jnp.float32

    def nrm(k, shape, scale):
        return jax.random.normal(k, shape, f32) * scale

    dt = jnp.exp(jax.random.uniform(ks[14], (DEPTH, 2, DN_HEADS), f32, math.log(1e-3), math.log(1e-1)))
    return {
        "x": nrm(ks[0], (BATCH, SEQ, D_MODEL), 1.0),
        "c": nrm(ks[1], (BATCH, D_MODEL), 1.0),
        "ctx": nrm(ks[2], (BATCH, CTX_LEN, D_MODEL), 1.0),
        "c_ctx": nrm(ks[3], (D_MODEL,), 1.0),
        "w_ada": nrm(ks[4], (DEPTH, D_MODEL, 6 * D_MODEL), 0.5 * D_MODEL ** -0.5),
        "b_ada": nrm(ks[5], (DEPTH, 6 * D_MODEL), 0.02),
        "norm_mix": 1.0 + nrm(ks[6], (DEPTH, D_MODEL), 0.02),
        "norm_ffn": 1.0 + nrm(ks[7], (DEPTH, D_MODEL), 0.02),
        "w_in": nrm(ks[8], (DEPTH, D_MODEL, IN_COLS), D_MODEL ** -0.5),
        "w_out": nrm(ks[9], (DEPTH, MIX_W, D_MODEL), MIX_W ** -0.5),
        "na_qk_gain": 1.0 + nrm(ks[10], (DEPTH, 2, HEAD_DIM), 0.02),
        "na_rpb": nrm(ks[11], (DEPTH, NA_HEADS, 2 * NA_WIN_ROWS - 1, 2 * NA_WIN_COLS - 1), 0.5),
        "dn_conv": nrm(ks[12], (DEPTH, DN_CONV_W, 3 * GROUP_W), DN_CONV_W ** -0.5),
        "dn_a_log": jnp.log(jax.random.uniform(ks[13], (DEPTH, 2, DN_HEADS), f32, 1.0, 16.0)),
        "dn_dt_bias": dt + jnp.log(-jnp.expm1(-dt)),
        "dn_norm": 1.0 + nrm(ks[15], (DEPTH, HEAD_DIM), 0.02),
        "ml_i_bias": nrm(ks[16], (DEPTH, 2, ML_HEADS), 0.1),
        "ml_f_bias": jax.random.uniform(ks[17], (DEPTH, 2, ML_HEADS), f32, 3.0, 6.0),
        "ml_norm": 1.0 + nrm(ks[18], (DEPTH, ML_HEADS, ML_V), 0.02),
        "wa_qk_gain": 1.0 + nrm(ks[19], (DEPTH, 2, HEAD_DIM), 0.02),
        "wa_sink": nrm(ks[20], (DEPTH, WA_HEADS), 0.5),
        "w_ffn_in": nrm(ks[21], (DEPTH, D_MODEL, 2 * FFN_HIDDEN), D_MODEL ** -0.5),
        "w_ffn_out": nrm(ks[22], (DEPTH, FFN_HIDDEN, D_MODEL), FFN_HIDDEN ** -0.5),
    }


def reference(x, c, ctx, c_ctx, w_ada, b_ada, norm_mix, norm_ffn, w_in, w_out,
              na_qk_gain, na_rpb, dn_conv, dn_a_log, dn_dt_bias, dn_norm,
              ml_i_bias, ml_f_bias, ml_norm, wa_qk_gain, wa_sink, w_ffn_in, w_ffn_out):
    cos, sin = axial_rope_tables(x.shape[1])
    silu_c = jax.nn.silu(c)
    silu_cc = jax.nn.silu(c_ctx)
    cuts = [NA_IN, NA_IN + DN_IN, NA_IN + DN_IN + ML_IN]
    for l in range(DEPTH):
        need_ctx = l < DEPTH - 1
        mod = jnp.split((silu_c @ w_ada[l] + b_ada[l])[:, None, :], 6, axis=-1)
        modc = jnp.split(silu_cc @ w_ada[l] + b_ada[l], 6, axis=-1)
        h = modulate(rmsnorm(x, norm_mix[l]), mod[0], mod[1])
        hc = modulate(rmsnorm(ctx, norm_mix[l]), modc[0], modc[1])
        pa, pb, pm, pw = jnp.split(h @ w_in[l], cuts, axis=-1)
        pca, pcb, pcm, pcw = jnp.split(hc @ w_in[l], cuts, axis=-1)
        ya, yca = neighbourhood_mixer(pa, pca, na_qk_gain[l], na_rpb[l], need_ctx)
        yb, ycb = deltanet_mixer(pb, pcb, dn_conv[l], dn_a_log[l], dn_dt_bias[l], dn_norm[l], need_ctx)
        ym, ycm = mlstm_mixer(pm, pcm, ml_i_bias[l], ml_f_bias[l], ml_norm[l], need_ctx)
        yw, ycw = window_mixer(pw, pcw, wa_qk_gain[l], wa_sink[l], cos, sin, need_ctx)
        x = x + mod[2] * (jnp.concatenate([ya, yb, ym, yw], axis=-1) @ w_out[l])
        x = x + mod[5] * swiglu(modulate(rmsnorm(x, norm_ffn[l]), mod[3], mod[4]), w_ffn_in[l], w_ffn_out[l])
        if need_ctx:
            ctx = ctx + modc[2] * (jnp.concatenate([yca, ycb, ycm, ycw], axis=-1) @ w_out[l])
            ctx = ctx + modc[5] * swiglu(modulate(rmsnorm(ctx, norm_ffn[l]), modc[3], modc[4]),
                                         w_ffn_in[l], w_ffn_out[l])
    return x
```

```python
import contextlib

import numpy as np
import concourse.bass as bass
import concourse.mybir as mybir
from concourse.bass_utils import run_bass_kernel_spmd
from concourse.alu_op_type import AluOpType as ALU

AF = mybir.ActivationFunctionType
AX = mybir.AxisListType
F32 = mybir.dt.float32
BF16 = mybir.dt.bfloat16
F32R = mybir.dt.float32r

ENGS = ('pe', 'act', 'dve', 'pool', 'sp')
NDMASEM = 12
NCCSEM = 56


class Buf:
    __slots__ = ('name', 'w', 'r')

    def __init__(self, name=''):
        self.name = name
        self.w = None
        self.r = []


class Op:
    __slots__ = ('eng', 'pos', 'fn', 'waits', 'flag', 'val', 'dma', 'sem', 'inc')


class Prog:
    def __init__(self, nc):
        self.nc = nc
        self.ops = {e: [] for e in ENGS}
        self.seen = {e: {} for e in ENGS}
        self.dma_last = [None] * (NDMASEM + NCCSEM)
        self.dma_cnt = [0] * (NDMASEM + NCCSEM)
        self.dma_tot = [0] * (NDMASEM + NCCSEM)
        self.dma_rr = 0
        self.cc_rr = 0
        self.ndma = 0

    def _dep(self, o, d, same_ok=False):
        if d is None:
            return
        E = o.eng
        if d.dma:
            key = ('d', d.sem)
            if self.seen[E].get(key, 0) >= d.val:
                return
            self.seen[E][key] = d.val
            o.waits.append(d)
        else:
            if d.eng == E and same_ok:
                return
            key = ('e', d.eng)
            if self.seen[E].get(key, -1) >= d.pos:
                return
            self.seen[E][key] = d.pos
            d.flag = True
            o.waits.append(d)

    def op(self, eng, fn, reads=(), writes=(), dma=False, pe_acc=False, inc=16, cc=False):
        o = Op()
        o.eng = eng
        o.pos = len(self.ops[eng])
        o.fn = fn
        o.waits = []
        o.flag = False
        o.val = None
        o.dma = dma
        o.sem = None
        for b in reads:
            self._dep(o, b.w)
        for b in writes:
            if not (pe_acc and b.w is not None and b.w.eng == 'pe' and eng == 'pe'):
                self._dep(o, b.w)
            for r in b.r:
                self._dep(o, r, same_ok=(not r.dma))
        if dma:
            if cc:
                s = NDMASEM + self.cc_rr
                self.cc_rr = (self.cc_rr + 1) % NCCSEM
            else:
                s = self.dma_rr
                self.dma_rr = (self.dma_rr + 1) % NDMASEM
            self._dep(o, self.dma_last[s])
            self.dma_cnt[s] += 1
            self.dma_tot[s] += inc
            o.sem = s
            o.inc = inc
            o.val = self.dma_tot[s]
            self.dma_last[s] = o
            self.ndma += 1
        self.ops[eng].append(o)
        for b in reads:
            if dma:
                b.r.append(o)
            else:
                b.r = [r for r in b.r if r.dma or r.eng != eng]
                b.r.append(o)
        for b in writes:
            b.w = o
            b.r = []
        return o

    def dma(self, q, out, in_, reads=(), writes=(), **kw):
        return self.op(q, lambda e: e.dma_start(out=out, in_=in_, **kw), reads, writes, dma=True)

    def barrier(self):
        lasts = {}
        for e in ENGS:
            for o in reversed(self.ops[e]):
                if (not o.dma) and o.fn is not None:
                    lasts[e] = o
                    break
        for E in ENGS:
            o = Op()
            o.eng = E
            o.pos = len(self.ops[E])
            o.fn = None
            o.waits = []
            o.flag = False
            o.val = None
            o.dma = False
            o.sem = None
            o.inc = 0
            for F in ENGS:
                if F != E and F in lasts:
                    self._dep(o, lasts[F])
            for d in self.dma_last[:NDMASEM]:
                if d is not None:
                    self._dep(o, d)
            self.ops[E].append(o)

    def finish(self):
        o = Op()
        o.eng = 'sp'
        o.pos = len(self.ops['sp'])
        o.fn = None
        o.waits = []
        o.flag = False
        o.val = None
        o.dma = False
        o.sem = None
        for d in self.dma_last:
            if d is not None:
                self._dep(o, d)
        self.ops['sp'].append(o)

    def emit(self):
        nc = self.nc
        self.finish()
        for e in ENGS:
            c = 0
            for o in self.ops[e]:
                if not o.dma and o.flag:
                    c += 1
                    o.val = c
        import contextlib
        with contextlib.ExitStack() as st:
            esem = {e: st.enter_context(nc.semaphore('s_' + e)) for e in ENGS}
            dsem = [st.enter_context(nc.semaphore('d%d' % i)) for i in range(NDMASEM + NCCSEM)]
            block = st.enter_context(nc.Block())

            def run(e):
                def body(eng):
                    for o in self.ops[e]:
                        for d in o.waits:
                            if d.dma:
                                eng.wait_ge(dsem[d.sem], d.val)
                            else:
                                eng.wait_ge(esem[d.eng], d.val)
                        if o.fn is None:
                            continue
                        ins = o.fn(eng)
                        if o.dma:
                            ins.then_inc(dsem[o.sem], o.inc)
                        elif o.flag:
                            ins.then_inc(esem[e], 1)
                return body

            block.tensor(run('pe'))
            block.scalar(run('act'))
            block.vector(run('dve'))
            block.gpsimd(run('pool'))
            block.sync(run('sp'))

import numpy as np

U32 = mybir.dt.uint32
DEPTH = 4
D = 2048
NTOK = 1088
TT = [(0, 512), (512, 512), (1024, 64)]
IN_COLS = 6176
FFN_H = 5632
EPS = 1e-6
NT = 4352
NB = 34
NCH = 68
SCALE = 128 ** -0.5

FM_TENSORS = [('na_q', 0), ('na_k', 512), ('na_v', 1024),
              ('dn_q', 1536), ('dn_k', 2048), ('dn_v', 2560), ('dn_g', 3072),
              ('ml_q', 3600), ('ml_k', 3856), ('ml_v', 4112), ('ml_o', 4624),
              ('wa_q', 5152), ('wa_k', 5664), ('wa_v', 5920)]
FM_WIDTH = {'ml_q': 64, 'ml_k': 64}
CT_ROWS = [('dn_b0', 3584 + 0), ('dn_b1', 3584 + 4), ('dn_a0', 3584 + 8), ('dn_a1', 3584 + 12),
           ('ml_i0', 5136 + 0), ('ml_i1', 5136 + 4), ('ml_f0', 5136 + 8), ('ml_f1', 5136 + 12)]


def idx_cols():
    cols = {}
    n = 0
    for name, _ in FM_TENSORS:
        for r in range(4):
            cols[(name, r)] = n
            cols[(name, r, 'c')] = n + 1
            n += 2
    for name, _ in CT_ROWS:
        cols[(name, 'lat')] = n
        cols[(name, 'ctx')] = n + 1
        n += 2
    for kc in range(16):
        cols[('y', kc)] = n
        n += 1
    return cols, n


IDXC, NIDX = idx_cols()


PCH = 256


def grow(r, cidx):
    cidx = np.asarray(cidx)
    start = (cidx // PCH) * PCH
    rows_q = np.minimum(PCH, IN_COLS - start)
    return 4 * start + r * rows_q + (cidx - start)


PCC = 512


def grow_ctx(r, cidx):
    cidx = np.asarray(cidx)
    start = (cidx // PCC) * PCC
    rows_q = np.minimum(PCC, IN_COLS - start)
    return 4 * start + r * rows_q + (cidx - start)


def make_idx(j):
    t = np.zeros((128, NIDX), np.uint32)
    p = np.arange(128)
    for name, base in FM_TENSORS:
        w = FM_WIDTH.get(name, 128)
        hj = (j // 2) if name in ('wa_k', 'wa_v') else j
        c0 = base + hj * w
        for r in range(4):
            t[:, IDXC[(name, r)]] = grow(r, c0 + np.minimum(p, w - 1))
            t[:, IDXC[(name, r, 'c')]] = grow_ctx(r, c0 + np.minimum(p, w - 1))
    for name, base in CT_ROWS:
        c = base + j
        rev = name.endswith('1')
        ci = np.arange(64)
        cn = (63 - ci) if rev else ci
        t[0:64, IDXC[(name, 'lat')]] = grow(cn // 16, c) * 16 + (cn % 16)
        rr = np.arange(4)
        rn = (3 - rr) if rev else rr
        t[0:4, IDXC[(name, 'ctx')]] = grow_ctx(rn, c)
    for kc in range(16):
        g, r = kc // 4, kc % 4
        t[:, IDXC[('y', kc)]] = ((g * 4 + j) * 4 + r) * 128 + p
    return t


class K:
    def __init__(self, arena_kb=204):
        self.nc = bass.Bass("TRN2", target_bir_lowering=False)
        self.st = contextlib.ExitStack()
        self.P = Prog(self.nc)
        self.psn = 0
        self.nring = 8
        self.words = arena_kb * 256
        self.arena = self.st.enter_context(self.nc.sbuf_tensor("arena", [128, self.words], F32))
        self.off = 0
        self.mark = 0
        self.pst = [self.st.enter_context(self.nc.psum_tensor("ps%d" % i, [128, 512], F32)) for i in range(8)]
        self.psb = [Buf('ps%d' % i) for i in range(8)]
        self.drams = {}

    def dram(self, name, shape, dt=F32, kind="ExternalInput"):
        if kind == "Internal":
            t = self.nc.dram_tensor(name, list(shape), dt).ap()
        else:
            t = self.nc.dram_tensor(name, list(shape), dt, kind=kind).ap()
        self.drams[name] = t
        return t

    def sb(self, shape, dt=F32):
        shape = list(shape)
        n = 1
        for s in shape[1:]:
            n *= s
        bpe = 2 if dt == BF16 else 4
        words = (n * bpe + 3) // 4
        words = (words + 7) // 8 * 8
        assert self.off + words <= self.words, ("SBUF arena overflow", self.off, words, self.words)
        ap = self.arena[0:shape[0], self.off:self.off + words]
        self.off += words
        if dt != F32:
            ap = ap.bitcast(dt)
        ap = ap[:, 0:n]
        if len(shape) == 3:
            ap = ap.rearrange("p (a b) -> p a b", b=shape[2])
        elif len(shape) == 4:
            ap = ap.rearrange("p (a b c) -> p a b c", b=shape[2], c=shape[3])
        return ap

    def persist(self):
        self.mark = self.off

    def phase(self):
        self.P.barrier()
        self.off = self.mark

    def nextps(self):
        i = self.psn % self.nring
        self.psn += 1
        return self.pst[i], self.psb[i]

    def done(self):
        self.P.emit()
        self.st.close()
        return self.nc


class Common:
    pass


def setup_common(k):
    P = k.P
    c = Common()
    c.idx_d = k.dram("idx", [128, NIDX], U32)
    c.I_d = k.dram("I128", [128, 128])
    c.J_d = k.dram("J128", [128, 128])
    c.idx = k.sb([128, NIDX], U32)
    c.I = k.sb([128, 128])
    c.J = k.sb([128, 128])
    c.ones = k.sb([128, 128])
    c.eps = k.sb([128, 1])
    c.bconst = Buf('const')
    P.dma('sp', c.idx, c.idx_d, writes=[c.bconst])
    P.dma('sp', c.I, c.I_d, writes=[c.bconst])
    P.dma('sp', c.J, c.J_d, writes=[c.bconst])
    P.op('dve', lambda e: e.memset(c.ones, 1.0), writes=[c.bconst])
    P.op('dve', lambda e: e.memset(c.eps, EPS), writes=[c.bconst])
    c.J64 = c.J[0:64, 64:128]
    return c


def gather_fm(k, c, G, dst, bdst, name, rows=128, lat_off=0, ctx_off=4096):
    P = k.P
    G_lat, G_ctx, bGl, bGc = G
    base = dict(FM_TENSORS)[name]
    wdt = FM_WIDTH.get(name, 128)
    gdeps = bGl[base // PCH:(base + 4 * wdt - 1) // PCH + 1]
    cdeps = bGc[base // PCC:(base + 4 * wdt - 1) // PCC + 1]
    for r in range(4):
        col = IDXC[(name, r)]
        colc = IDXC[(name, r, 'c')]
        P.op('pool', lambda e, r=r, col=col: e.indirect_dma_start(
            out=dst[0:rows, lat_off + r * 1024: lat_off + (r + 1) * 1024], out_offset=None, in_=G_lat,
            in_offset=bass.IndirectOffsetOnAxis(ap=c.idx[0:rows, col:col + 1], axis=0)),
            reads=gdeps + [c.bconst], writes=[bdst], dma=True)
        P.op('pool', lambda e, r=r, colc=colc: e.indirect_dma_start(
            out=dst[0:rows, ctx_off + r * 64: ctx_off + (r + 1) * 64], out_offset=None, in_=G_ctx,
            in_offset=bass.IndirectOffsetOnAxis(ap=c.idx[0:rows, colc:colc + 1], axis=0)),
            reads=cdeps + [c.bconst], writes=[bdst], dma=True)


def gather_ct(k, c, G, dst, bdst, name):
    P = k.P
    G_lat, G_ctx, bGl, bGc = G
    base = dict(CT_ROWS)[name]
    gdeps = bGl[base // PCH:(base + 3) // PCH + 1]
    cdeps = bGc[base // PCC:(base + 3) // PCC + 1]
    G64 = G_lat.rearrange("r (a b) -> (r a) b", b=64)
    cl, cc = IDXC[(name, 'lat')], IDXC[(name, 'ctx')]
    P.op('pool', lambda e: e.indirect_dma_start(out=dst[4:68, :], out_offset=None, in_=G64,
                                                in_offset=bass.IndirectOffsetOnAxis(ap=c.idx[0:64, cl:cl + 1], axis=0)),
         reads=gdeps + [c.bconst], writes=[bdst], dma=True)
    P.op('pool', lambda e: e.indirect_dma_start(out=dst[0:4, :], out_offset=None, in_=G_ctx,
                                                in_offset=bass.IndirectOffsetOnAxis(ap=c.idx[0:4, cc:cc + 1], axis=0)),
         reads=cdeps + [c.bconst], writes=[bdst], dma=True)


def mm_evac(k, out_rows, out_cols, lhsT, rhs, reads, dst, bdst, eng_i=0, scale_col=None, extra_reads=()):
    P = k.P
    pt, pb = k.nextps()
    P.op('pe', lambda e: e.matmul(pt[0:out_rows, 0:out_cols], lhsT, rhs, start=True, stop=True), reads=list(reads), writes=[pb])
    if scale_col is not None:
        P.op('dve', lambda e: e.tensor_scalar(out=dst, in0=pt[0:out_rows, 0:out_cols], scalar1=scale_col, scalar2=None, op0=ALU.mult),
             reads=[pb] + list(extra_reads), writes=[bdst])
    elif eng_i % 2 == 0:
        P.op('act', lambda e: e.activation(out=dst, in_=pt[0:out_rows, 0:out_cols], func=AF.Copy), reads=[pb], writes=[bdst])
    else:
        P.op('dve', lambda e: e.tensor_copy(out=dst, in_=pt[0:out_rows, 0:out_cols]), reads=[pb], writes=[bdst])


def store_y_fm(k, yT, byT, ybuf, bybuf, g, lat_off, ctx_off):
    P = k.P
    bys, G_y, bGy, groups = bybuf
    for jt in range(4):
        q = 'sp' if jt % 2 == 0 else 'act'
        P.dma(q, ybuf[g, jt, :, 0:1024], yT[:, lat_off + jt * 1024: lat_off + (jt + 1) * 1024], reads=[byT], writes=[bys[g * 4 + jt]])
        P.dma(q, ybuf[g, jt, :, 1024:1088], yT[:, ctx_off + jt * 64: ctx_off + (jt + 1) * 64], reads=[byT], writes=[bys[g * 4 + jt]])
    yb2 = ybuf.rearrange("g j d n -> (g j d) n")
    for jt in range(4):
        q = g * 4 + jt
        P.op('pool', lambda e, q=q: e.collective_compute(
            "AllGather", ALU.bypass, replica_groups=groups, ins=[yb2[q * 128:(q + 1) * 128, :].opt()], outs=[G_y[q * 512:(q + 1) * 512, :].opt()]),
            reads=[bys[q]], writes=[bGy[q]], dma=True, inc=1, cc=True)


def qknorm(k, bufs, src, dst, gain, tiles, rope=None):
    P = k.P
    (ones, bones, epst, beps, scr, bscr, bsrc, bdst, bg) = bufs
    for i, (t0, tn, dorope) in enumerate(tiles):
        s0, s1, s2 = scr[(3 * i) % 6], scr[(3 * i + 1) % 6], scr[(3 * i + 2) % 6]
        b0, b1, b2 = bscr[(3 * i) % 6], bscr[(3 * i + 1) % 6], bscr[(3 * i + 2) % 6]
        P.op('act', lambda e, s0=s0, t0=t0, tn=tn: e.activation(out=s0[:, 0:tn], in_=src[:, t0:t0 + tn], func=AF.Square),
             reads=[bsrc], writes=[b0])
        pt, pb = k.nextps()
        P.op('pe', lambda e, pt=pt, s0=s0, tn=tn: e.matmul(pt[:, 0:tn], ones, s0[:, 0:tn], start=True, stop=True),
             reads=[bones, b0], writes=[pb])
        P.op('act', lambda e, pt=pt, s1=s1, tn=tn: e.activation(out=s1[:, 0:tn], in_=pt[:, 0:tn], func=AF.Sqrt,
                                                                  bias=epst, scale=1.0 / 128), reads=[pb, beps], writes=[b1])
        P.op('dve', lambda e, s1=s1, tn=tn: e.reciprocal(out=s1[:, 0:tn], in_=s1[:, 0:tn]), reads=[b1], writes=[b1])
        if not dorope:
            P.op('dve', lambda e, s1=s1, t0=t0, tn=tn: e.scalar_tensor_tensor(
                out=dst[:, t0:t0 + tn], in0=src[:, t0:t0 + tn], scalar=gain, in1=s1[:, 0:tn], op0=ALU.mult, op1=ALU.mult),
                reads=[bsrc, b1, bg], writes=[bdst])
        else:
            C, S, RmT, brope = rope
            P.op('dve', lambda e, s1=s1, s2=s2, t0=t0, tn=tn: e.scalar_tensor_tensor(
                out=s2[:, 0:tn], in0=src[:, t0:t0 + tn], scalar=gain, in1=s1[:, 0:tn], op0=ALU.mult, op1=ALU.mult),
                reads=[bsrc, b1, bg], writes=[b2])
            pr, prb = k.nextps()
            P.op('pe', lambda e, pr=pr, s2=s2, tn=tn: e.matmul(pr[:, 0:tn], RmT, s2[:, 0:tn], start=True, stop=True),
                 reads=[brope, b2], writes=[prb])
            P.op('dve', lambda e, pr=pr, s0=s0, t0=t0, tn=tn: e.tensor_tensor(
                out=s0[:, 0:tn], in0=pr[:, 0:tn], in1=S[:, t0:t0 + tn], op=ALU.mult), reads=[prb, brope], writes=[b0])
            P.op('pool', lambda e, s1=s1, s2=s2, t0=t0, tn=tn: e.tensor_tensor(
                out=s1[:, 0:tn], in0=s2[:, 0:tn], in1=C[:, t0:t0 + tn], op=ALU.mult), reads=[b2, brope], writes=[b1])
            P.op('dve', lambda e, s0=s0, s1=s1, t0=t0, tn=tn: e.tensor_tensor(
                out=dst[:, t0:t0 + tn], in0=s0[:, 0:tn], in1=s1[:, 0:tn], op=ALU.add), reads=[b0, b1], writes=[bdst])


def attn_params(k, kind):
    pr = {}
    if kind == 'wa':
        pr['gains'] = k.dram("wa_gains", [DEPTH, 128, 2])
        pr['cos'] = k.dram("cosT", [128, 4096])
        pr['sin'] = k.dram("sinT", [128, 4096])
        pr['rm'] = k.dram("rmT", [128, 128])
        pr['sink'] = k.dram("wa_sinkb", [DEPTH, 128, 1])
        pr['mask'] = k.dram("wamask", [128, 256])
    else:
        pr['gains'] = k.dram("na_gains", [DEPTH, 128, 2])
        pr['bias'] = k.dram("na_bias", [DEPTH, 128, 5, 7 * 128])
    return pr


def emit_attn(k, c, kind, l, G, ybuf, bybuf, pr):
    P = k.P
    g_slot = 0 if kind == 'na' else 3
    pre = 'na' if kind == 'na' else 'wa'
    q = k.sb([128, NT]); kk = k.sb([128, NT]); vT = k.sb([128, NT])
    qb = k.sb([128, NT], BF16); kb = k.sb([128, NT], BF16)
    V1 = k.sb([128, NB, 129], BF16)
    g = k.sb([128, 2])
    scr = [k.sb([128, 512]) for _ in range(6)]
    bq, bk, bv, bqb, bkb, bV, bg = (Buf(n) for n in 'q k v qb kb V g'.split())
    bscr = [Buf('scr%d' % i) for i in range(6)]
    bo = [Buf('o%d' % i) for i in range(NB)]
    ones, bones, epst, beps = c.ones, c.bconst, c.eps, c.bconst

    gather_fm(k, c, G, q, bq, pre + '_q')
    gather_fm(k, c, G, kk, bk, pre + '_k')
    gather_fm(k, c, G, vT, bv, pre + '_v')
    P.dma('sp', g, pr['gains'][l], writes=[bg])
    P.op('pool', lambda e: e.memset(V1[:, :, 128:129], 1.0), writes=[bV])
    rope = None
    if kind == 'wa':
        C = k.sb([128, 4096]); S = k.sb([128, 4096]); RmT = k.sb([128, 128])
        sk = k.sb([128, 1]); es = k.sb([128, 1]); msk = k.sb([128, 256], BF16)
        brope, bsk, bes, bmsk = Buf('rope'), Buf('sk'), Buf('es'), Buf('msk')
        P.dma('sp', C, pr['cos'], writes=[brope])
        P.dma('act', S, pr['sin'], writes=[brope])
        P.dma('sp', RmT, pr['rm'], writes=[brope])
        P.dma('sp', sk, pr['sink'][l], writes=[bsk])
        P.dma('pool', msk, pr['mask'], writes=[bmsk])
        P.op('act', lambda e: e.activation(out=es, in_=sk, func=AF.Exp), reads=[bsk], writes=[bes])
        rope = (C, S, RmT, brope)
    else:
        bias = k.sb([128, 5, 7 * 128])
        bbias = Buf('bias')
        P.dma('sp', bias, pr['bias'][l], writes=[bbias])
    for n in range(NB):
        mm_evac(k, 128, 128, vT[:, n * 128:(n + 1) * 128], c.I, [bv, c.bconst], V1[:, n, 0:128], bV, eng_i=n)

    tiles_q = [(i * 512, 512, kind == 'wa') for i in range(8)] + [(4096, 256, False)]
    qknorm(k, (ones, bones, epst, beps, scr, bscr, bq, bqb, bg), q, qb, g[:, 0:1], tiles_q, rope)
    qknorm(k, (ones, bones, epst, beps, scr, bscr, bk, bkb, bg), kk, kb, g[:, 1:2], tiles_q, rope)

    osb = q.rearrange("p (n d) -> p n d", d=128)
    yT = kk
    eA = [k.sb([128, 512], BF16) for _ in range(2)]
    eB = [k.sb([128, 512], BF16) for _ in range(2)]
    tmpf = [k.sb([128, 512]) for _ in range(2)]
    rd = [k.sb([128, 1]) for _ in range(2)]
    beA = [Buf('eA0'), Buf('eA1')]; beB = [Buf('eB0'), Buf('eB1')]
    btmp = [Buf('tf0'), Buf('tf1')]; brd = [Buf('rd0'), Buf('rd1')]

    def smat(pt, pb, ci, ch, n):
        P.op('pe', lambda e: e.matmul(pt[:, ci * 128:(ci + 1) * 128], kb[:, ch * 128:(ch + 1) * 128],
                                      qb[:, n * 128:(n + 1) * 128], start=True, stop=True),
             reads=[bkb, bqb], writes=[pb], pe_acc=(ci > 0))

    for n in range(NB):
        s = n % 2
        groups = []
        if kind == 'wa':
            if n < 32:
                A = [n, 32, 33]
                B = ([n - 1] if n > 0 else []) + ([n + 1] if n < 31 else [])
                Bm = ([0] if n > 0 else []) + ([1] if n < 31 else [])
            else:
                A, B, Bm = [32, 33], [], []
            pa, pab = k.nextps()
            for ci, ch in enumerate(A):
                smat(pa, pab, ci, ch, n)
            P.op('act', lambda e, pa=pa, s=s, w=len(A) * 128: e.activation(
                out=eA[s][:, 0:w], in_=pa[:, 0:w], func=AF.Exp, scale=SCALE), reads=[pab], writes=[beA[s]])
            groups.append((eA[s], beA[s], A))
            if B:
                pbt, pbb = k.nextps()
                for ci, ch in enumerate(B):
                    smat(pbt, pbb, ci, ch, n)
                w = len(B) * 128
                P.op('act', lambda e, pbt=pbt, s=s, w=w: e.activation(
                    out=eB[s][:, 0:w], in_=pbt[:, 0:w], func=AF.Exp, scale=SCALE), reads=[pbb], writes=[beB[s]])
                for ci, mi in enumerate(Bm):
                    P.op('pool', lambda e, s=s, ci=ci, mi=mi: e.tensor_tensor(
                        out=eB[s][:, ci * 128:(ci + 1) * 128], in0=eB[s][:, ci * 128:(ci + 1) * 128],
                        in1=msk[:, mi * 128:(mi + 1) * 128], op=ALU.mult), reads=[beB[s], bmsk], writes=[beB[s]])
                groups.append((eB[s], beB[s], B))
        else:
            if n < 32:
                if n == 0:
                    cls, offs = 0, [0, 1, 2, 3]
                elif n == 1:
                    cls, offs = 1, [-1, 0, 1, 2]
                elif n == 30:
                    cls, offs = 3, [-2, -1, 0, 1]
                elif n == 31:
                    cls, offs = 4, [-3, -2, -1, 0]
                else:
                    cls, offs = 2, [-2, -1, 0, 1, 2]
                chs = [n + o for o in offs] + [32, 33]
                G1, G2 = chs[:4], chs[4:]
                col = 0
                for (et, ebf, Gc) in ((eA[s], beA[s], G1), (eB[s], beB[s], G2)):
                    pt, pb = k.nextps()
                    for ci, ch in enumerate(Gc):
                        smat(pt, pb, ci, ch, n)
                    w = len(Gc) * 128
                    P.op('dve', lambda e, pt=pt, s=s, w=w, col=col, cls=cls: e.scalar_tensor_tensor(
                        out=tmpf[s][:, 0:w], in0=pt[:, 0:w], scalar=SCALE, in1=bias[:, cls, col:col + w],
                        op0=ALU.mult, op1=ALU.add), reads=[pb, bbias], writes=[btmp[s]])
                    P.op('act', lambda e, et=et, s=s, w=w: e.activation(
                        out=et[:, 0:w], in_=tmpf[s][:, 0:w], func=AF.Exp), reads=[btmp[s]], writes=[ebf])
                    groups.append((et, ebf, Gc))
                    col += w
            else:
                A = [32, 33]
                pa, pab = k.nextps()
                for ci, ch in enumerate(A):
                    smat(pa, pab, ci, ch, n)
                P.op('act', lambda e, pa=pa, s=s: e.activation(
                    out=eA[s][:, 0:256], in_=pa[:, 0:256], func=AF.Exp, scale=SCALE), reads=[pab], writes=[beA[s]])
                groups.append((eA[s], beA[s], A))
        po, pob = k.nextps()
        tot = sum(len(Gc) for _, _, Gc in groups)
        cnt = 0
        for (et, ebf, Gc) in groups:
            for ci, ch in enumerate(Gc):
                P.op('pe', lambda e, po=po, et=et, ci=ci, ch=ch, cnt=cnt, tot=tot: e.matmul(
                    po[:, 0:129], et[:, ci * 128:(ci + 1) * 128], V1[:, ch, :], start=(cnt == 0), stop=(cnt == tot - 1)),
                    reads=[ebf, bV], writes=[pob], pe_acc=(cnt > 0))
                cnt += 1
        if kind == 'wa':
            P.op('dve', lambda e, po=po, s=s: e.tensor_scalar(
                out=rd[s], in0=po[:, 128:129], scalar1=es[:, 0:1], scalar2=None, op0=ALU.add), reads=[pob, bes], writes=[brd[s]])
            P.op('dve', lambda e, s=s: e.reciprocal(out=rd[s], in_=rd[s]), reads=[brd[s]], writes=[brd[s]])
        else:
            P.op('dve', lambda e, po=po, s=s: e.reciprocal(out=rd[s], in_=po[:, 128:129]), reads=[pob], writes=[brd[s]])
        P.op('dve', lambda e, po=po, s=s, n=n: e.tensor_scalar(
            out=osb[:, n, :], in0=po[:, 0:128], scalar1=rd[s][:, 0:1], scalar2=None, op0=ALU.mult),
            reads=[pob, brd[s]], writes=[bo[n], bq])
    for n in range(NB):
        mm_evac(k, 128, 128, osb[:, n, :], c.I, [bo[n], c.bconst], yT[:, n * 128:(n + 1) * 128], bk, eng_i=n)
    store_y_fm(k, yT, bk, ybuf, bybuf, g_slot, 0, 4096)


def bc_inner(ap2d, n):
    return bass.AP(ap2d.tensor, ap2d.offset, [list(ap2d.ap[0]), list(ap2d.ap[1]), [0, n]])


def bc_mid(ap2d, cnt):
    return bass.AP(ap2d.tensor, ap2d.offset, [list(ap2d.ap[0]), [0, cnt], list(ap2d.ap[1])])


def interleave(gens):
    gens = list(gens)
    while gens:
        for g in list(gens):
            try:
                next(g)
            except StopIteration:
                gens.remove(g)


def cn_of(cp):
    return (3 - cp) if cp < 4 else (4 + 63 - (cp - 4))


def flip_ct(k, c, src, bsrc, dst, bdst, tmp, btmp):
    mm_evac(k, 64, NCH, src, c.I[0:NCH, 0:NCH], [bsrc, c.bconst], tmp[0], btmp[0], eng_i=1)
    mm_evac(k, 64, NCH, c.J64, tmp[0], [c.bconst, btmp[0]], tmp[1], btmp[1], eng_i=1)
    mm_evac(k, NCH, 64, tmp[1], c.I[0:64, 0:64], [btmp[1], c.bconst], dst, bdst, eng_i=1)


def ml_params(k):
    return {'gb': k.dram("ml_gb", [DEPTH, NCH, 4]), 'gn': k.dram("ml_gn", [DEPTH, 64, 128]), 'tri': k.dram("ml_tri", [64, 64])}


def emit_ml(k, c, l, G, ybuf, bybuf, pr):
    P = k.P
    L = NT
    qT = k.sb([64, L]); kT = k.sb([64, L]); kt = k.sb([64, NCH, 64]); V1 = k.sb([64, NCH, 129])
    tmpA = k.sb([128, L])
    hh = [k.sb([64, NCH, 128]) for _ in range(2)]
    ct = [k.sb([NCH, 64]) for _ in range(12)]
    cc = [k.sb([NCH, 1]) for _ in range(4)]
    crow = [k.sb([1, NCH]) for _ in range(8)]
    cols = k.sb([64, 5, NCH]); abc = k.sb([64, 2, NCH])
    ftmp = [k.sb([64, NCH]) for _ in range(2)]
    gb = k.sb([NCH, 4]); tri = k.sb([64, 64])
    one1 = k.sb([1, 64]); onect = k.sb([NCH, 64]); zeroct = k.sb([NCH, 64])
    Cst = k.sb([64, 129])
    stm = [k.sb([64, 64]) for _ in range(2)]; kw = [k.sb([64, 64]) for _ in range(2)]
    nd = [k.sb([64, 129]) for _ in range(2)]; rdn = [k.sb([64, 1]) for _ in range(2)]
    clsb = [k.sb([64, 129]) for _ in range(2)]; bclsb = [Buf('cl0'), Buf('cl1')]
    gn = k.sb([64, 128]); ssq = k.sb([64, NCH])
    I68 = c.I[0:NCH, 0:NCH]
    bI = c.bconst
    bqT, bkT, bkt, bV, bA, bgb, btri, bone1, bC, bgn, bssq, bcols, babc, bconst = (Buf(n) for n in
        'qT kT kt V tmpA gb tri one1 C gn ssq cols abc const'.split())
    bhh = [Buf('hh0'), Buf('hh1')]
    bct = [Buf('ct%d' % i) for i in range(12)]; bcc = [Buf('cc%d' % i) for i in range(4)]
    bcrow = [Buf('crow%d' % i) for i in range(8)]; bftmp = [Buf('ft0'), Buf('ft1')]
    bstm = [Buf('stm0'), Buf('stm1')]; bkw = [Buf('kw0'), Buf('kw1')]; bnd = [Buf('nd0'), Buf('nd1')]
    brdn = [Buf('rdn0'), Buf('rdn1')]

    P.dma('sp', gb, pr['gb'][l], writes=[bgb])
    P.dma('sp', tri, pr['tri'], writes=[btri])
    P.dma('act', gn, pr['gn'][l], writes=[bgn])
    P.op('dve', lambda e: e.memset(one1, 1.0), writes=[bone1])
    P.op('dve', lambda e: e.memset(onect, 1.0), writes=[bconst])
    P.op('dve', lambda e: e.memset(zeroct, 0.0), writes=[bconst])

    def rop(eng, fn, reads, writes):
        P.op(eng, fn, reads=reads, writes=writes)

    def tr_col(src_ap, bsrc, dst_ap, bdst, m, n):
        pt, pb = k.nextps()
        P.op('pe', lambda e: e.matmul(pt[0:n, 0:m], src_ap, c.I[0:m, 0:m], start=True, stop=True), reads=[bsrc, bI], writes=[pb])
        P.op('dve', lambda e: e.tensor_copy(out=dst_ap, in_=pt[0:n, 0:m]), reads=[pb], writes=[bdst])

    for d in range(2):
        if d == 0:
            gather_fm(k, c, G, qT, bqT, 'ml_q', rows=64, lat_off=256, ctx_off=0)
            gather_fm(k, c, G, kT, bkT, 'ml_k', rows=64, lat_off=256, ctx_off=0)
            gather_fm(k, c, G, tmpA, bA, 'ml_v', lat_off=256, ctx_off=0)
            P.op('dve', lambda e: e.tensor_scalar(out=kT, in0=kT, scalar1=0.125, scalar2=None, op0=ALU.mult), reads=[bkT], writes=[bkT])
            P.op('pool', lambda e: e.memset(V1[:, :, 128:129], 1.0), writes=[bV])
            for ch in range(NCH):
                sl = slice(ch * 64, (ch + 1) * 64)
                mm_evac(k, 64, 64, kT[:, sl], c.I[0:64, 0:64], [bkT, bI], kt[:, ch, :], bkt, eng_i=ch)
                mm_evac(k, 64, 128, tmpA[:, sl], c.I, [bA, bI], V1[:, ch, 0:128], bV, eng_i=ch + 1)
        else:
            qtok = tmpA[0:64, :].rearrange("p (a b) -> p a b", b=64)
            for ch in range(NCH):
                sl = slice(ch * 64, (ch + 1) * 64)
                mm_evac(k, 64, 64, qT[:, sl], c.I[0:64, 0:64], [bqT, bI], qtok[:, ch, :], bA, eng_i=ch)
            for cp in range(NCH):
                cn = cn_of(cp)
                sl = slice(cp * 64, (cp + 1) * 64)
                mm_evac(k, 64, 64, qtok[:, cn, :], c.J64, [bA, bI], qT[:, sl], bqT, eng_i=cp)
                mm_evac(k, 64, 64, kt[:, cn, :], c.J64, [bkt, bI], kT[:, sl], bkT, eng_i=cp + 1)
            for cp in range(NCH):
                cn = cn_of(cp)
                if cp > cn:
                    continue
                pairs = [(cp, cn)] if cp == cn else [(cp, cn), (cn, cp)]
                for (tsr, bt_, wdt) in ((kt, bkt, 64), (V1, bV, 128)):
                    pts = []
                    for (dst_c, src_c) in pairs:
                        pt, pb = k.nextps()
                        P.op('pe', lambda e, pt=pt, tsr=tsr, src_c=src_c, wdt=wdt: e.matmul(
                            pt[0:64, 0:wdt], c.J64, tsr[:, src_c, 0:wdt], start=True, stop=True), reads=[bI, bt_], writes=[pb])
                        pts.append((pt, pb, dst_c))
                    for (pt, pb, dst_c) in pts:
                        P.op('act', lambda e, pt=pt, tsr=tsr, dst_c=dst_c, wdt=wdt: e.activation(
                            out=tsr[:, dst_c, 0:wdt], in_=pt[0:64, 0:wdt], func=AF.Copy), reads=[pb], writes=[bt_])
        P.op('pool', lambda e: e.memset(Cst, 0.0), writes=[bC])
        T_ = ct
        if d == 0:
            gather_ct(k, c, G, T_[0], bct[0], 'ml_i0')
            gather_ct(k, c, G, T_[1], bct[1], 'ml_f0')
        else:
            gather_ct(k, c, G, T_[2], bct[2], 'ml_i1')
            flip_ct(k, c, T_[2], bct[2], T_[0], bct[0], ftmp, bftmp)
            gather_ct(k, c, G, T_[2], bct[2], 'ml_f1')
            flip_ct(k, c, T_[2], bct[2], T_[1], bct[1], ftmp, bftmp)
        rop('dve', lambda e, d=d: e.tensor_scalar(out=T_[1], in0=T_[1], scalar1=gb[:, 2 + d:3 + d], scalar2=None, op0=ALU.add),
            [bct[1], bgb], [bct[1]])
        rop('act', lambda e: e.activation(out=T_[2], in_=T_[1], func=AF.Exp, scale=-1.0), [bct[1]], [bct[2]])
        rop('act', lambda e: e.activation(out=T_[2], in_=T_[2], func=AF.Ln, bias=onect[:, 0:1]), [bct[2], bconst], [bct[2]])
        rop('dve', lambda e: e.tensor_scalar(out=T_[1], in0=T_[2], scalar1=-1.0, scalar2=None, op0=ALU.mult), [bct[2]], [bct[1]])
        rop('dve', lambda e: e.tensor_tensor_scan(out=T_[2], data0=onect, data1=T_[1], initial=0.0, op0=ALU.mult, op1=ALU.add),
            [bct[1], bconst], [bct[2]])
        rop('dve', lambda e, d=d: e.scalar_tensor_tensor(out=T_[3], in0=T_[0], scalar=gb[:, d:d + 1], in1=T_[2],
                                                          op0=ALU.add, op1=ALU.subtract), [bct[0], bgb, bct[2]], [bct[3]])
        rop('dve', lambda e: e.tensor_tensor_scan(out=T_[4], data0=zeroct, data1=T_[3], initial=-1e30, op0=ALU.add, op1=ALU.max),
            [bct[3], bconst], [bct[4]])
        rop('dve', lambda e: e.tensor_copy(out=cc[0], in_=T_[2][:, 63:64]), [bct[2]], [bcc[0]])
        rop('dve', lambda e: e.tensor_copy(out=cc[1], in_=T_[4][:, 63:64]), [bct[4]], [bcc[1]])
        rop('dve', lambda e: e.tensor_tensor(out=cc[2], in0=cc[0], in1=cc[1], op=ALU.add), [bcc[0], bcc[1]], [bcc[2]])
        CR = crow
        tr_col(cc[0], bcc[0], CR[0], bcrow[0], NCH, 1)
        tr_col(cc[2], bcc[2], CR[2], bcrow[2], NCH, 1)
        rop('dve', lambda e: e.tensor_tensor_scan(out=CR[3], data0=CR[0], data1=CR[2], initial=0.0, op0=ALU.add, op1=ALU.max),
            [bcrow[0], bcrow[2]], [bcrow[3]])
        rop('dve', lambda e: e.memset(CR[4][:, 0:1], 0.0), [], [bcrow[4]])
        rop('dve', lambda e: e.tensor_copy(out=CR[4][:, 1:NCH], in_=CR[3][:, 0:NCH - 1]), [bcrow[3]], [bcrow[4]])
        rop('dve', lambda e: e.tensor_tensor(out=CR[5], in0=CR[0], in1=CR[4], op=ALU.add), [bcrow[0], bcrow[4]], [bcrow[5]])
        rop('dve', lambda e: e.tensor_tensor(out=CR[5], in0=CR[5], in1=CR[3], op=ALU.subtract), [bcrow[5], bcrow[3]], [bcrow[5]])
        rop('act', lambda e: e.activation(out=CR[5], in_=CR[5], func=AF.Exp), [bcrow[5]], [bcrow[5]])
        rop('dve', lambda e: e.tensor_tensor(out=CR[6], in0=CR[2], in1=CR[3], op=ALU.subtract), [bcrow[2], bcrow[3]], [bcrow[6]])
        rop('act', lambda e: e.activation(out=CR[6], in_=CR[6], func=AF.Exp), [bcrow[6]], [bcrow[6]])
        pt, pb = k.nextps()
        P.op('pe', lambda e, pt=pt: e.matmul(pt[0:NCH, 0:1], CR[4][0:1, :], one1[0:1, 0:1], start=True, stop=True),
             reads=[bcrow[4], bone1], writes=[pb])
        P.op('dve', lambda e, pt=pt: e.tensor_copy(out=cc[3], in_=pt[0:NCH, 0:1]), reads=[pb], writes=[bcc[3]])
        rop('dve', lambda e: e.tensor_scalar(out=T_[5], in0=T_[4], scalar1=cc[3][:, 0:1], scalar2=None, op0=ALU.max), [bct[4], bcc[3]], [bct[5]])
        rop('act', lambda e: e.activation(out=T_[6], in_=T_[3], func=AF.Exp), [bct[3]], [bct[6]])
        rop('act', lambda e: e.activation(out=T_[7], in_=T_[5], func=AF.Exp, scale=-1.0), [bct[5]], [bct[7]])
        rop('dve', lambda e: e.tensor_scalar(out=T_[8], in0=T_[5], scalar1=cc[3][:, 0:1], scalar2=None, op0=ALU.subtract), [bct[5], bcc[3]], [bct[8]])
        rop('act', lambda e: e.activation(out=T_[8], in_=T_[8], func=AF.Exp, scale=-1.0), [bct[8]], [bct[8]])
        rop('dve', lambda e: e.tensor_tensor(out=T_[9], in0=T_[2], in1=T_[5], op=ALU.add), [bct[2], bct[5]], [bct[9]])
        rop('act', lambda e: e.activation(out=T_[9], in_=T_[9], func=AF.Exp, scale=-1.0), [bct[9]], [bct[9]])
        rop('dve', lambda e: e.tensor_scalar(out=T_[10], in0=T_[3], scalar1=cc[1][:, 0:1], scalar2=None, op0=ALU.subtract), [bct[3], bcc[1]], [bct[10]])
        rop('act', lambda e: e.activation(out=T_[10], in_=T_[10], func=AF.Exp), [bct[10]], [bct[10]])
        for qi, ti in enumerate([6, 7, 8, 9, 10]):
            tr_col(T_[ti], bct[ti], cols[:, qi, :], bcols, NCH, 64)
        for qi, ci in enumerate([5, 6]):
            pt, pb = k.nextps()
            P.op('pe', lambda e, pt=pt, ci=ci: e.matmul(pt[0:64, 0:NCH], one1[0:1, 0:64], CR[ci][0:1, :], start=True, stop=True),
                 reads=[bcrow[ci], bone1], writes=[pb])
            P.op('dve', lambda e, pt=pt, qi=qi: e.tensor_copy(out=abc[:, qi, :], in_=pt[0:64, 0:NCH]), reads=[pb], writes=[babc])
        def pre(ch):
            s = ch % 2
            sl = slice(ch * 64, (ch + 1) * 64)
            pS, pSb = k.nextps()
            P.op('pe', lambda e: e.matmul(pS[0:64, 0:64], kT[:, sl], qT[:, sl], start=True, stop=True), reads=[bkT, bqT], writes=[pSb])
            P.op('dve', lambda e: e.scalar_tensor_tensor(out=stm[s], in0=pS[0:64, 0:64], scalar=cols[:, 0, ch:ch + 1], in1=tri,
                                                         op0=ALU.mult, op1=ALU.mult), reads=[pSb, bcols, btri], writes=[bstm[s]])
            P.op('pool', lambda e: e.tensor_scalar(out=kw[s], in0=kt[:, ch, :], scalar1=cols[:, 4, ch:ch + 1], scalar2=None, op0=ALU.mult),
                 reads=[bkt, bcols], writes=[bkw[s]])
            yield
            pA, pAb = k.nextps()
            P.op('pe', lambda e: e.matmul(pA[0:64, 0:129], stm[s], V1[:, ch, :], start=True, stop=True), reads=[bstm[s], bV], writes=[pAb])
            pC, pCb = k.nextps()
            P.op('pe', lambda e: e.matmul(pC[0:64, 0:129], kw[s], V1[:, ch, :], start=True, stop=True), reads=[bkw[s], bV], writes=[pCb])
            yield
            P.op('dve', lambda e: e.tensor_scalar(out=nd[s], in0=pA[0:64, 0:129], scalar1=cols[:, 1, ch:ch + 1], scalar2=None, op0=ALU.mult),
                 reads=[pAb, bcols], writes=[bnd[s]])
            P.op('act', lambda e: e.activation(out=clsb[s], in_=pC[0:64, 0:129], func=AF.Copy), reads=[pCb], writes=[bclsb[s]])
            yield

        def post(ch, d=d):
            s = ch % 2
            sl = slice(ch * 64, (ch + 1) * 64)
            pB, pBb = k.nextps()
            P.op('pe', lambda e: e.matmul(pB[0:64, 0:129], qT[:, sl], Cst, start=True, stop=True), reads=[bqT, bC], writes=[pBb])
            yield
            P.op('dve', lambda e: e.tensor_scalar(out=Cst, in0=Cst, scalar1=abc[:, 0, ch:ch + 1], scalar2=None, op0=ALU.mult),
                 reads=[bC, babc], writes=[bC])
            P.op('dve', lambda e: e.scalar_tensor_tensor(out=Cst, in0=clsb[s], scalar=abc[:, 1, ch:ch + 1], in1=Cst, op0=ALU.mult, op1=ALU.add),
                 reads=[bclsb[s], babc, bC], writes=[bC])
            yield
            P.op('dve', lambda e: e.scalar_tensor_tensor(out=nd[s], in0=pB[0:64, 0:129], scalar=cols[:, 2, ch:ch + 1], in1=nd[s],
                                                         op0=ALU.mult, op1=ALU.add), reads=[pBb, bcols, bnd[s]], writes=[bnd[s]])
            P.op('dve', lambda e: e.scalar_tensor_tensor(out=rdn[s], in0=nd[s][:, 128:129], scalar=-1.0, in1=nd[s][:, 128:129],
                                                         op0=ALU.mult, op1=ALU.max), reads=[bnd[s]], writes=[brdn[s]])
            yield
            P.op('dve', lambda e: e.tensor_scalar(out=rdn[s], in0=rdn[s], scalar1=cols[:, 3, ch:ch + 1], scalar2=None, op0=ALU.max),
                 reads=[brdn[s], bcols], writes=[brdn[s]])
            P.op('dve', lambda e: e.reciprocal(out=rdn[s], in_=rdn[s]), reads=[brdn[s]], writes=[brdn[s]])
            P.op('dve', lambda e: e.tensor_scalar(out=hh[d][:, ch, :], in0=nd[s][:, 0:128], scalar1=rdn[s][:, 0:1], scalar2=None, op0=ALU.mult),
                 reads=[bnd[s], brdn[s]], writes=[bhh[d]])
            yield

        for _ in pre(0):
            pass
        for ch in range(NCH):
            interleave([post(ch)] + ([pre(ch + 1)] if ch + 1 < NCH else []))
    for cn in range(NCH):
        cp = cn_of(cn)
        pt, pb = k.nextps()
        P.op('pe', lambda e, pt=pt, cp=cp: e.matmul(pt[0:64, 0:128], c.J64, hh[1][:, cp, :], start=True, stop=True),
             reads=[bI, bhh[1]], writes=[pb])
        P.op('dve', lambda e, pt=pt, cn=cn: e.tensor_tensor(out=hh[0][:, cn, :], in0=pt[0:64, 0:128], in1=hh[0][:, cn, :], op=ALU.add),
             reads=[pb, bhh[0]], writes=[bhh[0]])
    og = V1
    gather_fm(k, c, G, tmpA, bA, 'ml_o', lat_off=256, ctx_off=0)
    for ch in range(NCH):
        mm_evac(k, 64, 128, tmpA[:, ch * 64:(ch + 1) * 64], c.I, [bA, bI], og[:, ch, 0:128], bV, eng_i=ch)
    Y = hh[0]
    T = hh[1]
    P.op('pool', lambda e: e.tensor_tensor(out=T, in0=Y, in1=Y, op=ALU.mult), reads=[bhh[0], bhh[1]], writes=[bhh[1]])
    P.op('dve', lambda e: e.tensor_reduce(out=ssq, in_=T, axis=AX.X, op=ALU.add), reads=[bhh[1]], writes=[bssq])
    P.op('act', lambda e: e.activation(out=ssq, in_=ssq, func=AF.Sqrt, bias=c.eps[0:64, :], scale=1.0 / 128), reads=[bssq, bI], writes=[bssq])
    P.op('dve', lambda e: e.reciprocal(out=ssq, in_=ssq), reads=[bssq], writes=[bssq])
    P.op('dve', lambda e: e.tensor_tensor(out=Y, in0=Y, in1=bc_inner(ssq, 128), op=ALU.mult), reads=[bhh[0], bssq], writes=[bhh[0]])
    P.op('pool', lambda e: e.tensor_tensor(out=Y, in0=Y, in1=bc_mid(gn, NCH), op=ALU.mult), reads=[bhh[0], bgn], writes=[bhh[0]])
    P.op('act', lambda e: e.activation(out=og[:, :, 0:128], in_=og[:, :, 0:128], func=AF.Sigmoid), reads=[bV], writes=[bV])
    P.op('dve', lambda e: e.tensor_tensor(out=Y, in0=Y, in1=og[:, :, 0:128], op=ALU.mult), reads=[bhh[0], bV], writes=[bhh[0]])
    for ch in range(NCH):
        mm_evac(k, 128, 64, Y[:, ch, :], c.I[0:64, 0:64], [bhh[0], bI], tmpA[:, ch * 64:(ch + 1) * 64], bA, eng_i=ch)
    store_y_fm(k, tmpA, bA, ybuf, bybuf, 2, 256, 0)


def dn_params(k):
    return {'cw': k.dram("dn_cw", [DEPTH, 2, 128, 3, 5]), 'sc': k.dram("dn_sc", [DEPTH, NCH, 4]),
            'gn': k.dram("dn_gn", [DEPTH, 64, 128]), 'mk': k.dram("dn_masks", [64, 2, 64])}


def emit_dn(k, cm, l, G, ybuf, bybuf, pr):
    P = k.P
    L = NT
    SEGS = [(0, 256), (256, 4096)]
    big1 = k.sb([128, 2 * L]); big2 = k.sb([128, 2 * L])
    X = [big2[:, 0:L], big2[:, L:2 * L], k.sb([128, L])]
    acc = k.sb([128, L])
    qd = big1[:, 0:L]; kd = big1[:, L:2 * L]
    DmT = k.sb([64, NCH, 64]); NB_ = k.sb([64, NCH, 64])
    O = k.sb([64, NCH, 128])
    cw = k.sb([128, 3, 5]); sc = k.sb([NCH, 4]); mk = k.sb([64, 2, 64])
    I128 = cm.I; J64 = cm.J64; ones = cm.ones; epst = cm.eps
    one1 = k.sb([1, 128]); onect = k.sb([NCH, 64])
    ct = [k.sb([NCH, 64]) for _ in range(6)]
    cc = [k.sb([NCH, 1]) for _ in range(3)]
    crow = k.sb([1, NCH])
    cols = k.sb([64, 4, NCH])
    eglb = k.sb([128, NCH])
    S = k.sb([128, 128])
    scr = [k.sb([128, 512]) for _ in range(4)]
    ttok = [k.sb([128, 128]) for _ in range(2)]
    ftmp = [k.sb([64, NCH]) for _ in range(2)]
    Qb = [[k.sb([64, 64]) for _ in range(2)] for _ in range(3)]
    QTb = [[k.sb([64, 64]) for _ in range(2)] for _ in range(3)]
    R = [k.sb([64, 64]) for _ in range(3)]
    QKD = [k.sb([64, 64]) for _ in range(3)]
    vtok = [k.sb([64, 128]) for _ in range(3)]
    kend = [k.sb([64, 128]) for _ in range(3)]
    z = [k.sb([64, 128]) for _ in range(3)]
    vnew = [k.sb([64, 128]) for _ in range(3)]
    o1c = [k.sb([64, 128]) for _ in range(3)]
    gn = k.sb([64, 128]); ssq = k.sb([64, NCH])

    bX = [Buf('xq'), Buf('xk'), Buf('xv')]
    (bacc, bqd, bkd, bDm, bNB, bO, bcw, bsc, bmk, bone1, bconst, bcrow, bcols, beglb, bS, bgn, bssq) = (
        Buf(n) for n in 'acc qd kd Dm NB O cw sc mk one1 const crow cols eglb S gn ssq'.split())
    bI = cm.bconst; bJ = cm.bconst; bones = cm.bconst; beps = cm.bconst
    bct = [Buf('ct%d' % i) for i in range(6)]
    bcc = [Buf('cc%d' % i) for i in range(3)]
    bscr = [Buf('scr%d' % i) for i in range(4)]
    bttok = [Buf('tt0'), Buf('tt1')]; bftmp = [Buf('ft0'), Buf('ft1')]
    bQ = [[Buf('Q%d%d' % (a, b)) for b in range(2)] for a in range(3)]
    bQT = [[Buf('QT%d%d' % (a, b)) for b in range(2)] for a in range(3)]
    bR, bQKD, bvtok, bkend, bz, bvnew, bo1c = ([Buf('%s%d' % (n_, i)) for i in range(3)] for n_ in ('R', 'QKD', 'vt', 'ke', 'z', 'vn', 'o1c'))

    P.dma('sp', sc, pr['sc'][l], writes=[bsc])
    P.dma('sp', mk, pr['mk'], writes=[bmk])
    P.dma('sp', gn, pr['gn'][l], writes=[bgn])
    P.op('dve', lambda e: e.memset(one1, 1.0), writes=[bone1])
    P.op('dve', lambda e: e.memset(onect, 1.0), writes=[bconst])

    def tr_col(src_ap, bsrc, dst_ap, bdst, m, n):
        pt, pb = k.nextps()
        P.op('pe', lambda e: e.matmul(pt[0:n, 0:m], src_ap, I128[0:m, 0:m], start=True, stop=True), reads=[bsrc, bI], writes=[pb])
        P.op('dve', lambda e: e.tensor_copy(out=dst_ap, in_=pt[0:n, 0:m]), reads=[pb], writes=[bdst])

    tiles = [(i * 512, 512) for i in range(8)] + [(4096, 256)]

    for d in range(2):
        P.dma('sp', cw, pr['cw'][l][d], writes=[bcw])
        if d == 0:
            for t, nm in enumerate(('dn_q', 'dn_k', 'dn_v')):
                gather_fm(k, cm, G, X[t], bX[t], nm, lat_off=256, ctx_off=0)
            gather_ct(k, cm, G, ct[0], bct[0], 'dn_b0')
            gather_ct(k, cm, G, ct[1], bct[1], 'dn_a0')
        else:
            for t, nm in enumerate(('dn_q', 'dn_k', 'dn_v')):
                gather_fm(k, cm, G, acc, bacc, nm, lat_off=256, ctx_off=0)
                for blk in range(NB):
                    bp = (1 - blk) if blk < 2 else (35 - blk)
                    s_ = blk % 2
                    mm_evac(k, 128, 128, acc[:, blk * 128:(blk + 1) * 128], I128, [bacc, bI], ttok[s_], bttok[s_], eng_i=blk)
                    mm_evac(k, 128, 128, ttok[s_], cm.J, [bttok[s_], bI], X[t][:, bp * 128:(bp + 1) * 128], bX[t], eng_i=blk + 1)
            gather_ct(k, cm, G, ct[5], bct[5], 'dn_b1')
            flip_ct(k, cm, ct[5], bct[5], ct[0], bct[0], ftmp, bftmp)
            gather_ct(k, cm, G, ct[5], bct[5], 'dn_a1')
            flip_ct(k, cm, ct[5], bct[5], ct[1], bct[1], ftmp, bftmp)
        P.op('pool', lambda e: e.memset(S[:], 0.0), writes=[bS])
        for t in range(3):
            for (s0, sn) in SEGS:
                P.op('dve', lambda e, t=t, s0=s0, sn=sn: e.tensor_scalar(
                    out=acc[:, s0:s0 + sn], in0=X[t][:, s0:s0 + sn], scalar1=cw[:, t, 2:3], scalar2=None, op0=ALU.mult),
                    reads=[bX[t], bcw], writes=[bacc])
                for tap in (0, 1, 3, 4):
                    sh = tap - 2
                    a0 = s0 + max(0, -sh)
                    a1 = s0 + sn - max(0, sh)
                    P.op('dve', lambda e, t=t, tap=tap, sh=sh, a0=a0, a1=a1: e.scalar_tensor_tensor(
                        out=acc[:, a0:a1], in0=X[t][:, a0 + sh:a1 + sh], scalar=cw[:, t, tap:tap + 1], in1=acc[:, a0:a1],
                        op0=ALU.mult, op1=ALU.add), reads=[bX[t], bcw, bacc], writes=[bacc])
            P.op('act', lambda e, t=t: e.activation(out=X[t], in_=acc[:], func=AF.Silu), reads=[bacc], writes=[bX[t]])
        for t in range(2):
            for i, (t0, tn) in enumerate(tiles):
                s0, s1 = scr[(2 * i) % 4], scr[(2 * i + 1) % 4]
                b0, b1 = bscr[(2 * i) % 4], bscr[(2 * i + 1) % 4]
                P.op('act', lambda e, t=t, s0=s0, t0=t0, tn=tn: e.activation(out=s0[:, 0:tn], in_=X[t][:, t0:t0 + tn], func=AF.Square),
                     reads=[bX[t]], writes=[b0])
                pt, pb = k.nextps()
                P.op('pe', lambda e, pt=pt, s0=s0, tn=tn: e.matmul(pt[:, 0:tn], ones[:], s0[:, 0:tn], start=True, stop=True),
                     reads=[bones, b0], writes=[pb])
                P.op('act', lambda e, pt=pt, s1=s1, tn=tn: e.activation(out=s1[:, 0:tn], in_=pt[:, 0:tn], func=AF.Sqrt, bias=epst[:], scale=1.0),
                     reads=[pb, beps], writes=[b1])
                P.op('dve', lambda e, s1=s1, tn=tn: e.reciprocal(out=s1[:, 0:tn], in_=s1[:, 0:tn]), reads=[b1], writes=[b1])
                sc_ = (128 ** -0.5) if t == 0 else 1.0
                P.op('dve', lambda e, t=t, s1=s1, t0=t0, tn=tn, sc_=sc_: e.scalar_tensor_tensor(
                    out=X[t][:, t0:t0 + tn], in0=X[t][:, t0:t0 + tn], scalar=sc_, in1=s1[:, 0:tn], op0=ALU.mult, op1=ALU.mult),
                    reads=[bX[t], b1], writes=[bX[t]])
        P.op('act', lambda e: e.activation(out=ct[0][:], in_=ct[0][:], func=AF.Sigmoid), reads=[bct[0]], writes=[bct[0]])
        P.op('act', lambda e, d=d: e.activation(out=cc[0][:], in_=sc[:, d:d + 1], func=AF.Exp), reads=[bsc], writes=[bcc[0]])
        P.op('dve', lambda e: e.tensor_scalar(out=cc[0][:], in0=cc[0][:], scalar1=-1.0, scalar2=None, op0=ALU.mult), reads=[bcc[0]], writes=[bcc[0]])
        P.op('act', lambda e, d=d: e.activation(out=ct[1][:], in_=ct[1][:], func=AF.Exp, bias=sc[:, 2 + d:3 + d]), reads=[bct[1], bsc], writes=[bct[1]])
        P.op('act', lambda e: e.activation(out=ct[1][:], in_=ct[1][:], func=AF.Ln, bias=onect[:, 0:1]), reads=[bct[1], bconst], writes=[bct[1]])
        P.op('dve', lambda e: e.tensor_scalar(out=ct[1][:], in0=ct[1][:], scalar1=cc[0][:, 0:1], scalar2=None, op0=ALU.mult),
             reads=[bct[1], bcc[0]], writes=[bct[1]])
        P.op('dve', lambda e: e.tensor_tensor_scan(out=ct[2][:], data0=onect[:], data1=ct[1][:], initial=0.0, op0=ALU.mult, op1=ALU.add),
             reads=[bct[1], bconst], writes=[bct[2]])
        P.op('dve', lambda e: e.tensor_copy(out=cc[1][:], in_=ct[2][:, 63:64]), reads=[bct[2]], writes=[bcc[1]])
        P.op('dve', lambda e: e.tensor_scalar(out=ct[3][:], in0=ct[2][:], scalar1=cc[1][:, 0:1], scalar2=None, op0=ALU.subtract),
             reads=[bct[2], bcc[1]], writes=[bct[3]])
        P.op('act', lambda e: e.activation(out=ct[3][:], in_=ct[3][:], func=AF.Exp, scale=-1.0), reads=[bct[3]], writes=[bct[3]])
        P.op('dve', lambda e: e.tensor_scalar(out=ct[4][:], in0=ct[0][:], scalar1=-1.0, scalar2=None, op0=ALU.mult), reads=[bct[0]], writes=[bct[4]])
        for qi, ti in enumerate([2, 0, 4, 3]):
            tr_col(ct[ti][:], bct[ti], cols[:, qi, :], bcols, NCH, 64)
        tr_col(cc[1][:], bcc[1], crow[:], bcrow, NCH, 1)
        P.op('act', lambda e: e.activation(out=crow[:], in_=crow[:], func=AF.Exp), reads=[bcrow], writes=[bcrow])
        pt, pb = k.nextps()
        P.op('pe', lambda e, pt=pt: e.matmul(pt[:, 0:NCH], one1[0:1, :], crow[0:1, :], start=True, stop=True), reads=[bone1, bcrow], writes=[pb])
        P.op('dve', lambda e, pt=pt: e.tensor_copy(out=eglb[:], in_=pt[:, 0:NCH]), reads=[pb], writes=[beglb])
        P.dma('sp', acc[0:1, :].rearrange("o (c i) -> o c i", i=64), ct[2], reads=[bct[2]], writes=[bacc])
        for i, (t0, tn) in enumerate(tiles):
            nck = tn // 64
            c0 = t0 // 64
            s0, s1 = scr[(2 * i) % 4], scr[(2 * i + 1) % 4]
            b0, b1 = bscr[(2 * i) % 4], bscr[(2 * i + 1) % 4]
            pt, pb = k.nextps()
            P.op('pe', lambda e, pt=pt, t0=t0, tn=tn: e.matmul(pt[:, 0:tn], one1[0:1, :], acc[0:1, t0:t0 + tn], start=True, stop=True),
                 reads=[bone1, bacc], writes=[pb])
            P.op('act', lambda e, pt=pt, s0=s0, tn=tn: e.activation(out=s0[:, 0:tn], in_=pt[:, 0:tn], func=AF.Exp), reads=[pb], writes=[b0])
            P.op('dve', lambda e, s0=s0, t0=t0, tn=tn: e.tensor_tensor(out=qd[:, t0:t0 + tn], in0=X[0][:, t0:t0 + tn], in1=s0[:, 0:tn], op=ALU.mult),
                 reads=[bX[0], b0], writes=[bqd])
            P.op('pool', lambda e, s0=s0, t0=t0, tn=tn: e.tensor_tensor(out=kd[:, t0:t0 + tn], in0=X[1][:, t0:t0 + tn], in1=s0[:, 0:tn], op=ALU.mult),
                 reads=[bX[1], b0], writes=[bkd])
            d3 = s1[0:64, 0:tn].rearrange("p (c i) -> p c i", i=64)
            p3 = pt[0:64, 0:tn].rearrange("p (c i) -> p c i", i=64)
            P.op('dve', lambda e, d3=d3, p3=p3, c0=c0, nck=nck: e.tensor_tensor(out=d3, in0=p3, in1=bc_inner(cols[:, 0, c0:c0 + nck], 64), op=ALU.subtract),
                 reads=[pb, bcols], writes=[b1])
            P.op('dve', lambda e, d3=d3, nck=nck: e.tensor_tensor(out=d3, in0=d3, in1=bc_mid(mk[:, 0, :], nck), op=ALU.add),
                 reads=[b1, bmk], writes=[b1])
            P.op('act', lambda e, d3=d3, c0=c0, nck=nck: e.activation(out=DmT[:, c0:c0 + nck, :], in_=d3, func=AF.Exp), reads=[b1], writes=[bDm])
            P.op('dve', lambda e, c0=c0, nck=nck: e.tensor_tensor(out=NB_[:, c0:c0 + nck, :], in0=DmT[:, c0:c0 + nck, :], in1=bc_mid(mk[:, 1, :], nck), op=ALU.mult),
                 reads=[bDm, bmk], writes=[bNB])
            P.op('dve', lambda e, c0=c0, nck=nck: e.tensor_tensor(out=NB_[:, c0:c0 + nck, :], in0=NB_[:, c0:c0 + nck, :], in1=bc_inner(cols[:, 2, c0:c0 + nck], 64), op=ALU.mult),
                 reads=[bNB, bcols], writes=[bNB])

        def prep(c):
            a = c % 3
            sl = slice(c * 64, (c + 1) * 64)
            Q, QT = Qb[a], QTb[a]
            bq, bqt = bQ[a], bQT[a]
            pt, pb = k.nextps()
            P.op('pe', lambda e: e.matmul(pt[0:64, 0:64], X[1][:, sl], X[1][:, sl], start=True, stop=True), reads=[bX[1]], writes=[pb])
            P.op('dve', lambda e: e.tensor_tensor(out=Q[0][:], in0=pt[0:64, 0:64], in1=NB_[:, c, :], op=ALU.mult), reads=[pb, bNB], writes=[bq[0]])
            yield
            pt2, pb2 = k.nextps()
            P.op('pe', lambda e: e.matmul(pt2[0:64, 0:64], Q[0][:], I128[0:64, 0:64], start=True, stop=True), reads=[bq[0], bI], writes=[pb2])
            P.op('act', lambda e: e.activation(out=QT[0][:], in_=pt2[0:64, 0:64], func=AF.Copy), reads=[pb2], writes=[bqt[0]])
            P.op('pool', lambda e: e.tensor_tensor(out=R[a][:], in0=Q[0][:], in1=I128[0:64, 0:64], op=ALU.add), reads=[bq[0], bI], writes=[bR[a]])
            yield
            cur = 0
            for it in range(1, 6):
                nx = 1 - cur
                pq, pqb = k.nextps()
                P.op('pe', lambda e, pq=pq, cur=cur: e.matmul(pq[0:64, 0:64], Q[cur][:], QT[cur][:], start=True, stop=True),
                     reads=[bq[cur], bqt[cur]], writes=[pqb])
                if it < 5:
                    pq2, pq2b = k.nextps()
                    P.op('pe', lambda e, pq2=pq2, cur=cur: e.matmul(pq2[0:64, 0:64], QT[cur][:], Q[cur][:], start=True, stop=True),
                         reads=[bq[cur], bqt[cur]], writes=[pq2b])
                P.op('act', lambda e, pq=pq, nx=nx: e.activation(out=QT[nx][:], in_=pq[0:64, 0:64], func=AF.Copy), reads=[pqb], writes=[bqt[nx]])
                if it < 5:
                    P.op('dve', lambda e, pq2=pq2, nx=nx: e.tensor_copy(out=Q[nx][:], in_=pq2[0:64, 0:64]), reads=[pq2b], writes=[bq[nx]])
                yield
                pr, prb = k.nextps()
                P.op('pe', lambda e, pr=pr, nx=nx: e.matmul(pr[0:64, 0:64], QT[nx][:], R[a][:], start=True, stop=True),
                     reads=[bqt[nx], bR[a]], writes=[prb])
                P.op('dve', lambda e, pr=pr: e.tensor_tensor(out=R[a][:], in0=pr[0:64, 0:64], in1=R[a][:], op=ALU.add), reads=[prb, bR[a]], writes=[bR[a]])
                yield
                cur = nx
            p1, p1b = k.nextps()
            P.op('pe', lambda e: e.matmul(p1[0:64, 0:64], X[1][:, sl], X[0][:, sl], start=True, stop=True), reads=[bX[0], bX[1]], writes=[p1b])
            P.op('dve', lambda e: e.tensor_tensor(out=QKD[a][:], in0=p1[0:64, 0:64], in1=DmT[:, c, :], op=ALU.mult), reads=[p1b, bDm], writes=[bQKD[a]])
            yield
            p2, p2b = k.nextps()
            P.op('pe', lambda e: e.matmul(p2[0:64, 0:128], X[2][:, sl], I128[:], start=True, stop=True), reads=[bX[2], bI], writes=[p2b])
            P.op('act', lambda e: e.activation(out=vtok[a][:], in_=p2[0:64, 0:128], func=AF.Copy), reads=[p2b], writes=[bvtok[a]])
            p3_, p3b = k.nextps()
            P.op('pe', lambda e: e.matmul(p3_[0:64, 0:128], X[1][:, sl], I128[:], start=True, stop=True), reads=[bX[1], bI], writes=[p3b])
            P.op('dve', lambda e: e.tensor_scalar(out=kend[a][:], in0=p3_[0:64, 0:128], scalar1=cols[:, 3, c:c + 1], scalar2=None, op0=ALU.mult),
                 reads=[p3b, bcols], writes=[bkend[a]])
            yield

        def seq(c):
            a = c % 3
            sl = slice(c * 64, (c + 1) * 64)
            pk, pkb = k.nextps()
            P.op('pe', lambda e: e.matmul(pk[0:64, 0:128], kd[:, sl], S[:], start=True, stop=True), reads=[bkd, bS], writes=[pkb])
            po, pob = k.pst[6 + c % 2], k.psb[6 + c % 2]
            P.op('pe', lambda e: e.matmul(po[0:64, 0:128], qd[:, sl], S[:], start=True, stop=False), reads=[bqd, bS], writes=[pob])
            yield
            P.op('dve', lambda e: e.tensor_tensor(out=z[a][:], in0=vtok[a][:], in1=pk[0:64, 0:128], op=ALU.subtract),
                 reads=[bvtok[a], pkb], writes=[bz[a]])
            yield
            ptz, ptzb = k.nextps()
            P.op('pe', lambda e: e.matmul(ptz[0:64, 0:128], R[a][:], z[a][:], start=True, stop=True), reads=[bR[a], bz[a]], writes=[ptzb])
            yield
            P.op('dve', lambda e: e.tensor_scalar(out=vnew[a][:], in0=ptz[0:64, 0:128], scalar1=cols[:, 1, c:c + 1], scalar2=None, op0=ALU.mult),
                 reads=[ptzb, bcols], writes=[bvnew[a]])
            yield
            psu, psub = k.nextps()
            P.op('pe', lambda e: e.matmul(psu[:, 0:128], kend[a][:], vnew[a][:], start=True, stop=True), reads=[bkend[a], bvnew[a]], writes=[psub])
            P.op('pe', lambda e: e.matmul(po[0:64, 0:128], QKD[a][:], vnew[a][:], start=False, stop=True), reads=[bQKD[a], bvnew[a]], writes=[pob],
                 pe_acc=True)
            yield
            P.op('dve', lambda e: e.scalar_tensor_tensor(out=S[:], in0=S[:], scalar=eglb[:, c:c + 1], in1=psu[:, 0:128], op0=ALU.mult, op1=ALU.add),
                 reads=[bS, beglb, psub], writes=[bS])
            if d == 0:
                P.op('act', lambda e: e.activation(out=O[:, c, :], in_=po[0:64, 0:128], func=AF.Copy), reads=[pob], writes=[bO])
            else:
                cn = (3 - c) if c < 4 else (4 + 63 - (c - 4))
                P.op('act', lambda e: e.activation(out=o1c[a][:], in_=po[0:64, 0:128], func=AF.Copy), reads=[pob], writes=[bo1c[a]])
                pj, pjb = k.nextps()
                P.op('pe', lambda e: e.matmul(pj[0:64, 0:128], J64, o1c[a][:], start=True, stop=True), reads=[bJ, bo1c[a]], writes=[pjb])
                P.op('dve', lambda e: e.tensor_tensor(out=O[:, cn, :], in0=pj[0:64, 0:128], in1=O[:, cn, :], op=ALU.add),
                     reads=[pjb, bO], writes=[bO])
            yield

        k.nring = 6
        for _ in prep(0):
            pass
        preps = {}
        if NCH > 1:
            preps[1] = prep(1)
        for c in range(NCH):
            if c + 2 < NCH:
                preps[c + 2] = prep(c + 2)
            sg = seq(c)
            live = [sg] + [preps[i] for i in (c + 1, c + 2) if i in preps]
            must = [sg] + ([preps[c + 1]] if (c + 1) in preps else [])
            while must:
                for g in list(live):
                    try:
                        next(g)
                    except StopIteration:
                        live.remove(g)
                        if g in must:
                            must.remove(g)
            preps.pop(c + 1, None)


    k.nring = 8
    gate = big1[0:64, :].rearrange("p (c d) -> p c d", d=128)
    T = big2[0:64, :].rearrange("p (c d) -> p c d", d=128)
    gather_fm(k, cm, G, acc, bacc, 'dn_g', lat_off=256, ctx_off=0)
    for ch in range(NCH):
        pt, pb = k.nextps()
        P.op('pe', lambda e, pt=pt, ch=ch: e.matmul(pt[0:64, 0:128], acc[:, ch * 64:(ch + 1) * 64], I128, start=True, stop=True),
             reads=[bacc, bI], writes=[pb])
        P.op('act', lambda e, pt=pt, ch=ch: e.activation(out=gate[:, ch, :], in_=pt[0:64, 0:128], func=AF.Silu), reads=[pb], writes=[bqd, bkd])
    P.op('pool', lambda e: e.tensor_tensor(out=T, in0=O, in1=O, op=ALU.mult), reads=[bO], writes=[bX[0], bX[1]])
    P.op('dve', lambda e: e.tensor_reduce(out=ssq, in_=T, axis=AX.X, op=ALU.add), reads=[bX[0], bX[1]], writes=[bssq])
    P.op('act', lambda e: e.activation(out=ssq, in_=ssq, func=AF.Sqrt, bias=epst[0:64, :], scale=1.0 / 128), reads=[bssq, beps], writes=[bssq])
    P.op('dve', lambda e: e.reciprocal(out=ssq, in_=ssq), reads=[bssq], writes=[bssq])
    P.op('dve', lambda e: e.tensor_tensor(out=O, in0=O, in1=bc_inner(ssq, 128), op=ALU.mult), reads=[bO, bssq], writes=[bO])
    P.op('pool', lambda e: e.tensor_tensor(out=O, in0=O, in1=bc_mid(gn, NCH), op=ALU.mult), reads=[bO, bgn], writes=[bO])
    P.op('dve', lambda e: e.tensor_tensor(out=O, in0=O, in1=gate, op=ALU.mult), reads=[bO, bqd, bkd], writes=[bO])
    for ch in range(NCH):
        mm_evac(k, 128, 64, O[:, ch, :], I128[0:64, 0:64], [bO, bI], acc[:, ch * 64:(ch + 1) * 64], bacc, eng_i=ch)
    store_y_fm(k, acc, bacc, ybuf, bybuf, 1, 256, 0)


def emit_mod(k, c):
    P = k.P
    c3_d = k.dram("c3", [128, 16, 3])
    w_d = k.dram("w_mod", [D, 12288])
    b_d = k.dram("b_mod", [128, 96])
    sel_d = k.dram("sel", [128, 2])
    gn_d = k.dram("gnorm", [128, 2 * DEPTH, 16])
    modbuf = k.dram("modbuf", [128, 288], kind="Internal")
    G_mod = k.dram("G_mod", [4 * 128, 288], kind="Internal")
    c.Mlat = k.sb([128, 4 * 96]); c.Mctx = k.sb([128, 4 * 96]); c.gnorm = k.sb([128, 2 * DEPTH, 16])
    c.bM = Buf('M')
    k.persist()
    sc = k.sb([128, 16, 3]); bt = k.sb([128, 96]); sel = k.sb([128, 2]); mt = k.sb([128, 96, 3])
    wt = [k.sb([128, 16, 512]) for _ in range(2)]
    M = k.sb([128, 4, 288])
    bsc, bbt, bsel, bmt, bMM, bmb, bGm = (Buf(n) for n in 'sc bt sel mt MM modbuf Gmod'.split())
    bw = [Buf('w0'), Buf('w1')]
    P.dma('sp', sc, c3_d, writes=[bsc])
    P.dma('sp', bt, b_d, writes=[bbt])
    P.dma('sp', sel, sel_d, writes=[bsel])
    P.dma('sp', c.gnorm, gn_d, writes=[c.bM])
    P.op('act', lambda e: e.activation(out=sc, in_=sc, func=AF.Silu), reads=[bsc], writes=[bsc])
    wv = w_d.rearrange("(kc p) n -> p kc n", p=128)
    for n in range(24):
        s = n % 2
        for hf in range(2):
            P.dma('sp' if hf == 0 else 'act', wt[s][:, hf * 8:(hf + 1) * 8, :], wv[:, hf * 8:(hf + 1) * 8, n * 512:(n + 1) * 512], writes=[bw[s]])
        for cb in range(4):
            cbl = n * 4 + cb
            pt, pb = k.nextps()
            for kc in range(16):
                P.op('pe', lambda e, pt=pt, s=s, kc=kc, cb=cb: e.matmul(pt[:, 0:3], wt[s][:, kc, cb * 128:(cb + 1) * 128], sc[:, kc, :],
                                                                      start=(kc == 0), stop=(kc == 15)),
                     reads=[bsc, bw[s]], writes=[pb], pe_acc=(kc > 0))
            P.op('dve', lambda e, pt=pt, cbl=cbl: e.tensor_scalar(out=mt[:, cbl, :], in0=pt[:, 0:3], scalar1=bt[:, cbl:cbl + 1], scalar2=None, op0=ALU.add),
                 reads=[pb, bbt], writes=[bmt])
    P.dma('sp', modbuf, mt.rearrange("p a b -> p (a b)"), reads=[bmt], writes=[bmb])
    P.op('pool', lambda e: e.collective_compute("AllGather", ALU.bypass, replica_groups=[[0, 1, 2, 3], [4, 5, 6, 7]], ins=[modbuf.opt()], outs=[G_mod.opt()]),
         reads=[bmb], writes=[bGm], dma=True, inc=1)
    P.dma('sp', M, G_mod.rearrange("(r p) x -> p r x", p=128), reads=[bGm], writes=[bMM])
    M3 = M.rearrange("p r (cb x) -> p (r cb) x", x=3)
    P.op('dve', lambda e: e.tensor_scalar(out=c.Mlat, in0=M3[:, :, 0], scalar1=sel[:, 0:1], scalar2=None, op0=ALU.mult), reads=[bMM, bsel], writes=[c.bM])
    P.op('dve', lambda e: e.scalar_tensor_tensor(out=c.Mlat, in0=M3[:, :, 1], scalar=sel[:, 1:2], in1=c.Mlat, op0=ALU.mult, op1=ALU.add),
         reads=[bMM, bsel, c.bM], writes=[c.bM])
    P.op('dve', lambda e: e.tensor_copy(out=c.Mctx, in_=M3[:, :, 2]), reads=[bMM], writes=[c.bM])


def dense_params(k):
    return {'w_in': k.dram("w_in", [DEPTH, D, IN_COLS]), 'w_out': k.dram("w_out", [DEPTH, D, D]),
            'w_fi': k.dram("w_fi", [DEPTH, D, 2 * FFN_H]), 'w_fo': k.dram("w_fo", [DEPTH, FFN_H, D])}


def emit_dense(k, c, l_cur, pr, xsrc, bxsrc, xdst, bxdst, Gy, pbufs):
    P = k.P
    first = l_cur is None
    l_next = 0 if first else l_cur + 1
    last = l_next >= DEPTH
    G_y, bGy = Gy
    p_lat, p_ctx, bp, bpc, G_lat, G_ctx, bG, bGc, groups = pbufs

    def ML(l, v):
        return c.Mlat[:, l * 96 + v * 16:l * 96 + (v + 1) * 16]

    def MC(l, v):
        return c.Mctx[:, l * 96 + v * 16:l * 96 + (v + 1) * 16]

    x = k.sb([128, 16, NTOK]); h = k.sb([128, 16, NTOK], BF16)
    wr = [k.sb([128, 12288], BF16) for _ in range(2)]
    act = [k.sb([128, 2, NTOK], BF16) for _ in range(2)]
    sg = [k.sb([128, 512], BF16) for _ in range(2)]
    stage = [k.sb([128, NTOK]) for _ in range(2)]
    rstd = k.sb([128, NTOK]); coef = k.sb([128, 4, 16])
    ones = c.ones; epst = c.eps
    bx = [Buf('x%d' % i) for i in range(16)]
    bh = Buf('h'); bwr = [Buf('wr0'), Buf('wr1')]; bact = [Buf('a0'), Buf('a1')]; bsg = [Buf('sg0'), Buf('sg1')]
    bstage = [Buf('st0'), Buf('st1')]; brstd = Buf('rstd'); bcoef = Buf('coef')
    bmv = c.bM; bones = c.bconst; beps = c.bconst

    xv = xsrc.rearrange("(kc p) n -> p kc n", p=128)
    for q4 in range(4):
        P.dma('sp' if q4 % 2 == 0 else 'act', x[:, q4 * 4:(q4 + 1) * 4, :], xv[:, q4 * 4:(q4 + 1) * 4, :],
              reads=[bxsrc], writes=bx[q4 * 4:(q4 + 1) * 4])
    if not first:
        for ci, sv in ((0, ML(l_cur, 4)), (1, MC(l_cur, 4))):
            P.op('dve', lambda e, ci=ci, sv=sv: e.scalar_tensor_tensor(out=coef[:, ci, :], in0=sv, scalar=1.0, in1=c.gnorm[:, l_cur, :],
                                                                       op0=ALU.add, op1=ALU.mult), reads=[bmv], writes=[bcoef])
    if not last:
        for ci, sv in ((2, ML(l_next, 1)), (3, MC(l_next, 1))):
            P.op('dve', lambda e, ci=ci, sv=sv: e.scalar_tensor_tensor(out=coef[:, ci, :], in0=sv, scalar=1.0, in1=c.gnorm[:, DEPTH + l_next, :],
                                                                       op0=ALU.add, op1=ALU.mult), reads=[bmv], writes=[bcoef])
    steps = []

    def wview(slot, off, kc, n):
        return wr[slot][:, off:off + kc * n].rearrange("p (kc n) -> p kc n", n=n)

    def norm(a_l, a_c, b_l, b_c):
        for ti, (t0, tn) in enumerate(TT):
            pt, pb = k.nextps()
            for kc in range(16):
                s = kc % 2
                P.op('act', lambda e, s=s, kc=kc, t0=t0, tn=tn: e.activation(
                    out=stage[s][:, 0:tn], in_=x[:, kc, t0:t0 + tn], func=AF.Square), reads=[bx[kc]], writes=[bstage[s]])
                P.op('pe', lambda e, pt=pt, s=s, tn=tn, kc=kc: e.matmul(
                    pt[:, 0:tn], ones, stage[s][:, 0:tn], start=(kc == 0), stop=(kc == 15)),
                    reads=[bones, bstage[s]], writes=[pb], pe_acc=(kc > 0))
            P.op('act', lambda e, pt=pt, t0=t0, tn=tn: e.activation(
                out=rstd[:, t0:t0 + tn], in_=pt[:, 0:tn], func=AF.Sqrt, bias=epst, scale=1.0 / D), reads=[pb, beps], writes=[brstd])
        P.op('dve', lambda e: e.reciprocal(out=rstd, in_=rstd), reads=[brstd], writes=[brstd])
        for kc in range(16):
            s = kc % 2
            P.op('dve', lambda e, s=s, kc=kc: e.tensor_tensor(out=stage[s], in0=x[:, kc, :], in1=rstd, op=ALU.mult),
                 reads=[bx[kc], brstd], writes=[bstage[s]])
            P.op('act', lambda e, s=s, kc=kc: e.activation(
                out=h[:, kc, 0:1024], in_=stage[s][:, 0:1024], func=AF.Identity, bias=b_l[:, kc:kc + 1], scale=a_l[:, kc:kc + 1]),
                reads=[bstage[s], bcoef, bmv], writes=[bh])
            P.op('act', lambda e, s=s, kc=kc: e.activation(
                out=h[:, kc, 1024:NTOK], in_=stage[s][:, 1024:NTOK], func=AF.Identity, bias=b_c[:, kc:kc + 1], scale=a_c[:, kc:kc + 1]),
                reads=[bstage[s], bcoef, bmv], writes=[bh])

    def resid_evac(pt, pb, dc, ti, gl, gc):
        t0, tn = TT[ti]
        g = gc if ti == 2 else gl
        P.op('dve', lambda e: e.scalar_tensor_tensor(
            out=x[:, dc, t0:t0 + tn], in0=pt[:, 0:tn], scalar=g[:, dc:dc + 1], in1=x[:, dc, t0:t0 + tn],
            op0=ALU.mult, op1=ALU.add), reads=[pb, bmv, bx[dc]], writes=[bx[dc]])

    if not first:
        w_out = pr['w_out'][l_cur]; w_fi = pr['w_fi'][l_cur]; w_fo = pr['w_fo'][l_cur]
        Gy2 = G_y

        def load_y():
            for kc in range(16):
                col = IDXC[('y', kc)]
                P.op('pool', lambda e, kc=kc, col=col: e.indirect_dma_start(
                    out=h[:, kc, :], out_offset=None, in_=Gy2, in_offset=bass.IndirectOffsetOnAxis(ap=c.idx[:, col:col + 1], axis=0)),
                    reads=bGy[(kc // 4) * 4:(kc // 4) * 4 + 4] + [c.bconst], writes=[bh], dma=True)
        steps.append(('call', load_y))
        for n4 in range(4):
            def ld(slot, n4=n4):
                P.dma('pool', wview(slot, 0, 16, 512), w_out.rearrange("(kc p) n -> p kc n", p=128)[:, :, n4 * 512:(n4 + 1) * 512],
                      writes=[bwr[slot]])

            def cp(slot, n4=n4):
                wv = wview(slot, 0, 16, 512)
                for m in range(4):
                    dc = n4 * 4 + m
                    for ti, (t0, tn) in enumerate(TT):
                        pt, pb = k.nextps()
                        for kc in range(16):
                            P.op('pe', lambda e, pt=pt, wv=wv, kc=kc, m=m, t0=t0, tn=tn: e.matmul(
                                pt[:, 0:tn], wv[:, kc, m * 128:(m + 1) * 128], h[:, kc, t0:t0 + tn],
                                start=(kc == 0), stop=(kc == 15)), reads=[bwr[slot], bh], writes=[pb], pe_acc=(kc > 0))
                        resid_evac(pt, pb, dc, ti, ML(l_cur, 2), MC(l_cur, 2))
            steps.append(('w', ld, cp))
        steps.append(('call', lambda: norm(coef[:, 0, :], coef[:, 1, :], ML(l_cur, 3), MC(l_cur, 3))))
        for g in range(FFN_H // 256):
            def ld(slot, g=g):
                wfv = w_fi.rearrange("(kc p) n -> p kc n", p=128)
                P.dma('pool', wview(slot, 0, 16, 256), wfv[:, :, g * 256:(g + 1) * 256], writes=[bwr[slot]])
                P.dma('pool', wview(slot, 4096, 16, 256), wfv[:, :, FFN_H + g * 256:FFN_H + (g + 1) * 256], writes=[bwr[slot]])
                P.dma('pool', wview(slot, 8192, 2, 2048), w_fo[g * 256:(g + 1) * 256, :].rearrange("(hc p) n -> p hc n", p=128),
                      writes=[bwr[slot]])

            def cp(slot, g=g):
                wg = wview(slot, 0, 16, 256); wu = wview(slot, 4096, 16, 256); wo = wview(slot, 8192, 2, 2048)
                a = g % 2
                for hc in range(2):
                    for ti, (t0, tn) in enumerate(TT):
                        pg, pgb = k.nextps()
                        pu, pub = k.nextps()
                        for (pt, pb, wv) in ((pg, pgb, wg), (pu, pub, wu)):
                            for kc in range(16):
                                P.op('pe', lambda e, pt=pt, wv=wv, kc=kc, hc=hc, t0=t0, tn=tn: e.matmul(
                                    pt[:, 0:tn], wv[:, kc, hc * 128:(hc + 1) * 128], h[:, kc, t0:t0 + tn],
                                    start=(kc == 0), stop=(kc == 15)), reads=[bwr[slot], bh], writes=[pb], pe_acc=(kc > 0))
                        s = (hc * 3 + ti) % 2
                        P.op('act', lambda e, s=s, pg=pg, tn=tn: e.activation(
                            out=sg[s][:, 0:tn], in_=pg[:, 0:tn], func=AF.Silu), reads=[pgb], writes=[bsg[s]])
                        P.op('dve', lambda e, s=s, pu=pu, a=a, hc=hc, t0=t0, tn=tn: e.tensor_tensor(
                            out=act[a][:, hc, t0:t0 + tn], in0=sg[s][:, 0:tn], in1=pu[:, 0:tn], op=ALU.mult),
                            reads=[bsg[s], pub], writes=[bact[a]])
                for dc in range(16):
                    for ti, (t0, tn) in enumerate(TT):
                        pt, pb = k.nextps()
                        for hc in range(2):
                            P.op('pe', lambda e, pt=pt, hc=hc, dc=dc, t0=t0, tn=tn: e.matmul(
                                pt[:, 0:tn], wo[:, hc, dc * 128:(dc + 1) * 128], act[a][:, hc, t0:t0 + tn],
                                start=(hc == 0), stop=(hc == 1)), reads=[bwr[slot], bact[a]], writes=[pb], pe_acc=(hc > 0))
                        resid_evac(pt, pb, dc, ti, ML(l_cur, 5), MC(l_cur, 5))
            steps.append(('w', ld, cp))

        def store_x():
            xov = xdst.rearrange("(kc p) n -> p kc n", p=128)
            for q4 in range(4):
                P.dma('sp' if q4 % 2 == 0 else 'act', xov[:, q4 * 4:(q4 + 1) * 4, :], x[:, q4 * 4:(q4 + 1) * 4, :],
                      reads=bx[q4 * 4:(q4 + 1) * 4], writes=[bxdst])
        steps.append(('call', store_x))
    if not last:
        w_in = pr['w_in'][l_next]
        steps.append(('call', lambda: norm(coef[:, 2, :], coef[:, 3, :], ML(l_next, 0), MC(l_next, 0))))
        ncol = [(i * 512, 512) for i in (7, 8, 9, 10, 3, 4, 5, 6, 0, 1, 2, 11)] + [(6144, 32)]
        for (c0, cn) in ncol:
            def ld(slot, c0=c0, cn=cn):
                P.dma('pool', wview(slot, 0, 16, cn), w_in.rearrange("(kc p) n -> p kc n", p=128)[:, :, c0:c0 + cn], writes=[bwr[slot]])

            def cp(slot, c0=c0, cn=cn):
                wv = wview(slot, 0, 16, cn)
                for m in range((cn + 127) // 128):
                    mw = min(128, cn - m * 128)
                    s = m % 2
                    for ti, (t0, tn) in enumerate(TT):
                        pt, pb = k.nextps()
                        for kc in range(16):
                            P.op('pe', lambda e, pt=pt, wv=wv, kc=kc, m=m, mw=mw, t0=t0, tn=tn: e.matmul(
                                pt[0:mw, 0:tn], wv[:, kc, m * 128:m * 128 + mw], h[:, kc, t0:t0 + tn],
                                start=(kc == 0), stop=(kc == 15)), reads=[bwr[slot], bh], writes=[pb], pe_acc=(kc > 0))
                        if ti == 1:
                            P.op('dve', lambda e, pt=pt, s=s, mw=mw, t0=t0, tn=tn: e.tensor_copy(
                                out=stage[s][0:mw, t0:t0 + tn], in_=pt[0:mw, 0:tn]), reads=[pb], writes=[bstage[s]])
                        else:
                            P.op('act', lambda e, pt=pt, s=s, mw=mw, t0=t0, tn=tn: e.activation(
                                out=stage[s][0:mw, t0:t0 + tn], in_=pt[0:mw, 0:tn], func=AF.Copy), reads=[pb], writes=[bstage[s]])
                    r0 = c0 + m * 128
                    q = r0 // PCH
                    qc = r0 // PCC
                    P.dma('sp', p_lat[r0:r0 + mw, :], stage[s][0:mw, 0:1024], reads=[bstage[s]], writes=[bp[q]])
                    P.dma('act', p_ctx[r0:r0 + mw, :], stage[s][0:mw, 1024:NTOK], reads=[bstage[s]], writes=[bpc[qc]])
                    if (r0 + mw) % PCH == 0 or (r0 + mw) == IN_COLS:
                        q0 = q * PCH
                        rq = min(PCH, IN_COLS - q0)
                        P.op('pool', lambda e, q0=q0, rq=rq: e.collective_compute(
                            "AllGather", ALU.bypass, replica_groups=groups, ins=[p_lat[q0:q0 + rq, :].opt()],
                            outs=[G_lat[4 * q0:4 * q0 + 4 * rq, :].opt()]), reads=[bp[q]], writes=[bG[q]], dma=True, inc=1, cc=True)
                    if (r0 + mw) % PCC == 0 or (r0 + mw) == IN_COLS:
                        q0 = qc * PCC
                        rq = min(PCC, IN_COLS - q0)
                        P.op('pool', lambda e, q0=q0, rq=rq: e.collective_compute(
                            "AllGather", ALU.bypass, replica_groups=groups, ins=[p_ctx[q0:q0 + rq, :].opt()],
                            outs=[G_ctx[4 * q0:4 * q0 + 4 * rq, :].opt()]), reads=[bpc[qc]], writes=[bGc[qc]], dma=True, inc=1, cc=True)
            steps.append(('w', ld, cp))

    wsteps = [i for i, s in enumerate(steps) if s[0] == 'w']
    slot_of = {si: j % 2 for j, si in enumerate(wsteps)}
    nxt = {wsteps[j]: wsteps[j + 1] for j in range(len(wsteps) - 1)}
    if wsteps:
        steps[wsteps[0]][1](slot_of[wsteps[0]])
    for i, s in enumerate(steps):
        if s[0] == 'call':
            s[1]()
        else:
            if i in nxt:
                steps[nxt[i]][1](slot_of[nxt[i]])
            s[2](slot_of[i])


def build_fused(depth=DEPTH, stop_after=None):
    k = K()
    P = k.P
    c = setup_common(k)
    xT = k.dram("xT", [D, NTOK])
    xo = k.dram("xo", [D, NTOK], kind="ExternalOutput")
    xspill = k.dram("xspill", [D, NTOK], kind="Internal")
    p_lat = k.dram("p_lat", [IN_COLS, 1024], kind="Internal")
    p_ctx = k.dram("p_ctx", [IN_COLS, 64], kind="Internal")
    G_lat = k.dram("G_lat", [4 * IN_COLS, 1024], kind="Internal")
    G_ctx = k.dram("G_ctx", [4 * IN_COLS, 64], kind="Internal")
    ybuf = k.dram("ybuf", [4, 4, 128, NTOK], kind="Internal")
    G_y = k.dram("G_y", [4 * 4 * 4 * 128, NTOK], kind="Internal")
    bxT, bxo, bxs = (Buf(n) for n in 'xT xo xspill'.split())
    NQ = (IN_COLS + PCH - 1) // PCH
    bp = [Buf('p%d' % i) for i in range(NQ)]
    bG = [Buf('G%d' % i) for i in range(NQ)]
    NQC = (IN_COLS + PCC - 1) // PCC
    bpc = [Buf('pc%d' % i) for i in range(NQC)]
    bGc = [Buf('Gc%d' % i) for i in range(NQC)]
    bys = [Buf('y%d' % i) for i in range(16)]
    bGy = [Buf('Gy%d' % i) for i in range(16)]
    prd = dense_params(k)
    pra = {'na': attn_params(k, 'na'), 'wa': attn_params(k, 'wa')}
    prm = ml_params(k)
    prn = dn_params(k)
    groups = [[0, 1, 2, 3], [4, 5, 6, 7]]
    emit_mod(k, c)
    k.phase()

    def stop(tag):
        if stop_after != tag:
            return False
        dbgM = k.dram("dbgM", [128, 2, 384], kind="ExternalOutput")
        dbgG = k.dram("dbgG", [4 * IN_COLS, 64], kind="ExternalOutput")
        dbgY = k.dram("dbgY", [4 * 2048, 64], kind="ExternalOutput")
        P.dma('sp', dbgM[:, 0, :], c.Mlat, reads=[c.bM])
        P.dma('sp', dbgM[:, 1, :], c.Mctx, reads=[c.bM])
        P.dma('sp', dbgG, G_ctx, reads=bGc)
        P.dma('sp', dbgY, G_y[:, 1024:1088], reads=bGy)
        P.dma('act', xo, xT, reads=[bxT], writes=[bxo])
        return True

    bybuf = (bys, G_y, bGy, groups)
    pb_all = (p_lat, p_ctx, bp, bpc, G_lat, G_ctx, bG, bGc, groups)

    if stop('mod'):
        return k.done()
    emit_dense(k, c, None, prd, xT, bxT, None, None, (G_y, bGy), pb_all)
    k.phase()
    if stop('d0'):
        return k.done()
    G = (G_lat, G_ctx, bG, bGc)
    for l in range(depth):
        for (tag, fn) in (('ml', lambda: emit_ml(k, c, l, G, ybuf, bybuf, prm)),
                          ('dn', lambda: emit_dn(k, c, l, G, ybuf, bybuf, prn)),
                          ('na', lambda: emit_attn(k, c, 'na', l, G, ybuf, bybuf, pra['na'])),
                          ('wa', lambda: emit_attn(k, c, 'wa', l, G, ybuf, bybuf, pra['wa']))):
            fn()
            k.phase()
            if stop('%s%d' % (tag, l)):
                return k.done()
        lastl = (l == depth - 1)
        src, bsrc = (xT, bxT) if l == 0 else (xspill, bxs)
        dst, bdst = (xo, bxo) if lastl else (xspill, bxs)
        emit_dense(k, c, l, prd, src, bsrc, dst, bdst, (G_y, bGy), pb_all)
        k.phase()
        if (not lastl) and stop('d%d' % (l + 1)):
            return k.done()
    return k.done()

import numpy as np

NCORES = 8
_PROG = {}


def _rope_tables():
    n = 4096
    t = np.arange(n)
    n_freq = 32
    inv_freq = (np.float32(10000.0) ** (-np.arange(n_freq, dtype=np.float32) / np.float32(n_freq))).astype(np.float32)
    pos = np.stack([t // 64, t % 64], -1).astype(np.float32)
    ang = pos[:, :, None] * inv_freq
    cos = np.cos(ang).astype(np.float32)
    sin = np.sin(ang).astype(np.float32)
    C = np.zeros((128, n), np.float32)
    S = np.zeros((128, n), np.float32)
    RmT = np.zeros((128, 128), np.float32)
    for d in range(128):
        a, tt, f = d // 64, (d // 32) % 2, d % 32
        C[d] = cos[:, a, f]
        S[d] = sin[:, a, f]
        if tt == 0:
            RmT[d + 32, d] = -1.0
        else:
            RmT[d - 32, d] = 1.0
    return C, S, RmT


def _na_bias_tables(rpb):
    NEG = -30000.0
    tab = np.zeros((128, 5, 7 * 128), np.float32)
    classes = [(0, [0, 1, 2, 3]), (1, [-1, 0, 1, 2]), (5, [-2, -1, 0, 1, 2]), (30, [-2, -1, 0, 1]), (31, [-3, -2, -1, 0])]
    kk = np.arange(128)
    qq = np.arange(128)
    for cls, (n, offs) in enumerate(classes):
        r = 2 * n + qq // 64
        qc = qq % 64
        rs = np.clip(r - 4, 0, 56)
        ws = np.clip(qc - 8, 0, 48)
        for ci, off in enumerate(offs):
            ch = n + off
            kr = 2 * ch + kk // 64
            kc = kk % 64
            ok = ((kr[:, None] >= rs[None, :]) & (kr[:, None] < rs[None, :] + 8)
                  & (kc[:, None] >= ws[None, :]) & (kc[:, None] < ws[None, :] + 16))
            dr = np.clip(kr[:, None] - r[None, :] + 7, 0, 14)
            dc = np.clip(kc[:, None] - qc[None, :], -15, 15) + 15
            tab[:, cls, ci * 128:(ci + 1) * 128] = np.where(ok, rpb[dr, dc], NEG)
    return tab


def _core_inputs(I, core, shared):
    b, j = core // 4, core % 4
    m = dict(shared)
    m["idx"] = make_idx(j)
    m["w_mod"] = I['w_ada'][j]
    m["b_mod"] = np.ascontiguousarray(I['b_ada'][j].reshape(96, 128).T)
    sel = np.zeros((128, 2), np.float32)
    sel[:, b] = 1.0
    m["sel"] = sel
    xc = np.concatenate([I['x'][b, j * 1024:(j + 1) * 1024], I['ctx'][b, j * 64:(j + 1) * 64]], 0)
    m["xT"] = np.ascontiguousarray(xc.T)
    m["wa_sinkb"] = np.ascontiguousarray(np.broadcast_to(I['wa_sink'][:, j][:, None, None], (4, 128, 1))).astype(np.float32)
    m["na_bias"] = np.stack([_na_bias_tables(I['na_rpb'][ll][j]) for ll in range(4)], 0)
    gbv = np.stack([I['ml_i_bias'][:, 0, j], I['ml_i_bias'][:, 1, j], I['ml_f_bias'][:, 0, j], I['ml_f_bias'][:, 1, j]], -1)
    m["ml_gb"] = np.ascontiguousarray(np.broadcast_to(gbv[:, None, :], (4, 68, 4))).astype(np.float32)
    m["ml_gn"] = np.ascontiguousarray(np.broadcast_to(I['ml_norm'][:, j][:, None, :], (4, 64, 128))).astype(np.float32)
    cw = np.stack([I['dn_conv'][:, :, t * 512 + j * 128: t * 512 + (j + 1) * 128] for t in range(3)], 1)
    cw2 = np.stack([cw, cw[:, :, ::-1, :]], 1)
    m["dn_cw"] = np.ascontiguousarray(cw2.transpose(0, 1, 4, 2, 3)).astype(np.float32)
    scv = np.stack([I['dn_a_log'][:, 0, j], I['dn_a_log'][:, 1, j], I['dn_dt_bias'][:, 0, j], I['dn_dt_bias'][:, 1, j]], -1)
    m["dn_sc"] = np.ascontiguousarray(np.broadcast_to(scv[:, None, :], (4, 68, 4))).astype(np.float32)
    return m


def kernel(**I):
    I = {k_: np.asarray(v, np.float32) for k_, v in I.items()}
    if 'nc' not in _PROG:
        _PROG['nc'] = build_fused()
    nc = _PROG['nc']
    C, S, RmT = _rope_tables()
    kk = np.arange(128)[:, None]
    qq = np.arange(128)[None, :]
    jj = np.arange(64)
    mk = np.zeros((64, 2, 64), np.float32)
    mk[:, 0, :] = np.where(jj[None, :] >= jj[:, None], 0.0, -30000.0)
    mk[:, 1, :] = (jj[None, :] > jj[:, None]).astype(np.float32)
    c3 = np.stack([I['c'][0], I['c'][1], I['c_ctx']], 0)
    gnorm = np.zeros((128, 8, 16), np.float32)
    for l in range(4):
        gnorm[:, l] = I['norm_ffn'][l].reshape(16, 128).T
        gnorm[:, 4 + l] = I['norm_mix'][l].reshape(16, 128).T
    shared = {
        "I128": np.eye(128, dtype=np.float32), "J128": np.ascontiguousarray(np.eye(128, dtype=np.float32)[::-1]),
        "c3": np.ascontiguousarray(c3.T.reshape(16, 128, 3).transpose(1, 0, 2)), "gnorm": gnorm,
        "w_in": I['w_in'], "w_out": I['w_out'], "w_fi": I['w_ffn_in'], "w_fo": I['w_ffn_out'],
        "na_gains": np.ascontiguousarray(I['na_qk_gain'].transpose(0, 2, 1)),
        "wa_gains": np.ascontiguousarray(I['wa_qk_gain'].transpose(0, 2, 1)),
        "cosT": C, "sinT": S, "rmT": RmT, "wamask": np.concatenate([(kk >= qq), (kk <= qq)], 1).astype(np.float32),
        "ml_tri": (jj[None, :] >= jj[:, None]).astype(np.float32),
        "dn_gn": np.ascontiguousarray(np.broadcast_to(I['dn_norm'][:, None, :], (4, 64, 128))).astype(np.float32),
        "dn_masks": mk,
    }
    ins = [_core_inputs(I, core, shared) for core in range(NCORES)]
    res = run_bass_kernel_spmd(nc, ins, core_ids=list(range(NCORES))).results
    out = np.zeros((2, 4096, 2048), np.float32)
    for core in range(NCORES):
        b, j = core // 4, core % 4
        out[b, j * 1024:(j + 1) * 1024] = res[core]["xo"][:, 0:1024].T
    return out
```

```python
import contextlib

import numpy as np
import concourse.bass as bass
import concourse.mybir as mybir
from concourse.bass_utils import run_bass_kernel_spmd
from concourse.alu_op_type import AluOpType as ALU

AF = mybir.ActivationFunctionType
AX = mybir.AxisListType
F32 = mybir.dt.float32
BF16 = mybir.dt.bfloat16
F32R = mybir.dt.float32r

ENGS = ('pe', 'act', 'dve', 'pool', 'sp')
NDMASEM = 12
NCCSEM = 56


class Buf:
    __slots__ = ('name', 'w', 'r')

    def __init__(self, name=''):
        self.name = name
        self.w = None
        self.r = []


class Op:
    __slots__ = ('eng', 'pos', 'fn', 'waits', 'flag', 'val', 'dma', 'sem', 'inc')


class Prog:
    def __init__(self, nc):
        self.nc = nc
        self.ops = {e: [] for e in ENGS}
        self.seen = {e: {} for e in ENGS}
        self.dma_last = [None] * (NDMASEM + NCCSEM)
        self.dma_cnt = [0] * (NDMASEM + NCCSEM)
        self.dma_tot = [0] * (NDMASEM + NCCSEM)
        self.dma_rr = 0
        self.cc_rr = 0
        self.ndma = 0

    def _dep(self, o, d, same_ok=False):
        if d is None:
            return
        E = o.eng
        if d.dma:
            key = ('d', d.sem)
            if self.seen[E].get(key, 0) >= d.val:
                return
            self.seen[E][key] = d.val
            o.waits.append(d)
        else:
            if d.eng == E and same_ok:
                return
            key = ('e', d.eng)
            if self.seen[E].get(key, -1) >= d.pos:
                return
            self.seen[E][key] = d.pos
            d.flag = True
            o.waits.append(d)

    def op(self, eng, fn, reads=(), writes=(), dma=False, pe_acc=False, inc=16, cc=False):
        o = Op()
        o.eng = eng
        o.pos = len(self.ops[eng])
        o.fn = fn
        o.waits = []
        o.flag = False
        o.val = None
        o.dma = dma
        o.sem = None
        for b in reads:
            self._dep(o, b.w)
        for b in writes:
            if not (pe_acc and b.w is not None and b.w.eng == 'pe' and eng == 'pe'):
                self._dep(o, b.w)
            for r in b.r:
                self._dep(o, r, same_ok=(not r.dma))
        if dma:
            if cc:
                s = NDMASEM + self.cc_rr
                self.cc_rr = (self.cc_rr + 1) % NCCSEM
            else:
                s = self.dma_rr
                self.dma_rr = (self.dma_rr + 1) % NDMASEM
            self._dep(o, self.dma_last[s])
            self.dma_cnt[s] += 1
            self.dma_tot[s] += inc
            o.sem = s
            o.inc = inc
            o.val = self.dma_tot[s]
            self.dma_last[s] = o
            self.ndma += 1
        self.ops[eng].append(o)
        for b in reads:
            if dma:
                b.r.append(o)
            else:
                b.r = [r for r in b.r if r.dma or r.eng != eng]
                b.r.append(o)
        for b in writes:
            b.w = o
            b.r = []
        return o

    def dma(self, q, out, in_, reads=(), writes=(), **kw):
        return self.op(q, lambda e: e.dma_start(out=out, in_=in_, **kw), reads, writes, dma=True)

    def barrier(self):
        lasts = {}
        for e in ENGS:
            for o in reversed(self.ops[e]):
                if (not o.dma) and o.fn is not None:
                    lasts[e] = o
                    break
        for E in ENGS:
            o = Op()
            o.eng = E
            o.pos = len(self.ops[E])
            o.fn = None
            o.waits = []
            o.flag = False
            o.val = None
            o.dma = False
            o.sem = None
            o.inc = 0
            for F in ENGS:
                if F != E and F in lasts:
                    self._dep(o, lasts[F])
            for d in self.dma_last[:NDMASEM]:
                if d is not None:
                    self._dep(o, d)
            self.ops[E].append(o)

    def finish(self):
        o = Op()
        o.eng = 'sp'
        o.pos = len(self.ops['sp'])
        o.fn = None
        o.waits = []
        o.flag = False
        o.val = None
        o.dma = False
        o.sem = None
        for d in self.dma_last:
            if d is not None:
                self._dep(o, d)
        self.ops['sp'].append(o)

    def emit(self):
        nc = self.nc
        self.finish()
        for e in ENGS:
            c = 0
            for o in self.ops[e]:
                if not o.dma and o.flag:
                    c += 1
                    o.val = c
        import contextlib
        with contextlib.ExitStack() as st:
            esem = {e: st.enter_context(nc.semaphore('s_' + e)) for e in ENGS}
            dsem = [st.enter_context(nc.semaphore('d%d' % i)) for i in range(NDMASEM + NCCSEM)]
            block = st.enter_context(nc.Block())

            def run(e):
                def body(eng):
                    for o in self.ops[e]:
                        for d in o.waits:
                            if d.dma:
                                eng.wait_ge(dsem[d.sem], d.val)
                            else:
                                eng.wait_ge(esem[d.eng], d.val)
                        if o.fn is None:
                            continue
                        ins = o.fn(eng)
                        if o.dma:
                            ins.then_inc(dsem[o.sem], o.inc)
                        elif o.flag:
                            ins.then_inc(esem[e], 1)
                return body

            block.tensor(run('pe'))
            block.scalar(run('act'))
            block.vector(run('dve'))
            block.gpsimd(run('pool'))
            block.sync(run('sp'))

import numpy as np

U32 = mybir.dt.uint32
DEPTH = 4
D = 2048
NTOK = 1088
TT = [(0, 512), (512, 512), (1024, 64)]
IN_COLS = 6176
FFN_H = 5632
EPS = 1e-6
NT = 4352
NB = 34
NCH = 68
SCALE = 128 ** -0.5

FM_TENSORS = [('na_q', 0), ('na_k', 512), ('na_v', 1024),
              ('dn_q', 1536), ('dn_k', 2048), ('dn_v', 2560), ('dn_g', 3072),
              ('ml_q', 3600), ('ml_k', 3856), ('ml_v', 4112), ('ml_o', 4624),
              ('wa_q', 5152), ('wa_k', 5664), ('wa_v', 5920)]
FM_WIDTH = {'ml_q': 64, 'ml_k': 64}
CT_ROWS = [('dn_b0', 3584 + 0), ('dn_b1', 3584 + 4), ('dn_a0', 3584 + 8), ('dn_a1', 3584 + 12),
           ('ml_i0', 5136 + 0), ('ml_i1', 5136 + 4), ('ml_f0', 5136 + 8), ('ml_f1', 5136 + 12)]


def idx_cols():
    cols = {}
    n = 0
    for name, _ in FM_TENSORS:
        for r in range(4):
            cols[(name, r)] = n
            cols[(name, r, 'c')] = n + 1
            n += 2
    for name, _ in CT_ROWS:
        cols[(name, 'lat')] = n
        cols[(name, 'ctx')] = n + 1
        n += 2
    for kc in range(16):
        cols[('y', kc)] = n
        n += 1
    return cols, n


IDXC, NIDX = idx_cols()


PCH = 256


def grow(r, cidx):
    cidx = np.asarray(cidx)
    start = (cidx // PCH) * PCH
    rows_q = np.minimum(PCH, IN_COLS - start)
    return 4 * start + r * rows_q + (cidx - start)


PCC = 256


def grow_ctx(r, cidx):
    cidx = np.asarray(cidx)
    start = (cidx // PCC) * PCC
    rows_q = np.minimum(PCC, IN_COLS - start)
    return 4 * start + r * rows_q + (cidx - start)


def make_idx(j):
    t = np.zeros((128, NIDX), np.uint32)
    p = np.arange(128)
    for name, base in FM_TENSORS:
        w = FM_WIDTH.get(name, 128)
        hj = (j // 2) if name in ('wa_k', 'wa_v') else j
        c0 = base + hj * w
        for r in range(4):
            t[:, IDXC[(name, r)]] = grow(r, c0 + np.minimum(p, w - 1))
            t[:, IDXC[(name, r, 'c')]] = grow_ctx(r, c0 + np.minimum(p, w - 1))
    for name, base in CT_ROWS:
        c = base + j
        rev = name.endswith('1')
        ci = np.arange(64)
        cn = (63 - ci) if rev else ci
        t[0:64, IDXC[(name, 'lat')]] = grow(cn // 16, c) * 16 + (cn % 16)
        rr = np.arange(4)
        rn = (3 - rr) if rev else rr
        t[0:4, IDXC[(name, 'ctx')]] = grow_ctx(rn, c)
    for kc in range(16):
        g, r = kc // 4, kc % 4
        t[:, IDXC[('y', kc)]] = ((g * 4 + j) * 4 + r) * 128 + p
    return t


class K:
    def __init__(self, arena_kb=204):
        self.nc = bass.Bass("TRN2", target_bir_lowering=False)
        self.st = contextlib.ExitStack()
        self.P = Prog(self.nc)
        self.psn = 0
        self.nring = 8
        self.words = arena_kb * 256
        self.arena = self.st.enter_context(self.nc.sbuf_tensor("arena", [128, self.words], F32))
        self.off = 0
        self.mark = 0
        self.pst = [self.st.enter_context(self.nc.psum_tensor("ps%d" % i, [128, 512], F32)) for i in range(8)]
        self.psb = [Buf('ps%d' % i) for i in range(8)]
        self.drams = {}

    def dram(self, name, shape, dt=F32, kind="ExternalInput"):
        if kind == "Internal":
            t = self.nc.dram_tensor(name, list(shape), dt).ap()
        else:
            t = self.nc.dram_tensor(name, list(shape), dt, kind=kind).ap()
        self.drams[name] = t
        return t

    def sb(self, shape, dt=F32):
        shape = list(shape)
        n = 1
        for s in shape[1:]:
            n *= s
        bpe = 2 if dt == BF16 else 4
        words = (n * bpe + 3) // 4
        words = (words + 7) // 8 * 8
        assert self.off + words <= self.words, ("SBUF arena overflow", self.off, words, self.words)
        ap = self.arena[0:shape[0], self.off:self.off + words]
        self.off += words
        if dt != F32:
            ap = ap.bitcast(dt)
        ap = ap[:, 0:n]
        if len(shape) == 3:
            ap = ap.rearrange("p (a b) -> p a b", b=shape[2])
        elif len(shape) == 4:
            ap = ap.rearrange("p (a b c) -> p a b c", b=shape[2], c=shape[3])
        return ap

    def persist(self):
        self.mark = self.off

    def phase(self):
        self.P.barrier()
        self.off = self.mark

    def nextps(self):
        i = self.psn % self.nring
        self.psn += 1
        return self.pst[i], self.psb[i]

    def done(self):
        self.P.emit()
        self.st.close()
        return self.nc


class Common:
    pass


def setup_common(k):
    P = k.P
    c = Common()
    c.idx_d = k.dram("idx", [128, NIDX], U32)
    c.I_d = k.dram("I128", [128, 128])
    c.J_d = k.dram("J128", [128, 128])
    c.idx = k.sb([128, NIDX], U32)
    c.I = k.sb([128, 128])
    c.J = k.sb([128, 128])
    c.ones = k.sb([128, 128])
    c.eps = k.sb([128, 1])
    c.bconst = Buf('const')
    P.dma('sp', c.idx, c.idx_d, writes=[c.bconst])
    P.dma('sp', c.I, c.I_d, writes=[c.bconst])
    P.dma('sp', c.J, c.J_d, writes=[c.bconst])
    P.op('dve', lambda e: e.memset(c.ones, 1.0), writes=[c.bconst])
    P.op('dve', lambda e: e.memset(c.eps, EPS), writes=[c.bconst])
    c.J64 = c.J[0:64, 64:128]
    return c


def gather_fm(k, c, G, dst, bdst, name, rows=128, lat_off=0, ctx_off=4096):
    P = k.P
    G_lat, G_ctx, bGl, bGc = G
    base = dict(FM_TENSORS)[name]
    wdt = FM_WIDTH.get(name, 128)
    gdeps = bGl[base // PCH:(base + 4 * wdt - 1) // PCH + 1]
    cdeps = bGc[base // PCC:(base + 4 * wdt - 1) // PCC + 1]
    for r in range(4):
        col = IDXC[(name, r)]
        colc = IDXC[(name, r, 'c')]
        P.op('pool', lambda e, r=r, col=col: e.indirect_dma_start(
            out=dst[0:rows, lat_off + r * 1024: lat_off + (r + 1) * 1024], out_offset=None, in_=G_lat,
            in_offset=bass.IndirectOffsetOnAxis(ap=c.idx[0:rows, col:col + 1], axis=0)),
            reads=gdeps + [c.bconst], writes=[bdst], dma=True)
        P.op('pool', lambda e, r=r, colc=colc: e.indirect_dma_start(
            out=dst[0:rows, ctx_off + r * 64: ctx_off + (r + 1) * 64], out_offset=None, in_=G_ctx,
            in_offset=bass.IndirectOffsetOnAxis(ap=c.idx[0:rows, colc:colc + 1], axis=0)),
            reads=cdeps + [c.bconst], writes=[bdst], dma=True)


def gather_ct(k, c, G, dst, bdst, name):
    P = k.P
    G_lat, G_ctx, bGl, bGc = G
    base = dict(CT_ROWS)[name]
    gdeps = bGl[base // PCH:(base + 3) // PCH + 1]
    cdeps = bGc[base // PCC:(base + 3) // PCC + 1]
    G64 = G_lat.rearrange("r (a b) -> (r a) b", b=64)
    cl, cc = IDXC[(name, 'lat')], IDXC[(name, 'ctx')]
    P.op('pool', lambda e: e.indirect_dma_start(out=dst[4:68, :], out_offset=None, in_=G64,
                                                in_offset=bass.IndirectOffsetOnAxis(ap=c.idx[0:64, cl:cl + 1], axis=0)),
         reads=gdeps + [c.bconst], writes=[bdst], dma=True)
    P.op('pool', lambda e: e.indirect_dma_start(out=dst[0:4, :], out_offset=None, in_=G_ctx,
                                                in_offset=bass.IndirectOffsetOnAxis(ap=c.idx[0:4, cc:cc + 1], axis=0)),
         reads=cdeps + [c.bconst], writes=[bdst], dma=True)


def mm_evac(k, out_rows, out_cols, lhsT, rhs, reads, dst, bdst, eng_i=0, scale_col=None, extra_reads=()):
    P = k.P
    pt, pb = k.nextps()
    P.op('pe', lambda e: e.matmul(pt[0:out_rows, 0:out_cols], lhsT, rhs, start=True, stop=True), reads=list(reads), writes=[pb])
    if scale_col is not None:
        P.op('dve', lambda e: e.tensor_scalar(out=dst, in0=pt[0:out_rows, 0:out_cols], scalar1=scale_col, scalar2=None, op0=ALU.mult),
             reads=[pb] + list(extra_reads), writes=[bdst])
    elif eng_i % 2 == 0:
        P.op('act', lambda e: e.activation(out=dst, in_=pt[0:out_rows, 0:out_cols], func=AF.Copy), reads=[pb], writes=[bdst])
    else:
        P.op('dve', lambda e: e.tensor_copy(out=dst, in_=pt[0:out_rows, 0:out_cols]), reads=[pb], writes=[bdst])


def store_y_fm(k, yT, byT, ybuf, bybuf, g, lat_off, ctx_off):
    P = k.P
    bys, G_y, bGy, groups = bybuf
    for jt in range(4):
        q = 'sp' if jt % 2 == 0 else 'act'
        P.dma(q, ybuf[g, jt, :, 0:1024], yT[:, lat_off + jt * 1024: lat_off + (jt + 1) * 1024], reads=[byT], writes=[bys[g * 4 + jt]])
        P.dma(q, ybuf[g, jt, :, 1024:1088], yT[:, ctx_off + jt * 64: ctx_off + (jt + 1) * 64], reads=[byT], writes=[bys[g * 4 + jt]])
    yb2 = ybuf.rearrange("g j d n -> (g j d) n")
    for jt in range(4):
        q = g * 4 + jt
        P.op('pool', lambda e, q=q: e.collective_compute(
            "AllGather", ALU.bypass, replica_groups=groups, ins=[yb2[q * 128:(q + 1) * 128, :].opt()], outs=[G_y[q * 512:(q + 1) * 512, :].opt()]),
            reads=[bys[q]], writes=[bGy[q]], dma=True, inc=1, cc=True)


def qknorm(k, bufs, src, dst, gain, tiles, rope=None):
    P = k.P
    (ones, bones, epst, beps, scr, bscr, bsrc, bdst, bg) = bufs
    for i, (t0, tn, dorope) in enumerate(tiles):
        s0, s1, s2 = scr[(3 * i) % 6], scr[(3 * i + 1) % 6], scr[(3 * i + 2) % 6]
        b0, b1, b2 = bscr[(3 * i) % 6], bscr[(3 * i + 1) % 6], bscr[(3 * i + 2) % 6]
        P.op('act', lambda e, s0=s0, t0=t0, tn=tn: e.activation(out=s0[:, 0:tn], in_=src[:, t0:t0 + tn], func=AF.Square),
             reads=[bsrc], writes=[b0])
        pt, pb = k.nextps()
        P.op('pe', lambda e, pt=pt, s0=s0, tn=tn: e.matmul(pt[:, 0:tn], ones, s0[:, 0:tn], start=True, stop=True),
             reads=[bones, b0], writes=[pb])
        P.op('act', lambda e, pt=pt, s1=s1, tn=tn: e.activation(out=s1[:, 0:tn], in_=pt[:, 0:tn], func=AF.Sqrt,
                                                                  bias=epst, scale=1.0 / 128), reads=[pb, beps], writes=[b1])
        P.op('dve', lambda e, s1=s1, tn=tn: e.reciprocal(out=s1[:, 0:tn], in_=s1[:, 0:tn]), reads=[b1], writes=[b1])
        if not dorope:
            P.op('dve', lambda e, s1=s1, t0=t0, tn=tn: e.scalar_tensor_tensor(
                out=dst[:, t0:t0 + tn], in0=src[:, t0:t0 + tn], scalar=gain, in1=s1[:, 0:tn], op0=ALU.mult, op1=ALU.mult),
                reads=[bsrc, b1, bg], writes=[bdst])
        else:
            C, S, RmT, brope = rope
            P.op('dve', lambda e, s1=s1, s2=s2, t0=t0, tn=tn: e.scalar_tensor_tensor(
                out=s2[:, 0:tn], in0=src[:, t0:t0 + tn], scalar=gain, in1=s1[:, 0:tn], op0=ALU.mult, op1=ALU.mult),
                reads=[bsrc, b1, bg], writes=[b2])
            pr, prb = k.nextps()
            P.op('pe', lambda e, pr=pr, s2=s2, tn=tn: e.matmul(pr[:, 0:tn], RmT, s2[:, 0:tn], start=True, stop=True),
                 reads=[brope, b2], writes=[prb])
            P.op('dve', lambda e, pr=pr, s0=s0, t0=t0, tn=tn: e.tensor_tensor(
                out=s0[:, 0:tn], in0=pr[:, 0:tn], in1=S[:, t0:t0 + tn], op=ALU.mult), reads=[prb, brope], writes=[b0])
            P.op('pool', lambda e, s1=s1, s2=s2, t0=t0, tn=tn: e.tensor_tensor(
                out=s1[:, 0:tn], in0=s2[:, 0:tn], in1=C[:, t0:t0 + tn], op=ALU.mult), reads=[b2, brope], writes=[b1])
            P.op('dve', lambda e, s0=s0, s1=s1, t0=t0, tn=tn: e.tensor_tensor(
                out=dst[:, t0:t0 + tn], in0=s0[:, 0:tn], in1=s1[:, 0:tn], op=ALU.add), reads=[b0, b1], writes=[bdst])


def attn_params(k, kind):
    pr = {}
    if kind == 'wa':
        pr['gains'] = k.dram("wa_gains", [DEPTH, 128, 2])
        pr['cos'] = k.dram("cosT", [128, 4096])
        pr['sin'] = k.dram("sinT", [128, 4096])
        pr['rm'] = k.dram("rmT", [128, 128])
        pr['sink'] = k.dram("wa_sinkb", [DEPTH, 128, 1])
        pr['mask'] = k.dram("wamask", [128, 256])
    else:
        pr['gains'] = k.dram("na_gains", [DEPTH, 128, 2])
        pr['bias'] = k.dram("na_bias", [DEPTH, 128, 5, 7 * 128])
    return pr


def emit_attn(k, c, kind, l, G, ybuf, bybuf, pr):
    P = k.P
    g_slot = 0 if kind == 'na' else 3
    pre = 'na' if kind == 'na' else 'wa'
    q = k.sb([128, NT]); kk = k.sb([128, NT]); vT = k.sb([128, NT])
    qb = k.sb([128, NT], BF16); kb = k.sb([128, NT], BF16)
    V1 = k.sb([128, NB, 129], BF16)
    g = k.sb([128, 2])
    scr = [k.sb([128, 512]) for _ in range(6)]
    bq, bk, bv, bqb, bkb, bV, bg = (Buf(n) for n in 'q k v qb kb V g'.split())
    bscr = [Buf('scr%d' % i) for i in range(6)]
    bo = [Buf('o%d' % i) for i in range(NB)]
    ones, bones, epst, beps = c.ones, c.bconst, c.eps, c.bconst

    gather_fm(k, c, G, q, bq, pre + '_q')
    gather_fm(k, c, G, kk, bk, pre + '_k')
    gather_fm(k, c, G, vT, bv, pre + '_v')
    P.dma('sp', g, pr['gains'][l], writes=[bg])
    P.op('pool', lambda e: e.memset(V1[:, :, 128:129], 1.0), writes=[bV])
    rope = None
    if kind == 'wa':
        C = k.sb([128, 4096]); S = k.sb([128, 4096]); RmT = k.sb([128, 128])
        sk = k.sb([128, 1]); es = k.sb([128, 1]); msk = k.sb([128, 256], BF16)
        brope, bsk, bes, bmsk = Buf('rope'), Buf('sk'), Buf('es'), Buf('msk')
        P.dma('sp', C, pr['cos'], writes=[brope])
        P.dma('act', S, pr['sin'], writes=[brope])
        P.dma('sp', RmT, pr['rm'], writes=[brope])
        P.dma('sp', sk, pr['sink'][l], writes=[bsk])
        P.dma('pool', msk, pr['mask'], writes=[bmsk])
        P.op('act', lambda e: e.activation(out=es, in_=sk, func=AF.Exp), reads=[bsk], writes=[bes])
        rope = (C, S, RmT, brope)
    else:
        bias = k.sb([128, 5, 7 * 128])
        bbias = Buf('bias')
        P.dma('sp', bias, pr['bias'][l], writes=[bbias])
    for n in range(NB):
        mm_evac(k, 128, 128, vT[:, n * 128:(n + 1) * 128], c.I, [bv, c.bconst], V1[:, n, 0:128], bV, eng_i=n)

    tiles_q = [(i * 512, 512, kind == 'wa') for i in range(8)] + [(4096, 256, False)]
    qknorm(k, (ones, bones, epst, beps, scr, bscr, bq, bqb, bg), q, qb, g[:, 0:1], tiles_q, rope)
    qknorm(k, (ones, bones, epst, beps, scr, bscr, bk, bkb, bg), kk, kb, g[:, 1:2], tiles_q, rope)

    osb = q.rearrange("p (n d) -> p n d", d=128)
    yT = kk
    eA = [k.sb([128, 512], BF16) for _ in range(2)]
    eB = [k.sb([128, 512], BF16) for _ in range(2)]
    tmpf = [k.sb([128, 512]) for _ in range(2)]
    rd = [k.sb([128, 1]) for _ in range(2)]
    beA = [Buf('eA0'), Buf('eA1')]; beB = [Buf('eB0'), Buf('eB1')]
    btmp = [Buf('tf0'), Buf('tf1')]; brd = [Buf('rd0'), Buf('rd1')]

    def smat(pt, pb, ci, ch, n):
        P.op('pe', lambda e: e.matmul(pt[:, ci * 128:(ci + 1) * 128], kb[:, ch * 128:(ch + 1) * 128],
                                      qb[:, n * 128:(n + 1) * 128], start=True, stop=True),
             reads=[bkb, bqb], writes=[pb], pe_acc=(ci > 0))

    def stage1(n):
        if True:
            s = n % 2
            groups = []
            if kind == 'wa':
                if n < 32:
                    A = [n, 32, 33]
                    B = ([n - 1] if n > 0 else []) + ([n + 1] if n < 31 else [])
                    Bm = ([0] if n > 0 else []) + ([1] if n < 31 else [])
                else:
                    A, B, Bm = [32, 33], [], []
                pa, pab = k.nextps()
                for ci, ch in enumerate(A):
                    smat(pa, pab, ci, ch, n)
                P.op('act', lambda e, pa=pa, s=s, w=len(A) * 128: e.activation(
                    out=eA[s][:, 0:w], in_=pa[:, 0:w], func=AF.Exp, scale=SCALE), reads=[pab], writes=[beA[s]])
                groups.append((eA[s], beA[s], A))
                if B:
                    pbt, pbb = k.nextps()
                    for ci, ch in enumerate(B):
                        smat(pbt, pbb, ci, ch, n)
                    w = len(B) * 128
                    P.op('act', lambda e, pbt=pbt, s=s, w=w: e.activation(
                        out=eB[s][:, 0:w], in_=pbt[:, 0:w], func=AF.Exp, scale=SCALE), reads=[pbb], writes=[beB[s]])
                    for ci, mi in enumerate(Bm):
                        P.op('pool', lambda e, s=s, ci=ci, mi=mi: e.tensor_tensor(
                            out=eB[s][:, ci * 128:(ci + 1) * 128], in0=eB[s][:, ci * 128:(ci + 1) * 128],
                            in1=msk[:, mi * 128:(mi + 1) * 128], op=ALU.mult), reads=[beB[s], bmsk], writes=[beB[s]])
                    groups.append((eB[s], beB[s], B))
            else:
                if n < 32:
                    if n == 0:
                        cls, offs = 0, [0, 1, 2, 3]
                    elif n == 1:
                        cls, offs = 1, [-1, 0, 1, 2]
                    elif n == 30:
                        cls, offs = 3, [-2, -1, 0, 1]
                    elif n == 31:
                        cls, offs = 4, [-3, -2, -1, 0]
                    else:
                        cls, offs = 2, [-2, -1, 0, 1, 2]
                    chs = [n + o for o in offs] + [32, 33]
                    G1, G2 = chs[:4], chs[4:]
                    col = 0
                    for (et, ebf, Gc) in ((eA[s], beA[s], G1), (eB[s], beB[s], G2)):
                        pt, pb = k.nextps()
                        for ci, ch in enumerate(Gc):
                            smat(pt, pb, ci, ch, n)
                        w = len(Gc) * 128
                        P.op('dve', lambda e, pt=pt, s=s, w=w, col=col, cls=cls: e.scalar_tensor_tensor(
                            out=tmpf[s][:, 0:w], in0=pt[:, 0:w], scalar=SCALE, in1=bias[:, cls, col:col + w],
                            op0=ALU.mult, op1=ALU.add), reads=[pb, bbias], writes=[btmp[s]])
                        P.op('act', lambda e, et=et, s=s, w=w: e.activation(
                            out=et[:, 0:w], in_=tmpf[s][:, 0:w], func=AF.Exp), reads=[btmp[s]], writes=[ebf])
                        groups.append((et, ebf, Gc))
                        col += w
                else:
                    A = [32, 33]
                    pa, pab = k.nextps()
                    for ci, ch in enumerate(A):
                        smat(pa, pab, ci, ch, n)
                    P.op('act', lambda e, pa=pa, s=s: e.activation(
                        out=eA[s][:, 0:256], in_=pa[:, 0:256], func=AF.Exp, scale=SCALE), reads=[pab], writes=[beA[s]])
                    groups.append((eA[s], beA[s], A))
            st1[n] = groups
            yield

    def stage2(n):
        if True:
            s = n % 2
            groups = st1.pop(n)
            po, pob = k.nextps()
            tot = sum(len(Gc) for _, _, Gc in groups)
            cnt = 0
            for (et, ebf, Gc) in groups:
                for ci, ch in enumerate(Gc):
                    P.op('pe', lambda e, po=po, et=et, ci=ci, ch=ch, cnt=cnt, tot=tot: e.matmul(
                        po[:, 0:129], et[:, ci * 128:(ci + 1) * 128], V1[:, ch, :], start=(cnt == 0), stop=(cnt == tot - 1)),
                        reads=[ebf, bV], writes=[pob], pe_acc=(cnt > 0))
                    cnt += 1
            if kind == 'wa':
                P.op('dve', lambda e, po=po, s=s: e.tensor_scalar(
                    out=rd[s], in0=po[:, 128:129], scalar1=es[:, 0:1], scalar2=None, op0=ALU.add), reads=[pob, bes], writes=[brd[s]])
                P.op('dve', lambda e, s=s: e.reciprocal(out=rd[s], in_=rd[s]), reads=[brd[s]], writes=[brd[s]])
            else:
                P.op('dve', lambda e, po=po, s=s: e.reciprocal(out=rd[s], in_=po[:, 128:129]), reads=[pob], writes=[brd[s]])
            P.op('dve', lambda e, po=po, s=s, n=n: e.tensor_scalar(
                out=osb[:, n, :], in0=po[:, 0:128], scalar1=rd[s][:, 0:1], scalar2=None, op0=ALU.mult),
                reads=[pob, brd[s]], writes=[bo[n], bq])
            yield

    st1 = {}
    for _ in stage1(0):
        pass
    for n in range(NB):
        interleave(([stage1(n + 1)] if n + 1 < NB else []) + [stage2(n)])
    for n in range(NB):
        mm_evac(k, 128, 128, osb[:, n, :], c.I, [bo[n], c.bconst], yT[:, n * 128:(n + 1) * 128], bk, eng_i=n)
    store_y_fm(k, yT, bk, ybuf, bybuf, g_slot, 0, 4096)


def bc_inner(ap2d, n):
    return bass.AP(ap2d.tensor, ap2d.offset, [list(ap2d.ap[0]), list(ap2d.ap[1]), [0, n]])


def bc_mid(ap2d, cnt):
    return bass.AP(ap2d.tensor, ap2d.offset, [list(ap2d.ap[0]), [0, cnt], list(ap2d.ap[1])])


def interleave(gens):
    gens = list(gens)
    while gens:
        for g in list(gens):
            try:
                next(g)
            except StopIteration:
                gens.remove(g)


def cn_of(cp):
    return (3 - cp) if cp < 4 else (4 + 63 - (cp - 4))


def flip_ct(k, c, src, bsrc, dst, bdst, tmp, btmp):
    mm_evac(k, 64, NCH, src, c.I[0:NCH, 0:NCH], [bsrc, c.bconst], tmp[0], btmp[0], eng_i=1)
    mm_evac(k, 64, NCH, c.J64, tmp[0], [c.bconst, btmp[0]], tmp[1], btmp[1], eng_i=1)
    mm_evac(k, NCH, 64, tmp[1], c.I[0:64, 0:64], [btmp[1], c.bconst], dst, bdst, eng_i=1)


def ml_params(k):
    return {'gb': k.dram("ml_gb", [DEPTH, NCH, 4]), 'gn': k.dram("ml_gn", [DEPTH, 64, 128]), 'tri': k.dram("ml_tri", [64, 64])}


def emit_ml(k, c, l, G, ybuf, bybuf, pr):
    P = k.P
    L = NT
    qT = k.sb([64, L]); kT = k.sb([64, L]); kt = k.sb([64, NCH, 64]); V1 = k.sb([64, NCH, 129])
    tmpA = k.sb([128, L])
    hh = [k.sb([64, NCH, 128]) for _ in range(2)]
    ct = [k.sb([NCH, 64]) for _ in range(12)]
    cc = [k.sb([NCH, 1]) for _ in range(4)]
    crow = [k.sb([1, NCH]) for _ in range(8)]
    cols = k.sb([64, 5, NCH]); abc = k.sb([64, 2, NCH])
    ftmp = [k.sb([64, NCH]) for _ in range(2)]
    gb = k.sb([NCH, 4]); tri = k.sb([64, 64])
    one1 = k.sb([1, 64]); onect = k.sb([NCH, 64]); zeroct = k.sb([NCH, 64])
    Cst = k.sb([64, 129])
    stm = [k.sb([64, 64]) for _ in range(2)]; kw = [k.sb([64, 64]) for _ in range(2)]
    nd = [k.sb([64, 129]) for _ in range(2)]; rdn = [k.sb([64, 1]) for _ in range(2)]
    clsb = [k.sb([64, 129]) for _ in range(2)]; bclsb = [Buf('cl0'), Buf('cl1')]
    gn = k.sb([64, 128]); ssq = k.sb([64, NCH])
    I68 = c.I[0:NCH, 0:NCH]
    bI = c.bconst
    bqT, bkT, bkt, bV, bA, bgb, btri, bone1, bC, bgn, bssq, bcols, babc, bconst = (Buf(n) for n in
        'qT kT kt V tmpA gb tri one1 C gn ssq cols abc const'.split())
    bhh = [Buf('hh0'), Buf('hh1')]
    bct = [Buf('ct%d' % i) for i in range(12)]; bcc = [Buf('cc%d' % i) for i in range(4)]
    bcrow = [Buf('crow%d' % i) for i in range(8)]; bftmp = [Buf('ft0'), Buf('ft1')]
    bstm = [Buf('stm0'), Buf('stm1')]; bkw = [Buf('kw0'), Buf('kw1')]; bnd = [Buf('nd0'), Buf('nd1')]
    brdn = [Buf('rdn0'), Buf('rdn1')]

    P.dma('sp', gb, pr['gb'][l], writes=[bgb])
    P.dma('sp', tri, pr['tri'], writes=[btri])
    P.dma('act', gn, pr['gn'][l], writes=[bgn])
    P.op('dve', lambda e: e.memset(one1, 1.0), writes=[bone1])
    P.op('dve', lambda e: e.memset(onect, 1.0), writes=[bconst])
    P.op('dve', lambda e: e.memset(zeroct, 0.0), writes=[bconst])

    def rop(eng, fn, reads, writes):
        P.op(eng, fn, reads=reads, writes=writes)

    def tr_col(src_ap, bsrc, dst_ap, bdst, m, n):
        pt, pb = k.nextps()
        P.op('pe', lambda e: e.matmul(pt[0:n, 0:m], src_ap, c.I[0:m, 0:m], start=True, stop=True), reads=[bsrc, bI], writes=[pb])
        P.op('dve', lambda e: e.tensor_copy(out=dst_ap, in_=pt[0:n, 0:m]), reads=[pb], writes=[bdst])

    for d in range(2):
        if d == 0:
            gather_fm(k, c, G, qT, bqT, 'ml_q', rows=64, lat_off=256, ctx_off=0)
            gather_fm(k, c, G, kT, bkT, 'ml_k', rows=64, lat_off=256, ctx_off=0)
            gather_fm(k, c, G, tmpA, bA, 'ml_v', lat_off=256, ctx_off=0)
            P.op('dve', lambda e: e.tensor_scalar(out=kT, in0=kT, scalar1=0.125, scalar2=None, op0=ALU.mult), reads=[bkT], writes=[bkT])
            P.op('pool', lambda e: e.memset(V1[:, :, 128:129], 1.0), writes=[bV])
            for ch in range(NCH):
                sl = slice(ch * 64, (ch + 1) * 64)
                mm_evac(k, 64, 64, kT[:, sl], c.I[0:64, 0:64], [bkT, bI], kt[:, ch, :], bkt, eng_i=ch)
                mm_evac(k, 64, 128, tmpA[:, sl], c.I, [bA, bI], V1[:, ch, 0:128], bV, eng_i=ch + 1)
        else:
            qtok = tmpA[0:64, :].rearrange("p (a b) -> p a b", b=64)
            for ch in range(NCH):
                sl = slice(ch * 64, (ch + 1) * 64)
                mm_evac(k, 64, 64, qT[:, sl], c.I[0:64, 0:64], [bqT, bI], qtok[:, ch, :], bA, eng_i=ch)
            for cp in range(NCH):
                cn = cn_of(cp)
                sl = slice(cp * 64, (cp + 1) * 64)
                mm_evac(k, 64, 64, qtok[:, cn, :], c.J64, [bA, bI], qT[:, sl], bqT, eng_i=cp)
                mm_evac(k, 64, 64, kt[:, cn, :], c.J64, [bkt, bI], kT[:, sl], bkT, eng_i=cp + 1)
            for cp in range(NCH):
                cn = cn_of(cp)
                if cp > cn:
                    continue
                pairs = [(cp, cn)] if cp == cn else [(cp, cn), (cn, cp)]
                for (tsr, bt_, wdt) in ((kt, bkt, 64), (V1, bV, 128)):
                    pts = []
                    for (dst_c, src_c) in pairs:
                        pt, pb = k.nextps()
                        P.op('pe', lambda e, pt=pt, tsr=tsr, src_c=src_c, wdt=wdt: e.matmul(
                            pt[0:64, 0:wdt], c.J64, tsr[:, src_c, 0:wdt], start=True, stop=True), reads=[bI, bt_], writes=[pb])
                        pts.append((pt, pb, dst_c))
                    for (pt, pb, dst_c) in pts:
                        P.op('act', lambda e, pt=pt, tsr=tsr, dst_c=dst_c, wdt=wdt: e.activation(
                            out=tsr[:, dst_c, 0:wdt], in_=pt[0:64, 0:wdt], func=AF.Copy), reads=[pb], writes=[bt_])
        P.op('pool', lambda e: e.memset(Cst, 0.0), writes=[bC])
        T_ = ct
        if d == 0:
            gather_ct(k, c, G, T_[0], bct[0], 'ml_i0')
            gather_ct(k, c, G, T_[1], bct[1], 'ml_f0')
        else:
            gather_ct(k, c, G, T_[2], bct[2], 'ml_i1')
            flip_ct(k, c, T_[2], bct[2], T_[0], bct[0], ftmp, bftmp)
            gather_ct(k, c, G, T_[2], bct[2], 'ml_f1')
            flip_ct(k, c, T_[2], bct[2], T_[1], bct[1], ftmp, bftmp)
        rop('dve', lambda e, d=d: e.tensor_scalar(out=T_[1], in0=T_[1], scalar1=gb[:, 2 + d:3 + d], scalar2=None, op0=ALU.add),
            [bct[1], bgb], [bct[1]])
        rop('act', lambda e: e.activation(out=T_[2], in_=T_[1], func=AF.Exp, scale=-1.0), [bct[1]], [bct[2]])
        rop('act', lambda e: e.activation(out=T_[2], in_=T_[2], func=AF.Ln, bias=onect[:, 0:1]), [bct[2], bconst], [bct[2]])
        rop('dve', lambda e: e.tensor_scalar(out=T_[1], in0=T_[2], scalar1=-1.0, scalar2=None, op0=ALU.mult), [bct[2]], [bct[1]])
        rop('dve', lambda e: e.tensor_tensor_scan(out=T_[2], data0=onect, data1=T_[1], initial=0.0, op0=ALU.mult, op1=ALU.add),
            [bct[1], bconst], [bct[2]])
        rop('dve', lambda e, d=d: e.scalar_tensor_tensor(out=T_[3], in0=T_[0], scalar=gb[:, d:d + 1], in1=T_[2],
                                                          op0=ALU.add, op1=ALU.subtract), [bct[0], bgb, bct[2]], [bct[3]])
        rop('dve', lambda e: e.tensor_tensor_scan(out=T_[4], data0=zeroct, data1=T_[3], initial=-1e30, op0=ALU.add, op1=ALU.max),
            [bct[3], bconst], [bct[4]])
        rop('dve', lambda e: e.tensor_copy(out=cc[0], in_=T_[2][:, 63:64]), [bct[2]], [bcc[0]])
        rop('dve', lambda e: e.tensor_copy(out=cc[1], in_=T_[4][:, 63:64]), [bct[4]], [bcc[1]])
        rop('dve', lambda e: e.tensor_tensor(out=cc[2], in0=cc[0], in1=cc[1], op=ALU.add), [bcc[0], bcc[1]], [bcc[2]])
        CR = crow
        tr_col(cc[0], bcc[0], CR[0], bcrow[0], NCH, 1)
        tr_col(cc[2], bcc[2], CR[2], bcrow[2], NCH, 1)
        rop('dve', lambda e: e.tensor_tensor_scan(out=CR[3], data0=CR[0], data1=CR[2], initial=0.0, op0=ALU.add, op1=ALU.max),
            [bcrow[0], bcrow[2]], [bcrow[3]])
        rop('dve', lambda e: e.memset(CR[4][:, 0:1], 0.0), [], [bcrow[4]])
        rop('dve', lambda e: e.tensor_copy(out=CR[4][:, 1:NCH], in_=CR[3][:, 0:NCH - 1]), [bcrow[3]], [bcrow[4]])
        rop('dve', lambda e: e.tensor_tensor(out=CR[5], in0=CR[0], in1=CR[4], op=ALU.add), [bcrow[0], bcrow[4]], [bcrow[5]])
        rop('dve', lambda e: e.tensor_tensor(out=CR[5], in0=CR[5], in1=CR[3], op=ALU.subtract), [bcrow[5], bcrow[3]], [bcrow[5]])
        rop('act', lambda e: e.activation(out=CR[5], in_=CR[5], func=AF.Exp), [bcrow[5]], [bcrow[5]])
        rop('dve', lambda e: e.tensor_tensor(out=CR[6], in0=CR[2], in1=CR[3], op=ALU.subtract), [bcrow[2], bcrow[3]], [bcrow[6]])
        rop('act', lambda e: e.activation(out=CR[6], in_=CR[6], func=AF.Exp), [bcrow[6]], [bcrow[6]])
        pt, pb = k.nextps()
        P.op('pe', lambda e, pt=pt: e.matmul(pt[0:NCH, 0:1], CR[4][0:1, :], one1[0:1, 0:1], start=True, stop=True),
             reads=[bcrow[4], bone1], writes=[pb])
        P.op('dve', lambda e, pt=pt: e.tensor_copy(out=cc[3], in_=pt[0:NCH, 0:1]), reads=[pb], writes=[bcc[3]])
        rop('dve', lambda e: e.tensor_scalar(out=T_[5], in0=T_[4], scalar1=cc[3][:, 0:1], scalar2=None, op0=ALU.max), [bct[4], bcc[3]], [bct[5]])
        rop('act', lambda e: e.activation(out=T_[6], in_=T_[3], func=AF.Exp), [bct[3]], [bct[6]])
        rop('act', lambda e: e.activation(out=T_[7], in_=T_[5], func=AF.Exp, scale=-1.0), [bct[5]], [bct[7]])
        rop('dve', lambda e: e.tensor_scalar(out=T_[8], in0=T_[5], scalar1=cc[3][:, 0:1], scalar2=None, op0=ALU.subtract), [bct[5], bcc[3]], [bct[8]])
        rop('act', lambda e: e.activation(out=T_[8], in_=T_[8], func=AF.Exp, scale=-1.0), [bct[8]], [bct[8]])
        rop('dve', lambda e: e.tensor_tensor(out=T_[9], in0=T_[2], in1=T_[5], op=ALU.add), [bct[2], bct[5]], [bct[9]])
        rop('act', lambda e: e.activation(out=T_[9], in_=T_[9], func=AF.Exp, scale=-1.0), [bct[9]], [bct[9]])
        rop('dve', lambda e: e.tensor_scalar(out=T_[10], in0=T_[3], scalar1=cc[1][:, 0:1], scalar2=None, op0=ALU.subtract), [bct[3], bcc[1]], [bct[10]])
        rop('act', lambda e: e.activation(out=T_[10], in_=T_[10], func=AF.Exp), [bct[10]], [bct[10]])
        for qi, ti in enumerate([6, 7, 8, 9, 10]):
            tr_col(T_[ti], bct[ti], cols[:, qi, :], bcols, NCH, 64)
        for qi, ci in enumerate([5, 6]):
            pt, pb = k.nextps()
            P.op('pe', lambda e, pt=pt, ci=ci: e.matmul(pt[0:64, 0:NCH], one1[0:1, 0:64], CR[ci][0:1, :], start=True, stop=True),
                 reads=[bcrow[ci], bone1], writes=[pb])
            P.op('dve', lambda e, pt=pt, qi=qi: e.tensor_copy(out=abc[:, qi, :], in_=pt[0:64, 0:NCH]), reads=[pb], writes=[babc])
        def pre(ch):
            s = ch % 2
            sl = slice(ch * 64, (ch + 1) * 64)
            pS, pSb = k.nextps()
            P.op('pe', lambda e: e.matmul(pS[0:64, 0:64], kT[:, sl], qT[:, sl], start=True, stop=True), reads=[bkT, bqT], writes=[pSb])
            P.op('dve', lambda e: e.scalar_tensor_tensor(out=stm[s], in0=pS[0:64, 0:64], scalar=cols[:, 0, ch:ch + 1], in1=tri,
                                                         op0=ALU.mult, op1=ALU.mult), reads=[pSb, bcols, btri], writes=[bstm[s]])
            P.op('pool', lambda e: e.tensor_scalar(out=kw[s], in0=kt[:, ch, :], scalar1=cols[:, 4, ch:ch + 1], scalar2=None, op0=ALU.mult),
                 reads=[bkt, bcols], writes=[bkw[s]])
            yield
            pA, pAb = k.nextps()
            P.op('pe', lambda e: e.matmul(pA[0:64, 0:129], stm[s], V1[:, ch, :], start=True, stop=True), reads=[bstm[s], bV], writes=[pAb])
            pC, pCb = k.nextps()
            P.op('pe', lambda e: e.matmul(pC[0:64, 0:129], kw[s], V1[:, ch, :], start=True, stop=True), reads=[bkw[s], bV], writes=[pCb])
            yield
            P.op('dve', lambda e: e.tensor_scalar(out=nd[s], in0=pA[0:64, 0:129], scalar1=cols[:, 1, ch:ch + 1], scalar2=None, op0=ALU.mult),
                 reads=[pAb, bcols], writes=[bnd[s]])
            P.op('act', lambda e: e.activation(out=clsb[s], in_=pC[0:64, 0:129], func=AF.Copy), reads=[pCb], writes=[bclsb[s]])
            yield

        def post(ch, d=d):
            s = ch % 2
            sl = slice(ch * 64, (ch + 1) * 64)
            pB, pBb = k.nextps()
            P.op('pe', lambda e: e.matmul(pB[0:64, 0:129], qT[:, sl], Cst, start=True, stop=True), reads=[bqT, bC], writes=[pBb])
            yield
            P.op('dve', lambda e: e.tensor_scalar(out=Cst, in0=Cst, scalar1=abc[:, 0, ch:ch + 1], scalar2=None, op0=ALU.mult),
                 reads=[bC, babc], writes=[bC])
            P.op('dve', lambda e: e.scalar_tensor_tensor(out=Cst, in0=clsb[s], scalar=abc[:, 1, ch:ch + 1], in1=Cst, op0=ALU.mult, op1=ALU.add),
                 reads=[bclsb[s], babc, bC], writes=[bC])
            yield
            P.op('dve', lambda e: e.scalar_tensor_tensor(out=nd[s], in0=pB[0:64, 0:129], scalar=cols[:, 2, ch:ch + 1], in1=nd[s],
                                                         op0=ALU.mult, op1=ALU.add), reads=[pBb, bcols, bnd[s]], writes=[bnd[s]])
            P.op('dve', lambda e: e.scalar_tensor_tensor(out=rdn[s], in0=nd[s][:, 128:129], scalar=-1.0, in1=nd[s][:, 128:129],
                                                         op0=ALU.mult, op1=ALU.max), reads=[bnd[s]], writes=[brdn[s]])
            yield
            P.op('dve', lambda e: e.tensor_scalar(out=rdn[s], in0=rdn[s], scalar1=cols[:, 3, ch:ch + 1], scalar2=None, op0=ALU.max),
                 reads=[brdn[s], bcols], writes=[brdn[s]])
            P.op('dve', lambda e: e.reciprocal(out=rdn[s], in_=rdn[s]), reads=[brdn[s]], writes=[brdn[s]])
            P.op('dve', lambda e: e.tensor_scalar(out=hh[d][:, ch, :], in0=nd[s][:, 0:128], scalar1=rdn[s][:, 0:1], scalar2=None, op0=ALU.mult),
                 reads=[bnd[s], brdn[s]], writes=[bhh[d]])
            yield

        for _ in pre(0):
            pass
        for ch in range(NCH):
            interleave([post(ch)] + ([pre(ch + 1)] if ch + 1 < NCH else []))
    for cn in range(NCH):
        cp = cn_of(cn)
        pt, pb = k.nextps()
        P.op('pe', lambda e, pt=pt, cp=cp: e.matmul(pt[0:64, 0:128], c.J64, hh[1][:, cp, :], start=True, stop=True),
             reads=[bI, bhh[1]], writes=[pb])
        P.op('dve', lambda e, pt=pt, cn=cn: e.tensor_tensor(out=hh[0][:, cn, :], in0=pt[0:64, 0:128], in1=hh[0][:, cn, :], op=ALU.add),
             reads=[pb, bhh[0]], writes=[bhh[0]])
    og = V1
    gather_fm(k, c, G, tmpA, bA, 'ml_o', lat_off=256, ctx_off=0)
    for ch in range(NCH):
        mm_evac(k, 64, 128, tmpA[:, ch * 64:(ch + 1) * 64], c.I, [bA, bI], og[:, ch, 0:128], bV, eng_i=ch)
    Y = hh[0]
    T = hh[1]
    P.op('pool', lambda e: e.tensor_tensor(out=T, in0=Y, in1=Y, op=ALU.mult), reads=[bhh[0], bhh[1]], writes=[bhh[1]])
    P.op('dve', lambda e: e.tensor_reduce(out=ssq, in_=T, axis=AX.X, op=ALU.add), reads=[bhh[1]], writes=[bssq])
    P.op('act', lambda e: e.activation(out=ssq, in_=ssq, func=AF.Sqrt, bias=c.eps[0:64, :], scale=1.0 / 128), reads=[bssq, bI], writes=[bssq])
    P.op('dve', lambda e: e.reciprocal(out=ssq, in_=ssq), reads=[bssq], writes=[bssq])
    P.op('dve', lambda e: e.tensor_tensor(out=Y, in0=Y, in1=bc_inner(ssq, 128), op=ALU.mult), reads=[bhh[0], bssq], writes=[bhh[0]])
    P.op('pool', lambda e: e.tensor_tensor(out=Y, in0=Y, in1=bc_mid(gn, NCH), op=ALU.mult), reads=[bhh[0], bgn], writes=[bhh[0]])
    P.op('act', lambda e: e.activation(out=og[:, :, 0:128], in_=og[:, :, 0:128], func=AF.Sigmoid), reads=[bV], writes=[bV])
    P.op('dve', lambda e: e.tensor_tensor(out=Y, in0=Y, in1=og[:, :, 0:128], op=ALU.mult), reads=[bhh[0], bV], writes=[bhh[0]])
    for ch in range(NCH):
        mm_evac(k, 128, 64, Y[:, ch, :], c.I[0:64, 0:64], [bhh[0], bI], tmpA[:, ch * 64:(ch + 1) * 64], bA, eng_i=ch)
    store_y_fm(k, tmpA, bA, ybuf, bybuf, 2, 256, 0)


def dn_params(k):
    return {'cw': k.dram("dn_cw", [DEPTH, 2, 128, 3, 5]), 'sc': k.dram("dn_sc", [DEPTH, NCH, 4]),
            'gn': k.dram("dn_gn", [DEPTH, 64, 128]), 'mk': k.dram("dn_masks", [64, 2, 64])}


def emit_dn(k, cm, l, G, ybuf, bybuf, pr):
    P = k.P
    L = NT
    SEGS = [(0, 256), (256, 4096)]
    big1 = k.sb([128, 2 * L]); big2 = k.sb([128, 2 * L])
    X = [big2[:, 0:L], big2[:, L:2 * L], k.sb([128, L])]
    acc = k.sb([128, L])
    qd = big1[:, 0:L]; kd = big1[:, L:2 * L]
    DmT = k.sb([64, NCH, 64]); NB_ = k.sb([64, NCH, 64])
    O = k.sb([64, NCH, 128])
    cw = k.sb([128, 3, 5]); sc = k.sb([NCH, 4]); mk = k.sb([64, 2, 64])
    I128 = cm.I; J64 = cm.J64; ones = cm.ones; epst = cm.eps
    one1 = k.sb([1, 128]); onect = k.sb([NCH, 64])
    ct = [k.sb([NCH, 64]) for _ in range(6)]
    cc = [k.sb([NCH, 1]) for _ in range(3)]
    crow = k.sb([1, NCH])
    cols = k.sb([64, 4, NCH])
    eglb = k.sb([128, NCH])
    S = k.sb([128, 128])
    scr = [k.sb([128, 512]) for _ in range(4)]
    ttok = [k.sb([128, 128]) for _ in range(2)]
    ftmp = [k.sb([64, NCH]) for _ in range(2)]
    Qb = [[k.sb([64, 64]) for _ in range(2)] for _ in range(3)]
    QTb = [[k.sb([64, 64]) for _ in range(2)] for _ in range(3)]
    R = [k.sb([64, 64]) for _ in range(3)]
    QKD = [k.sb([64, 64]) for _ in range(3)]
    vtok = [k.sb([64, 128]) for _ in range(3)]
    kend = [k.sb([64, 128]) for _ in range(3)]
    z = [k.sb([64, 128]) for _ in range(3)]
    vnew = [k.sb([64, 128]) for _ in range(3)]
    o1c = [k.sb([64, 128]) for _ in range(3)]
    gn = k.sb([64, 128]); ssq = k.sb([64, NCH])

    bX = [Buf('xq'), Buf('xk'), Buf('xv')]
    (bacc, bqd, bkd, bDm, bNB, bO, bcw, bsc, bmk, bone1, bconst, bcrow, bcols, beglb, bS, bgn, bssq) = (
        Buf(n) for n in 'acc qd kd Dm NB O cw sc mk one1 const crow cols eglb S gn ssq'.split())
    bI = cm.bconst; bJ = cm.bconst; bones = cm.bconst; beps = cm.bconst
    bct = [Buf('ct%d' % i) for i in range(6)]
    bcc = [Buf('cc%d' % i) for i in range(3)]
    bscr = [Buf('scr%d' % i) for i in range(4)]
    bttok = [Buf('tt0'), Buf('tt1')]; bftmp = [Buf('ft0'), Buf('ft1')]
    bQ = [[Buf('Q%d%d' % (a, b)) for b in range(2)] for a in range(3)]
    bQT = [[Buf('QT%d%d' % (a, b)) for b in range(2)] for a in range(3)]
    bR, bQKD, bvtok, bkend, bz, bvnew, bo1c = ([Buf('%s%d' % (n_, i)) for i in range(3)] for n_ in ('R', 'QKD', 'vt', 'ke', 'z', 'vn', 'o1c'))

    P.dma('sp', sc, pr['sc'][l], writes=[bsc])
    P.dma('sp', mk, pr['mk'], writes=[bmk])
    P.dma('sp', gn, pr['gn'][l], writes=[bgn])
    P.op('dve', lambda e: e.memset(one1, 1.0), writes=[bone1])
    P.op('dve', lambda e: e.memset(onect, 1.0), writes=[bconst])

    def tr_col(src_ap, bsrc, dst_ap, bdst, m, n):
        pt, pb = k.nextps()
        P.op('pe', lambda e: e.matmul(pt[0:n, 0:m], src_ap, I128[0:m, 0:m], start=True, stop=True), reads=[bsrc, bI], writes=[pb])
        P.op('dve', lambda e: e.tensor_copy(out=dst_ap, in_=pt[0:n, 0:m]), reads=[pb], writes=[bdst])

    tiles = [(i * 512, 512) for i in range(8)] + [(4096, 256)]

    for d in range(2):
        P.dma('sp', cw, pr['cw'][l][d], writes=[bcw])
        if d == 0:
            for t, nm in enumerate(('dn_q', 'dn_k', 'dn_v')):
                gather_fm(k, cm, G, X[t], bX[t], nm, lat_off=256, ctx_off=0)
            gather_ct(k, cm, G, ct[0], bct[0], 'dn_b0')
            gather_ct(k, cm, G, ct[1], bct[1], 'dn_a0')
        else:
            for t, nm in enumerate(('dn_q', 'dn_k', 'dn_v')):
                gather_fm(k, cm, G, acc, bacc, nm, lat_off=256, ctx_off=0)
                for blk in range(NB):
                    bp = (1 - blk) if blk < 2 else (35 - blk)
                    s_ = blk % 2
                    mm_evac(k, 128, 128, acc[:, blk * 128:(blk + 1) * 128], I128, [bacc, bI], ttok[s_], bttok[s_], eng_i=blk)
                    mm_evac(k, 128, 128, ttok[s_], cm.J, [bttok[s_], bI], X[t][:, bp * 128:(bp + 1) * 128], bX[t], eng_i=blk + 1)
            gather_ct(k, cm, G, ct[5], bct[5], 'dn_b1')
            flip_ct(k, cm, ct[5], bct[5], ct[0], bct[0], ftmp, bftmp)
            gather_ct(k, cm, G, ct[5], bct[5], 'dn_a1')
            flip_ct(k, cm, ct[5], bct[5], ct[1], bct[1], ftmp, bftmp)
        P.op('pool', lambda e: e.memset(S[:], 0.0), writes=[bS])
        for t in range(3):
            for (s0, sn) in SEGS:
                P.op('dve', lambda e, t=t, s0=s0, sn=sn: e.tensor_scalar(
                    out=acc[:, s0:s0 + sn], in0=X[t][:, s0:s0 + sn], scalar1=cw[:, t, 2:3], scalar2=None, op0=ALU.mult),
                    reads=[bX[t], bcw], writes=[bacc])
                for tap in (0, 1, 3, 4):
                    sh = tap - 2
                    a0 = s0 + max(0, -sh)
                    a1 = s0 + sn - max(0, sh)
                    P.op('dve', lambda e, t=t, tap=tap, sh=sh, a0=a0, a1=a1: e.scalar_tensor_tensor(
                        out=acc[:, a0:a1], in0=X[t][:, a0 + sh:a1 + sh], scalar=cw[:, t, tap:tap + 1], in1=acc[:, a0:a1],
                        op0=ALU.mult, op1=ALU.add), reads=[bX[t], bcw, bacc], writes=[bacc])
            P.op('act', lambda e, t=t: e.activation(out=X[t], in_=acc[:], func=AF.Silu), reads=[bacc], writes=[bX[t]])
        for t in range(2):
            for i, (t0, tn) in enumerate(tiles):
                s0, s1 = scr[(2 * i) % 4], scr[(2 * i + 1) % 4]
                b0, b1 = bscr[(2 * i) % 4], bscr[(2 * i + 1) % 4]
                P.op('act', lambda e, t=t, s0=s0, t0=t0, tn=tn: e.activation(out=s0[:, 0:tn], in_=X[t][:, t0:t0 + tn], func=AF.Square),
                     reads=[bX[t]], writes=[b0])
                pt, pb = k.nextps()
                P.op('pe', lambda e, pt=pt, s0=s0, tn=tn: e.matmul(pt[:, 0:tn], ones[:], s0[:, 0:tn], start=True, stop=True),
                     reads=[bones, b0], writes=[pb])
                P.op('act', lambda e, pt=pt, s1=s1, tn=tn: e.activation(out=s1[:, 0:tn], in_=pt[:, 0:tn], func=AF.Sqrt, bias=epst[:], scale=1.0),
                     reads=[pb, beps], writes=[b1])
                P.op('dve', lambda e, s1=s1, tn=tn: e.reciprocal(out=s1[:, 0:tn], in_=s1[:, 0:tn]), reads=[b1], writes=[b1])
                sc_ = (128 ** -0.5) if t == 0 else 1.0
                P.op('dve', lambda e, t=t, s1=s1, t0=t0, tn=tn, sc_=sc_: e.scalar_tensor_tensor(
                    out=X[t][:, t0:t0 + tn], in0=X[t][:, t0:t0 + tn], scalar=sc_, in1=s1[:, 0:tn], op0=ALU.mult, op1=ALU.mult),
                    reads=[bX[t], b1], writes=[bX[t]])
        P.op('act', lambda e: e.activation(out=ct[0][:], in_=ct[0][:], func=AF.Sigmoid), reads=[bct[0]], writes=[bct[0]])
        P.op('act', lambda e, d=d: e.activation(out=cc[0][:], in_=sc[:, d:d + 1], func=AF.Exp), reads=[bsc], writes=[bcc[0]])
        P.op('dve', lambda e: e.tensor_scalar(out=cc[0][:], in0=cc[0][:], scalar1=-1.0, scalar2=None, op0=ALU.mult), reads=[bcc[0]], writes=[bcc[0]])
        P.op('act', lambda e, d=d: e.activation(out=ct[1][:], in_=ct[1][:], func=AF.Exp, bias=sc[:, 2 + d:3 + d]), reads=[bct[1], bsc], writes=[bct[1]])
        P.op('act', lambda e: e.activation(out=ct[1][:], in_=ct[1][:], func=AF.Ln, bias=onect[:, 0:1]), reads=[bct[1], bconst], writes=[bct[1]])
        P.op('dve', lambda e: e.tensor_scalar(out=ct[1][:], in0=ct[1][:], scalar1=cc[0][:, 0:1], scalar2=None, op0=ALU.mult),
             reads=[bct[1], bcc[0]], writes=[bct[1]])
        P.op('dve', lambda e: e.tensor_tensor_scan(out=ct[2][:], data0=onect[:], data1=ct[1][:], initial=0.0, op0=ALU.mult, op1=ALU.add),
             reads=[bct[1], bconst], writes=[bct[2]])
        P.op('dve', lambda e: e.tensor_copy(out=cc[1][:], in_=ct[2][:, 63:64]), reads=[bct[2]], writes=[bcc[1]])
        P.op('dve', lambda e: e.tensor_scalar(out=ct[3][:], in0=ct[2][:], scalar1=cc[1][:, 0:1], scalar2=None, op0=ALU.subtract),
             reads=[bct[2], bcc[1]], writes=[bct[3]])
        P.op('act', lambda e: e.activation(out=ct[3][:], in_=ct[3][:], func=AF.Exp, scale=-1.0), reads=[bct[3]], writes=[bct[3]])
        P.op('dve', lambda e: e.tensor_scalar(out=ct[4][:], in0=ct[0][:], scalar1=-1.0, scalar2=None, op0=ALU.mult), reads=[bct[0]], writes=[bct[4]])
        for qi, ti in enumerate([2, 0, 4, 3]):
            tr_col(ct[ti][:], bct[ti], cols[:, qi, :], bcols, NCH, 64)
        tr_col(cc[1][:], bcc[1], crow[:], bcrow, NCH, 1)
        P.op('act', lambda e: e.activation(out=crow[:], in_=crow[:], func=AF.Exp), reads=[bcrow], writes=[bcrow])
        pt, pb = k.nextps()
        P.op('pe', lambda e, pt=pt: e.matmul(pt[:, 0:NCH], one1[0:1, :], crow[0:1, :], start=True, stop=True), reads=[bone1, bcrow], writes=[pb])
        P.op('dve', lambda e, pt=pt: e.tensor_copy(out=eglb[:], in_=pt[:, 0:NCH]), reads=[pb], writes=[beglb])
        P.dma('sp', acc[0:1, :].rearrange("o (c i) -> o c i", i=64), ct[2], reads=[bct[2]], writes=[bacc])
        for i, (t0, tn) in enumerate(tiles):
            nck = tn // 64
            c0 = t0 // 64
            s0, s1 = scr[(2 * i) % 4], scr[(2 * i + 1) % 4]
            b0, b1 = bscr[(2 * i) % 4], bscr[(2 * i + 1) % 4]
            pt, pb = k.nextps()
            P.op('pe', lambda e, pt=pt, t0=t0, tn=tn: e.matmul(pt[:, 0:tn], one1[0:1, :], acc[0:1, t0:t0 + tn], start=True, stop=True),
                 reads=[bone1, bacc], writes=[pb])
            P.op('act', lambda e, pt=pt, s0=s0, tn=tn: e.activation(out=s0[:, 0:tn], in_=pt[:, 0:tn], func=AF.Exp), reads=[pb], writes=[b0])
            P.op('dve', lambda e, s0=s0, t0=t0, tn=tn: e.tensor_tensor(out=qd[:, t0:t0 + tn], in0=X[0][:, t0:t0 + tn], in1=s0[:, 0:tn], op=ALU.mult),
                 reads=[bX[0], b0], writes=[bqd])
            P.op('pool', lambda e, s0=s0, t0=t0, tn=tn: e.tensor_tensor(out=kd[:, t0:t0 + tn], in0=X[1][:, t0:t0 + tn], in1=s0[:, 0:tn], op=ALU.mult),
                 reads=[bX[1], b0], writes=[bkd])
            d3 = s1[0:64, 0:tn].rearrange("p (c i) -> p c i", i=64)
            p3 = pt[0:64, 0:tn].rearrange("p (c i) -> p c i", i=64)
            P.op('dve', lambda e, d3=d3, p3=p3, c0=c0, nck=nck: e.tensor_tensor(out=d3, in0=p3, in1=bc_inner(cols[:, 0, c0:c0 + nck], 64), op=ALU.subtract),
                 reads=[pb, bcols], writes=[b1])
            P.op('dve', lambda e, d3=d3, nck=nck: e.tensor_tensor(out=d3, in0=d3, in1=bc_mid(mk[:, 0, :], nck), op=ALU.add),
                 reads=[b1, bmk], writes=[b1])
            P.op('act', lambda e, d3=d3, c0=c0, nck=nck: e.activation(out=DmT[:, c0:c0 + nck, :], in_=d3, func=AF.Exp), reads=[b1], writes=[bDm])
            P.op('dve', lambda e, c0=c0, nck=nck: e.tensor_tensor(out=NB_[:, c0:c0 + nck, :], in0=DmT[:, c0:c0 + nck, :], in1=bc_mid(mk[:, 1, :], nck), op=ALU.mult),
                 reads=[bDm, bmk], writes=[bNB])
            P.op('dve', lambda e, c0=c0, nck=nck: e.tensor_tensor(out=NB_[:, c0:c0 + nck, :], in0=NB_[:, c0:c0 + nck, :], in1=bc_inner(cols[:, 2, c0:c0 + nck], 64), op=ALU.mult),
                 reads=[bNB, bcols], writes=[bNB])

        def prep(c):
            a = c % 3
            sl = slice(c * 64, (c + 1) * 64)
            Q, QT = Qb[a], QTb[a]
            bq, bqt = bQ[a], bQT[a]
            pt, pb = k.nextps()
            P.op('pe', lambda e: e.matmul(pt[0:64, 0:64], X[1][:, sl], X[1][:, sl], start=True, stop=True), reads=[bX[1]], writes=[pb])
            P.op('dve', lambda e: e.tensor_tensor(out=Q[0][:], in0=pt[0:64, 0:64], in1=NB_[:, c, :], op=ALU.mult), reads=[pb, bNB], writes=[bq[0]])
            yield
            pt2, pb2 = k.nextps()
            P.op('pe', lambda e: e.matmul(pt2[0:64, 0:64], Q[0][:], I128[0:64, 0:64], start=True, stop=True), reads=[bq[0], bI], writes=[pb2])
            P.op('act', lambda e: e.activation(out=QT[0][:], in_=pt2[0:64, 0:64], func=AF.Copy), reads=[pb2], writes=[bqt[0]])
            P.op('pool', lambda e: e.tensor_tensor(out=R[a][:], in0=Q[0][:], in1=I128[0:64, 0:64], op=ALU.add), reads=[bq[0], bI], writes=[bR[a]])
            yield
            cur = 0
            for it in range(1, 6):
                nx = 1 - cur
                pq, pqb = k.nextps()
                P.op('pe', lambda e, pq=pq, cur=cur: e.matmul(pq[0:64, 0:64], Q[cur][:], QT[cur][:], start=True, stop=True),
                     reads=[bq[cur], bqt[cur]], writes=[pqb])
                if it < 5:
                    pq2, pq2b = k.nextps()
                    P.op('pe', lambda e, pq2=pq2, cur=cur: e.matmul(pq2[0:64, 0:64], QT[cur][:], Q[cur][:], start=True, stop=True),
                         reads=[bq[cur], bqt[cur]], writes=[pq2b])
                P.op('act', lambda e, pq=pq, nx=nx: e.activation(out=QT[nx][:], in_=pq[0:64, 0:64], func=AF.Copy), reads=[pqb], writes=[bqt[nx]])
                if it < 5:
                    P.op('dve', lambda e, pq2=pq2, nx=nx: e.tensor_copy(out=Q[nx][:], in_=pq2[0:64, 0:64]), reads=[pq2b], writes=[bq[nx]])
                yield
                pr, prb = k.nextps()
                P.op('pe', lambda e, pr=pr, nx=nx: e.matmul(pr[0:64, 0:64], QT[nx][:], R[a][:], start=True, stop=True),
                     reads=[bqt[nx], bR[a]], writes=[prb])
                P.op('dve', lambda e, pr=pr: e.tensor_tensor(out=R[a][:], in0=pr[0:64, 0:64], in1=R[a][:], op=ALU.add), reads=[prb, bR[a]], writes=[bR[a]])
                yield
                cur = nx
            p1, p1b = k.nextps()
            P.op('pe', lambda e: e.matmul(p1[0:64, 0:64], X[1][:, sl], X[0][:, sl], start=True, stop=True), reads=[bX[0], bX[1]], writes=[p1b])
            P.op('dve', lambda e: e.tensor_tensor(out=QKD[a][:], in0=p1[0:64, 0:64], in1=DmT[:, c, :], op=ALU.mult), reads=[p1b, bDm], writes=[bQKD[a]])
            yield
            p2, p2b = k.nextps()
            P.op('pe', lambda e: e.matmul(p2[0:64, 0:128], X[2][:, sl], I128[:], start=True, stop=True), reads=[bX[2], bI], writes=[p2b])
            P.op('act', lambda e: e.activation(out=vtok[a][:], in_=p2[0:64, 0:128], func=AF.Copy), reads=[p2b], writes=[bvtok[a]])
            p3_, p3b = k.nextps()
            P.op('pe', lambda e: e.matmul(p3_[0:64, 0:128], X[1][:, sl], I128[:], start=True, stop=True), reads=[bX[1], bI], writes=[p3b])
            P.op('dve', lambda e: e.tensor_scalar(out=kend[a][:], in0=p3_[0:64, 0:128], scalar1=cols[:, 3, c:c + 1], scalar2=None, op0=ALU.mult),
                 reads=[p3b, bcols], writes=[bkend[a]])
            yield

        def seq(c):
            a = c % 3
            sl = slice(c * 64, (c + 1) * 64)
            pk, pkb = k.nextps()
            P.op('pe', lambda e: e.matmul(pk[0:64, 0:128], kd[:, sl], S[:], start=True, stop=True), reads=[bkd, bS], writes=[pkb])
            po, pob = k.pst[6 + c % 2], k.psb[6 + c % 2]
            P.op('pe', lambda e: e.matmul(po[0:64, 0:128], qd[:, sl], S[:], start=True, stop=False), reads=[bqd, bS], writes=[pob])
            yield
            P.op('dve', lambda e: e.tensor_tensor(out=z[a][:], in0=vtok[a][:], in1=pk[0:64, 0:128], op=ALU.subtract),
                 reads=[bvtok[a], pkb], writes=[bz[a]])
            yield
            ptz, ptzb = k.nextps()
            P.op('pe', lambda e: e.matmul(ptz[0:64, 0:128], R[a][:], z[a][:], start=True, stop=True), reads=[bR[a], bz[a]], writes=[ptzb])
            yield
            P.op('dve', lambda e: e.tensor_scalar(out=vnew[a][:], in0=ptz[0:64, 0:128], scalar1=cols[:, 1, c:c + 1], scalar2=None, op0=ALU.mult),
                 reads=[ptzb, bcols], writes=[bvnew[a]])
            yield
            psu, psub = k.nextps()
            P.op('pe', lambda e: e.matmul(psu[:, 0:128], kend[a][:], vnew[a][:], start=True, stop=True), reads=[bkend[a], bvnew[a]], writes=[psub])
            P.op('pe', lambda e: e.matmul(po[0:64, 0:128], QKD[a][:], vnew[a][:], start=False, stop=True), reads=[bQKD[a], bvnew[a]], writes=[pob],
                 pe_acc=True)
            yield
            P.op('dve', lambda e: e.scalar_tensor_tensor(out=S[:], in0=S[:], scalar=eglb[:, c:c + 1], in1=psu[:, 0:128], op0=ALU.mult, op1=ALU.add),
                 reads=[bS, beglb, psub], writes=[bS])
            if d == 0:
                P.op('act', lambda e: e.activation(out=O[:, c, :], in_=po[0:64, 0:128], func=AF.Copy), reads=[pob], writes=[bO])
            else:
                cn = (3 - c) if c < 4 else (4 + 63 - (c - 4))
                P.op('act', lambda e: e.activation(out=o1c[a][:], in_=po[0:64, 0:128], func=AF.Copy), reads=[pob], writes=[bo1c[a]])
                pj, pjb = k.nextps()
                P.op('pe', lambda e: e.matmul(pj[0:64, 0:128], J64, o1c[a][:], start=True, stop=True), reads=[bJ, bo1c[a]], writes=[pjb])
                P.op('dve', lambda e: e.tensor_tensor(out=O[:, cn, :], in0=pj[0:64, 0:128], in1=O[:, cn, :], op=ALU.add),
                     reads=[pjb, bO], writes=[bO])
            yield

        k.nring = 6
        for _ in prep(0):
            pass
        preps = {}
        if NCH > 1:
            preps[1] = prep(1)
        for c in range(NCH):
            if c + 2 < NCH:
                preps[c + 2] = prep(c + 2)
            sg = seq(c)
            live = [sg] + [preps[i] for i in (c + 1, c + 2) if i in preps]
            must = [sg] + ([preps[c + 1]] if (c + 1) in preps else [])
            while must:
                for g in list(live):
                    try:
                        next(g)
                    except StopIteration:
                        live.remove(g)
                        if g in must:
                            must.remove(g)
            preps.pop(c + 1, None)


    k.nring = 8
    gate = big1[0:64, :].rearrange("p (c d) -> p c d", d=128)
    T = big2[0:64, :].rearrange("p (c d) -> p c d", d=128)
    gather_fm(k, cm, G, acc, bacc, 'dn_g', lat_off=256, ctx_off=0)
    for ch in range(NCH):
        pt, pb = k.nextps()
        P.op('pe', lambda e, pt=pt, ch=ch: e.matmul(pt[0:64, 0:128], acc[:, ch * 64:(ch + 1) * 64], I128, start=True, stop=True),
             reads=[bacc, bI], writes=[pb])
        P.op('act', lambda e, pt=pt, ch=ch: e.activation(out=gate[:, ch, :], in_=pt[0:64, 0:128], func=AF.Silu), reads=[pb], writes=[bqd, bkd])
    P.op('pool', lambda e: e.tensor_tensor(out=T, in0=O, in1=O, op=ALU.mult), reads=[bO], writes=[bX[0], bX[1]])
    P.op('dve', lambda e: e.tensor_reduce(out=ssq, in_=T, axis=AX.X, op=ALU.add), reads=[bX[0], bX[1]], writes=[bssq])
    P.op('act', lambda e: e.activation(out=ssq, in_=ssq, func=AF.Sqrt, bias=epst[0:64, :], scale=1.0 / 128), reads=[bssq, beps], writes=[bssq])
    P.op('dve', lambda e: e.reciprocal(out=ssq, in_=ssq), reads=[bssq], writes=[bssq])
    P.op('dve', lambda e: e.tensor_tensor(out=O, in0=O, in1=bc_inner(ssq, 128), op=ALU.mult), reads=[bO, bssq], writes=[bO])
    P.op('pool', lambda e: e.tensor_tensor(out=O, in0=O, in1=bc_mid(gn, NCH), op=ALU.mult), reads=[bO, bgn], writes=[bO])
    P.op('dve', lambda e: e.tensor_tensor(out=O, in0=O, in1=gate, op=ALU.mult), reads=[bO, bqd, bkd], writes=[bO])
    for ch in range(NCH):
        mm_evac(k, 128, 64, O[:, ch, :], I128[0:64, 0:64], [bO, bI], acc[:, ch * 64:(ch + 1) * 64], bacc, eng_i=ch)
    store_y_fm(k, acc, bacc, ybuf, bybuf, 1, 256, 0)


def emit_mod(k, c):
    P = k.P
    c3_d = k.dram("c3", [128, 16, 3])
    w_d = k.dram("w_mod", [D, 12288])
    b_d = k.dram("b_mod", [128, 96])
    sel_d = k.dram("sel", [128, 2])
    gn_d = k.dram("gnorm", [128, 2 * DEPTH, 16])
    modbuf = k.dram("modbuf", [128, 288], kind="Internal")
    G_mod = k.dram("G_mod", [4 * 128, 288], kind="Internal")
    c.Mlat = k.sb([128, 4 * 96]); c.Mctx = k.sb([128, 4 * 96]); c.gnorm = k.sb([128, 2 * DEPTH, 16])
    c.bM = Buf('M')
    k.persist()
    sc = k.sb([128, 16, 3]); bt = k.sb([128, 96]); sel = k.sb([128, 2]); mt = k.sb([128, 96, 3])
    wt = [k.sb([128, 16, 512]) for _ in range(2)]
    M = k.sb([128, 4, 288])
    bsc, bbt, bsel, bmt, bMM, bmb, bGm = (Buf(n) for n in 'sc bt sel mt MM modbuf Gmod'.split())
    bw = [Buf('w0'), Buf('w1')]
    P.dma('sp', sc, c3_d, writes=[bsc])
    P.dma('sp', bt, b_d, writes=[bbt])
    P.dma('sp', sel, sel_d, writes=[bsel])
    P.dma('sp', c.gnorm, gn_d, writes=[c.bM])
    P.op('act', lambda e: e.activation(out=sc, in_=sc, func=AF.Silu), reads=[bsc], writes=[bsc])
    wv = w_d.rearrange("(kc p) n -> p kc n", p=128)
    for n in range(24):
        s = n % 2
        for hf in range(2):
            P.dma('sp' if hf == 0 else 'act', wt[s][:, hf * 8:(hf + 1) * 8, :], wv[:, hf * 8:(hf + 1) * 8, n * 512:(n + 1) * 512], writes=[bw[s]])
        for cb in range(4):
            cbl = n * 4 + cb
            pt, pb = k.nextps()
            for kc in range(16):
                P.op('pe', lambda e, pt=pt, s=s, kc=kc, cb=cb: e.matmul(pt[:, 0:3], wt[s][:, kc, cb * 128:(cb + 1) * 128], sc[:, kc, :],
                                                                      start=(kc == 0), stop=(kc == 15)),
                     reads=[bsc, bw[s]], writes=[pb], pe_acc=(kc > 0))
            P.op('dve', lambda e, pt=pt, cbl=cbl: e.tensor_scalar(out=mt[:, cbl, :], in0=pt[:, 0:3], scalar1=bt[:, cbl:cbl + 1], scalar2=None, op0=ALU.add),
                 reads=[pb, bbt], writes=[bmt])
    P.dma('sp', modbuf, mt.rearrange("p a b -> p (a b)"), reads=[bmt], writes=[bmb])
    P.op('pool', lambda e: e.collective_compute("AllGather", ALU.bypass, replica_groups=[[0, 1, 2, 3], [4, 5, 6, 7]], ins=[modbuf.opt()], outs=[G_mod.opt()]),
         reads=[bmb], writes=[bGm], dma=True, inc=1)
    P.dma('sp', M, G_mod.rearrange("(r p) x -> p r x", p=128), reads=[bGm], writes=[bMM])
    M3 = M.rearrange("p r (cb x) -> p (r cb) x", x=3)
    P.op('dve', lambda e: e.tensor_scalar(out=c.Mlat, in0=M3[:, :, 0], scalar1=sel[:, 0:1], scalar2=None, op0=ALU.mult), reads=[bMM, bsel], writes=[c.bM])
    P.op('dve', lambda e: e.scalar_tensor_tensor(out=c.Mlat, in0=M3[:, :, 1], scalar=sel[:, 1:2], in1=c.Mlat, op0=ALU.mult, op1=ALU.add),
         reads=[bMM, bsel, c.bM], writes=[c.bM])
    P.op('dve', lambda e: e.tensor_copy(out=c.Mctx, in_=M3[:, :, 2]), reads=[bMM], writes=[c.bM])


def dense_params(k):
    return {'w_in': k.dram("w_in", [DEPTH, D, IN_COLS]), 'w_out': k.dram("w_out", [DEPTH, D, D]),
            'w_fi': k.dram("w_fi", [DEPTH, D, 2 * FFN_H]), 'w_fo': k.dram("w_fo", [DEPTH, FFN_H, D])}


def emit_dense(k, c, l_cur, pr, xsrc, bxsrc, xdst, bxdst, Gy, pbufs):
    P = k.P
    first = l_cur is None
    l_next = 0 if first else l_cur + 1
    last = l_next >= DEPTH
    G_y, bGy = Gy
    p_lat, p_ctx, bp, bpc, G_lat, G_ctx, bG, bGc, groups = pbufs

    def ML(l, v):
        return c.Mlat[:, l * 96 + v * 16:l * 96 + (v + 1) * 16]

    def MC(l, v):
        return c.Mctx[:, l * 96 + v * 16:l * 96 + (v + 1) * 16]

    x = k.sb([128, 16, NTOK]); h = k.sb([128, 16, NTOK], BF16)
    wr = [k.sb([128, 12288], BF16) for _ in range(2)]
    act = [k.sb([128, 2, NTOK], BF16) for _ in range(2)]
    sg = [k.sb([128, 512], BF16) for _ in range(2)]
    stage = [k.sb([128, NTOK]) for _ in range(2)]
    rstd = k.sb([128, NTOK]); coef = k.sb([128, 4, 16])
    ones = c.ones; epst = c.eps
    bx = [Buf('x%d' % i) for i in range(16)]
    bh = Buf('h'); bwr = [Buf('wr0'), Buf('wr1')]; bact = [Buf('a0'), Buf('a1')]; bsg = [Buf('sg0'), Buf('sg1')]
    bstage = [Buf('st0'), Buf('st1')]; brstd = Buf('rstd'); bcoef = Buf('coef')
    bmv = c.bM; bones = c.bconst; beps = c.bconst

    xv = xsrc.rearrange("(kc p) n -> p kc n", p=128)
    for q4 in range(4):
        P.dma('sp' if q4 % 2 == 0 else 'act', x[:, q4 * 4:(q4 + 1) * 4, :], xv[:, q4 * 4:(q4 + 1) * 4, :],
              reads=[bxsrc], writes=bx[q4 * 4:(q4 + 1) * 4])
    if not first:
        for ci, sv in ((0, ML(l_cur, 4)), (1, MC(l_cur, 4))):
            P.op('dve', lambda e, ci=ci, sv=sv: e.scalar_tensor_tensor(out=coef[:, ci, :], in0=sv, scalar=1.0, in1=c.gnorm[:, l_cur, :],
                                                                       op0=ALU.add, op1=ALU.mult), reads=[bmv], writes=[bcoef])
    if not last:
        for ci, sv in ((2, ML(l_next, 1)), (3, MC(l_next, 1))):
            P.op('dve', lambda e, ci=ci, sv=sv: e.scalar_tensor_tensor(out=coef[:, ci, :], in0=sv, scalar=1.0, in1=c.gnorm[:, DEPTH + l_next, :],
                                                                       op0=ALU.add, op1=ALU.mult), reads=[bmv], writes=[bcoef])
    steps = []

    def wview(slot, off, kc, n):
        return wr[slot][:, off:off + kc * n].rearrange("p (kc n) -> p kc n", n=n)

    def norm(a_l, a_c, b_l, b_c):
        for ti, (t0, tn) in enumerate(TT):
            pt, pb = k.nextps()
            for kc in range(16):
                s = kc % 2
                P.op('act', lambda e, s=s, kc=kc, t0=t0, tn=tn: e.activation(
                    out=stage[s][:, 0:tn], in_=x[:, kc, t0:t0 + tn], func=AF.Square), reads=[bx[kc]], writes=[bstage[s]])
                P.op('pe', lambda e, pt=pt, s=s, tn=tn, kc=kc: e.matmul(
                    pt[:, 0:tn], ones, stage[s][:, 0:tn], start=(kc == 0), stop=(kc == 15)),
                    reads=[bones, bstage[s]], writes=[pb], pe_acc=(kc > 0))
            P.op('act', lambda e, pt=pt, t0=t0, tn=tn: e.activation(
                out=rstd[:, t0:t0 + tn], in_=pt[:, 0:tn], func=AF.Sqrt, bias=epst, scale=1.0 / D), reads=[pb, beps], writes=[brstd])
        P.op('dve', lambda e: e.reciprocal(out=rstd, in_=rstd), reads=[brstd], writes=[brstd])
        for kc in range(16):
            s = kc % 2
            P.op('dve', lambda e, s=s, kc=kc: e.tensor_tensor(out=stage[s], in0=x[:, kc, :], in1=rstd, op=ALU.mult),
                 reads=[bx[kc], brstd], writes=[bstage[s]])
            P.op('act', lambda e, s=s, kc=kc: e.activation(
                out=h[:, kc, 0:1024], in_=stage[s][:, 0:1024], func=AF.Identity, bias=b_l[:, kc:kc + 1], scale=a_l[:, kc:kc + 1]),
                reads=[bstage[s], bcoef, bmv], writes=[bh])
            P.op('act', lambda e, s=s, kc=kc: e.activation(
                out=h[:, kc, 1024:NTOK], in_=stage[s][:, 1024:NTOK], func=AF.Identity, bias=b_c[:, kc:kc + 1], scale=a_c[:, kc:kc + 1]),
                reads=[bstage[s], bcoef, bmv], writes=[bh])

    def resid_evac(pt, pb, dc, ti, gl, gc):
        t0, tn = TT[ti]
        g = gc if ti == 2 else gl
        P.op('dve', lambda e: e.scalar_tensor_tensor(
            out=x[:, dc, t0:t0 + tn], in0=pt[:, 0:tn], scalar=g[:, dc:dc + 1], in1=x[:, dc, t0:t0 + tn],
            op0=ALU.mult, op1=ALU.add), reads=[pb, bmv, bx[dc]], writes=[bx[dc]])

    if not first:
        w_out = pr['w_out'][l_cur]; w_fi = pr['w_fi'][l_cur]; w_fo = pr['w_fo'][l_cur]
        Gy2 = G_y

        def load_y():
            for kc in range(16):
                col = IDXC[('y', kc)]
                P.op('pool', lambda e, kc=kc, col=col: e.indirect_dma_start(
                    out=h[:, kc, :], out_offset=None, in_=Gy2, in_offset=bass.IndirectOffsetOnAxis(ap=c.idx[:, col:col + 1], axis=0)),
                    reads=bGy[(kc // 4) * 4:(kc // 4) * 4 + 4] + [c.bconst], writes=[bh], dma=True)
        steps.append(('call', load_y))
        for n4 in range(4):
            def ld(slot, n4=n4):
                P.dma('pool', wview(slot, 0, 16, 512), w_out.rearrange("(kc p) n -> p kc n", p=128)[:, :, n4 * 512:(n4 + 1) * 512],
                      writes=[bwr[slot]])

            def cp(slot, n4=n4):
                wv = wview(slot, 0, 16, 512)
                for m in range(4):
                    dc = n4 * 4 + m
                    for ti, (t0, tn) in enumerate(TT):
                        pt, pb = k.nextps()
                        for kc in range(16):
                            P.op('pe', lambda e, pt=pt, wv=wv, kc=kc, m=m, t0=t0, tn=tn: e.matmul(
                                pt[:, 0:tn], wv[:, kc, m * 128:(m + 1) * 128], h[:, kc, t0:t0 + tn],
                                start=(kc == 0), stop=(kc == 15)), reads=[bwr[slot], bh], writes=[pb], pe_acc=(kc > 0))
                        resid_evac(pt, pb, dc, ti, ML(l_cur, 2), MC(l_cur, 2))
            steps.append(('w', ld, cp))
        steps.append(('call', lambda: norm(coef[:, 0, :], coef[:, 1, :], ML(l_cur, 3), MC(l_cur, 3))))
        for g in range(FFN_H // 256):
            def ld(slot, g=g):
                wfv = w_fi.rearrange("(kc p) n -> p kc n", p=128)
                P.dma('pool', wview(slot, 0, 16, 256), wfv[:, :, g * 256:(g + 1) * 256], writes=[bwr[slot]])
                P.dma('pool', wview(slot, 4096, 16, 256), wfv[:, :, FFN_H + g * 256:FFN_H + (g + 1) * 256], writes=[bwr[slot]])
                P.dma('pool', wview(slot, 8192, 2, 2048), w_fo[g * 256:(g + 1) * 256, :].rearrange("(hc p) n -> p hc n", p=128),
                      writes=[bwr[slot]])

            def cp(slot, g=g):
                wg = wview(slot, 0, 16, 256); wu = wview(slot, 4096, 16, 256); wo = wview(slot, 8192, 2, 2048)
                a = g % 2
                for hc in range(2):
                    for ti, (t0, tn) in enumerate(TT):
                        pg, pgb = k.nextps()
                        pu, pub = k.nextps()
                        for (pt, pb, wv) in ((pg, pgb, wg), (pu, pub, wu)):
                            for kc in range(16):
                                P.op('pe', lambda e, pt=pt, wv=wv, kc=kc, hc=hc, t0=t0, tn=tn: e.matmul(
                                    pt[:, 0:tn], wv[:, kc, hc * 128:(hc + 1) * 128], h[:, kc, t0:t0 + tn],
                                    start=(kc == 0), stop=(kc == 15)), reads=[bwr[slot], bh], writes=[pb], pe_acc=(kc > 0))
                        s = (hc * 3 + ti) % 2
                        P.op('act', lambda e, s=s, pg=pg, tn=tn: e.activation(
                            out=sg[s][:, 0:tn], in_=pg[:, 0:tn], func=AF.Silu), reads=[pgb], writes=[bsg[s]])
                        P.op('dve', lambda e, s=s, pu=pu, a=a, hc=hc, t0=t0, tn=tn: e.tensor_tensor(
                            out=act[a][:, hc, t0:t0 + tn], in0=sg[s][:, 0:tn], in1=pu[:, 0:tn], op=ALU.mult),
                            reads=[bsg[s], pub], writes=[bact[a]])
                for dc in range(16):
                    for ti, (t0, tn) in enumerate(TT):
                        pt, pb = k.nextps()
                        for hc in range(2):
                            P.op('pe', lambda e, pt=pt, hc=hc, dc=dc, t0=t0, tn=tn: e.matmul(
                                pt[:, 0:tn], wo[:, hc, dc * 128:(dc + 1) * 128], act[a][:, hc, t0:t0 + tn],
                                start=(hc == 0), stop=(hc == 1)), reads=[bwr[slot], bact[a]], writes=[pb], pe_acc=(hc > 0))
                        resid_evac(pt, pb, dc, ti, ML(l_cur, 5), MC(l_cur, 5))
            steps.append(('w', ld, cp))

        def store_x():
            xov = xdst.rearrange("(kc p) n -> p kc n", p=128)
            for q4 in range(4):
                P.dma('sp' if q4 % 2 == 0 else 'act', xov[:, q4 * 4:(q4 + 1) * 4, :], x[:, q4 * 4:(q4 + 1) * 4, :],
                      reads=bx[q4 * 4:(q4 + 1) * 4], writes=[bxdst])
        steps.append(('call', store_x))
    if not last:
        w_in = pr['w_in'][l_next]
        steps.append(('call', lambda: norm(coef[:, 2, :], coef[:, 3, :], ML(l_next, 0), MC(l_next, 0))))
        ncol = [(i * 512, 512) for i in (0, 1, 2, 10, 11)] + [(6144, 32)] + [(i * 512, 512) for i in (7, 8, 9, 3, 4, 5, 6)]
        for (c0, cn) in ncol:
            def ld(slot, c0=c0, cn=cn):
                P.dma('pool', wview(slot, 0, 16, cn), w_in.rearrange("(kc p) n -> p kc n", p=128)[:, :, c0:c0 + cn], writes=[bwr[slot]])

            def cp(slot, c0=c0, cn=cn):
                wv = wview(slot, 0, 16, cn)
                for m in range((cn + 127) // 128):
                    mw = min(128, cn - m * 128)
                    s = m % 2
                    for ti, (t0, tn) in enumerate(TT):
                        pt, pb = k.nextps()
                        for kc in range(16):
                            P.op('pe', lambda e, pt=pt, wv=wv, kc=kc, m=m, mw=mw, t0=t0, tn=tn: e.matmul(
                                pt[0:mw, 0:tn], wv[:, kc, m * 128:m * 128 + mw], h[:, kc, t0:t0 + tn],
                                start=(kc == 0), stop=(kc == 15)), reads=[bwr[slot], bh], writes=[pb], pe_acc=(kc > 0))
                        if ti == 1:
                            P.op('dve', lambda e, pt=pt, s=s, mw=mw, t0=t0, tn=tn: e.tensor_copy(
                                out=stage[s][0:mw, t0:t0 + tn], in_=pt[0:mw, 0:tn]), reads=[pb], writes=[bstage[s]])
                        else:
                            P.op('act', lambda e, pt=pt, s=s, mw=mw, t0=t0, tn=tn: e.activation(
                                out=stage[s][0:mw, t0:t0 + tn], in_=pt[0:mw, 0:tn], func=AF.Copy), reads=[pb], writes=[bstage[s]])
                    r0 = c0 + m * 128
                    q = r0 // PCH
                    qc = r0 // PCC
                    P.dma('sp', p_lat[r0:r0 + mw, :], stage[s][0:mw, 0:1024], reads=[bstage[s]], writes=[bp[q]])
                    P.dma('act', p_ctx[r0:r0 + mw, :], stage[s][0:mw, 1024:NTOK], reads=[bstage[s]], writes=[bpc[qc]])
                    if (r0 + mw) % PCH == 0 or (r0 + mw) == IN_COLS:
                        q0 = q * PCH
                        rq = min(PCH, IN_COLS - q0)
                        P.op('pool', lambda e, q0=q0, rq=rq: e.collective_compute(
                            "AllGather", ALU.bypass, replica_groups=groups, ins=[p_lat[q0:q0 + rq, :].opt()],
                            outs=[G_lat[4 * q0:4 * q0 + 4 * rq, :].opt()]), reads=[bp[q]], writes=[bG[q]], dma=True, inc=1, cc=True)
                    if (r0 + mw) % PCC == 0 or (r0 + mw) == IN_COLS:
                        q0 = qc * PCC
                        rq = min(PCC, IN_COLS - q0)
                        P.op('pool', lambda e, q0=q0, rq=rq: e.collective_compute(
                            "AllGather", ALU.bypass, replica_groups=groups, ins=[p_ctx[q0:q0 + rq, :].opt()],
                            outs=[G_ctx[4 * q0:4 * q0 + 4 * rq, :].opt()]), reads=[bpc[qc]], writes=[bGc[qc]], dma=True, inc=1, cc=True)
            steps.append(('w', ld, cp))

    wsteps = [i for i, s in enumerate(steps) if s[0] == 'w']
    slot_of = {si: j % 2 for j, si in enumerate(wsteps)}
    nxt = {wsteps[j]: wsteps[j + 1] for j in range(len(wsteps) - 1)}
    if wsteps:
        steps[wsteps[0]][1](slot_of[wsteps[0]])
    for i, s in enumerate(steps):
        if s[0] == 'call':
            s[1]()
        else:
            if i in nxt:
                steps[nxt[i]][1](slot_of[nxt[i]])
            s[2](slot_of[i])


def build_fused(depth=DEPTH, stop_after=None):
    k = K()
    P = k.P
    c = setup_common(k)
    xT = k.dram("xT", [D, NTOK])
    xo = k.dram("xo", [D, NTOK], kind="ExternalOutput")
    xspill = k.dram("xspill", [D, NTOK], kind="Internal")
    p_lat = k.dram("p_lat", [IN_COLS, 1024], kind="Internal")
    p_ctx = k.dram("p_ctx", [IN_COLS, 64], kind="Internal")
    G_lat = k.dram("G_lat", [4 * IN_COLS, 1024], kind="Internal")
    G_ctx = k.dram("G_ctx", [4 * IN_COLS, 64], kind="Internal")
    ybuf = k.dram("ybuf", [4, 4, 128, NTOK], kind="Internal")
    G_y = k.dram("G_y", [4 * 4 * 4 * 128, NTOK], kind="Internal")
    bxT, bxo, bxs = (Buf(n) for n in 'xT xo xspill'.split())
    NQ = (IN_COLS + PCH - 1) // PCH
    bp = [Buf('p%d' % i) for i in range(NQ)]
    bG = [Buf('G%d' % i) for i in range(NQ)]
    NQC = (IN_COLS + PCC - 1) // PCC
    bpc = [Buf('pc%d' % i) for i in range(NQC)]
    bGc = [Buf('Gc%d' % i) for i in range(NQC)]
    bys = [Buf('y%d' % i) for i in range(16)]
    bGy = [Buf('Gy%d' % i) for i in range(16)]
    prd = dense_params(k)
    pra = {'na': attn_params(k, 'na'), 'wa': attn_params(k, 'wa')}
    prm = ml_params(k)
    prn = dn_params(k)
    groups = [[0, 1, 2, 3], [4, 5, 6, 7]]
    emit_mod(k, c)
    k.phase()

    def stop(tag):
        if stop_after != tag:
            return False
        dbgM = k.dram("dbgM", [128, 2, 384], kind="ExternalOutput")
        dbgG = k.dram("dbgG", [4 * IN_COLS, 64], kind="ExternalOutput")
        dbgY = k.dram("dbgY", [4 * 2048, 64], kind="ExternalOutput")
        P.dma('sp', dbgM[:, 0, :], c.Mlat, reads=[c.bM])
        P.dma('sp', dbgM[:, 1, :], c.Mctx, reads=[c.bM])
        P.dma('sp', dbgG, G_ctx, reads=bGc)
        P.dma('sp', dbgY, G_y[:, 1024:1088], reads=bGy)
        P.dma('act', xo, xT, reads=[bxT], writes=[bxo])
        return True

    bybuf = (bys, G_y, bGy, groups)
    pb_all = (p_lat, p_ctx, bp, bpc, G_lat, G_ctx, bG, bGc, groups)

    if stop('mod'):
        return k.done()
    emit_dense(k, c, None, prd, xT, bxT, None, None, (G_y, bGy), pb_all)
    k.phase()
    if stop('d0'):
        return k.done()
    G = (G_lat, G_ctx, bG, bGc)
    for l in range(depth):
        for (tag, fn) in (('na', lambda: emit_attn(k, c, 'na', l, G, ybuf, bybuf, pra['na'])),
                          ('wa', lambda: emit_attn(k, c, 'wa', l, G, ybuf, bybuf, pra['wa'])),
                          ('ml', lambda: emit_ml(k, c, l, G, ybuf, bybuf, prm)),
                          ('dn', lambda: emit_dn(k, c, l, G, ybuf, bybuf, prn))):
            fn()
            k.phase()
            if stop('%s%d' % (tag, l)):
                return k.done()
        lastl = (l == depth - 1)
        src, bsrc = (xT, bxT) if l == 0 else (xspill, bxs)
        dst, bdst = (xo, bxo) if lastl else (xspill, bxs)
        emit_dense(k, c, l, prd, src, bsrc, dst, bdst, (G_y, bGy), pb_all)
        k.phase()
        if (not lastl) and stop('d%d' % (l + 1)):
            return k.done()
    return k.done()

import numpy as np

NCORES = 8
_PROG = {}


def _rope_tables():
    n = 4096
    t = np.arange(n)
    n_freq = 32
    inv_freq = (np.float32(10000.0) ** (-np.arange(n_freq, dtype=np.float32) / np.float32(n_freq))).astype(np.float32)
    pos = np.stack([t // 64, t % 64], -1).astype(np.float32)
    ang = pos[:, :, None] * inv_freq
    cos = np.cos(ang).astype(np.float32)
    sin = np.sin(ang).astype(np.float32)
    C = np.zeros((128, n), np.float32)
    S = np.zeros((128, n), np.float32)
    RmT = np.zeros((128, 128), np.float32)
    for d in range(128):
        a, tt, f = d // 64, (d // 32) % 2, d % 32
        C[d] = cos[:, a, f]
        S[d] = sin[:, a, f]
        if tt == 0:
            RmT[d + 32, d] = -1.0
        else:
            RmT[d - 32, d] = 1.0
    return C, S, RmT


def _na_bias_tables(rpb):
    NEG = -30000.0
    tab = np.zeros((128, 5, 7 * 128), np.float32)
    classes = [(0, [0, 1, 2, 3]), (1, [-1, 0, 1, 2]), (5, [-2, -1, 0, 1, 2]), (30, [-2, -1, 0, 1]), (31, [-3, -2, -1, 0])]
    kk = np.arange(128)
    qq = np.arange(128)
    for cls, (n, offs) in enumerate(classes):
        r = 2 * n + qq // 64
        qc = qq % 64
        rs = np.clip(r - 4, 0, 56)
        ws = np.clip(qc - 8, 0, 48)
        for ci, off in enumerate(offs):
            ch = n + off
            kr = 2 * ch + kk // 64
            kc = kk % 64
            ok = ((kr[:, None] >= rs[None, :]) & (kr[:, None] < rs[None, :] + 8)
                  & (kc[:, None] >= ws[None, :]) & (kc[:, None] < ws[None, :] + 16))
            dr = np.clip(kr[:, None] - r[None, :] + 7, 0, 14)
            dc = np.clip(kc[:, None] - qc[None, :], -15, 15) + 15
            tab[:, cls, ci * 128:(ci + 1) * 128] = np.where(ok, rpb[dr, dc], NEG)
    return tab


def _core_inputs(I, core, shared):
    b, j = core // 4, core % 4
    m = dict(shared)
    m["idx"] = make_idx(j)
    m["w_mod"] = I['w_ada'][j]
    m["b_mod"] = np.ascontiguousarray(I['b_ada'][j].reshape(96, 128).T)
    sel = np.zeros((128, 2), np.float32)
    sel[:, b] = 1.0
    m["sel"] = sel
    xc = np.concatenate([I['x'][b, j * 1024:(j + 1) * 1024], I['ctx'][b, j * 64:(j + 1) * 64]], 0)
    m["xT"] = np.ascontiguousarray(xc.T)
    m["wa_sinkb"] = np.ascontiguousarray(np.broadcast_to(I['wa_sink'][:, j][:, None, None], (4, 128, 1))).astype(np.float32)
    m["na_bias"] = np.stack([_na_bias_tables(I['na_rpb'][ll][j]) for ll in range(4)], 0)
    gbv = np.stack([I['ml_i_bias'][:, 0, j], I['ml_i_bias'][:, 1, j], I['ml_f_bias'][:, 0, j], I['ml_f_bias'][:, 1, j]], -1)
    m["ml_gb"] = np.ascontiguousarray(np.broadcast_to(gbv[:, None, :], (4, 68, 4))).astype(np.float32)
    m["ml_gn"] = np.ascontiguousarray(np.broadcast_to(I['ml_norm'][:, j][:, None, :], (4, 64, 128))).astype(np.float32)
    cw = np.stack([I['dn_conv'][:, :, t * 512 + j * 128: t * 512 + (j + 1) * 128] for t in range(3)], 1)
    cw2 = np.stack([cw, cw[:, :, ::-1, :]], 1)
    m["dn_cw"] = np.ascontiguousarray(cw2.transpose(0, 1, 4, 2, 3)).astype(np.float32)
    scv = np.stack([I['dn_a_log'][:, 0, j], I['dn_a_log'][:, 1, j], I['dn_dt_bias'][:, 0, j], I['dn_dt_bias'][:, 1, j]], -1)
    m["dn_sc"] = np.ascontiguousarray(np.broadcast_to(scv[:, None, :], (4, 68, 4))).astype(np.float32)
    return m


def kernel(**I):
    I = {k_: np.asarray(v, np.float32) for k_, v in I.items()}
    if 'nc' not in _PROG:
        _PROG['nc'] = build_fused()
    nc = _PROG['nc']
    C, S, RmT = _rope_tables()
    kk = np.arange(128)[:, None]
    qq = np.arange(128)[None, :]
    jj = np.arange(64)
    mk = np.zeros((64, 2, 64), np.float32)
    mk[:, 0, :] = np.where(jj[None, :] >= jj[:, None], 0.0, -30000.0)
    mk[:, 1, :] = (jj[None, :] > jj[:, None]).astype(np.float32)
    c3 = np.stack([I['c'][0], I['c'][1], I['c_ctx']], 0)
    gnorm = np.zeros((128, 8, 16), np.float32)
    for l in range(4):
        gnorm[:, l] = I['norm_ffn'][l].reshape(16, 128).T
        gnorm[:, 4 + l] = I['norm_mix'][l].reshape(16, 128).T
    shared = {
        "I128": np.eye(128, dtype=np.float32), "J128": np.ascontiguousarray(np.eye(128, dtype=np.float32)[::-1]),
        "c3": np.ascontiguousarray(c3.T.reshape(16, 128, 3).transpose(1, 0, 2)), "gnorm": gnorm,
        "w_in": I['w_in'], "w_out": I['w_out'], "w_fi": I['w_ffn_in'], "w_fo": I['w_ffn_out'],
        "na_gains": np.ascontiguousarray(I['na_qk_gain'].transpose(0, 2, 1)),
        "wa_gains": np.ascontiguousarray(I['wa_qk_gain'].transpose(0, 2, 1)),
        "cosT": C, "sinT": S, "rmT": RmT, "wamask": np.concatenate([(kk >= qq), (kk <= qq)], 1).astype(np.float32),
        "ml_tri": (jj[None, :] >= jj[:, None]).astype(np.float32),
        "dn_gn": np.ascontiguousarray(np.broadcast_to(I['dn_norm'][:, None, :], (4, 64, 128))).astype(np.float32),
        "dn_masks": mk,
    }
    ins = [_core_inputs(I, core, shared) for core in range(NCORES)]
    res = run_bass_kernel_spmd(nc, ins, core_ids=list(range(NCORES))).results
    out = np.zeros((2, 4096, 2048), np.float32)
    for core in range(NCORES):
        b, j = core // 4, core % 4
        out[b, j * 1024:(j + 1) * 1024] = res[core]["xo"][:, 0:1024].T
    return out
```

```python
import contextlib

import numpy as np
import concourse.bass as bass
import concourse.mybir as mybir
from concourse.bass_utils import run_bass_kernel_spmd
from concourse.alu_op_type import AluOpType as ALU

AF = mybir.ActivationFunctionType
AX = mybir.AxisListType
F32 = mybir.dt.float32
BF16 = mybir.dt.bfloat16
F32R = mybir.dt.float32r

ENGS = ('pe', 'act', 'dve', 'pool', 'sp')
NDMASEM = 12
NCCSEM = 56


class Buf:
    __slots__ = ('name', 'w', 'r')

    def __init__(self, name=''):
        self.name = name
        self.w = None
        self.r = []


class Op:
    __slots__ = ('eng', 'pos', 'fn', 'waits', 'flag', 'val', 'dma', 'sem', 'inc')


class Prog:
    def __init__(self, nc):
        self.nc = nc
        self.ops = {e: [] for e in ENGS}
        self.seen = {e: {} for e in ENGS}
        self.dma_last = [None] * (NDMASEM + NCCSEM)
        self.dma_cnt = [0] * (NDMASEM + NCCSEM)
        self.dma_tot = [0] * (NDMASEM + NCCSEM)
        self.dma_rr = 0
        self.cc_rr = 0
        self.ndma = 0

    def _dep(self, o, d, same_ok=False):
        if d is None:
            return
        E = o.eng
        if d.dma:
            key = ('d', d.sem)
            if self.seen[E].get(key, 0) >= d.val:
                return
            self.seen[E][key] = d.val
            o.waits.append(d)
        else:
            if d.eng == E and same_ok:
                return
            key = ('e', d.eng)
            if self.seen[E].get(key, -1) >= d.pos:
                return
            self.seen[E][key] = d.pos
            d.flag = True
            o.waits.append(d)

    def op(self, eng, fn, reads=(), writes=(), dma=False, pe_acc=False, inc=16, cc=False):
        o = Op()
        o.eng = eng
        o.pos = len(self.ops[eng])
        o.fn = fn
        o.waits = []
        o.flag = False
        o.val = None
        o.dma = dma
        o.sem = None
        for b in reads:
            self._dep(o, b.w)
        for b in writes:
            if not (pe_acc and b.w is not None and b.w.eng == 'pe' and eng == 'pe'):
                self._dep(o, b.w)
            for r in b.r:
                self._dep(o, r, same_ok=(not r.dma))
        if dma:
            if cc:
                s = NDMASEM + self.cc_rr
                self.cc_rr = (self.cc_rr + 1) % NCCSEM
            else:
                s = self.dma_rr
                self.dma_rr = (self.dma_rr + 1) % NDMASEM
            self._dep(o, self.dma_last[s])
            self.dma_cnt[s] += 1
            self.dma_tot[s] += inc
            o.sem = s
            o.inc = inc
            o.val = self.dma_tot[s]
            self.dma_last[s] = o
            self.ndma += 1
        self.ops[eng].append(o)
        for b in reads:
            if dma:
                b.r.append(o)
            else:
                b.r = [r for r in b.r if r.dma or r.eng != eng]
                b.r.append(o)
        for b in writes:
            b.w = o
            b.r = []
        return o

    def dma(self, q, out, in_, reads=(), writes=(), **kw):
        return self.op(q, lambda e: e.dma_start(out=out, in_=in_, **kw), reads, writes, dma=True)

    def barrier(self):
        lasts = {}
        for e in ENGS:
            for o in reversed(self.ops[e]):
                if (not o.dma) and o.fn is not None:
                    lasts[e] = o
                    break
        for E in ENGS:
            o = Op()
            o.eng = E
            o.pos = len(self.ops[E])
            o.fn = None
            o.waits = []
            o.flag = False
            o.val = None
            o.dma = False
            o.sem = None
            o.inc = 0
            for F in ENGS:
                if F != E and F in lasts:
                    self._dep(o, lasts[F])
            for d in self.dma_last[:NDMASEM]:
                if d is not None:
                    self._dep(o, d)
            self.ops[E].append(o)

    def finish(self):
        o = Op()
        o.eng = 'sp'
        o.pos = len(self.ops['sp'])
        o.fn = None
        o.waits = []
        o.flag = False
        o.val = None
        o.dma = False
        o.sem = None
        for d in self.dma_last:
            if d is not None:
                self._dep(o, d)
        self.ops['sp'].append(o)

    def emit(self):
        nc = self.nc
        self.finish()
        for e in ENGS:
            c = 0
            for o in self.ops[e]:
                if not o.dma and o.flag:
                    c += 1
                    o.val = c
        import contextlib
        with contextlib.ExitStack() as st:
            esem = {e: st.enter_context(nc.semaphore('s_' + e)) for e in ENGS}
            dsem = [st.enter_context(nc.semaphore('d%d' % i)) for i in range(NDMASEM + NCCSEM)]
            block = st.enter_context(nc.Block())

            def run(e):
                def body(eng):
                    for o in self.ops[e]:
                        for d in o.waits:
                            if d.dma:
                                eng.wait_ge(dsem[d.sem], d.val)
                            else:
                                eng.wait_ge(esem[d.eng], d.val)
                        if o.fn is None:
                            continue
                        ins = o.fn(eng)
                        if o.dma:
                            ins.then_inc(dsem[o.sem], o.inc)
                        elif o.flag:
                            ins.then_inc(esem[e], 1)
                return body

            block.tensor(run('pe'))
            block.scalar(run('act'))
            block.vector(run('dve'))
            block.gpsimd(run('pool'))
            block.sync(run('sp'))

import numpy as np

U32 = mybir.dt.uint32
DEPTH = 4
D = 2048
NTOK = 1088
TT = [(0, 512), (512, 512), (1024, 64)]
IN_COLS = 6176
FFN_H = 5632
EPS = 1e-6
NT = 4352
NB = 34
NCH = 68
SCALE = 128 ** -0.5

FM_TENSORS = [('na_q', 0), ('na_k', 512), ('na_v', 1024),
              ('dn_q', 1536), ('dn_k', 2048), ('dn_v', 2560), ('dn_g', 3072),
              ('ml_q', 3600), ('ml_k', 3856), ('ml_v', 4112), ('ml_o', 4624),
              ('wa_q', 5152), ('wa_k', 5664), ('wa_v', 5920)]
FM_WIDTH = {'ml_q': 64, 'ml_k': 64}
CT_ROWS = [('dn_b0', 3584 + 0), ('dn_b1', 3584 + 4), ('dn_a0', 3584 + 8), ('dn_a1', 3584 + 12),
           ('ml_i0', 5136 + 0), ('ml_i1', 5136 + 4), ('ml_f0', 5136 + 8), ('ml_f1', 5136 + 12)]


def idx_cols():
    cols = {}
    n = 0
    for name, _ in FM_TENSORS:
        for r in range(4):
            cols[(name, r)] = n
            cols[(name, r, 'c')] = n + 1
            n += 2
    for name, _ in CT_ROWS:
        cols[(name, 'lat')] = n
        cols[(name, 'ctx')] = n + 1
        n += 2
    for kc in range(16):
        cols[('y', kc)] = n
        n += 1
    return cols, n


IDXC, NIDX = idx_cols()


PCH = 256


def grow(r, cidx):
    cidx = np.asarray(cidx)
    start = (cidx // PCH) * PCH
    rows_q = np.minimum(PCH, IN_COLS - start)
    return 4 * start + r * rows_q + (cidx - start)


PCC = 256


def grow_ctx(r, cidx):
    cidx = np.asarray(cidx)
    start = (cidx // PCC) * PCC
    rows_q = np.minimum(PCC, IN_COLS - start)
    return 4 * start + r * rows_q + (cidx - start)


def make_idx(j):
    t = np.zeros((128, NIDX), np.uint32)
    p = np.arange(128)
    for name, base in FM_TENSORS:
        w = FM_WIDTH.get(name, 128)
        hj = (j // 2) if name in ('wa_k', 'wa_v') else j
        c0 = base + hj * w
        for r in range(4):
            t[:, IDXC[(name, r)]] = grow(r, c0 + np.minimum(p, w - 1))
            t[:, IDXC[(name, r, 'c')]] = grow_ctx(r, c0 + np.minimum(p, w - 1))
    for name, base in CT_ROWS:
        c = base + j
        rev = name.endswith('1')
        ci = np.arange(64)
        cn = (63 - ci) if rev else ci
        t[0:64, IDXC[(name, 'lat')]] = grow(cn // 16, c) * 16 + (cn % 16)
        rr = np.arange(4)
        rn = (3 - rr) if rev else rr
        t[0:4, IDXC[(name, 'ctx')]] = grow_ctx(rn, c)
    for kc in range(16):
        g, r = kc // 4, kc % 4
        t[:, IDXC[('y', kc)]] = ((g * 4 + j) * 4 + r) * 128 + p
    return t


class K:
    def __init__(self, arena_kb=204):
        self.nc = bass.Bass("TRN2", target_bir_lowering=False)
        self.st = contextlib.ExitStack()
        self.P = Prog(self.nc)
        self.psn = 0
        self.nring = 8
        self.words = arena_kb * 256
        self.arena = self.st.enter_context(self.nc.sbuf_tensor("arena", [128, self.words], F32))
        self.off = 0
        self.mark = 0
        self.pst = [self.st.enter_context(self.nc.psum_tensor("ps%d" % i, [128, 512], F32)) for i in range(8)]
        self.psb = [Buf('ps%d' % i) for i in range(8)]
        self.drams = {}

    def dram(self, name, shape, dt=F32, kind="ExternalInput"):
        if kind == "Internal":
            t = self.nc.dram_tensor(name, list(shape), dt).ap()
        else:
            t = self.nc.dram_tensor(name, list(shape), dt, kind=kind).ap()
        self.drams[name] = t
        return t

    def sb(self, shape, dt=F32):
        shape = list(shape)
        n = 1
        for s in shape[1:]:
            n *= s
        bpe = 2 if dt == BF16 else 4
        words = (n * bpe + 3) // 4
        words = (words + 7) // 8 * 8
        assert self.off + words <= self.words, ("SBUF arena overflow", self.off, words, self.words)
        ap = self.arena[0:shape[0], self.off:self.off + words]
        self.off += words
        if dt != F32:
            ap = ap.bitcast(dt)
        ap = ap[:, 0:n]
        if len(shape) == 3:
            ap = ap.rearrange("p (a b) -> p a b", b=shape[2])
        elif len(shape) == 4:
            ap = ap.rearrange("p (a b c) -> p a b c", b=shape[2], c=shape[3])
        return ap

    def persist(self):
        self.mark = self.off

    def phase(self):
        self.P.barrier()
        self.off = self.mark

    def nextps(self):
        i = self.psn % self.nring
        self.psn += 1
        return self.pst[i], self.psb[i]

    def done(self):
        self.P.emit()
        self.st.close()
        return self.nc


class Common:
    pass


def setup_common(k):
    P = k.P
    c = Common()
    c.idx_d = k.dram("idx", [128, NIDX], U32)
    c.I_d = k.dram("I128", [128, 128])
    c.J_d = k.dram("J128", [128, 128])
    c.idx = k.sb([128, NIDX], U32)
    c.I = k.sb([128, 128])
    c.J = k.sb([128, 128])
    c.ones = k.sb([128, 128])
    c.eps = k.sb([128, 1])
    c.bconst = Buf('const')
    P.dma('sp', c.idx, c.idx_d, writes=[c.bconst])
    P.dma('sp', c.I, c.I_d, writes=[c.bconst])
    P.dma('sp', c.J, c.J_d, writes=[c.bconst])
    P.op('dve', lambda e: e.memset(c.ones, 1.0), writes=[c.bconst])
    P.op('dve', lambda e: e.memset(c.eps, EPS), writes=[c.bconst])
    c.J64 = c.J[0:64, 64:128]
    return c


def gather_fm(k, c, G, dst, bdst, name, rows=128, lat_off=0, ctx_off=4096):
    P = k.P
    G_lat, G_ctx, bGl, bGc = G
    base = dict(FM_TENSORS)[name]
    wdt = FM_WIDTH.get(name, 128)
    gdeps = bGl[base // PCH:(base + 4 * wdt - 1) // PCH + 1]
    cdeps = bGc[base // PCC:(base + 4 * wdt - 1) // PCC + 1]
    for r in range(4):
        col = IDXC[(name, r)]
        colc = IDXC[(name, r, 'c')]
        P.op('pool', lambda e, r=r, col=col: e.indirect_dma_start(
            out=dst[0:rows, lat_off + r * 1024: lat_off + (r + 1) * 1024], out_offset=None, in_=G_lat,
            in_offset=bass.IndirectOffsetOnAxis(ap=c.idx[0:rows, col:col + 1], axis=0)),
            reads=gdeps + [c.bconst], writes=[bdst], dma=True)
        P.op('pool', lambda e, r=r, colc=colc: e.indirect_dma_start(
            out=dst[0:rows, ctx_off + r * 64: ctx_off + (r + 1) * 64], out_offset=None, in_=G_ctx,
            in_offset=bass.IndirectOffsetOnAxis(ap=c.idx[0:rows, colc:colc + 1], axis=0)),
            reads=cdeps + [c.bconst], writes=[bdst], dma=True)


def gather_ct(k, c, G, dst, bdst, name):
    P = k.P
    G_lat, G_ctx, bGl, bGc = G
    base = dict(CT_ROWS)[name]
    gdeps = bGl[base // PCH:(base + 3) // PCH + 1]
    cdeps = bGc[base // PCC:(base + 3) // PCC + 1]
    G64 = G_lat.rearrange("r (a b) -> (r a) b", b=64)
    cl, cc = IDXC[(name, 'lat')], IDXC[(name, 'ctx')]
    P.op('pool', lambda e: e.indirect_dma_start(out=dst[4:68, :], out_offset=None, in_=G64,
                                                in_offset=bass.IndirectOffsetOnAxis(ap=c.idx[0:64, cl:cl + 1], axis=0)),
         reads=gdeps + [c.bconst], writes=[bdst], dma=True)
    P.op('pool', lambda e: e.indirect_dma_start(out=dst[0:4, :], out_offset=None, in_=G_ctx,
                                                in_offset=bass.IndirectOffsetOnAxis(ap=c.idx[0:4, cc:cc + 1], axis=0)),
         reads=cdeps + [c.bconst], writes=[bdst], dma=True)


def mm_evac(k, out_rows, out_cols, lhsT, rhs, reads, dst, bdst, eng_i=0, scale_col=None, extra_reads=()):
    P = k.P
    pt, pb = k.nextps()
    P.op('pe', lambda e: e.matmul(pt[0:out_rows, 0:out_cols], lhsT, rhs, start=True, stop=True), reads=list(reads), writes=[pb])
    if scale_col is not None:
        P.op('dve', lambda e: e.tensor_scalar(out=dst, in0=pt[0:out_rows, 0:out_cols], scalar1=scale_col, scalar2=None, op0=ALU.mult),
             reads=[pb] + list(extra_reads), writes=[bdst])
    elif eng_i % 2 == 0:
        P.op('act', lambda e: e.activation(out=dst, in_=pt[0:out_rows, 0:out_cols], func=AF.Copy), reads=[pb], writes=[bdst])
    else:
        P.op('dve', lambda e: e.tensor_copy(out=dst, in_=pt[0:out_rows, 0:out_cols]), reads=[pb], writes=[bdst])


def store_y_fm(k, yT, byT, ybuf, bybuf, g, lat_off, ctx_off):
    P = k.P
    bys, G_y, bGy, groups = bybuf
    for jt in range(4):
        q = 'sp' if jt % 2 == 0 else 'act'
        P.dma(q, ybuf[g, jt, :, 0:1024], yT[:, lat_off + jt * 1024: lat_off + (jt + 1) * 1024], reads=[byT], writes=[bys[g * 4 + jt]])
        P.dma(q, ybuf[g, jt, :, 1024:1088], yT[:, ctx_off + jt * 64: ctx_off + (jt + 1) * 64], reads=[byT], writes=[bys[g * 4 + jt]])
    yb2 = ybuf.rearrange("g j d n -> (g j d) n")
    for jt in range(4):
        q = g * 4 + jt
        P.op('pool', lambda e, q=q: e.collective_compute(
            "AllGather", ALU.bypass, replica_groups=groups, dma_qos="P2", ins=[yb2[q * 128:(q + 1) * 128, :].opt()], outs=[G_y[q * 512:(q + 1) * 512, :].opt()]),
            reads=[bys[q]], writes=[bGy[q]], dma=True, inc=1, cc=True)


def qknorm(k, bufs, src, dst, gain, tiles, rope=None):
    P = k.P
    (ones, bones, epst, beps, scr, bscr, bsrc, bdst, bg) = bufs
    for i, (t0, tn, dorope) in enumerate(tiles):
        s0, s1, s2 = scr[(3 * i) % 6], scr[(3 * i + 1) % 6], scr[(3 * i + 2) % 6]
        b0, b1, b2 = bscr[(3 * i) % 6], bscr[(3 * i + 1) % 6], bscr[(3 * i + 2) % 6]
        P.op('act', lambda e, s0=s0, t0=t0, tn=tn: e.activation(out=s0[:, 0:tn], in_=src[:, t0:t0 + tn], func=AF.Square),
             reads=[bsrc], writes=[b0])
        pt, pb = k.nextps()
        P.op('pe', lambda e, pt=pt, s0=s0, tn=tn: e.matmul(pt[:, 0:tn], ones, s0[:, 0:tn], start=True, stop=True),
             reads=[bones, b0], writes=[pb])
        P.op('act', lambda e, pt=pt, s1=s1, tn=tn: e.activation(out=s1[:, 0:tn], in_=pt[:, 0:tn], func=AF.Sqrt,
                                                                  bias=epst, scale=1.0 / 128), reads=[pb, beps], writes=[b1])
        P.op('dve', lambda e, s1=s1, tn=tn: e.reciprocal(out=s1[:, 0:tn], in_=s1[:, 0:tn]), reads=[b1], writes=[b1])
        if not dorope:
            P.op('dve', lambda e, s1=s1, t0=t0, tn=tn: e.scalar_tensor_tensor(
                out=dst[:, t0:t0 + tn], in0=src[:, t0:t0 + tn], scalar=gain, in1=s1[:, 0:tn], op0=ALU.mult, op1=ALU.mult),
                reads=[bsrc, b1, bg], writes=[bdst])
        else:
            C, S, RmT, brope = rope
            P.op('dve', lambda e, s1=s1, s2=s2, t0=t0, tn=tn: e.scalar_tensor_tensor(
                out=s2[:, 0:tn], in0=src[:, t0:t0 + tn], scalar=gain, in1=s1[:, 0:tn], op0=ALU.mult, op1=ALU.mult),
                reads=[bsrc, b1, bg], writes=[b2])
            pr, prb = k.nextps()
            P.op('pe', lambda e, pr=pr, s2=s2, tn=tn: e.matmul(pr[:, 0:tn], RmT, s2[:, 0:tn], start=True, stop=True),
                 reads=[brope, b2], writes=[prb])
            P.op('dve', lambda e, pr=pr, s0=s0, t0=t0, tn=tn: e.tensor_tensor(
                out=s0[:, 0:tn], in0=pr[:, 0:tn], in1=S[:, t0:t0 + tn], op=ALU.mult), reads=[prb, brope], writes=[b0])
            P.op('pool', lambda e, s1=s1, s2=s2, t0=t0, tn=tn: e.tensor_tensor(
                out=s1[:, 0:tn], in0=s2[:, 0:tn], in1=C[:, t0:t0 + tn], op=ALU.mult), reads=[b2, brope], writes=[b1])
            P.op('dve', lambda e, s0=s0, s1=s1, t0=t0, tn=tn: e.tensor_tensor(
                out=dst[:, t0:t0 + tn], in0=s0[:, 0:tn], in1=s1[:, 0:tn], op=ALU.add), reads=[b0, b1], writes=[bdst])


def attn_params(k, kind):
    pr = {}
    if kind == 'wa':
        pr['gains'] = k.dram("wa_gains", [DEPTH, 128, 2])
        pr['cos'] = k.dram("cosT", [128, 4096])
        pr['sin'] = k.dram("sinT", [128, 4096])
        pr['rm'] = k.dram("rmT", [128, 128])
        pr['sink'] = k.dram("wa_sinkb", [DEPTH, 128, 1])
        pr['mask'] = k.dram("wamask", [128, 256])
    else:
        pr['gains'] = k.dram("na_gains", [DEPTH, 128, 2])
        pr['bias'] = k.dram("na_bias", [DEPTH, 128, 5, 7 * 128])
    return pr


def emit_attn(k, c, kind, l, G, ybuf, bybuf, pr):
    P = k.P
    g_slot = 0 if kind == 'na' else 3
    pre = 'na' if kind == 'na' else 'wa'
    q = k.sb([128, NT]); kk = k.sb([128, NT]); vT = k.sb([128, NT])
    qb = k.sb([128, NT], BF16); kb = k.sb([128, NT], BF16)
    V1 = k.sb([128, NB, 129], BF16)
    g = k.sb([128, 2])
    scr = [k.sb([128, 512]) for _ in range(6)]
    bq, bk, bv, bqb, bkb, bV, bg = (Buf(n) for n in 'q k v qb kb V g'.split())
    bscr = [Buf('scr%d' % i) for i in range(6)]
    bo = [Buf('o%d' % i) for i in range(NB)]
    ones, bones, epst, beps = c.ones, c.bconst, c.eps, c.bconst

    gather_fm(k, c, G, q, bq, pre + '_q')
    gather_fm(k, c, G, kk, bk, pre + '_k')
    gather_fm(k, c, G, vT, bv, pre + '_v')
    P.dma('sp', g, pr['gains'][l], writes=[bg])
    P.op('pool', lambda e: e.memset(V1[:, :, 128:129], 1.0), writes=[bV])
    rope = None
    if kind == 'wa':
        C = k.sb([128, 4096]); S = k.sb([128, 4096]); RmT = k.sb([128, 128])
        sk = k.sb([128, 1]); es = k.sb([128, 1]); msk = k.sb([128, 256], BF16)
        brope, bsk, bes, bmsk = Buf('rope'), Buf('sk'), Buf('es'), Buf('msk')
        P.dma('sp', C, pr['cos'], writes=[brope])
        P.dma('act', S, pr['sin'], writes=[brope])
        P.dma('sp', RmT, pr['rm'], writes=[brope])
        P.dma('sp', sk, pr['sink'][l], writes=[bsk])
        P.dma('pool', msk, pr['mask'], writes=[bmsk])
        P.op('act', lambda e: e.activation(out=es, in_=sk, func=AF.Exp), reads=[bsk], writes=[bes])
        rope = (C, S, RmT, brope)
    else:
        bias = k.sb([128, 5, 7 * 128])
        bbias = Buf('bias')
        P.dma('sp', bias, pr['bias'][l], writes=[bbias])
    for n in range(NB):
        mm_evac(k, 128, 128, vT[:, n * 128:(n + 1) * 128], c.I, [bv, c.bconst], V1[:, n, 0:128], bV, eng_i=n)

    tiles_q = [(i * 512, 512, kind == 'wa') for i in range(8)] + [(4096, 256, False)]
    qknorm(k, (ones, bones, epst, beps, scr, bscr, bq, bqb, bg), q, qb, g[:, 0:1], tiles_q, rope)
    qknorm(k, (ones, bones, epst, beps, scr, bscr, bk, bkb, bg), kk, kb, g[:, 1:2], tiles_q, rope)

    osb = q.rearrange("p (n d) -> p n d", d=128)
    yT = kk
    eA = [k.sb([128, 512], BF16) for _ in range(2)]
    eB = [k.sb([128, 512], BF16) for _ in range(2)]
    tmpf = [k.sb([128, 512]) for _ in range(2)]
    rd = [k.sb([128, 1]) for _ in range(2)]
    beA = [Buf('eA0'), Buf('eA1')]; beB = [Buf('eB0'), Buf('eB1')]
    btmp = [Buf('tf0'), Buf('tf1')]; brd = [Buf('rd0'), Buf('rd1')]

    def smat(pt, pb, ci, ch, n):
        P.op('pe', lambda e: e.matmul(pt[:, ci * 128:(ci + 1) * 128], kb[:, ch * 128:(ch + 1) * 128],
                                      qb[:, n * 128:(n + 1) * 128], start=True, stop=True),
             reads=[bkb, bqb], writes=[pb], pe_acc=(ci > 0))

    def stage1(n):
        if True:
            s = n % 2
            groups = []
            if kind == 'wa':
                if n < 32:
                    A = [n, 32, 33]
                    B = ([n - 1] if n > 0 else []) + ([n + 1] if n < 31 else [])
                    Bm = ([0] if n > 0 else []) + ([1] if n < 31 else [])
                else:
                    A, B, Bm = [32, 33], [], []
                pa, pab = k.nextps()
                for ci, ch in enumerate(A):
                    smat(pa, pab, ci, ch, n)
                P.op('act', lambda e, pa=pa, s=s, w=len(A) * 128: e.activation(
                    out=eA[s][:, 0:w], in_=pa[:, 0:w], func=AF.Exp, scale=SCALE), reads=[pab], writes=[beA[s]])
                groups.append((eA[s], beA[s], A))
                if B:
                    pbt, pbb = k.nextps()
                    for ci, ch in enumerate(B):
                        smat(pbt, pbb, ci, ch, n)
                    w = len(B) * 128
                    P.op('act', lambda e, pbt=pbt, s=s, w=w: e.activation(
                        out=eB[s][:, 0:w], in_=pbt[:, 0:w], func=AF.Exp, scale=SCALE), reads=[pbb], writes=[beB[s]])
                    for ci, mi in enumerate(Bm):
                        P.op('pool', lambda e, s=s, ci=ci, mi=mi: e.tensor_tensor(
                            out=eB[s][:, ci * 128:(ci + 1) * 128], in0=eB[s][:, ci * 128:(ci + 1) * 128],
                            in1=msk[:, mi * 128:(mi + 1) * 128], op=ALU.mult), reads=[beB[s], bmsk], writes=[beB[s]])
                    groups.append((eB[s], beB[s], B))
            else:
                if n < 32:
                    if n == 0:
                        cls, offs = 0, [0, 1, 2, 3]
                    elif n == 1:
                        cls, offs = 1, [-1, 0, 1, 2]
                    elif n == 30:
                        cls, offs = 3, [-2, -1, 0, 1]
                    elif n == 31:
                        cls, offs = 4, [-3, -2, -1, 0]
                    else:
                        cls, offs = 2, [-2, -1, 0, 1, 2]
                    chs = [n + o for o in offs] + [32, 33]
                    G1, G2 = chs[:4], chs[4:]
                    col = 0
                    for (et, ebf, Gc) in ((eA[s], beA[s], G1), (eB[s], beB[s], G2)):
                        pt, pb = k.nextps()
                        for ci, ch in enumerate(Gc):
                            smat(pt, pb, ci, ch, n)
                        w = len(Gc) * 128
                        P.op('dve', lambda e, pt=pt, s=s, w=w, col=col, cls=cls: e.scalar_tensor_tensor(
                            out=tmpf[s][:, 0:w], in0=pt[:, 0:w], scalar=SCALE, in1=bias[:, cls, col:col + w],
                            op0=ALU.mult, op1=ALU.add), reads=[pb, bbias], writes=[btmp[s]])
                        P.op('act', lambda e, et=et, s=s, w=w: e.activation(
                            out=et[:, 0:w], in_=tmpf[s][:, 0:w], func=AF.Exp), reads=[btmp[s]], writes=[ebf])
                        groups.append((et, ebf, Gc))
                        col += w
                else:
                    A = [32, 33]
                    pa, pab = k.nextps()
                    for ci, ch in enumerate(A):
                        smat(pa, pab, ci, ch, n)
                    P.op('act', lambda e, pa=pa, s=s: e.activation(
                        out=eA[s][:, 0:256], in_=pa[:, 0:256], func=AF.Exp, scale=SCALE), reads=[pab], writes=[beA[s]])
                    groups.append((eA[s], beA[s], A))
            st1[n] = groups
            yield

    def stage2(n):
        if True:
            s = n % 2
            groups = st1.pop(n)
            po, pob = k.nextps()
            tot = sum(len(Gc) for _, _, Gc in groups)
            cnt = 0
            for (et, ebf, Gc) in groups:
                for ci, ch in enumerate(Gc):
                    P.op('pe', lambda e, po=po, et=et, ci=ci, ch=ch, cnt=cnt, tot=tot: e.matmul(
                        po[:, 0:129], et[:, ci * 128:(ci + 1) * 128], V1[:, ch, :], start=(cnt == 0), stop=(cnt == tot - 1)),
                        reads=[ebf, bV], writes=[pob], pe_acc=(cnt > 0))
                    cnt += 1
            if kind == 'wa':
                P.op('dve', lambda e, po=po, s=s: e.tensor_scalar(
                    out=rd[s], in0=po[:, 128:129], scalar1=es[:, 0:1], scalar2=None, op0=ALU.add), reads=[pob, bes], writes=[brd[s]])
                P.op('dve', lambda e, s=s: e.reciprocal(out=rd[s], in_=rd[s]), reads=[brd[s]], writes=[brd[s]])
            else:
                P.op('dve', lambda e, po=po, s=s: e.reciprocal(out=rd[s], in_=po[:, 128:129]), reads=[pob], writes=[brd[s]])
            P.op('dve', lambda e, po=po, s=s, n=n: e.tensor_scalar(
                out=osb[:, n, :], in0=po[:, 0:128], scalar1=rd[s][:, 0:1], scalar2=None, op0=ALU.mult),
                reads=[pob, brd[s]], writes=[bo[n], bq])
            yield

    st1 = {}
    for _ in stage1(0):
        pass
    for n in range(NB):
        interleave(([stage1(n + 1)] if n + 1 < NB else []) + [stage2(n)])
    for n in range(NB):
        mm_evac(k, 128, 128, osb[:, n, :], c.I, [bo[n], c.bconst], yT[:, n * 128:(n + 1) * 128], bk, eng_i=n)
    store_y_fm(k, yT, bk, ybuf, bybuf, g_slot, 0, 4096)


def bc_inner(ap2d, n):
    return bass.AP(ap2d.tensor, ap2d.offset, [list(ap2d.ap[0]), list(ap2d.ap[1]), [0, n]])


def bc_mid(ap2d, cnt):
    return bass.AP(ap2d.tensor, ap2d.offset, [list(ap2d.ap[0]), [0, cnt], list(ap2d.ap[1])])


def interleave(gens):
    gens = list(gens)
    while gens:
        for g in list(gens):
            try:
                next(g)
            except StopIteration:
                gens.remove(g)


def cn_of(cp):
    return (3 - cp) if cp < 4 else (4 + 63 - (cp - 4))


def flip_ct(k, c, src, bsrc, dst, bdst, tmp, btmp):
    mm_evac(k, 64, NCH, src, c.I[0:NCH, 0:NCH], [bsrc, c.bconst], tmp[0], btmp[0], eng_i=1)
    mm_evac(k, 64, NCH, c.J64, tmp[0], [c.bconst, btmp[0]], tmp[1], btmp[1], eng_i=1)
    mm_evac(k, NCH, 64, tmp[1], c.I[0:64, 0:64], [btmp[1], c.bconst], dst, bdst, eng_i=1)


def ml_params(k):
    return {'gb': k.dram("ml_gb", [DEPTH, NCH, 4]), 'gn': k.dram("ml_gn", [DEPTH, 64, 128]), 'tri': k.dram("ml_tri", [64, 64])}


def emit_ml(k, c, l, G, ybuf, bybuf, pr):
    P = k.P
    L = NT
    qT = k.sb([64, L]); kT = k.sb([64, L]); kt = k.sb([64, NCH, 64]); V1 = k.sb([64, NCH, 129])
    tmpA = k.sb([128, L])
    hh = [k.sb([64, NCH, 128]) for _ in range(2)]
    ct = [k.sb([NCH, 64]) for _ in range(12)]
    cc = [k.sb([NCH, 1]) for _ in range(4)]
    crow = [k.sb([1, NCH]) for _ in range(8)]
    cols = k.sb([64, 5, NCH]); abc = k.sb([64, 2, NCH])
    ftmp = [k.sb([64, NCH]) for _ in range(2)]
    gb = k.sb([NCH, 4]); tri = k.sb([64, 64])
    one1 = k.sb([1, 64]); onect = k.sb([NCH, 64]); zeroct = k.sb([NCH, 64])
    Cst = k.sb([64, 129])
    stm = [k.sb([64, 64]) for _ in range(2)]; kw = [k.sb([64, 64]) for _ in range(2)]
    nd = [k.sb([64, 129]) for _ in range(2)]; rdn = [k.sb([64, 1]) for _ in range(2)]
    clsb = [k.sb([64, 129]) for _ in range(2)]; bclsb = [Buf('cl0'), Buf('cl1')]
    gn = k.sb([64, 128]); ssq = k.sb([64, NCH])
    I68 = c.I[0:NCH, 0:NCH]
    bI = c.bconst
    bqT, bkT, bkt, bV, bA, bgb, btri, bone1, bC, bgn, bssq, bcols, babc, bconst = (Buf(n) for n in
        'qT kT kt V tmpA gb tri one1 C gn ssq cols abc const'.split())
    bhh = [Buf('hh0'), Buf('hh1')]
    bct = [Buf('ct%d' % i) for i in range(12)]; bcc = [Buf('cc%d' % i) for i in range(4)]
    bcrow = [Buf('crow%d' % i) for i in range(8)]; bftmp = [Buf('ft0'), Buf('ft1')]
    bstm = [Buf('stm0'), Buf('stm1')]; bkw = [Buf('kw0'), Buf('kw1')]; bnd = [Buf('nd0'), Buf('nd1')]
    brdn = [Buf('rdn0'), Buf('rdn1')]

    P.dma('sp', gb, pr['gb'][l], writes=[bgb])
    P.dma('sp', tri, pr['tri'], writes=[btri])
    P.dma('act', gn, pr['gn'][l], writes=[bgn])
    P.op('dve', lambda e: e.memset(one1, 1.0), writes=[bone1])
    P.op('dve', lambda e: e.memset(onect, 1.0), writes=[bconst])
    P.op('dve', lambda e: e.memset(zeroct, 0.0), writes=[bconst])

    def rop(eng, fn, reads, writes):
        P.op(eng, fn, reads=reads, writes=writes)

    def tr_col(src_ap, bsrc, dst_ap, bdst, m, n):
        pt, pb = k.nextps()
        P.op('pe', lambda e: e.matmul(pt[0:n, 0:m], src_ap, c.I[0:m, 0:m], start=True, stop=True), reads=[bsrc, bI], writes=[pb])
        P.op('dve', lambda e: e.tensor_copy(out=dst_ap, in_=pt[0:n, 0:m]), reads=[pb], writes=[bdst])

    for d in range(2):
        if d == 0:
            gather_fm(k, c, G, qT, bqT, 'ml_q', rows=64, lat_off=256, ctx_off=0)
            gather_fm(k, c, G, kT, bkT, 'ml_k', rows=64, lat_off=256, ctx_off=0)
            gather_fm(k, c, G, tmpA, bA, 'ml_v', lat_off=256, ctx_off=0)
            P.op('dve', lambda e: e.tensor_scalar(out=kT, in0=kT, scalar1=0.125, scalar2=None, op0=ALU.mult), reads=[bkT], writes=[bkT])
            P.op('pool', lambda e: e.memset(V1[:, :, 128:129], 1.0), writes=[bV])
            for ch in range(NCH):
                sl = slice(ch * 64, (ch + 1) * 64)
                mm_evac(k, 64, 64, kT[:, sl], c.I[0:64, 0:64], [bkT, bI], kt[:, ch, :], bkt, eng_i=ch)
                mm_evac(k, 64, 128, tmpA[:, sl], c.I, [bA, bI], V1[:, ch, 0:128], bV, eng_i=ch + 1)
        else:
            qtok = tmpA[0:64, :].rearrange("p (a b) -> p a b", b=64)
            for ch in range(NCH):
                sl = slice(ch * 64, (ch + 1) * 64)
                mm_evac(k, 64, 64, qT[:, sl], c.I[0:64, 0:64], [bqT, bI], qtok[:, ch, :], bA, eng_i=ch)
            for cp in range(NCH):
                cn = cn_of(cp)
                sl = slice(cp * 64, (cp + 1) * 64)
                mm_evac(k, 64, 64, qtok[:, cn, :], c.J64, [bA, bI], qT[:, sl], bqT, eng_i=cp)
                mm_evac(k, 64, 64, kt[:, cn, :], c.J64, [bkt, bI], kT[:, sl], bkT, eng_i=cp + 1)
            for cp in range(NCH):
                cn = cn_of(cp)
                if cp > cn:
                    continue
                pairs = [(cp, cn)] if cp == cn else [(cp, cn), (cn, cp)]
                for (tsr, bt_, wdt) in ((kt, bkt, 64), (V1, bV, 128)):
                    pts = []
                    for (dst_c, src_c) in pairs:
                        pt, pb = k.nextps()
                        P.op('pe', lambda e, pt=pt, tsr=tsr, src_c=src_c, wdt=wdt: e.matmul(
                            pt[0:64, 0:wdt], c.J64, tsr[:, src_c, 0:wdt], start=True, stop=True), reads=[bI, bt_], writes=[pb])
                        pts.append((pt, pb, dst_c))
                    for (pt, pb, dst_c) in pts:
                        P.op('act', lambda e, pt=pt, tsr=tsr, dst_c=dst_c, wdt=wdt: e.activation(
                            out=tsr[:, dst_c, 0:wdt], in_=pt[0:64, 0:wdt], func=AF.Copy), reads=[pb], writes=[bt_])
        P.op('pool', lambda e: e.memset(Cst, 0.0), writes=[bC])
        T_ = ct
        if d == 0:
            gather_ct(k, c, G, T_[0], bct[0], 'ml_i0')
            gather_ct(k, c, G, T_[1], bct[1], 'ml_f0')
        else:
            gather_ct(k, c, G, T_[2], bct[2], 'ml_i1')
            flip_ct(k, c, T_[2], bct[2], T_[0], bct[0], ftmp, bftmp)
            gather_ct(k, c, G, T_[2], bct[2], 'ml_f1')
            flip_ct(k, c, T_[2], bct[2], T_[1], bct[1], ftmp, bftmp)
        rop('dve', lambda e, d=d: e.tensor_scalar(out=T_[1], in0=T_[1], scalar1=gb[:, 2 + d:3 + d], scalar2=None, op0=ALU.add),
            [bct[1], bgb], [bct[1]])
        rop('act', lambda e: e.activation(out=T_[2], in_=T_[1], func=AF.Exp, scale=-1.0), [bct[1]], [bct[2]])
        rop('act', lambda e: e.activation(out=T_[2], in_=T_[2], func=AF.Ln, bias=onect[:, 0:1]), [bct[2], bconst], [bct[2]])
        rop('dve', lambda e: e.tensor_scalar(out=T_[1], in0=T_[2], scalar1=-1.0, scalar2=None, op0=ALU.mult), [bct[2]], [bct[1]])
        rop('dve', lambda e: e.tensor_tensor_scan(out=T_[2], data0=onect, data1=T_[1], initial=0.0, op0=ALU.mult, op1=ALU.add),
            [bct[1], bconst], [bct[2]])
        rop('dve', lambda e, d=d: e.scalar_tensor_tensor(out=T_[3], in0=T_[0], scalar=gb[:, d:d + 1], in1=T_[2],
                                                          op0=ALU.add, op1=ALU.subtract), [bct[0], bgb, bct[2]], [bct[3]])
        rop('dve', lambda e: e.tensor_tensor_scan(out=T_[4], data0=zeroct, data1=T_[3], initial=-1e30, op0=ALU.add, op1=ALU.max),
            [bct[3], bconst], [bct[4]])
        rop('dve', lambda e: e.tensor_copy(out=cc[0], in_=T_[2][:, 63:64]), [bct[2]], [bcc[0]])
        rop('dve', lambda e: e.tensor_copy(out=cc[1], in_=T_[4][:, 63:64]), [bct[4]], [bcc[1]])
        rop('dve', lambda e: e.tensor_tensor(out=cc[2], in0=cc[0], in1=cc[1], op=ALU.add), [bcc[0], bcc[1]], [bcc[2]])
        CR = crow
        tr_col(cc[0], bcc[0], CR[0], bcrow[0], NCH, 1)
        tr_col(cc[2], bcc[2], CR[2], bcrow[2], NCH, 1)
        rop('dve', lambda e: e.tensor_tensor_scan(out=CR[3], data0=CR[0], data1=CR[2], initial=0.0, op0=ALU.add, op1=ALU.max),
            [bcrow[0], bcrow[2]], [bcrow[3]])
        rop('dve', lambda e: e.memset(CR[4][:, 0:1], 0.0), [], [bcrow[4]])
        rop('dve', lambda e: e.tensor_copy(out=CR[4][:, 1:NCH], in_=CR[3][:, 0:NCH - 1]), [bcrow[3]], [bcrow[4]])
        rop('dve', lambda e: e.tensor_tensor(out=CR[5], in0=CR[0], in1=CR[4], op=ALU.add), [bcrow[0], bcrow[4]], [bcrow[5]])
        rop('dve', lambda e: e.tensor_tensor(out=CR[5], in0=CR[5], in1=CR[3], op=ALU.subtract), [bcrow[5], bcrow[3]], [bcrow[5]])
        rop('act', lambda e: e.activation(out=CR[5], in_=CR[5], func=AF.Exp), [bcrow[5]], [bcrow[5]])
        rop('dve', lambda e: e.tensor_tensor(out=CR[6], in0=CR[2], in1=CR[3], op=ALU.subtract), [bcrow[2], bcrow[3]], [bcrow[6]])
        rop('act', lambda e: e.activation(out=CR[6], in_=CR[6], func=AF.Exp), [bcrow[6]], [bcrow[6]])
        pt, pb = k.nextps()
        P.op('pe', lambda e, pt=pt: e.matmul(pt[0:NCH, 0:1], CR[4][0:1, :], one1[0:1, 0:1], start=True, stop=True),
             reads=[bcrow[4], bone1], writes=[pb])
        P.op('dve', lambda e, pt=pt: e.tensor_copy(out=cc[3], in_=pt[0:NCH, 0:1]), reads=[pb], writes=[bcc[3]])
        rop('dve', lambda e: e.tensor_scalar(out=T_[5], in0=T_[4], scalar1=cc[3][:, 0:1], scalar2=None, op0=ALU.max), [bct[4], bcc[3]], [bct[5]])
        rop('act', lambda e: e.activation(out=T_[6], in_=T_[3], func=AF.Exp), [bct[3]], [bct[6]])
        rop('act', lambda e: e.activation(out=T_[7], in_=T_[5], func=AF.Exp, scale=-1.0), [bct[5]], [bct[7]])
        rop('dve', lambda e: e.tensor_scalar(out=T_[8], in0=T_[5], scalar1=cc[3][:, 0:1], scalar2=None, op0=ALU.subtract), [bct[5], bcc[3]], [bct[8]])
        rop('act', lambda e: e.activation(out=T_[8], in_=T_[8], func=AF.Exp, scale=-1.0), [bct[8]], [bct[8]])
        rop('dve', lambda e: e.tensor_tensor(out=T_[9], in0=T_[2], in1=T_[5], op=ALU.add), [bct[2], bct[5]], [bct[9]])
        rop('act', lambda e: e.activation(out=T_[9], in_=T_[9], func=AF.Exp, scale=-1.0), [bct[9]], [bct[9]])
        rop('dve', lambda e: e.tensor_scalar(out=T_[10], in0=T_[3], scalar1=cc[1][:, 0:1], scalar2=None, op0=ALU.subtract), [bct[3], bcc[1]], [bct[10]])
        rop('act', lambda e: e.activation(out=T_[10], in_=T_[10], func=AF.Exp), [bct[10]], [bct[10]])
        for qi, ti in enumerate([6, 7, 8, 9, 10]):
            tr_col(T_[ti], bct[ti], cols[:, qi, :], bcols, NCH, 64)
        for qi, ci in enumerate([5, 6]):
            pt, pb = k.nextps()
            P.op('pe', lambda e, pt=pt, ci=ci: e.matmul(pt[0:64, 0:NCH], one1[0:1, 0:64], CR[ci][0:1, :], start=True, stop=True),
                 reads=[bcrow[ci], bone1], writes=[pb])
            P.op('dve', lambda e, pt=pt, qi=qi: e.tensor_copy(out=abc[:, qi, :], in_=pt[0:64, 0:NCH]), reads=[pb], writes=[babc])
        def pre(ch):
            s = ch % 2
            sl = slice(ch * 64, (ch + 1) * 64)
            pS, pSb = k.nextps()
            P.op('pe', lambda e: e.matmul(pS[0:64, 0:64], kT[:, sl], qT[:, sl], start=True, stop=True), reads=[bkT, bqT], writes=[pSb])
            P.op('dve', lambda e: e.scalar_tensor_tensor(out=stm[s], in0=pS[0:64, 0:64], scalar=cols[:, 0, ch:ch + 1], in1=tri,
                                                         op0=ALU.mult, op1=ALU.mult), reads=[pSb, bcols, btri], writes=[bstm[s]])
            P.op('pool', lambda e: e.tensor_scalar(out=kw[s], in0=kt[:, ch, :], scalar1=cols[:, 4, ch:ch + 1], scalar2=None, op0=ALU.mult),
                 reads=[bkt, bcols], writes=[bkw[s]])
            yield
            pA, pAb = k.nextps()
            P.op('pe', lambda e: e.matmul(pA[0:64, 0:129], stm[s], V1[:, ch, :], start=True, stop=True), reads=[bstm[s], bV], writes=[pAb])
            pC, pCb = k.nextps()
            P.op('pe', lambda e: e.matmul(pC[0:64, 0:129], kw[s], V1[:, ch, :], start=True, stop=True), reads=[bkw[s], bV], writes=[pCb])
            yield
            P.op('dve', lambda e: e.tensor_scalar(out=nd[s], in0=pA[0:64, 0:129], scalar1=cols[:, 1, ch:ch + 1], scalar2=None, op0=ALU.mult),
                 reads=[pAb, bcols], writes=[bnd[s]])
            P.op('act', lambda e: e.activation(out=clsb[s], in_=pC[0:64, 0:129], func=AF.Copy), reads=[pCb], writes=[bclsb[s]])
            yield

        def post(ch, d=d):
            s = ch % 2
            sl = slice(ch * 64, (ch + 1) * 64)
            pB, pBb = k.nextps()
            P.op('pe', lambda e: e.matmul(pB[0:64, 0:129], qT[:, sl], Cst, start=True, stop=True), reads=[bqT, bC], writes=[pBb])
            yield
            P.op('dve', lambda e: e.tensor_scalar(out=Cst, in0=Cst, scalar1=abc[:, 0, ch:ch + 1], scalar2=None, op0=ALU.mult),
                 reads=[bC, babc], writes=[bC])
            P.op('dve', lambda e: e.scalar_tensor_tensor(out=Cst, in0=clsb[s], scalar=abc[:, 1, ch:ch + 1], in1=Cst, op0=ALU.mult, op1=ALU.add),
                 reads=[bclsb[s], babc, bC], writes=[bC])
            yield
            P.op('dve', lambda e: e.scalar_tensor_tensor(out=nd[s], in0=pB[0:64, 0:129], scalar=cols[:, 2, ch:ch + 1], in1=nd[s],
                                                         op0=ALU.mult, op1=ALU.add), reads=[pBb, bcols, bnd[s]], writes=[bnd[s]])
            P.op('dve', lambda e: e.scalar_tensor_tensor(out=rdn[s], in0=nd[s][:, 128:129], scalar=-1.0, in1=nd[s][:, 128:129],
                                                         op0=ALU.mult, op1=ALU.max), reads=[bnd[s]], writes=[brdn[s]])
            yield
            P.op('dve', lambda e: e.tensor_scalar(out=rdn[s], in0=rdn[s], scalar1=cols[:, 3, ch:ch + 1], scalar2=None, op0=ALU.max),
                 reads=[brdn[s], bcols], writes=[brdn[s]])
            P.op('dve', lambda e: e.reciprocal(out=rdn[s], in_=rdn[s]), reads=[brdn[s]], writes=[brdn[s]])
            P.op('dve', lambda e: e.tensor_scalar(out=hh[d][:, ch, :], in0=nd[s][:, 0:128], scalar1=rdn[s][:, 0:1], scalar2=None, op0=ALU.mult),
                 reads=[bnd[s], brdn[s]], writes=[bhh[d]])
            yield

        for _ in pre(0):
            pass
        for ch in range(NCH):
            interleave([post(ch)] + ([pre(ch + 1)] if ch + 1 < NCH else []))
    for cn in range(NCH):
        cp = cn_of(cn)
        pt, pb = k.nextps()
        P.op('pe', lambda e, pt=pt, cp=cp: e.matmul(pt[0:64, 0:128], c.J64, hh[1][:, cp, :], start=True, stop=True),
             reads=[bI, bhh[1]], writes=[pb])
        P.op('dve', lambda e, pt=pt, cn=cn: e.tensor_tensor(out=hh[0][:, cn, :], in0=pt[0:64, 0:128], in1=hh[0][:, cn, :], op=ALU.add),
             reads=[pb, bhh[0]], writes=[bhh[0]])
    og = V1
    gather_fm(k, c, G, tmpA, bA, 'ml_o', lat_off=256, ctx_off=0)
    for ch in range(NCH):
        mm_evac(k, 64, 128, tmpA[:, ch * 64:(ch + 1) * 64], c.I, [bA, bI], og[:, ch, 0:128], bV, eng_i=ch)
    Y = hh[0]
    T = hh[1]
    P.op('pool', lambda e: e.tensor_tensor(out=T, in0=Y, in1=Y, op=ALU.mult), reads=[bhh[0], bhh[1]], writes=[bhh[1]])
    P.op('dve', lambda e: e.tensor_reduce(out=ssq, in_=T, axis=AX.X, op=ALU.add), reads=[bhh[1]], writes=[bssq])
    P.op('act', lambda e: e.activation(out=ssq, in_=ssq, func=AF.Sqrt, bias=c.eps[0:64, :], scale=1.0 / 128), reads=[bssq, bI], writes=[bssq])
    P.op('dve', lambda e: e.reciprocal(out=ssq, in_=ssq), reads=[bssq], writes=[bssq])
    P.op('dve', lambda e: e.tensor_tensor(out=Y, in0=Y, in1=bc_inner(ssq, 128), op=ALU.mult), reads=[bhh[0], bssq], writes=[bhh[0]])
    P.op('pool', lambda e: e.tensor_tensor(out=Y, in0=Y, in1=bc_mid(gn, NCH), op=ALU.mult), reads=[bhh[0], bgn], writes=[bhh[0]])
    P.op('act', lambda e: e.activation(out=og[:, :, 0:128], in_=og[:, :, 0:128], func=AF.Sigmoid), reads=[bV], writes=[bV])
    P.op('dve', lambda e: e.tensor_tensor(out=Y, in0=Y, in1=og[:, :, 0:128], op=ALU.mult), reads=[bhh[0], bV], writes=[bhh[0]])
    for ch in range(NCH):
        mm_evac(k, 128, 64, Y[:, ch, :], c.I[0:64, 0:64], [bhh[0], bI], tmpA[:, ch * 64:(ch + 1) * 64], bA, eng_i=ch)
    store_y_fm(k, tmpA, bA, ybuf, bybuf, 2, 256, 0)


def dn_params(k):
    return {'cw': k.dram("dn_cw", [DEPTH, 2, 128, 3, 5]), 'sc': k.dram("dn_sc", [DEPTH, NCH, 4]),
            'gn': k.dram("dn_gn", [DEPTH, 64, 128]), 'mk': k.dram("dn_masks", [64, 2, 64])}


def emit_dn(k, cm, l, G, ybuf, bybuf, pr):
    P = k.P
    L = NT
    SEGS = [(0, 256), (256, 4096)]
    big1 = k.sb([128, 2 * L]); big2 = k.sb([128, 2 * L])
    X = [big2[:, 0:L], big2[:, L:2 * L], k.sb([128, L])]
    acc = k.sb([128, L])
    qd = big1[:, 0:L]; kd = big1[:, L:2 * L]
    DmT = k.sb([64, NCH, 64]); NB_ = k.sb([64, NCH, 64])
    O = k.sb([64, NCH, 128])
    cw = k.sb([128, 3, 5]); sc = k.sb([NCH, 4]); mk = k.sb([64, 2, 64])
    I128 = cm.I; J64 = cm.J64; ones = cm.ones; epst = cm.eps
    one1 = k.sb([1, 128]); onect = k.sb([NCH, 64])
    ct = [k.sb([NCH, 64]) for _ in range(6)]
    cc = [k.sb([NCH, 1]) for _ in range(3)]
    crow = k.sb([1, NCH])
    cols = k.sb([64, 4, NCH])
    eglb = k.sb([128, NCH])
    S = k.sb([128, 128])
    scr = [k.sb([128, 512]) for _ in range(4)]
    ttok = [k.sb([128, 128]) for _ in range(2)]
    ftmp = [k.sb([64, NCH]) for _ in range(2)]
    Qb = [[k.sb([64, 64]) for _ in range(2)] for _ in range(3)]
    QTb = [[k.sb([64, 64]) for _ in range(2)] for _ in range(3)]
    R = [k.sb([64, 64]) for _ in range(3)]
    QKD = [k.sb([64, 64]) for _ in range(3)]
    vtok = [k.sb([64, 128]) for _ in range(3)]
    kend = [k.sb([64, 128]) for _ in range(3)]
    z = [k.sb([64, 128]) for _ in range(3)]
    vnew = [k.sb([64, 128]) for _ in range(3)]
    o1c = [k.sb([64, 128]) for _ in range(3)]
    gn = k.sb([64, 128]); ssq = k.sb([64, NCH])

    bX = [Buf('xq'), Buf('xk'), Buf('xv')]
    (bacc, bqd, bkd, bDm, bNB, bO, bcw, bsc, bmk, bone1, bconst, bcrow, bcols, beglb, bS, bgn, bssq) = (
        Buf(n) for n in 'acc qd kd Dm NB O cw sc mk one1 const crow cols eglb S gn ssq'.split())
    bI = cm.bconst; bJ = cm.bconst; bones = cm.bconst; beps = cm.bconst
    bct = [Buf('ct%d' % i) for i in range(6)]
    bcc = [Buf('cc%d' % i) for i in range(3)]
    bscr = [Buf('scr%d' % i) for i in range(4)]
    bttok = [Buf('tt0'), Buf('tt1')]; bftmp = [Buf('ft0'), Buf('ft1')]
    bQ = [[Buf('Q%d%d' % (a, b)) for b in range(2)] for a in range(3)]
    bQT = [[Buf('QT%d%d' % (a, b)) for b in range(2)] for a in range(3)]
    bR, bQKD, bvtok, bkend, bz, bvnew, bo1c = ([Buf('%s%d' % (n_, i)) for i in range(3)] for n_ in ('R', 'QKD', 'vt', 'ke', 'z', 'vn', 'o1c'))

    P.dma('sp', sc, pr['sc'][l], writes=[bsc])
    P.dma('sp', mk, pr['mk'], writes=[bmk])
    P.dma('sp', gn, pr['gn'][l], writes=[bgn])
    P.op('dve', lambda e: e.memset(one1, 1.0), writes=[bone1])
    P.op('dve', lambda e: e.memset(onect, 1.0), writes=[bconst])

    def tr_col(src_ap, bsrc, dst_ap, bdst, m, n):
        pt, pb = k.nextps()
        P.op('pe', lambda e: e.matmul(pt[0:n, 0:m], src_ap, I128[0:m, 0:m], start=True, stop=True), reads=[bsrc, bI], writes=[pb])
        P.op('dve', lambda e: e.tensor_copy(out=dst_ap, in_=pt[0:n, 0:m]), reads=[pb], writes=[bdst])

    tiles = [(i * 512, 512) for i in range(8)] + [(4096, 256)]

    for d in range(2):
        P.dma('sp', cw, pr['cw'][l][d], writes=[bcw])
        if d == 0:
            for t, nm in enumerate(('dn_q', 'dn_k', 'dn_v')):
                gather_fm(k, cm, G, X[t], bX[t], nm, lat_off=256, ctx_off=0)
            gather_ct(k, cm, G, ct[0], bct[0], 'dn_b0')
            gather_ct(k, cm, G, ct[1], bct[1], 'dn_a0')
        else:
            for t, nm in enumerate(('dn_q', 'dn_k', 'dn_v')):
                gather_fm(k, cm, G, acc, bacc, nm, lat_off=256, ctx_off=0)
                for blk in range(NB):
                    bp = (1 - blk) if blk < 2 else (35 - blk)
                    s_ = blk % 2
                    mm_evac(k, 128, 128, acc[:, blk * 128:(blk + 1) * 128], I128, [bacc, bI], ttok[s_], bttok[s_], eng_i=blk)
                    mm_evac(k, 128, 128, ttok[s_], cm.J, [bttok[s_], bI], X[t][:, bp * 128:(bp + 1) * 128], bX[t], eng_i=blk + 1)
            gather_ct(k, cm, G, ct[5], bct[5], 'dn_b1')
            flip_ct(k, cm, ct[5], bct[5], ct[0], bct[0], ftmp, bftmp)
            gather_ct(k, cm, G, ct[5], bct[5], 'dn_a1')
            flip_ct(k, cm, ct[5], bct[5], ct[1], bct[1], ftmp, bftmp)
        P.op('pool', lambda e: e.memset(S[:], 0.0), writes=[bS])
        for t in range(3):
            for (s0, sn) in SEGS:
                P.op('dve', lambda e, t=t, s0=s0, sn=sn: e.tensor_scalar(
                    out=acc[:, s0:s0 + sn], in0=X[t][:, s0:s0 + sn], scalar1=cw[:, t, 2:3], scalar2=None, op0=ALU.mult),
                    reads=[bX[t], bcw], writes=[bacc])
                for tap in (0, 1, 3, 4):
                    sh = tap - 2
                    a0 = s0 + max(0, -sh)
                    a1 = s0 + sn - max(0, sh)
                    P.op('dve', lambda e, t=t, tap=tap, sh=sh, a0=a0, a1=a1: e.scalar_tensor_tensor(
                        out=acc[:, a0:a1], in0=X[t][:, a0 + sh:a1 + sh], scalar=cw[:, t, tap:tap + 1], in1=acc[:, a0:a1],
                        op0=ALU.mult, op1=ALU.add), reads=[bX[t], bcw, bacc], writes=[bacc])
            P.op('act', lambda e, t=t: e.activation(out=X[t], in_=acc[:], func=AF.Silu), reads=[bacc], writes=[bX[t]])
        for t in range(2):
            for i, (t0, tn) in enumerate(tiles):
                s0, s1 = scr[(2 * i) % 4], scr[(2 * i + 1) % 4]
                b0, b1 = bscr[(2 * i) % 4], bscr[(2 * i + 1) % 4]
                P.op('act', lambda e, t=t, s0=s0, t0=t0, tn=tn: e.activation(out=s0[:, 0:tn], in_=X[t][:, t0:t0 + tn], func=AF.Square),
                     reads=[bX[t]], writes=[b0])
                pt, pb = k.nextps()
                P.op('pe', lambda e, pt=pt, s0=s0, tn=tn: e.matmul(pt[:, 0:tn], ones[:], s0[:, 0:tn], start=True, stop=True),
                     reads=[bones, b0], writes=[pb])
                P.op('act', lambda e, pt=pt, s1=s1, tn=tn: e.activation(out=s1[:, 0:tn], in_=pt[:, 0:tn], func=AF.Sqrt, bias=epst[:], scale=1.0),
                     reads=[pb, beps], writes=[b1])
                P.op('dve', lambda e, s1=s1, tn=tn: e.reciprocal(out=s1[:, 0:tn], in_=s1[:, 0:tn]), reads=[b1], writes=[b1])
                sc_ = (128 ** -0.5) if t == 0 else 1.0
                P.op('dve', lambda e, t=t, s1=s1, t0=t0, tn=tn, sc_=sc_: e.scalar_tensor_tensor(
                    out=X[t][:, t0:t0 + tn], in0=X[t][:, t0:t0 + tn], scalar=sc_, in1=s1[:, 0:tn], op0=ALU.mult, op1=ALU.mult),
                    reads=[bX[t], b1], writes=[bX[t]])
        P.op('act', lambda e: e.activation(out=ct[0][:], in_=ct[0][:], func=AF.Sigmoid), reads=[bct[0]], writes=[bct[0]])
        P.op('act', lambda e, d=d: e.activation(out=cc[0][:], in_=sc[:, d:d + 1], func=AF.Exp), reads=[bsc], writes=[bcc[0]])
        P.op('dve', lambda e: e.tensor_scalar(out=cc[0][:], in0=cc[0][:], scalar1=-1.0, scalar2=None, op0=ALU.mult), reads=[bcc[0]], writes=[bcc[0]])
        P.op('act', lambda e, d=d: e.activation(out=ct[1][:], in_=ct[1][:], func=AF.Exp, bias=sc[:, 2 + d:3 + d]), reads=[bct[1], bsc], writes=[bct[1]])
        P.op('act', lambda e: e.activation(out=ct[1][:], in_=ct[1][:], func=AF.Ln, bias=onect[:, 0:1]), reads=[bct[1], bconst], writes=[bct[1]])
        P.op('dve', lambda e: e.tensor_scalar(out=ct[1][:], in0=ct[1][:], scalar1=cc[0][:, 0:1], scalar2=None, op0=ALU.mult),
             reads=[bct[1], bcc[0]], writes=[bct[1]])
        P.op('dve', lambda e: e.tensor_tensor_scan(out=ct[2][:], data0=onect[:], data1=ct[1][:], initial=0.0, op0=ALU.mult, op1=ALU.add),
             reads=[bct[1], bconst], writes=[bct[2]])
        P.op('dve', lambda e: e.tensor_copy(out=cc[1][:], in_=ct[2][:, 63:64]), reads=[bct[2]], writes=[bcc[1]])
        P.op('dve', lambda e: e.tensor_scalar(out=ct[3][:], in0=ct[2][:], scalar1=cc[1][:, 0:1], scalar2=None, op0=ALU.subtract),
             reads=[bct[2], bcc[1]], writes=[bct[3]])
        P.op('act', lambda e: e.activation(out=ct[3][:], in_=ct[3][:], func=AF.Exp, scale=-1.0), reads=[bct[3]], writes=[bct[3]])
        P.op('dve', lambda e: e.tensor_scalar(out=ct[4][:], in0=ct[0][:], scalar1=-1.0, scalar2=None, op0=ALU.mult), reads=[bct[0]], writes=[bct[4]])
        for qi, ti in enumerate([2, 0, 4, 3]):
            tr_col(ct[ti][:], bct[ti], cols[:, qi, :], bcols, NCH, 64)
        tr_col(cc[1][:], bcc[1], crow[:], bcrow, NCH, 1)
        P.op('act', lambda e: e.activation(out=crow[:], in_=crow[:], func=AF.Exp), reads=[bcrow], writes=[bcrow])
        pt, pb = k.nextps()
        P.op('pe', lambda e, pt=pt: e.matmul(pt[:, 0:NCH], one1[0:1, :], crow[0:1, :], start=True, stop=True), reads=[bone1, bcrow], writes=[pb])
        P.op('dve', lambda e, pt=pt: e.tensor_copy(out=eglb[:], in_=pt[:, 0:NCH]), reads=[pb], writes=[beglb])
        P.dma('sp', acc[0:1, :].rearrange("o (c i) -> o c i", i=64), ct[2], reads=[bct[2]], writes=[bacc])
        for i, (t0, tn) in enumerate(tiles):
            nck = tn // 64
            c0 = t0 // 64
            s0, s1 = scr[(2 * i) % 4], scr[(2 * i + 1) % 4]
            b0, b1 = bscr[(2 * i) % 4], bscr[(2 * i + 1) % 4]
            pt, pb = k.nextps()
            P.op('pe', lambda e, pt=pt, t0=t0, tn=tn: e.matmul(pt[:, 0:tn], one1[0:1, :], acc[0:1, t0:t0 + tn], start=True, stop=True),
                 reads=[bone1, bacc], writes=[pb])
            P.op('act', lambda e, pt=pt, s0=s0, tn=tn: e.activation(out=s0[:, 0:tn], in_=pt[:, 0:tn], func=AF.Exp), reads=[pb], writes=[b0])
            P.op('dve', lambda e, s0=s0, t0=t0, tn=tn: e.tensor_tensor(out=qd[:, t0:t0 + tn], in0=X[0][:, t0:t0 + tn], in1=s0[:, 0:tn], op=ALU.mult),
                 reads=[bX[0], b0], writes=[bqd])
            P.op('pool', lambda e, s0=s0, t0=t0, tn=tn: e.tensor_tensor(out=kd[:, t0:t0 + tn], in0=X[1][:, t0:t0 + tn], in1=s0[:, 0:tn], op=ALU.mult),
                 reads=[bX[1], b0], writes=[bkd])
            d3 = s1[0:64, 0:tn].rearrange("p (c i) -> p c i", i=64)
            p3 = pt[0:64, 0:tn].rearrange("p (c i) -> p c i", i=64)
            P.op('dve', lambda e, d3=d3, p3=p3, c0=c0, nck=nck: e.tensor_tensor(out=d3, in0=p3, in1=bc_inner(cols[:, 0, c0:c0 + nck], 64), op=ALU.subtract),
                 reads=[pb, bcols], writes=[b1])
            P.op('dve', lambda e, d3=d3, nck=nck: e.tensor_tensor(out=d3, in0=d3, in1=bc_mid(mk[:, 0, :], nck), op=ALU.add),
                 reads=[b1, bmk], writes=[b1])
            P.op('act', lambda e, d3=d3, c0=c0, nck=nck: e.activation(out=DmT[:, c0:c0 + nck, :], in_=d3, func=AF.Exp), reads=[b1], writes=[bDm])
            P.op('dve', lambda e, c0=c0, nck=nck: e.tensor_tensor(out=NB_[:, c0:c0 + nck, :], in0=DmT[:, c0:c0 + nck, :], in1=bc_mid(mk[:, 1, :], nck), op=ALU.mult),
                 reads=[bDm, bmk], writes=[bNB])
            P.op('dve', lambda e, c0=c0, nck=nck: e.tensor_tensor(out=NB_[:, c0:c0 + nck, :], in0=NB_[:, c0:c0 + nck, :], in1=bc_inner(cols[:, 2, c0:c0 + nck], 64), op=ALU.mult),
                 reads=[bNB, bcols], writes=[bNB])

        def prep(c):
            a = c % 3
            sl = slice(c * 64, (c + 1) * 64)
            Q, QT = Qb[a], QTb[a]
            bq, bqt = bQ[a], bQT[a]
            pt, pb = k.nextps()
            P.op('pe', lambda e: e.matmul(pt[0:64, 0:64], X[1][:, sl], X[1][:, sl], start=True, stop=True), reads=[bX[1]], writes=[pb])
            P.op('dve', lambda e: e.tensor_tensor(out=Q[0][:], in0=pt[0:64, 0:64], in1=NB_[:, c, :], op=ALU.mult), reads=[pb, bNB], writes=[bq[0]])
            yield
            pt2, pb2 = k.nextps()
            P.op('pe', lambda e: e.matmul(pt2[0:64, 0:64], Q[0][:], I128[0:64, 0:64], start=True, stop=True), reads=[bq[0], bI], writes=[pb2])
            P.op('act', lambda e: e.activation(out=QT[0][:], in_=pt2[0:64, 0:64], func=AF.Copy), reads=[pb2], writes=[bqt[0]])
            P.op('pool', lambda e: e.tensor_tensor(out=R[a][:], in0=Q[0][:], in1=I128[0:64, 0:64], op=ALU.add), reads=[bq[0], bI], writes=[bR[a]])
            yield
            cur = 0
            for it in range(1, 6):
                nx = 1 - cur
                pq, pqb = k.nextps()
                P.op('pe', lambda e, pq=pq, cur=cur: e.matmul(pq[0:64, 0:64], Q[cur][:], QT[cur][:], start=True, stop=True),
                     reads=[bq[cur], bqt[cur]], writes=[pqb])
                if it < 5:
                    pq2, pq2b = k.nextps()
                    P.op('pe', lambda e, pq2=pq2, cur=cur: e.matmul(pq2[0:64, 0:64], QT[cur][:], Q[cur][:], start=True, stop=True),
                         reads=[bq[cur], bqt[cur]], writes=[pq2b])
                P.op('act', lambda e, pq=pq, nx=nx: e.activation(out=QT[nx][:], in_=pq[0:64, 0:64], func=AF.Copy), reads=[pqb], writes=[bqt[nx]])
                if it < 5:
                    P.op('dve', lambda e, pq2=pq2, nx=nx: e.tensor_copy(out=Q[nx][:], in_=pq2[0:64, 0:64]), reads=[pq2b], writes=[bq[nx]])
                yield
                pr, prb = k.nextps()
                P.op('pe', lambda e, pr=pr, nx=nx: e.matmul(pr[0:64, 0:64], QT[nx][:], R[a][:], start=True, stop=True),
                     reads=[bqt[nx], bR[a]], writes=[prb])
                P.op('dve', lambda e, pr=pr: e.tensor_tensor(out=R[a][:], in0=pr[0:64, 0:64], in1=R[a][:], op=ALU.add), reads=[prb, bR[a]], writes=[bR[a]])
                yield
                cur = nx
            p1, p1b = k.nextps()
            P.op('pe', lambda e: e.matmul(p1[0:64, 0:64], X[1][:, sl], X[0][:, sl], start=True, stop=True), reads=[bX[0], bX[1]], writes=[p1b])
            P.op('dve', lambda e: e.tensor_tensor(out=QKD[a][:], in0=p1[0:64, 0:64], in1=DmT[:, c, :], op=ALU.mult), reads=[p1b, bDm], writes=[bQKD[a]])
            yield
            p2, p2b = k.nextps()
            P.op('pe', lambda e: e.matmul(p2[0:64, 0:128], X[2][:, sl], I128[:], start=True, stop=True), reads=[bX[2], bI], writes=[p2b])
            P.op('act', lambda e: e.activation(out=vtok[a][:], in_=p2[0:64, 0:128], func=AF.Copy), reads=[p2b], writes=[bvtok[a]])
            p3_, p3b = k.nextps()
            P.op('pe', lambda e: e.matmul(p3_[0:64, 0:128], X[1][:, sl], I128[:], start=True, stop=True), reads=[bX[1], bI], writes=[p3b])
            P.op('dve', lambda e: e.tensor_scalar(out=kend[a][:], in0=p3_[0:64, 0:128], scalar1=cols[:, 3, c:c + 1], scalar2=None, op0=ALU.mult),
                 reads=[p3b, bcols], writes=[bkend[a]])
            yield

        def seq(c):
            a = c % 3
            sl = slice(c * 64, (c + 1) * 64)
            pk, pkb = k.nextps()
            P.op('pe', lambda e: e.matmul(pk[0:64, 0:128], kd[:, sl], S[:], start=True, stop=True), reads=[bkd, bS], writes=[pkb])
            po, pob = k.pst[6 + c % 2], k.psb[6 + c % 2]
            P.op('pe', lambda e: e.matmul(po[0:64, 0:128], qd[:, sl], S[:], start=True, stop=False), reads=[bqd, bS], writes=[pob])
            yield
            P.op('dve', lambda e: e.tensor_tensor(out=z[a][:], in0=vtok[a][:], in1=pk[0:64, 0:128], op=ALU.subtract),
                 reads=[bvtok[a], pkb], writes=[bz[a]])
            yield
            ptz, ptzb = k.nextps()
            P.op('pe', lambda e: e.matmul(ptz[0:64, 0:128], R[a][:], z[a][:], start=True, stop=True), reads=[bR[a], bz[a]], writes=[ptzb])
            yield
            P.op('dve', lambda e: e.tensor_scalar(out=vnew[a][:], in0=ptz[0:64, 0:128], scalar1=cols[:, 1, c:c + 1], scalar2=None, op0=ALU.mult),
                 reads=[ptzb, bcols], writes=[bvnew[a]])
            yield
            psu, psub = k.nextps()
            P.op('pe', lambda e: e.matmul(psu[:, 0:128], kend[a][:], vnew[a][:], start=True, stop=True), reads=[bkend[a], bvnew[a]], writes=[psub])
            P.op('pe', lambda e: e.matmul(po[0:64, 0:128], QKD[a][:], vnew[a][:], start=False, stop=True), reads=[bQKD[a], bvnew[a]], writes=[pob],
                 pe_acc=True)
            yield
            P.op('dve', lambda e: e.scalar_tensor_tensor(out=S[:], in0=S[:], scalar=eglb[:, c:c + 1], in1=psu[:, 0:128], op0=ALU.mult, op1=ALU.add),
                 reads=[bS, beglb, psub], writes=[bS])
            if d == 0:
                P.op('act', lambda e: e.activation(out=O[:, c, :], in_=po[0:64, 0:128], func=AF.Copy), reads=[pob], writes=[bO])
            else:
                cn = (3 - c) if c < 4 else (4 + 63 - (c - 4))
                P.op('act', lambda e: e.activation(out=o1c[a][:], in_=po[0:64, 0:128], func=AF.Copy), reads=[pob], writes=[bo1c[a]])
                pj, pjb = k.nextps()
                P.op('pe', lambda e: e.matmul(pj[0:64, 0:128], J64, o1c[a][:], start=True, stop=True), reads=[bJ, bo1c[a]], writes=[pjb])
                P.op('dve', lambda e: e.tensor_tensor(out=O[:, cn, :], in0=pj[0:64, 0:128], in1=O[:, cn, :], op=ALU.add),
                     reads=[pjb, bO], writes=[bO])
            yield

        k.nring = 6
        for _ in prep(0):
            pass
        preps = {}
        if NCH > 1:
            preps[1] = prep(1)
        for c in range(NCH):
            if c + 2 < NCH:
                preps[c + 2] = prep(c + 2)
            sg = seq(c)
            live = [sg] + [preps[i] for i in (c + 1, c + 2) if i in preps]
            must = [sg] + ([preps[c + 1]] if (c + 1) in preps else [])
            while must:
                for g in list(live):
                    try:
                        next(g)
                    except StopIteration:
                        live.remove(g)
                        if g in must:
                            must.remove(g)
            preps.pop(c + 1, None)


    k.nring = 8
    gate = big1[0:64, :].rearrange("p (c d) -> p c d", d=128)
    T = big2[0:64, :].rearrange("p (c d) -> p c d", d=128)
    gather_fm(k, cm, G, acc, bacc, 'dn_g', lat_off=256, ctx_off=0)
    for ch in range(NCH):
        pt, pb = k.nextps()
        P.op('pe', lambda e, pt=pt, ch=ch: e.matmul(pt[0:64, 0:128], acc[:, ch * 64:(ch + 1) * 64], I128, start=True, stop=True),
             reads=[bacc, bI], writes=[pb])
        P.op('act', lambda e, pt=pt, ch=ch: e.activation(out=gate[:, ch, :], in_=pt[0:64, 0:128], func=AF.Silu), reads=[pb], writes=[bqd, bkd])
    P.op('pool', lambda e: e.tensor_tensor(out=T, in0=O, in1=O, op=ALU.mult), reads=[bO], writes=[bX[0], bX[1]])
    P.op('dve', lambda e: e.tensor_reduce(out=ssq, in_=T, axis=AX.X, op=ALU.add), reads=[bX[0], bX[1]], writes=[bssq])
    P.op('act', lambda e: e.activation(out=ssq, in_=ssq, func=AF.Sqrt, bias=epst[0:64, :], scale=1.0 / 128), reads=[bssq, beps], writes=[bssq])
    P.op('dve', lambda e: e.reciprocal(out=ssq, in_=ssq), reads=[bssq], writes=[bssq])
    P.op('dve', lambda e: e.tensor_tensor(out=O, in0=O, in1=bc_inner(ssq, 128), op=ALU.mult), reads=[bO, bssq], writes=[bO])
    P.op('pool', lambda e: e.tensor_tensor(out=O, in0=O, in1=bc_mid(gn, NCH), op=ALU.mult), reads=[bO, bgn], writes=[bO])
    P.op('dve', lambda e: e.tensor_tensor(out=O, in0=O, in1=gate, op=ALU.mult), reads=[bO, bqd, bkd], writes=[bO])
    for ch in range(NCH):
        mm_evac(k, 128, 64, O[:, ch, :], I128[0:64, 0:64], [bO, bI], acc[:, ch * 64:(ch + 1) * 64], bacc, eng_i=ch)
    store_y_fm(k, acc, bacc, ybuf, bybuf, 1, 256, 0)


def emit_mod(k, c):
    P = k.P
    c3_d = k.dram("c3", [128, 16, 3])
    w_d = k.dram("w_mod", [D, 12288])
    b_d = k.dram("b_mod", [128, 96])
    sel_d = k.dram("sel", [128, 2])
    gn_d = k.dram("gnorm", [128, 2 * DEPTH, 16])
    modbuf = k.dram("modbuf", [128, 288], kind="Internal")
    G_mod = k.dram("G_mod", [4 * 128, 288], kind="Internal")
    c.Mlat = k.sb([128, 4 * 96]); c.Mctx = k.sb([128, 4 * 96]); c.gnorm = k.sb([128, 2 * DEPTH, 16])
    c.bM = Buf('M')
    k.persist()
    sc = k.sb([128, 16, 3]); bt = k.sb([128, 96]); sel = k.sb([128, 2]); mt = k.sb([128, 96, 3])
    wt = [k.sb([128, 16, 512]) for _ in range(2)]
    M = k.sb([128, 4, 288])
    bsc, bbt, bsel, bmt, bMM, bmb, bGm = (Buf(n) for n in 'sc bt sel mt MM modbuf Gmod'.split())
    bw = [Buf('w0'), Buf('w1')]
    P.dma('sp', sc, c3_d, writes=[bsc])
    P.dma('sp', bt, b_d, writes=[bbt])
    P.dma('sp', sel, sel_d, writes=[bsel])
    P.dma('sp', c.gnorm, gn_d, writes=[c.bM])
    P.op('act', lambda e: e.activation(out=sc, in_=sc, func=AF.Silu), reads=[bsc], writes=[bsc])
    wv = w_d.rearrange("(kc p) n -> p kc n", p=128)
    for n in range(24):
        s = n % 2
        for hf in range(2):
            P.dma('sp' if hf == 0 else 'act', wt[s][:, hf * 8:(hf + 1) * 8, :], wv[:, hf * 8:(hf + 1) * 8, n * 512:(n + 1) * 512], writes=[bw[s]])
        for cb in range(4):
            cbl = n * 4 + cb
            pt, pb = k.nextps()
            for kc in range(16):
                P.op('pe', lambda e, pt=pt, s=s, kc=kc, cb=cb: e.matmul(pt[:, 0:3], wt[s][:, kc, cb * 128:(cb + 1) * 128], sc[:, kc, :],
                                                                      start=(kc == 0), stop=(kc == 15)),
                     reads=[bsc, bw[s]], writes=[pb], pe_acc=(kc > 0))
            P.op('dve', lambda e, pt=pt, cbl=cbl: e.tensor_scalar(out=mt[:, cbl, :], in0=pt[:, 0:3], scalar1=bt[:, cbl:cbl + 1], scalar2=None, op0=ALU.add),
                 reads=[pb, bbt], writes=[bmt])
    P.dma('sp', modbuf, mt.rearrange("p a b -> p (a b)"), reads=[bmt], writes=[bmb])
    P.op('pool', lambda e: e.collective_compute("AllGather", ALU.bypass, replica_groups=[[0, 1, 2, 3], [4, 5, 6, 7]], dma_qos="P2", ins=[modbuf.opt()], outs=[G_mod.opt()]),
         reads=[bmb], writes=[bGm], dma=True, inc=1)
    P.dma('sp', M, G_mod.rearrange("(r p) x -> p r x", p=128), reads=[bGm], writes=[bMM])
    M3 = M.rearrange("p r (cb x) -> p (r cb) x", x=3)
    P.op('dve', lambda e: e.tensor_scalar(out=c.Mlat, in0=M3[:, :, 0], scalar1=sel[:, 0:1], scalar2=None, op0=ALU.mult), reads=[bMM, bsel], writes=[c.bM])
    P.op('dve', lambda e: e.scalar_tensor_tensor(out=c.Mlat, in0=M3[:, :, 1], scalar=sel[:, 1:2], in1=c.Mlat, op0=ALU.mult, op1=ALU.add),
         reads=[bMM, bsel, c.bM], writes=[c.bM])
    P.op('dve', lambda e: e.tensor_copy(out=c.Mctx, in_=M3[:, :, 2]), reads=[bMM], writes=[c.bM])


def dense_params(k):
    return {'w_in': k.dram("w_in", [DEPTH, D, IN_COLS]), 'w_out': k.dram("w_out", [DEPTH, D, D]),
            'w_fi': k.dram("w_fi", [DEPTH, D, 2 * FFN_H]), 'w_fo': k.dram("w_fo", [DEPTH, FFN_H, D])}


def emit_dense(k, c, l_cur, pr, xsrc, bxsrc, xdst, bxdst, Gy, pbufs):
    P = k.P
    first = l_cur is None
    l_next = 0 if first else l_cur + 1
    last = l_next >= DEPTH
    G_y, bGy = Gy
    p_lat, p_ctx, bp, bpc, G_lat, G_ctx, bG, bGc, groups = pbufs

    def ML(l, v):
        return c.Mlat[:, l * 96 + v * 16:l * 96 + (v + 1) * 16]

    def MC(l, v):
        return c.Mctx[:, l * 96 + v * 16:l * 96 + (v + 1) * 16]

    x = k.sb([128, 16, NTOK]); h = k.sb([128, 16, NTOK], BF16)
    wr = [k.sb([128, 12288], BF16) for _ in range(2)]
    act = [k.sb([128, 2, NTOK], BF16) for _ in range(2)]
    sg = [k.sb([128, 512], BF16) for _ in range(2)]
    stage = [k.sb([128, NTOK]) for _ in range(2)]
    rstd = k.sb([128, NTOK]); coef = k.sb([128, 4, 16])
    ones = c.ones; epst = c.eps
    bx = [Buf('x%d' % i) for i in range(16)]
    bh = Buf('h'); bwr = [Buf('wr0'), Buf('wr1')]; bact = [Buf('a0'), Buf('a1')]; bsg = [Buf('sg0'), Buf('sg1')]
    bstage = [Buf('st0'), Buf('st1')]; brstd = Buf('rstd'); bcoef = Buf('coef')
    bmv = c.bM; bones = c.bconst; beps = c.bconst

    xv = xsrc.rearrange("(kc p) n -> p kc n", p=128)
    for q4 in range(4):
        P.dma('sp' if q4 % 2 == 0 else 'act', x[:, q4 * 4:(q4 + 1) * 4, :], xv[:, q4 * 4:(q4 + 1) * 4, :],
              reads=[bxsrc], writes=bx[q4 * 4:(q4 + 1) * 4])
    if not first:
        for ci, sv in ((0, ML(l_cur, 4)), (1, MC(l_cur, 4))):
            P.op('dve', lambda e, ci=ci, sv=sv: e.scalar_tensor_tensor(out=coef[:, ci, :], in0=sv, scalar=1.0, in1=c.gnorm[:, l_cur, :],
                                                                       op0=ALU.add, op1=ALU.mult), reads=[bmv], writes=[bcoef])
    if not last:
        for ci, sv in ((2, ML(l_next, 1)), (3, MC(l_next, 1))):
            P.op('dve', lambda e, ci=ci, sv=sv: e.scalar_tensor_tensor(out=coef[:, ci, :], in0=sv, scalar=1.0, in1=c.gnorm[:, DEPTH + l_next, :],
                                                                       op0=ALU.add, op1=ALU.mult), reads=[bmv], writes=[bcoef])
    steps = []

    def wview(slot, off, kc, n):
        return wr[slot][:, off:off + kc * n].rearrange("p (kc n) -> p kc n", n=n)

    def norm(a_l, a_c, b_l, b_c):
        for ti, (t0, tn) in enumerate(TT):
            pt, pb = k.nextps()
            for kc in range(16):
                s = kc % 2
                P.op('act', lambda e, s=s, kc=kc, t0=t0, tn=tn: e.activation(
                    out=stage[s][:, 0:tn], in_=x[:, kc, t0:t0 + tn], func=AF.Square), reads=[bx[kc]], writes=[bstage[s]])
                P.op('pe', lambda e, pt=pt, s=s, tn=tn, kc=kc: e.matmul(
                    pt[:, 0:tn], ones, stage[s][:, 0:tn], start=(kc == 0), stop=(kc == 15)),
                    reads=[bones, bstage[s]], writes=[pb], pe_acc=(kc > 0))
            P.op('act', lambda e, pt=pt, t0=t0, tn=tn: e.activation(
                out=rstd[:, t0:t0 + tn], in_=pt[:, 0:tn], func=AF.Sqrt, bias=epst, scale=1.0 / D), reads=[pb, beps], writes=[brstd])
        P.op('dve', lambda e: e.reciprocal(out=rstd, in_=rstd), reads=[brstd], writes=[brstd])
        for kc in range(16):
            s = kc % 2
            P.op('dve', lambda e, s=s, kc=kc: e.tensor_tensor(out=stage[s], in0=x[:, kc, :], in1=rstd, op=ALU.mult),
                 reads=[bx[kc], brstd], writes=[bstage[s]])
            P.op('act', lambda e, s=s, kc=kc: e.activation(
                out=h[:, kc, 0:1024], in_=stage[s][:, 0:1024], func=AF.Identity, bias=b_l[:, kc:kc + 1], scale=a_l[:, kc:kc + 1]),
                reads=[bstage[s], bcoef, bmv], writes=[bh])
            P.op('act', lambda e, s=s, kc=kc: e.activation(
                out=h[:, kc, 1024:NTOK], in_=stage[s][:, 1024:NTOK], func=AF.Identity, bias=b_c[:, kc:kc + 1], scale=a_c[:, kc:kc + 1]),
                reads=[bstage[s], bcoef, bmv], writes=[bh])

    def resid_evac(pt, pb, dc, ti, gl, gc):
        t0, tn = TT[ti]
        g = gc if ti == 2 else gl
        P.op('dve', lambda e: e.scalar_tensor_tensor(
            out=x[:, dc, t0:t0 + tn], in0=pt[:, 0:tn], scalar=g[:, dc:dc + 1], in1=x[:, dc, t0:t0 + tn],
            op0=ALU.mult, op1=ALU.add), reads=[pb, bmv, bx[dc]], writes=[bx[dc]])

    if not first:
        w_out = pr['w_out'][l_cur]; w_fi = pr['w_fi'][l_cur]; w_fo = pr['w_fo'][l_cur]
        Gy2 = G_y

        def load_y():
            for kc in range(16):
                col = IDXC[('y', kc)]
                P.op('pool', lambda e, kc=kc, col=col: e.indirect_dma_start(
                    out=h[:, kc, :], out_offset=None, in_=Gy2, in_offset=bass.IndirectOffsetOnAxis(ap=c.idx[:, col:col + 1], axis=0)),
                    reads=bGy[(kc // 4) * 4:(kc // 4) * 4 + 4] + [c.bconst], writes=[bh], dma=True)
        steps.append(('call', load_y))
        for n4 in range(4):
            def ld(slot, n4=n4):
                P.dma('pool', wview(slot, 0, 16, 512), w_out.rearrange("(kc p) n -> p kc n", p=128)[:, :, n4 * 512:(n4 + 1) * 512],
                      writes=[bwr[slot]])

            def cp(slot, n4=n4):
                wv = wview(slot, 0, 16, 512)
                for m in range(4):
                    dc = n4 * 4 + m
                    for ti, (t0, tn) in enumerate(TT):
                        pt, pb = k.nextps()
                        for kc in range(16):
                            P.op('pe', lambda e, pt=pt, wv=wv, kc=kc, m=m, t0=t0, tn=tn: e.matmul(
                                pt[:, 0:tn], wv[:, kc, m * 128:(m + 1) * 128], h[:, kc, t0:t0 + tn],
                                start=(kc == 0), stop=(kc == 15)), reads=[bwr[slot], bh], writes=[pb], pe_acc=(kc > 0))
                        resid_evac(pt, pb, dc, ti, ML(l_cur, 2), MC(l_cur, 2))
            steps.append(('w', ld, cp))
        steps.append(('call', lambda: norm(coef[:, 0, :], coef[:, 1, :], ML(l_cur, 3), MC(l_cur, 3))))
        for g in range(FFN_H // 256):
            def ld(slot, g=g):
                wfv = w_fi.rearrange("(kc p) n -> p kc n", p=128)
                P.dma('pool', wview(slot, 0, 16, 256), wfv[:, :, g * 256:(g + 1) * 256], writes=[bwr[slot]])
                P.dma('pool', wview(slot, 4096, 16, 256), wfv[:, :, FFN_H + g * 256:FFN_H + (g + 1) * 256], writes=[bwr[slot]])
                P.dma('pool', wview(slot, 8192, 2, 2048), w_fo[g * 256:(g + 1) * 256, :].rearrange("(hc p) n -> p hc n", p=128),
                      writes=[bwr[slot]])

            def cp(slot, g=g):
                wg = wview(slot, 0, 16, 256); wu = wview(slot, 4096, 16, 256); wo = wview(slot, 8192, 2, 2048)
                a = g % 2
                for hc in range(2):
                    for ti, (t0, tn) in enumerate(TT):
                        pg, pgb = k.nextps()
                        pu, pub = k.nextps()
                        for (pt, pb, wv) in ((pg, pgb, wg), (pu, pub, wu)):
                            for kc in range(16):
                                P.op('pe', lambda e, pt=pt, wv=wv, kc=kc, hc=hc, t0=t0, tn=tn: e.matmul(
                                    pt[:, 0:tn], wv[:, kc, hc * 128:(hc + 1) * 128], h[:, kc, t0:t0 + tn],
                                    start=(kc == 0), stop=(kc == 15)), reads=[bwr[slot], bh], writes=[pb], pe_acc=(kc > 0))
                        s = (hc * 3 + ti) % 2
                        P.op('act', lambda e, s=s, pg=pg, tn=tn: e.activation(
                            out=sg[s][:, 0:tn], in_=pg[:, 0:tn], func=AF.Silu), reads=[pgb], writes=[bsg[s]])
                        P.op('dve', lambda e, s=s, pu=pu, a=a, hc=hc, t0=t0, tn=tn: e.tensor_tensor(
                            out=act[a][:, hc, t0:t0 + tn], in0=sg[s][:, 0:tn], in1=pu[:, 0:tn], op=ALU.mult),
                            reads=[bsg[s], pub], writes=[bact[a]])
                for dc in range(16):
                    for ti, (t0, tn) in enumerate(TT):
                        pt, pb = k.nextps()
                        for hc in range(2):
                            P.op('pe', lambda e, pt=pt, hc=hc, dc=dc, t0=t0, tn=tn: e.matmul(
                                pt[:, 0:tn], wo[:, hc, dc * 128:(dc + 1) * 128], act[a][:, hc, t0:t0 + tn],
                                start=(hc == 0), stop=(hc == 1)), reads=[bwr[slot], bact[a]], writes=[pb], pe_acc=(hc > 0))
                        resid_evac(pt, pb, dc, ti, ML(l_cur, 5), MC(l_cur, 5))
            steps.append(('w', ld, cp))

        def store_x():
            xov = xdst.rearrange("(kc p) n -> p kc n", p=128)
            for q4 in range(4):
                P.dma('sp' if q4 % 2 == 0 else 'act', xov[:, q4 * 4:(q4 + 1) * 4, :], x[:, q4 * 4:(q4 + 1) * 4, :],
                      reads=bx[q4 * 4:(q4 + 1) * 4], writes=[bxdst])
        steps.append(('call', store_x))
    if not last:
        w_in = pr['w_in'][l_next]
        steps.append(('call', lambda: norm(coef[:, 2, :], coef[:, 3, :], ML(l_next, 0), MC(l_next, 0))))
        ncol = [(i * 512, 512) for i in (0, 1, 2, 10, 11)] + [(6144, 32)] + [(i * 512, 512) for i in (7, 8, 9, 3, 4, 5, 6)]
        for (c0, cn) in ncol:
            def ld(slot, c0=c0, cn=cn):
                P.dma('pool', wview(slot, 0, 16, cn), w_in.rearrange("(kc p) n -> p kc n", p=128)[:, :, c0:c0 + cn], writes=[bwr[slot]])

            def cp(slot, c0=c0, cn=cn):
                wv = wview(slot, 0, 16, cn)
                for m in range((cn + 127) // 128):
                    mw = min(128, cn - m * 128)
                    s = m % 2
                    for ti, (t0, tn) in enumerate(TT):
                        pt, pb = k.nextps()
                        for kc in range(16):
                            P.op('pe', lambda e, pt=pt, wv=wv, kc=kc, m=m, mw=mw, t0=t0, tn=tn: e.matmul(
                                pt[0:mw, 0:tn], wv[:, kc, m * 128:m * 128 + mw], h[:, kc, t0:t0 + tn],
                                start=(kc == 0), stop=(kc == 15)), reads=[bwr[slot], bh], writes=[pb], pe_acc=(kc > 0))
                        if ti == 1:
                            P.op('dve', lambda e, pt=pt, s=s, mw=mw, t0=t0, tn=tn: e.tensor_copy(
                                out=stage[s][0:mw, t0:t0 + tn], in_=pt[0:mw, 0:tn]), reads=[pb], writes=[bstage[s]])
                        else:
                            P.op('act', lambda e, pt=pt, s=s, mw=mw, t0=t0, tn=tn: e.activation(
                                out=stage[s][0:mw, t0:t0 + tn], in_=pt[0:mw, 0:tn], func=AF.Copy), reads=[pb], writes=[bstage[s]])
                    r0 = c0 + m * 128
                    q = r0 // PCH
                    qc = r0 // PCC
                    P.dma('sp', p_lat[r0:r0 + mw, :], stage[s][0:mw, 0:1024], reads=[bstage[s]], writes=[bp[q]])
                    P.dma('act', p_ctx[r0:r0 + mw, :], stage[s][0:mw, 1024:NTOK], reads=[bstage[s]], writes=[bpc[qc]])
                    if (r0 + mw) % PCH == 0 or (r0 + mw) == IN_COLS:
                        q0 = q * PCH
                        rq = min(PCH, IN_COLS - q0)
                        P.op('pool', lambda e, q0=q0, rq=rq: e.collective_compute(
                            "AllGather", ALU.bypass, replica_groups=groups, dma_qos="P2", ins=[p_lat[q0:q0 + rq, :].opt()],
                            outs=[G_lat[4 * q0:4 * q0 + 4 * rq, :].opt()]), reads=[bp[q]], writes=[bG[q]], dma=True, inc=1, cc=True)
                    if (r0 + mw) % PCC == 0 or (r0 + mw) == IN_COLS:
                        q0 = qc * PCC
                        rq = min(PCC, IN_COLS - q0)
                        P.op('pool', lambda e, q0=q0, rq=rq: e.collective_compute(
                            "AllGather", ALU.bypass, replica_groups=groups, dma_qos="P2", ins=[p_ctx[q0:q0 + rq, :].opt()],
                            outs=[G_ctx[4 * q0:4 * q0 + 4 * rq, :].opt()]), reads=[bpc[qc]], writes=[bGc[qc]], dma=True, inc=1, cc=True)
            steps.append(('w', ld, cp))

    wsteps = [i for i, s in enumerate(steps) if s[0] == 'w']
    slot_of = {si: j % 2 for j, si in enumerate(wsteps)}
    nxt = {wsteps[j]: wsteps[j + 1] for j in range(len(wsteps) - 1)}
    if wsteps:
        steps[wsteps[0]][1](slot_of[wsteps[0]])
    for i, s in enumerate(steps):
        if s[0] == 'call':
            s[1]()
        else:
            if i in nxt:
                steps[nxt[i]][1](slot_of[nxt[i]])
            s[2](slot_of[i])


def build_fused(depth=DEPTH, stop_after=None):
    k = K()
    P = k.P
    c = setup_common(k)
    xT = k.dram("xT", [D, NTOK])
    xo = k.dram("xo", [D, NTOK], kind="ExternalOutput")
    xspill = k.dram("xspill", [D, NTOK], kind="Internal")
    p_lat = k.dram("p_lat", [IN_COLS, 1024], kind="Internal")
    p_ctx = k.dram("p_ctx", [IN_COLS, 64], kind="Internal")
    G_lat = k.dram("G_lat", [4 * IN_COLS, 1024], kind="Internal")
    G_ctx = k.dram("G_ctx", [4 * IN_COLS, 64], kind="Internal")
    ybuf = k.dram("ybuf", [4, 4, 128, NTOK], kind="Internal")
    G_y = k.dram("G_y", [4 * 4 * 4 * 128, NTOK], kind="Internal")
    bxT, bxo, bxs = (Buf(n) for n in 'xT xo xspill'.split())
    NQ = (IN_COLS + PCH - 1) // PCH
    bp = [Buf('p%d' % i) for i in range(NQ)]
    bG = [Buf('G%d' % i) for i in range(NQ)]
    NQC = (IN_COLS + PCC - 1) // PCC
    bpc = [Buf('pc%d' % i) for i in range(NQC)]
    bGc = [Buf('Gc%d' % i) for i in range(NQC)]
    bys = [Buf('y%d' % i) for i in range(16)]
    bGy = [Buf('Gy%d' % i) for i in range(16)]
    prd = dense_params(k)
    pra = {'na': attn_params(k, 'na'), 'wa': attn_params(k, 'wa')}
    prm = ml_params(k)
    prn = dn_params(k)
    groups = [[0, 1, 2, 3], [4, 5, 6, 7]]
    emit_mod(k, c)
    k.phase()

    def stop(tag):
        if stop_after != tag:
            return False
        dbgM = k.dram("dbgM", [128, 2, 384], kind="ExternalOutput")
        dbgG = k.dram("dbgG", [4 * IN_COLS, 64], kind="ExternalOutput")
        dbgY = k.dram("dbgY", [4 * 2048, 64], kind="ExternalOutput")
        P.dma('sp', dbgM[:, 0, :], c.Mlat, reads=[c.bM])
        P.dma('sp', dbgM[:, 1, :], c.Mctx, reads=[c.bM])
        P.dma('sp', dbgG, G_ctx, reads=bGc)
        P.dma('sp', dbgY, G_y[:, 1024:1088], reads=bGy)
        P.dma('act', xo, xT, reads=[bxT], writes=[bxo])
        return True

    bybuf = (bys, G_y, bGy, groups)
    pb_all = (p_lat, p_ctx, bp, bpc, G_lat, G_ctx, bG, bGc, groups)

    if stop('mod'):
        return k.done()
    emit_dense(k, c, None, prd, xT, bxT, None, None, (G_y, bGy), pb_all)
    k.phase()
    if stop('d0'):
        return k.done()
    G = (G_lat, G_ctx, bG, bGc)
    for l in range(depth):
        for (tag, fn) in (('na', lambda: emit_attn(k, c, 'na', l, G, ybuf, bybuf, pra['na'])),
                          ('wa', lambda: emit_attn(k, c, 'wa', l, G, ybuf, bybuf, pra['wa'])),
                          ('ml', lambda: emit_ml(k, c, l, G, ybuf, bybuf, prm)),
                          ('dn', lambda: emit_dn(k, c, l, G, ybuf, bybuf, prn))):
            fn()
            k.phase()
            if stop('%s%d' % (tag, l)):
                return k.done()
        lastl = (l == depth - 1)
        src, bsrc = (xT, bxT) if l == 0 else (xspill, bxs)
        dst, bdst = (xo, bxo) if lastl else (xspill, bxs)
        emit_dense(k, c, l, prd, src, bsrc, dst, bdst, (G_y, bGy), pb_all)
        k.phase()
        if (not lastl) and stop('d%d' % (l + 1)):
            return k.done()
    return k.done()

import numpy as np

NCORES = 8
_PROG = {}


def _rope_tables():
    n = 4096
    t = np.arange(n)
    n_freq = 32
    inv_freq = (np.float32(10000.0) ** (-np.arange(n_freq, dtype=np.float32) / np.float32(n_freq))).astype(np.float32)
    pos = np.stack([t // 64, t % 64], -1).astype(np.float32)
    ang = pos[:, :, None] * inv_freq
    cos = np.cos(ang).astype(np.float32)
    sin = np.sin(ang).astype(np.float32)
    C = np.zeros((128, n), np.float32)
    S = np.zeros((128, n), np.float32)
    RmT = np.zeros((128, 128), np.float32)
    for d in range(128):
        a, tt, f = d // 64, (d // 32) % 2, d % 32
        C[d] = cos[:, a, f]
        S[d] = sin[:, a, f]
        if tt == 0:
            RmT[d + 32, d] = -1.0
        else:
            RmT[d - 32, d] = 1.0
    return C, S, RmT


def _na_bias_tables(rpb):
    NEG = -30000.0
    tab = np.zeros((128, 5, 7 * 128), np.float32)
    classes = [(0, [0, 1, 2, 3]), (1, [-1, 0, 1, 2]), (5, [-2, -1, 0, 1, 2]), (30, [-2, -1, 0, 1]), (31, [-3, -2, -1, 0])]
    kk = np.arange(128)
    qq = np.arange(128)
    for cls, (n, offs) in enumerate(classes):
        r = 2 * n + qq // 64
        qc = qq % 64
        rs = np.clip(r - 4, 0, 56)
        ws = np.clip(qc - 8, 0, 48)
        for ci, off in enumerate(offs):
            ch = n + off
            kr = 2 * ch + kk // 64
            kc = kk % 64
            ok = ((kr[:, None] >= rs[None, :]) & (kr[:, None] < rs[None, :] + 8)
                  & (kc[:, None] >= ws[None, :]) & (kc[:, None] < ws[None, :] + 16))
            dr = np.clip(kr[:, None] - r[None, :] + 7, 0, 14)
            dc = np.clip(kc[:, None] - qc[None, :], -15, 15) + 15
            tab[:, cls, ci * 128:(ci + 1) * 128] = np.where(ok, rpb[dr, dc], NEG)
    return tab


def _core_inputs(I, core, shared):
    b, j = core // 4, core % 4
    m = dict(shared)
    m["idx"] = make_idx(j)
    m["w_mod"] = I['w_ada'][j]
    m["b_mod"] = np.ascontiguousarray(I['b_ada'][j].reshape(96, 128).T)
    sel = np.zeros((128, 2), np.float32)
    sel[:, b] = 1.0
    m["sel"] = sel
    xc = np.concatenate([I['x'][b, j * 1024:(j + 1) * 1024], I['ctx'][b, j * 64:(j + 1) * 64]], 0)
    m["xT"] = np.ascontiguousarray(xc.T)
    m["wa_sinkb"] = np.ascontiguousarray(np.broadcast_to(I['wa_sink'][:, j][:, None, None], (4, 128, 1))).astype(np.float32)
    m["na_bias"] = np.stack([_na_bias_tables(I['na_rpb'][ll][j]) for ll in range(4)], 0)
    gbv = np.stack([I['ml_i_bias'][:, 0, j], I['ml_i_bias'][:, 1, j], I['ml_f_bias'][:, 0, j], I['ml_f_bias'][:, 1, j]], -1)
    m["ml_gb"] = np.ascontiguousarray(np.broadcast_to(gbv[:, None, :], (4, 68, 4))).astype(np.float32)
    m["ml_gn"] = np.ascontiguousarray(np.broadcast_to(I['ml_norm'][:, j][:, None, :], (4, 64, 128))).astype(np.float32)
    cw = np.stack([I['dn_conv'][:, :, t * 512 + j * 128: t * 512 + (j + 1) * 128] for t in range(3)], 1)
    cw2 = np.stack([cw, cw[:, :, ::-1, :]], 1)
    m["dn_cw"] = np.ascontiguousarray(cw2.transpose(0, 1, 4, 2, 3)).astype(np.float32)
    scv = np.stack([I['dn_a_log'][:, 0, j], I['dn_a_log'][:, 1, j], I['dn_dt_bias'][:, 0, j], I['dn_dt_bias'][:, 1, j]], -1)
    m["dn_sc"] = np.ascontiguousarray(np.broadcast_to(scv[:, None, :], (4, 68, 4))).astype(np.float32)
    return m


def kernel(**I):
    I = {k_: np.asarray(v, np.float32) for k_, v in I.items()}
    if 'nc' not in _PROG:
        _PROG['nc'] = build_fused()
    nc = _PROG['nc']
    C, S, RmT = _rope_tables()
    kk = np.arange(128)[:, None]
    qq = np.arange(128)[None, :]
    jj = np.arange(64)
    mk = np.zeros((64, 2, 64), np.float32)
    mk[:, 0, :] = np.where(jj[None, :] >= jj[:, None], 0.0, -30000.0)
    mk[:, 1, :] = (jj[None, :] > jj[:, None]).astype(np.float32)
    c3 = np.stack([I['c'][0], I['c'][1], I['c_ctx']], 0)
    gnorm = np.zeros((128, 8, 16), np.float32)
    for l in range(4):
        gnorm[:, l] = I['norm_ffn'][l].reshape(16, 128).T
        gnorm[:, 4 + l] = I['norm_mix'][l].reshape(16, 128).T
    shared = {
        "I128": np.eye(128, dtype=np.float32), "J128": np.ascontiguousarray(np.eye(128, dtype=np.float32)[::-1]),
        "c3": np.ascontiguousarray(c3.T.reshape(16, 128, 3).transpose(1, 0, 2)), "gnorm": gnorm,
        "w_in": I['w_in'], "w_out": I['w_out'], "w_fi": I['w_ffn_in'], "w_fo": I['w_ffn_out'],
        "na_gains": np.ascontiguousarray(I['na_qk_gain'].transpose(0, 2, 1)),
        "wa_gains": np.ascontiguousarray(I['wa_qk_gain'].transpose(0, 2, 1)),
        "cosT": C, "sinT": S, "rmT": RmT, "wamask": np.concatenate([(kk >= qq), (kk <= qq)], 1).astype(np.float32),
        "ml_tri": (jj[None, :] >= jj[:, None]).astype(np.float32),
        "dn_gn": np.ascontiguousarray(np.broadcast_to(I['dn_norm'][:, None, :], (4, 64, 128))).astype(np.float32),
        "dn_masks": mk,
    }
    ins = [_core_inputs(I, core, shared) for core in range(NCORES)]
    res = run_bass_kernel_spmd(nc, ins, core_ids=list(range(NCORES))).results
    out = np.zeros((2, 4096, 2048), np.float32)
    for core in range(NCORES):
        b, j = core // 4, core % 4
        out[b, j * 1024:(j + 1) * 1024] = res[core]["xo"][:, 0:1024].T
    return out
```

```python
import contextlib

import numpy as np
import concourse.bass as bass
import concourse.mybir as mybir
from concourse.bass_utils import run_bass_kernel_spmd
from concourse.alu_op_type import AluOpType as ALU

AF = mybir.ActivationFunctionType
AX = mybir.AxisListType
F32 = mybir.dt.float32
BF16 = mybir.dt.bfloat16
F32R = mybir.dt.float32r

ENGS = ('pe', 'act', 'dve', 'pool', 'sp')
NDMASEM = 12
NCCSEM = 56


class Buf:
    __slots__ = ('name', 'w', 'r')

    def __init__(self, name=''):
        self.name = name
        self.w = None
        self.r = []


class Op:
    __slots__ = ('eng', 'pos', 'fn', 'waits', 'flag', 'val', 'dma', 'sem', 'inc')


class Prog:
    def __init__(self, nc):
        self.nc = nc
        self.ops = {e: [] for e in ENGS}
        self.seen = {e: {} for e in ENGS}
        self.dma_last = [None] * (NDMASEM + NCCSEM)
        self.dma_cnt = [0] * (NDMASEM + NCCSEM)
        self.dma_tot = [0] * (NDMASEM + NCCSEM)
        self.dma_rr = 0
        self.cc_rr = 0
        self.ndma = 0

    def _dep(self, o, d, same_ok=False):
        if d is None:
            return
        E = o.eng
        if d.dma:
            key = ('d', d.sem)
            if self.seen[E].get(key, 0) >= d.val:
                return
            self.seen[E][key] = d.val
            o.waits.append(d)
        else:
            if d.eng == E and same_ok:
                return
            key = ('e', d.eng)
            if self.seen[E].get(key, -1) >= d.pos:
                return
            self.seen[E][key] = d.pos
            d.flag = True
            o.waits.append(d)

    def op(self, eng, fn, reads=(), writes=(), dma=False, pe_acc=False, inc=16, cc=False):
        o = Op()
        o.eng = eng
        o.pos = len(self.ops[eng])
        o.fn = fn
        o.waits = []
        o.flag = False
        o.val = None
        o.dma = dma
        o.sem = None
        for b in reads:
            self._dep(o, b.w)
        for b in writes:
            if not (pe_acc and b.w is not None and b.w.eng == 'pe' and eng == 'pe'):
                self._dep(o, b.w)
            for r in b.r:
                self._dep(o, r, same_ok=(not r.dma))
        if dma:
            if cc:
                s = NDMASEM + self.cc_rr
                self.cc_rr = (self.cc_rr + 1) % NCCSEM
            else:
                s = self.dma_rr
                self.dma_rr = (self.dma_rr + 1) % NDMASEM
            self._dep(o, self.dma_last[s])
            self.dma_cnt[s] += 1
            self.dma_tot[s] += inc
            o.sem = s
            o.inc = inc
            o.val = self.dma_tot[s]
            self.dma_last[s] = o
            self.ndma += 1
        self.ops[eng].append(o)
        for b in reads:
            if dma:
                b.r.append(o)
            else:
                b.r = [r for r in b.r if r.dma or r.eng != eng]
                b.r.append(o)
        for b in writes:
            b.w = o
            b.r = []
        return o

    def dma(self, q, out, in_, reads=(), writes=(), **kw):
        return self.op(q, lambda e: e.dma_start(out=out, in_=in_, **kw), reads, writes, dma=True)

    def barrier(self):
        lasts = {}
        for e in ENGS:
            for o in reversed(self.ops[e]):
                if (not o.dma) and o.fn is not None:
                    lasts[e] = o
                    break
        for E in ENGS:
            o = Op()
            o.eng = E
            o.pos = len(self.ops[E])
            o.fn = None
            o.waits = []
            o.flag = False
            o.val = None
            o.dma = False
            o.sem = None
            o.inc = 0
            for F in ENGS:
                if F != E and F in lasts:
                    self._dep(o, lasts[F])
            for d in self.dma_last[:NDMASEM]:
                if d is not None:
                    self._dep(o, d)
            self.ops[E].append(o)

    def finish(self):
        o = Op()
        o.eng = 'sp'
        o.pos = len(self.ops['sp'])
        o.fn = None
        o.waits = []
        o.flag = False
        o.val = None
        o.dma = False
        o.sem = None
        for d in self.dma_last:
            if d is not None:
                self._dep(o, d)
        self.ops['sp'].append(o)

    def emit(self):
        nc = self.nc
        self.finish()
        for e in ENGS:
            c = 0
            for o in self.ops[e]:
                if not o.dma and o.flag:
                    c += 1
                    o.val = c
        import contextlib
        with contextlib.ExitStack() as st:
            esem = {e: st.enter_context(nc.semaphore('s_' + e)) for e in ENGS}
            dsem = [st.enter_context(nc.semaphore('d%d' % i)) for i in range(NDMASEM + NCCSEM)]
            block = st.enter_context(nc.Block())

            def run(e):
                def body(eng):
                    for o in self.ops[e]:
                        for d in o.waits:
                            if d.dma:
                                eng.wait_ge(dsem[d.sem], d.val)
                            else:
                                eng.wait_ge(esem[d.eng], d.val)
                        if o.fn is None:
                            continue
                        ins = o.fn(eng)
                        if o.dma:
                            ins.then_inc(dsem[o.sem], o.inc)
                        elif o.flag:
                            ins.then_inc(esem[e], 1)
                return body

            block.tensor(run('pe'))
            block.scalar(run('act'))
            block.vector(run('dve'))
            block.gpsimd(run('pool'))
            block.sync(run('sp'))

import numpy as np

U32 = mybir.dt.uint32
DEPTH = 4
D = 2048
NTOK = 1088
TT = [(0, 512), (512, 512), (1024, 64)]
IN_COLS = 6176
FFN_H = 5632
EPS = 1e-6
NT = 4352
NB = 34
NCH = 68
SCALE = 128 ** -0.5

FM_TENSORS = [('na_q', 0), ('na_k', 512), ('na_v', 1024),
              ('dn_q', 1536), ('dn_k', 2048), ('dn_v', 2560), ('dn_g', 3072),
              ('ml_q', 3600), ('ml_k', 3856), ('ml_v', 4112), ('ml_o', 4624),
              ('wa_q', 5152), ('wa_k', 5664), ('wa_v', 5920)]
FM_WIDTH = {'ml_q': 64, 'ml_k': 64}
CT_ROWS = [('dn_b0', 3584 + 0), ('dn_b1', 3584 + 4), ('dn_a0', 3584 + 8), ('dn_a1', 3584 + 12),
           ('ml_i0', 5136 + 0), ('ml_i1', 5136 + 4), ('ml_f0', 5136 + 8), ('ml_f1', 5136 + 12)]


def idx_cols():
    cols = {}
    n = 0
    for name, _ in FM_TENSORS:
        for r in range(4):
            cols[(name, r)] = n
            cols[(name, r, 'c')] = n + 1
            n += 2
    for name, _ in CT_ROWS:
        cols[(name, 'lat')] = n
        cols[(name, 'ctx')] = n + 1
        n += 2
    for kc in range(16):
        cols[('y', kc)] = n
        n += 1
    return cols, n


IDXC, NIDX = idx_cols()


PCH = 256


def grow(r, cidx):
    cidx = np.asarray(cidx)
    start = (cidx // PCH) * PCH
    rows_q = np.minimum(PCH, IN_COLS - start)
    return 4 * start + r * rows_q + (cidx - start)


PCC = 256


def grow_ctx(r, cidx):
    cidx = np.asarray(cidx)
    start = (cidx // PCC) * PCC
    rows_q = np.minimum(PCC, IN_COLS - start)
    return 4 * start + r * rows_q + (cidx - start)


def make_idx(j):
    t = np.zeros((128, NIDX), np.uint32)
    p = np.arange(128)
    for name, base in FM_TENSORS:
        w = FM_WIDTH.get(name, 128)
        hj = (j // 2) if name in ('wa_k', 'wa_v') else j
        c0 = base + hj * w
        for r in range(4):
            t[:, IDXC[(name, r)]] = grow(r, c0 + np.minimum(p, w - 1))
            t[:, IDXC[(name, r, 'c')]] = grow_ctx(r, c0 + np.minimum(p, w - 1))
    for name, base in CT_ROWS:
        c = base + j
        rev = name.endswith('1')
        ci = np.arange(64)
        cn = (63 - ci) if rev else ci
        t[0:64, IDXC[(name, 'lat')]] = grow(cn // 16, c) * 16 + (cn % 16)
        rr = np.arange(4)
        rn = (3 - rr) if rev else rr
        t[0:4, IDXC[(name, 'ctx')]] = grow_ctx(rn, c)
    for kc in range(16):
        g, r = kc // 4, kc % 4
        t[:, IDXC[('y', kc)]] = ((g * 4 + j) * 4 + r) * 128 + p
    return t


class K:
    def __init__(self, arena_kb=204):
        self.nc = bass.Bass("TRN2", target_bir_lowering=False)
        self.st = contextlib.ExitStack()
        self.P = Prog(self.nc)
        self.psn = 0
        self.nring = 8
        self.words = arena_kb * 256
        self.arena = self.st.enter_context(self.nc.sbuf_tensor("arena", [128, self.words], F32))
        self.off = 0
        self.mark = 0
        self.pst = [self.st.enter_context(self.nc.psum_tensor("ps%d" % i, [128, 512], F32)) for i in range(8)]
        self.psb = [Buf('ps%d' % i) for i in range(8)]
        self.drams = {}

    def dram(self, name, shape, dt=F32, kind="ExternalInput"):
        if kind == "Internal":
            t = self.nc.dram_tensor(name, list(shape), dt).ap()
        else:
            t = self.nc.dram_tensor(name, list(shape), dt, kind=kind).ap()
        self.drams[name] = t
        return t

    def sb(self, shape, dt=F32):
        shape = list(shape)
        n = 1
        for s in shape[1:]:
            n *= s
        bpe = 2 if dt == BF16 else 4
        words = (n * bpe + 3) // 4
        words = (words + 7) // 8 * 8
        assert self.off + words <= self.words, ("SBUF arena overflow", self.off, words, self.words)
        ap = self.arena[0:shape[0], self.off:self.off + words]
        self.off += words
        if dt != F32:
            ap = ap.bitcast(dt)
        ap = ap[:, 0:n]
        if len(shape) == 3:
            ap = ap.rearrange("p (a b) -> p a b", b=shape[2])
        elif len(shape) == 4:
            ap = ap.rearrange("p (a b c) -> p a b c", b=shape[2], c=shape[3])
        return ap

    def persist(self):
        self.mark = self.off

    def phase(self):
        self.P.barrier()
        self.off = self.mark

    def nextps(self):
        i = self.psn % self.nring
        self.psn += 1
        return self.pst[i], self.psb[i]

    def done(self):
        self.P.emit()
        self.st.close()
        return self.nc


class Common:
    pass


def setup_common(k):
    P = k.P
    c = Common()
    c.idx_d = k.dram("idx", [128, NIDX], U32)
    c.I_d = k.dram("I128", [128, 128])
    c.J_d = k.dram("J128", [128, 128])
    c.idx = k.sb([128, NIDX], U32)
    c.I = k.sb([128, 128])
    c.J = k.sb([128, 128])
    c.ones = k.sb([128, 128])
    c.eps = k.sb([128, 1])
    c.bconst = Buf('const')
    P.dma('sp', c.idx, c.idx_d, writes=[c.bconst])
    P.dma('sp', c.I, c.I_d, writes=[c.bconst])
    P.dma('sp', c.J, c.J_d, writes=[c.bconst])
    P.op('dve', lambda e: e.memset(c.ones, 1.0), writes=[c.bconst])
    P.op('dve', lambda e: e.memset(c.eps, EPS), writes=[c.bconst])
    c.J64 = c.J[0:64, 64:128]
    return c


def gather_fm(k, c, G, dst, bdst, name, rows=128, lat_off=0, ctx_off=4096):
    P = k.P
    G_lat, G_ctx, bGl, bGc = G
    base = dict(FM_TENSORS)[name]
    wdt = FM_WIDTH.get(name, 128)
    gdeps = bGl[base // PCH:(base + 4 * wdt - 1) // PCH + 1]
    cdeps = bGc[base // PCC:(base + 4 * wdt - 1) // PCC + 1]
    for r in range(4):
        col = IDXC[(name, r)]
        colc = IDXC[(name, r, 'c')]
        P.op('pool', lambda e, r=r, col=col: e.indirect_dma_start(
            out=dst[0:rows, lat_off + r * 1024: lat_off + (r + 1) * 1024], out_offset=None, in_=G_lat,
            in_offset=bass.IndirectOffsetOnAxis(ap=c.idx[0:rows, col:col + 1], axis=0)),
            reads=gdeps + [c.bconst], writes=[bdst], dma=True)
        P.op('pool', lambda e, r=r, colc=colc: e.indirect_dma_start(
            out=dst[0:rows, ctx_off + r * 64: ctx_off + (r + 1) * 64], out_offset=None, in_=G_ctx,
            in_offset=bass.IndirectOffsetOnAxis(ap=c.idx[0:rows, colc:colc + 1], axis=0)),
            reads=cdeps + [c.bconst], writes=[bdst], dma=True)


def gather_ct(k, c, G, dst, bdst, name):
    P = k.P
    G_lat, G_ctx, bGl, bGc = G
    base = dict(CT_ROWS)[name]
    gdeps = bGl[base // PCH:(base + 3) // PCH + 1]
    cdeps = bGc[base // PCC:(base + 3) // PCC + 1]
    G64 = G_lat.rearrange("r (a b) -> (r a) b", b=64)
    cl, cc = IDXC[(name, 'lat')], IDXC[(name, 'ctx')]
    P.op('pool', lambda e: e.indirect_dma_start(out=dst[4:68, :], out_offset=None, in_=G64,
                                                in_offset=bass.IndirectOffsetOnAxis(ap=c.idx[0:64, cl:cl + 1], axis=0)),
         reads=gdeps + [c.bconst], writes=[bdst], dma=True)
    P.op('pool', lambda e: e.indirect_dma_start(out=dst[0:4, :], out_offset=None, in_=G_ctx,
                                                in_offset=bass.IndirectOffsetOnAxis(ap=c.idx[0:4, cc:cc + 1], axis=0)),
         reads=cdeps + [c.bconst], writes=[bdst], dma=True)


def mm_evac(k, out_rows, out_cols, lhsT, rhs, reads, dst, bdst, eng_i=0, scale_col=None, extra_reads=()):
    P = k.P
    pt, pb = k.nextps()
    P.op('pe', lambda e: e.matmul(pt[0:out_rows, 0:out_cols], lhsT, rhs, start=True, stop=True), reads=list(reads), writes=[pb])
    if scale_col is not None:
        P.op('dve', lambda e: e.tensor_scalar(out=dst, in0=pt[0:out_rows, 0:out_cols], scalar1=scale_col, scalar2=None, op0=ALU.mult),
             reads=[pb] + list(extra_reads), writes=[bdst])
    elif eng_i % 2 == 0:
        P.op('act', lambda e: e.activation(out=dst, in_=pt[0:out_rows, 0:out_cols], func=AF.Copy), reads=[pb], writes=[bdst])
    else:
        P.op('dve', lambda e: e.tensor_copy(out=dst, in_=pt[0:out_rows, 0:out_cols]), reads=[pb], writes=[bdst])


def store_y_fm(k, yT, byT, ybuf, bybuf, g, lat_off, ctx_off):
    P = k.P
    bys, G_y, bGy, groups = bybuf
    for jt in range(4):
        q = 'sp' if jt % 2 == 0 else 'act'
        P.dma(q, ybuf[g, jt, :, 0:1024], yT[:, lat_off + jt * 1024: lat_off + (jt + 1) * 1024], reads=[byT], writes=[bys[g * 4 + jt]])
        P.dma(q, ybuf[g, jt, :, 1024:1088], yT[:, ctx_off + jt * 64: ctx_off + (jt + 1) * 64], reads=[byT], writes=[bys[g * 4 + jt]])
    yb2 = ybuf.rearrange("g j d n -> (g j d) n")
    for jt in range(4):
        q = g * 4 + jt
        P.op('pool', lambda e, q=q: e.collective_compute(
            "AllGather", ALU.bypass, replica_groups=groups, dma_qos="P3", ins=[yb2[q * 128:(q + 1) * 128, :].opt()], outs=[G_y[q * 512:(q + 1) * 512, :].opt()]),
            reads=[bys[q]], writes=[bGy[q]], dma=True, inc=1, cc=True)


def qknorm(k, bufs, src, dst, gain, tiles, rope=None):
    P = k.P
    (ones, bones, epst, beps, scr, bscr, bsrc, bdst, bg) = bufs
    for i, (t0, tn, dorope) in enumerate(tiles):
        s0, s1, s2 = scr[(3 * i) % 6], scr[(3 * i + 1) % 6], scr[(3 * i + 2) % 6]
        b0, b1, b2 = bscr[(3 * i) % 6], bscr[(3 * i + 1) % 6], bscr[(3 * i + 2) % 6]
        P.op('act', lambda e, s0=s0, t0=t0, tn=tn: e.activation(out=s0[:, 0:tn], in_=src[:, t0:t0 + tn], func=AF.Square),
             reads=[bsrc], writes=[b0])
        pt, pb = k.nextps()
        P.op('pe', lambda e, pt=pt, s0=s0, tn=tn: e.matmul(pt[:, 0:tn], ones, s0[:, 0:tn], start=True, stop=True),
             reads=[bones, b0], writes=[pb])
        P.op('act', lambda e, pt=pt, s1=s1, tn=tn: e.activation(out=s1[:, 0:tn], in_=pt[:, 0:tn], func=AF.Sqrt,
                                                                  bias=epst, scale=1.0 / 128), reads=[pb, beps], writes=[b1])
        P.op('dve', lambda e, s1=s1, tn=tn: e.reciprocal(out=s1[:, 0:tn], in_=s1[:, 0:tn]), reads=[b1], writes=[b1])
        if not dorope:
            P.op('dve', lambda e, s1=s1, t0=t0, tn=tn: e.scalar_tensor_tensor(
                out=dst[:, t0:t0 + tn], in0=src[:, t0:t0 + tn], scalar=gain, in1=s1[:, 0:tn], op0=ALU.mult, op1=ALU.mult),
                reads=[bsrc, b1, bg], writes=[bdst])
        else:
            C, S, RmT, brope = rope
            P.op('dve', lambda e, s1=s1, s2=s2, t0=t0, tn=tn: e.scalar_tensor_tensor(
                out=s2[:, 0:tn], in0=src[:, t0:t0 + tn], scalar=gain, in1=s1[:, 0:tn], op0=ALU.mult, op1=ALU.mult),
                reads=[bsrc, b1, bg], writes=[b2])
            pr, prb = k.nextps()
            P.op('pe', lambda e, pr=pr, s2=s2, tn=tn: e.matmul(pr[:, 0:tn], RmT, s2[:, 0:tn], start=True, stop=True),
                 reads=[brope, b2], writes=[prb])
            P.op('dve', lambda e, pr=pr, s0=s0, t0=t0, tn=tn: e.tensor_tensor(
                out=s0[:, 0:tn], in0=pr[:, 0:tn], in1=S[:, t0:t0 + tn], op=ALU.mult), reads=[prb, brope], writes=[b0])
            P.op('pool', lambda e, s1=s1, s2=s2, t0=t0, tn=tn: e.tensor_tensor(
                out=s1[:, 0:tn], in0=s2[:, 0:tn], in1=C[:, t0:t0 + tn], op=ALU.mult), reads=[b2, brope], writes=[b1])
            P.op('dve', lambda e, s0=s0, s1=s1, t0=t0, tn=tn: e.tensor_tensor(
                out=dst[:, t0:t0 + tn], in0=s0[:, 0:tn], in1=s1[:, 0:tn], op=ALU.add), reads=[b0, b1], writes=[bdst])


def attn_params(k, kind):
    pr = {}
    if kind == 'wa':
        pr['gains'] = k.dram("wa_gains", [DEPTH, 128, 2])
        pr['cos'] = k.dram("cosT", [128, 4096])
        pr['sin'] = k.dram("sinT", [128, 4096])
        pr['rm'] = k.dram("rmT", [128, 128])
        pr['sink'] = k.dram("wa_sinkb", [DEPTH, 128, 1])
        pr['mask'] = k.dram("wamask", [128, 256])
    else:
        pr['gains'] = k.dram("na_gains", [DEPTH, 128, 2])
        pr['bias'] = k.dram("na_bias", [DEPTH, 128, 5, 7 * 128])
    return pr


def emit_attn(k, c, kind, l, G, ybuf, bybuf, pr):
    P = k.P
    g_slot = 0 if kind == 'na' else 3
    pre = 'na' if kind == 'na' else 'wa'
    q = k.sb([128, NT]); kk = k.sb([128, NT]); vT = k.sb([128, NT])
    qb = k.sb([128, NT], BF16); kb = k.sb([128, NT], BF16)
    V1 = k.sb([128, NB, 129], BF16)
    g = k.sb([128, 2])
    scr = [k.sb([128, 512]) for _ in range(6)]
    bq, bk, bv, bqb, bkb, bV, bg = (Buf(n) for n in 'q k v qb kb V g'.split())
    bscr = [Buf('scr%d' % i) for i in range(6)]
    bo = [Buf('o%d' % i) for i in range(NB)]
    ones, bones, epst, beps = c.ones, c.bconst, c.eps, c.bconst

    gather_fm(k, c, G, q, bq, pre + '_q')
    gather_fm(k, c, G, kk, bk, pre + '_k')
    gather_fm(k, c, G, vT, bv, pre + '_v')
    P.dma('sp', g, pr['gains'][l], writes=[bg])
    P.op('pool', lambda e: e.memset(V1[:, :, 128:129], 1.0), writes=[bV])
    rope = None
    if kind == 'wa':
        C = k.sb([128, 4096]); S = k.sb([128, 4096]); RmT = k.sb([128, 128])
        sk = k.sb([128, 1]); es = k.sb([128, 1]); msk = k.sb([128, 256], BF16)
        brope, bsk, bes, bmsk = Buf('rope'), Buf('sk'), Buf('es'), Buf('msk')
        P.dma('sp', C, pr['cos'], writes=[brope])
        P.dma('act', S, pr['sin'], writes=[brope])
        P.dma('sp', RmT, pr['rm'], writes=[brope])
        P.dma('sp', sk, pr['sink'][l], writes=[bsk])
        P.dma('pool', msk, pr['mask'], writes=[bmsk])
        P.op('act', lambda e: e.activation(out=es, in_=sk, func=AF.Exp), reads=[bsk], writes=[bes])
        rope = (C, S, RmT, brope)
    else:
        bias = k.sb([128, 5, 7 * 128])
        bbias = Buf('bias')
        P.dma('sp', bias, pr['bias'][l], writes=[bbias])
    for n in range(NB):
        mm_evac(k, 128, 128, vT[:, n * 128:(n + 1) * 128], c.I, [bv, c.bconst], V1[:, n, 0:128], bV, eng_i=n)

    tiles_q = [(i * 512, 512, kind == 'wa') for i in range(8)] + [(4096, 256, False)]
    qknorm(k, (ones, bones, epst, beps, scr, bscr, bq, bqb, bg), q, qb, g[:, 0:1], tiles_q, rope)
    qknorm(k, (ones, bones, epst, beps, scr, bscr, bk, bkb, bg), kk, kb, g[:, 1:2], tiles_q, rope)

    osb = q.rearrange("p (n d) -> p n d", d=128)
    yT = kk
    eA = [k.sb([128, 512], BF16) for _ in range(2)]
    eB = [k.sb([128, 512], BF16) for _ in range(2)]
    tmpf = [k.sb([128, 512]) for _ in range(2)]
    rd = [k.sb([128, 1]) for _ in range(2)]
    beA = [Buf('eA0'), Buf('eA1')]; beB = [Buf('eB0'), Buf('eB1')]
    btmp = [Buf('tf0'), Buf('tf1')]; brd = [Buf('rd0'), Buf('rd1')]

    def smat(pt, pb, ci, ch, n):
        P.op('pe', lambda e: e.matmul(pt[:, ci * 128:(ci + 1) * 128], kb[:, ch * 128:(ch + 1) * 128],
                                      qb[:, n * 128:(n + 1) * 128], start=True, stop=True),
             reads=[bkb, bqb], writes=[pb], pe_acc=(ci > 0))

    def stage1(n):
        if True:
            s = n % 2
            groups = []
            if kind == 'wa':
                if n < 32:
                    A = [n, 32, 33]
                    B = ([n - 1] if n > 0 else []) + ([n + 1] if n < 31 else [])
                    Bm = ([0] if n > 0 else []) + ([1] if n < 31 else [])
                else:
                    A, B, Bm = [32, 33], [], []
                pa, pab = k.nextps()
                for ci, ch in enumerate(A):
                    smat(pa, pab, ci, ch, n)
                P.op('act', lambda e, pa=pa, s=s, w=len(A) * 128: e.activation(
                    out=eA[s][:, 0:w], in_=pa[:, 0:w], func=AF.Exp, scale=SCALE), reads=[pab], writes=[beA[s]])
                groups.append((eA[s], beA[s], A))
                if B:
                    pbt, pbb = k.nextps()
                    for ci, ch in enumerate(B):
                        smat(pbt, pbb, ci, ch, n)
                    w = len(B) * 128
                    P.op('act', lambda e, pbt=pbt, s=s, w=w: e.activation(
                        out=eB[s][:, 0:w], in_=pbt[:, 0:w], func=AF.Exp, scale=SCALE), reads=[pbb], writes=[beB[s]])
                    for ci, mi in enumerate(Bm):
                        P.op('pool', lambda e, s=s, ci=ci, mi=mi: e.tensor_tensor(
                            out=eB[s][:, ci * 128:(ci + 1) * 128], in0=eB[s][:, ci * 128:(ci + 1) * 128],
                            in1=msk[:, mi * 128:(mi + 1) * 128], op=ALU.mult), reads=[beB[s], bmsk], writes=[beB[s]])
                    groups.append((eB[s], beB[s], B))
            else:
                if n < 32:
                    if n == 0:
                        cls, offs = 0, [0, 1, 2, 3]
                    elif n == 1:
                        cls, offs = 1, [-1, 0, 1, 2]
                    elif n == 30:
                        cls, offs = 3, [-2, -1, 0, 1]
                    elif n == 31:
                        cls, offs = 4, [-3, -2, -1, 0]
                    else:
                        cls, offs = 2, [-2, -1, 0, 1, 2]
                    chs = [n + o for o in offs] + [32, 33]
                    G1, G2 = chs[:4], chs[4:]
                    col = 0
                    for (et, ebf, Gc) in ((eA[s], beA[s], G1), (eB[s], beB[s], G2)):
                        pt, pb = k.nextps()
                        for ci, ch in enumerate(Gc):
                            smat(pt, pb, ci, ch, n)
                        w = len(Gc) * 128
                        P.op('dve', lambda e, pt=pt, s=s, w=w, col=col, cls=cls: e.scalar_tensor_tensor(
                            out=tmpf[s][:, 0:w], in0=pt[:, 0:w], scalar=SCALE, in1=bias[:, cls, col:col + w],
                            op0=ALU.mult, op1=ALU.add), reads=[pb, bbias], writes=[btmp[s]])
                        P.op('act', lambda e, et=et, s=s, w=w: e.activation(
                            out=et[:, 0:w], in_=tmpf[s][:, 0:w], func=AF.Exp), reads=[btmp[s]], writes=[ebf])
                        groups.append((et, ebf, Gc))
                        col += w
                else:
                    A = [32, 33]
                    pa, pab = k.nextps()
                    for ci, ch in enumerate(A):
                        smat(pa, pab, ci, ch, n)
                    P.op('act', lambda e, pa=pa, s=s: e.activation(
                        out=eA[s][:, 0:256], in_=pa[:, 0:256], func=AF.Exp, scale=SCALE), reads=[pab], writes=[beA[s]])
                    groups.append((eA[s], beA[s], A))
            st1[n] = groups
            yield

    def stage2(n):
        if True:
            s = n % 2
            groups = st1.pop(n)
            po, pob = k.nextps()
            tot = sum(len(Gc) for _, _, Gc in groups)
            cnt = 0
            for (et, ebf, Gc) in groups:
                for ci, ch in enumerate(Gc):
                    P.op('pe', lambda e, po=po, et=et, ci=ci, ch=ch, cnt=cnt, tot=tot: e.matmul(
                        po[:, 0:129], et[:, ci * 128:(ci + 1) * 128], V1[:, ch, :], start=(cnt == 0), stop=(cnt == tot - 1)),
                        reads=[ebf, bV], writes=[pob], pe_acc=(cnt > 0))
                    cnt += 1
            if kind == 'wa':
                P.op('dve', lambda e, po=po, s=s: e.tensor_scalar(
                    out=rd[s], in0=po[:, 128:129], scalar1=es[:, 0:1], scalar2=None, op0=ALU.add), reads=[pob, bes], writes=[brd[s]])
                P.op('dve', lambda e, s=s: e.reciprocal(out=rd[s], in_=rd[s]), reads=[brd[s]], writes=[brd[s]])
            else:
                P.op('dve', lambda e, po=po, s=s: e.reciprocal(out=rd[s], in_=po[:, 128:129]), reads=[pob], writes=[brd[s]])
            P.op('dve', lambda e, po=po, s=s, n=n: e.tensor_scalar(
                out=osb[:, n, :], in0=po[:, 0:128], scalar1=rd[s][:, 0:1], scalar2=None, op0=ALU.mult),
                reads=[pob, brd[s]], writes=[bo[n], bq])
            yield

    st1 = {}
    for _ in stage1(0):
        pass
    for n in range(NB):
        interleave(([stage1(n + 1)] if n + 1 < NB else []) + [stage2(n)])
    for n in range(NB):
        mm_evac(k, 128, 128, osb[:, n, :], c.I, [bo[n], c.bconst], yT[:, n * 128:(n + 1) * 128], bk, eng_i=n)
    store_y_fm(k, yT, bk, ybuf, bybuf, g_slot, 0, 4096)


def bc_inner(ap2d, n):
    return bass.AP(ap2d.tensor, ap2d.offset, [list(ap2d.ap[0]), list(ap2d.ap[1]), [0, n]])


def bc_mid(ap2d, cnt):
    return bass.AP(ap2d.tensor, ap2d.offset, [list(ap2d.ap[0]), [0, cnt], list(ap2d.ap[1])])


def interleave(gens):
    gens = list(gens)
    while gens:
        for g in list(gens):
            try:
                next(g)
            except StopIteration:
                gens.remove(g)


def cn_of(cp):
    return (3 - cp) if cp < 4 else (4 + 63 - (cp - 4))


def flip_ct(k, c, src, bsrc, dst, bdst, tmp, btmp):
    mm_evac(k, 64, NCH, src, c.I[0:NCH, 0:NCH], [bsrc, c.bconst], tmp[0], btmp[0], eng_i=1)
    mm_evac(k, 64, NCH, c.J64, tmp[0], [c.bconst, btmp[0]], tmp[1], btmp[1], eng_i=1)
    mm_evac(k, NCH, 64, tmp[1], c.I[0:64, 0:64], [btmp[1], c.bconst], dst, bdst, eng_i=1)


def ml_params(k):
    return {'gb': k.dram("ml_gb", [DEPTH, NCH, 4]), 'gn': k.dram("ml_gn", [DEPTH, 64, 128]), 'tri': k.dram("ml_tri", [64, 64])}


def emit_ml(k, c, l, G, ybuf, bybuf, pr):
    P = k.P
    L = NT
    qT = k.sb([64, L]); kT = k.sb([64, L]); kt = k.sb([64, NCH, 64]); V1 = k.sb([64, NCH, 129])
    tmpA = k.sb([128, L])
    hh = [k.sb([64, NCH, 128]) for _ in range(2)]
    ct = [k.sb([NCH, 64]) for _ in range(12)]
    cc = [k.sb([NCH, 1]) for _ in range(4)]
    crow = [k.sb([1, NCH]) for _ in range(8)]
    cols = k.sb([64, 5, NCH]); abc = k.sb([64, 2, NCH])
    ftmp = [k.sb([64, NCH]) for _ in range(2)]
    gb = k.sb([NCH, 4]); tri = k.sb([64, 64])
    one1 = k.sb([1, 64]); onect = k.sb([NCH, 64]); zeroct = k.sb([NCH, 64])
    Cst = k.sb([64, 129])
    stm = [k.sb([64, 64]) for _ in range(2)]; kw = [k.sb([64, 64]) for _ in range(2)]
    nd = [k.sb([64, 129]) for _ in range(2)]; rdn = [k.sb([64, 1]) for _ in range(2)]
    clsb = [k.sb([64, 129]) for _ in range(2)]; bclsb = [Buf('cl0'), Buf('cl1')]
    gn = k.sb([64, 128]); ssq = k.sb([64, NCH])
    I68 = c.I[0:NCH, 0:NCH]
    bI = c.bconst
    bqT, bkT, bkt, bV, bA, bgb, btri, bone1, bC, bgn, bssq, bcols, babc, bconst = (Buf(n) for n in
        'qT kT kt V tmpA gb tri one1 C gn ssq cols abc const'.split())
    bhh = [Buf('hh0'), Buf('hh1')]
    bct = [Buf('ct%d' % i) for i in range(12)]; bcc = [Buf('cc%d' % i) for i in range(4)]
    bcrow = [Buf('crow%d' % i) for i in range(8)]; bftmp = [Buf('ft0'), Buf('ft1')]
    bstm = [Buf('stm0'), Buf('stm1')]; bkw = [Buf('kw0'), Buf('kw1')]; bnd = [Buf('nd0'), Buf('nd1')]
    brdn = [Buf('rdn0'), Buf('rdn1')]

    P.dma('sp', gb, pr['gb'][l], writes=[bgb])
    P.dma('sp', tri, pr['tri'], writes=[btri])
    P.dma('act', gn, pr['gn'][l], writes=[bgn])
    P.op('dve', lambda e: e.memset(one1, 1.0), writes=[bone1])
    P.op('dve', lambda e: e.memset(onect, 1.0), writes=[bconst])
    P.op('dve', lambda e: e.memset(zeroct, 0.0), writes=[bconst])

    def rop(eng, fn, reads, writes):
        P.op(eng, fn, reads=reads, writes=writes)

    def tr_col(src_ap, bsrc, dst_ap, bdst, m, n):
        pt, pb = k.nextps()
        P.op('pe', lambda e: e.matmul(pt[0:n, 0:m], src_ap, c.I[0:m, 0:m], start=True, stop=True), reads=[bsrc, bI], writes=[pb])
        P.op('dve', lambda e: e.tensor_copy(out=dst_ap, in_=pt[0:n, 0:m]), reads=[pb], writes=[bdst])

    for d in range(2):
        if d == 0:
            gather_fm(k, c, G, qT, bqT, 'ml_q', rows=64, lat_off=256, ctx_off=0)
            gather_fm(k, c, G, kT, bkT, 'ml_k', rows=64, lat_off=256, ctx_off=0)
            gather_fm(k, c, G, tmpA, bA, 'ml_v', lat_off=256, ctx_off=0)
            P.op('dve', lambda e: e.tensor_scalar(out=kT, in0=kT, scalar1=0.125, scalar2=None, op0=ALU.mult), reads=[bkT], writes=[bkT])
            P.op('pool', lambda e: e.memset(V1[:, :, 128:129], 1.0), writes=[bV])
            for ch in range(NCH):
                sl = slice(ch * 64, (ch + 1) * 64)
                mm_evac(k, 64, 64, kT[:, sl], c.I[0:64, 0:64], [bkT, bI], kt[:, ch, :], bkt, eng_i=ch)
                mm_evac(k, 64, 128, tmpA[:, sl], c.I, [bA, bI], V1[:, ch, 0:128], bV, eng_i=ch + 1)
        else:
            qtok = tmpA[0:64, :].rearrange("p (a b) -> p a b", b=64)
            for ch in range(NCH):
                sl = slice(ch * 64, (ch + 1) * 64)
                mm_evac(k, 64, 64, qT[:, sl], c.I[0:64, 0:64], [bqT, bI], qtok[:, ch, :], bA, eng_i=ch)
            for cp in range(NCH):
                cn = cn_of(cp)
                sl = slice(cp * 64, (cp + 1) * 64)
                mm_evac(k, 64, 64, qtok[:, cn, :], c.J64, [bA, bI], qT[:, sl], bqT, eng_i=cp)
                mm_evac(k, 64, 64, kt[:, cn, :], c.J64, [bkt, bI], kT[:, sl], bkT, eng_i=cp + 1)
            for cp in range(NCH):
                cn = cn_of(cp)
                if cp > cn:
                    continue
                pairs = [(cp, cn)] if cp == cn else [(cp, cn), (cn, cp)]
                for (tsr, bt_, wdt) in ((kt, bkt, 64), (V1, bV, 128)):
                    pts = []
                    for (dst_c, src_c) in pairs:
                        pt, pb = k.nextps()
                        P.op('pe', lambda e, pt=pt, tsr=tsr, src_c=src_c, wdt=wdt: e.matmul(
                            pt[0:64, 0:wdt], c.J64, tsr[:, src_c, 0:wdt], start=True, stop=True), reads=[bI, bt_], writes=[pb])
                        pts.append((pt, pb, dst_c))
                    for (pt, pb, dst_c) in pts:
                        P.op('act', lambda e, pt=pt, tsr=tsr, dst_c=dst_c, wdt=wdt: e.activation(
                            out=tsr[:, dst_c, 0:wdt], in_=pt[0:64, 0:wdt], func=AF.Copy), reads=[pb], writes=[bt_])
        P.op('pool', lambda e: e.memset(Cst, 0.0), writes=[bC])
        T_ = ct
        if d == 0:
            gather_ct(k, c, G, T_[0], bct[0], 'ml_i0')
            gather_ct(k, c, G, T_[1], bct[1], 'ml_f0')
        else:
            gather_ct(k, c, G, T_[2], bct[2], 'ml_i1')
            flip_ct(k, c, T_[2], bct[2], T_[0], bct[0], ftmp, bftmp)
            gather_ct(k, c, G, T_[2], bct[2], 'ml_f1')
            flip_ct(k, c, T_[2], bct[2], T_[1], bct[1], ftmp, bftmp)
        rop('dve', lambda e, d=d: e.tensor_scalar(out=T_[1], in0=T_[1], scalar1=gb[:, 2 + d:3 + d], scalar2=None, op0=ALU.add),
            [bct[1], bgb], [bct[1]])
        rop('act', lambda e: e.activation(out=T_[2], in_=T_[1], func=AF.Exp, scale=-1.0), [bct[1]], [bct[2]])
        rop('act', lambda e: e.activation(out=T_[2], in_=T_[2], func=AF.Ln, bias=onect[:, 0:1]), [bct[2], bconst], [bct[2]])
        rop('dve', lambda e: e.tensor_scalar(out=T_[1], in0=T_[2], scalar1=-1.0, scalar2=None, op0=ALU.mult), [bct[2]], [bct[1]])
        rop('dve', lambda e: e.tensor_tensor_scan(out=T_[2], data0=onect, data1=T_[1], initial=0.0, op0=ALU.mult, op1=ALU.add),
            [bct[1], bconst], [bct[2]])
        rop('dve', lambda e, d=d: e.scalar_tensor_tensor(out=T_[3], in0=T_[0], scalar=gb[:, d:d + 1], in1=T_[2],
                                                          op0=ALU.add, op1=ALU.subtract), [bct[0], bgb, bct[2]], [bct[3]])
        rop('dve', lambda e: e.tensor_tensor_scan(out=T_[4], data0=zeroct, data1=T_[3], initial=-1e30, op0=ALU.add, op1=ALU.max),
            [bct[3], bconst], [bct[4]])
        rop('dve', lambda e: e.tensor_copy(out=cc[0], in_=T_[2][:, 63:64]), [bct[2]], [bcc[0]])
        rop('dve', lambda e: e.tensor_copy(out=cc[1], in_=T_[4][:, 63:64]), [bct[4]], [bcc[1]])
        rop('dve', lambda e: e.tensor_tensor(out=cc[2], in0=cc[0], in1=cc[1], op=ALU.add), [bcc[0], bcc[1]], [bcc[2]])
        CR = crow
        tr_col(cc[0], bcc[0], CR[0], bcrow[0], NCH, 1)
        tr_col(cc[2], bcc[2], CR[2], bcrow[2], NCH, 1)
        rop('dve', lambda e: e.tensor_tensor_scan(out=CR[3], data0=CR[0], data1=CR[2], initial=0.0, op0=ALU.add, op1=ALU.max),
            [bcrow[0], bcrow[2]], [bcrow[3]])
        rop('dve', lambda e: e.memset(CR[4][:, 0:1], 0.0), [], [bcrow[4]])
        rop('dve', lambda e: e.tensor_copy(out=CR[4][:, 1:NCH], in_=CR[3][:, 0:NCH - 1]), [bcrow[3]], [bcrow[4]])
        rop('dve', lambda e: e.tensor_tensor(out=CR[5], in0=CR[0], in1=CR[4], op=ALU.add), [bcrow[0], bcrow[4]], [bcrow[5]])
        rop('dve', lambda e: e.tensor_tensor(out=CR[5], in0=CR[5], in1=CR[3], op=ALU.subtract), [bcrow[5], bcrow[3]], [bcrow[5]])
        rop('act', lambda e: e.activation(out=CR[5], in_=CR[5], func=AF.Exp), [bcrow[5]], [bcrow[5]])
        rop('dve', lambda e: e.tensor_tensor(out=CR[6], in0=CR[2], in1=CR[3], op=ALU.subtract), [bcrow[2], bcrow[3]], [bcrow[6]])
        rop('act', lambda e: e.activation(out=CR[6], in_=CR[6], func=AF.Exp), [bcrow[6]], [bcrow[6]])
        pt, pb = k.nextps()
        P.op('pe', lambda e, pt=pt: e.matmul(pt[0:NCH, 0:1], CR[4][0:1, :], one1[0:1, 0:1], start=True, stop=True),
             reads=[bcrow[4], bone1], writes=[pb])
        P.op('dve', lambda e, pt=pt: e.tensor_copy(out=cc[3], in_=pt[0:NCH, 0:1]), reads=[pb], writes=[bcc[3]])
        rop('dve', lambda e: e.tensor_scalar(out=T_[5], in0=T_[4], scalar1=cc[3][:, 0:1], scalar2=None, op0=ALU.max), [bct[4], bcc[3]], [bct[5]])
        rop('act', lambda e: e.activation(out=T_[6], in_=T_[3], func=AF.Exp), [bct[3]], [bct[6]])
        rop('act', lambda e: e.activation(out=T_[7], in_=T_[5], func=AF.Exp, scale=-1.0), [bct[5]], [bct[7]])
        rop('dve', lambda e: e.tensor_scalar(out=T_[8], in0=T_[5], scalar1=cc[3][:, 0:1], scalar2=None, op0=ALU.subtract), [bct[5], bcc[3]], [bct[8]])
        rop('act', lambda e: e.activation(out=T_[8], in_=T_[8], func=AF.Exp, scale=-1.0), [bct[8]], [bct[8]])
        rop('dve', lambda e: e.tensor_tensor(out=T_[9], in0=T_[2], in1=T_[5], op=ALU.add), [bct[2], bct[5]], [bct[9]])
        rop('act', lambda e: e.activation(out=T_[9], in_=T_[9], func=AF.Exp, scale=-1.0), [bct[9]], [bct[9]])
        rop('dve', lambda e: e.tensor_scalar(out=T_[10], in0=T_[3], scalar1=cc[1][:, 0:1], scalar2=None, op0=ALU.subtract), [bct[3], bcc[1]], [bct[10]])
        rop('act', lambda e: e.activation(out=T_[10], in_=T_[10], func=AF.Exp), [bct[10]], [bct[10]])
        for qi, ti in enumerate([6, 7, 8, 9, 10]):
            tr_col(T_[ti], bct[ti], cols[:, qi, :], bcols, NCH, 64)
        for qi, ci in enumerate([5, 6]):
            pt, pb = k.nextps()
            P.op('pe', lambda e, pt=pt, ci=ci: e.matmul(pt[0:64, 0:NCH], one1[0:1, 0:64], CR[ci][0:1, :], start=True, stop=True),
                 reads=[bcrow[ci], bone1], writes=[pb])
            P.op('dve', lambda e, pt=pt, qi=qi: e.tensor_copy(out=abc[:, qi, :], in_=pt[0:64, 0:NCH]), reads=[pb], writes=[babc])
        def pre(ch):
            s = ch % 2
            sl = slice(ch * 64, (ch + 1) * 64)
            pS, pSb = k.nextps()
            P.op('pe', lambda e: e.matmul(pS[0:64, 0:64], kT[:, sl], qT[:, sl], start=True, stop=True), reads=[bkT, bqT], writes=[pSb])
            P.op('dve', lambda e: e.scalar_tensor_tensor(out=stm[s], in0=pS[0:64, 0:64], scalar=cols[:, 0, ch:ch + 1], in1=tri,
                                                         op0=ALU.mult, op1=ALU.mult), reads=[pSb, bcols, btri], writes=[bstm[s]])
            P.op('pool', lambda e: e.tensor_scalar(out=kw[s], in0=kt[:, ch, :], scalar1=cols[:, 4, ch:ch + 1], scalar2=None, op0=ALU.mult),
                 reads=[bkt, bcols], writes=[bkw[s]])
            yield
            pA, pAb = k.nextps()
            P.op('pe', lambda e: e.matmul(pA[0:64, 0:129], stm[s], V1[:, ch, :], start=True, stop=True), reads=[bstm[s], bV], writes=[pAb])
            pC, pCb = k.nextps()
            P.op('pe', lambda e: e.matmul(pC[0:64, 0:129], kw[s], V1[:, ch, :], start=True, stop=True), reads=[bkw[s], bV], writes=[pCb])
            yield
            P.op('dve', lambda e: e.tensor_scalar(out=nd[s], in0=pA[0:64, 0:129], scalar1=cols[:, 1, ch:ch + 1], scalar2=None, op0=ALU.mult),
                 reads=[pAb, bcols], writes=[bnd[s]])
            P.op('act', lambda e: e.activation(out=clsb[s], in_=pC[0:64, 0:129], func=AF.Copy), reads=[pCb], writes=[bclsb[s]])
            yield

        def post(ch, d=d):
            s = ch % 2
            sl = slice(ch * 64, (ch + 1) * 64)
            pB, pBb = k.nextps()
            P.op('pe', lambda e: e.matmul(pB[0:64, 0:129], qT[:, sl], Cst, start=True, stop=True), reads=[bqT, bC], writes=[pBb])
            yield
            P.op('dve', lambda e: e.tensor_scalar(out=Cst, in0=Cst, scalar1=abc[:, 0, ch:ch + 1], scalar2=None, op0=ALU.mult),
                 reads=[bC, babc], writes=[bC])
            P.op('dve', lambda e: e.scalar_tensor_tensor(out=Cst, in0=clsb[s], scalar=abc[:, 1, ch:ch + 1], in1=Cst, op0=ALU.mult, op1=ALU.add),
                 reads=[bclsb[s], babc, bC], writes=[bC])
            yield
            P.op('dve', lambda e: e.scalar_tensor_tensor(out=nd[s], in0=pB[0:64, 0:129], scalar=cols[:, 2, ch:ch + 1], in1=nd[s],
                                                         op0=ALU.mult, op1=ALU.add), reads=[pBb, bcols, bnd[s]], writes=[bnd[s]])
            P.op('dve', lambda e: e.scalar_tensor_tensor(out=rdn[s], in0=nd[s][:, 128:129], scalar=-1.0, in1=nd[s][:, 128:129],
                                                         op0=ALU.mult, op1=ALU.max), reads=[bnd[s]], writes=[brdn[s]])
            yield
            P.op('dve', lambda e: e.tensor_scalar(out=rdn[s], in0=rdn[s], scalar1=cols[:, 3, ch:ch + 1], scalar2=None, op0=ALU.max),
                 reads=[brdn[s], bcols], writes=[brdn[s]])
            P.op('dve', lambda e: e.reciprocal(out=rdn[s], in_=rdn[s]), reads=[brdn[s]], writes=[brdn[s]])
            P.op('dve', lambda e: e.tensor_scalar(out=hh[d][:, ch, :], in0=nd[s][:, 0:128], scalar1=rdn[s][:, 0:1], scalar2=None, op0=ALU.mult),
                 reads=[bnd[s], brdn[s]], writes=[bhh[d]])
            yield

        for _ in pre(0):
            pass
        for ch in range(NCH):
            interleave([post(ch)] + ([pre(ch + 1)] if ch + 1 < NCH else []))
    for cn in range(NCH):
        cp = cn_of(cn)
        pt, pb = k.nextps()
        P.op('pe', lambda e, pt=pt, cp=cp: e.matmul(pt[0:64, 0:128], c.J64, hh[1][:, cp, :], start=True, stop=True),
             reads=[bI, bhh[1]], writes=[pb])
        P.op('dve', lambda e, pt=pt, cn=cn: e.tensor_tensor(out=hh[0][:, cn, :], in0=pt[0:64, 0:128], in1=hh[0][:, cn, :], op=ALU.add),
             reads=[pb, bhh[0]], writes=[bhh[0]])
    og = V1
    gather_fm(k, c, G, tmpA, bA, 'ml_o', lat_off=256, ctx_off=0)
    for ch in range(NCH):
        mm_evac(k, 64, 128, tmpA[:, ch * 64:(ch + 1) * 64], c.I, [bA, bI], og[:, ch, 0:128], bV, eng_i=ch)
    Y = hh[0]
    T = hh[1]
    P.op('pool', lambda e: e.tensor_tensor(out=T, in0=Y, in1=Y, op=ALU.mult), reads=[bhh[0], bhh[1]], writes=[bhh[1]])
    P.op('dve', lambda e: e.tensor_reduce(out=ssq, in_=T, axis=AX.X, op=ALU.add), reads=[bhh[1]], writes=[bssq])
    P.op('act', lambda e: e.activation(out=ssq, in_=ssq, func=AF.Sqrt, bias=c.eps[0:64, :], scale=1.0 / 128), reads=[bssq, bI], writes=[bssq])
    P.op('dve', lambda e: e.reciprocal(out=ssq, in_=ssq), reads=[bssq], writes=[bssq])
    P.op('dve', lambda e: e.tensor_tensor(out=Y, in0=Y, in1=bc_inner(ssq, 128), op=ALU.mult), reads=[bhh[0], bssq], writes=[bhh[0]])
    P.op('pool', lambda e: e.tensor_tensor(out=Y, in0=Y, in1=bc_mid(gn, NCH), op=ALU.mult), reads=[bhh[0], bgn], writes=[bhh[0]])
    P.op('act', lambda e: e.activation(out=og[:, :, 0:128], in_=og[:, :, 0:128], func=AF.Sigmoid), reads=[bV], writes=[bV])
    P.op('dve', lambda e: e.tensor_tensor(out=Y, in0=Y, in1=og[:, :, 0:128], op=ALU.mult), reads=[bhh[0], bV], writes=[bhh[0]])
    for ch in range(NCH):
        mm_evac(k, 128, 64, Y[:, ch, :], c.I[0:64, 0:64], [bhh[0], bI], tmpA[:, ch * 64:(ch + 1) * 64], bA, eng_i=ch)
    store_y_fm(k, tmpA, bA, ybuf, bybuf, 2, 256, 0)


def dn_params(k):
    return {'cw': k.dram("dn_cw", [DEPTH, 2, 128, 3, 5]), 'sc': k.dram("dn_sc", [DEPTH, NCH, 4]),
            'gn': k.dram("dn_gn", [DEPTH, 64, 128]), 'mk': k.dram("dn_masks", [64, 2, 64])}


def emit_dn(k, cm, l, G, ybuf, bybuf, pr):
    P = k.P
    L = NT
    SEGS = [(0, 256), (256, 4096)]
    big1 = k.sb([128, 2 * L]); big2 = k.sb([128, 2 * L])
    X = [big2[:, 0:L], big2[:, L:2 * L], k.sb([128, L])]
    acc = k.sb([128, L])
    qd = big1[:, 0:L]; kd = big1[:, L:2 * L]
    DmT = k.sb([64, NCH, 64]); NB_ = k.sb([64, NCH, 64])
    O = k.sb([64, NCH, 128])
    cw = k.sb([128, 3, 5]); sc = k.sb([NCH, 4]); mk = k.sb([64, 2, 64])
    I128 = cm.I; J64 = cm.J64; ones = cm.ones; epst = cm.eps
    one1 = k.sb([1, 128]); onect = k.sb([NCH, 64])
    ct = [k.sb([NCH, 64]) for _ in range(6)]
    cc = [k.sb([NCH, 1]) for _ in range(3)]
    crow = k.sb([1, NCH])
    cols = k.sb([64, 4, NCH])
    eglb = k.sb([128, NCH])
    S = k.sb([128, 128])
    scr = [k.sb([128, 512]) for _ in range(4)]
    ttok = [k.sb([128, 128]) for _ in range(2)]
    ftmp = [k.sb([64, NCH]) for _ in range(2)]
    Qb = [[k.sb([64, 64]) for _ in range(2)] for _ in range(3)]
    QTb = [[k.sb([64, 64]) for _ in range(2)] for _ in range(3)]
    R = [k.sb([64, 64]) for _ in range(3)]
    QKD = [k.sb([64, 64]) for _ in range(3)]
    vtok = [k.sb([64, 128]) for _ in range(3)]
    kend = [k.sb([64, 128]) for _ in range(3)]
    z = [k.sb([64, 128]) for _ in range(3)]
    vnew = [k.sb([64, 128]) for _ in range(3)]
    o1c = [k.sb([64, 128]) for _ in range(3)]
    gn = k.sb([64, 128]); ssq = k.sb([64, NCH])

    bX = [Buf('xq'), Buf('xk'), Buf('xv')]
    (bacc, bqd, bkd, bDm, bNB, bO, bcw, bsc, bmk, bone1, bconst, bcrow, bcols, beglb, bS, bgn, bssq) = (
        Buf(n) for n in 'acc qd kd Dm NB O cw sc mk one1 const crow cols eglb S gn ssq'.split())
    bI = cm.bconst; bJ = cm.bconst; bones = cm.bconst; beps = cm.bconst
    bct = [Buf('ct%d' % i) for i in range(6)]
    bcc = [Buf('cc%d' % i) for i in range(3)]
    bscr = [Buf('scr%d' % i) for i in range(4)]
    bttok = [Buf('tt0'), Buf('tt1')]; bftmp = [Buf('ft0'), Buf('ft1')]
    bQ = [[Buf('Q%d%d' % (a, b)) for b in range(2)] for a in range(3)]
    bQT = [[Buf('QT%d%d' % (a, b)) for b in range(2)] for a in range(3)]
    bR, bQKD, bvtok, bkend, bz, bvnew, bo1c = ([Buf('%s%d' % (n_, i)) for i in range(3)] for n_ in ('R', 'QKD', 'vt', 'ke', 'z', 'vn', 'o1c'))

    P.dma('sp', sc, pr['sc'][l], writes=[bsc])
    P.dma('sp', mk, pr['mk'], writes=[bmk])
    P.dma('sp', gn, pr['gn'][l], writes=[bgn])
    P.op('dve', lambda e: e.memset(one1, 1.0), writes=[bone1])
    P.op('dve', lambda e: e.memset(onect, 1.0), writes=[bconst])

    def tr_col(src_ap, bsrc, dst_ap, bdst, m, n):
        pt, pb = k.nextps()
        P.op('pe', lambda e: e.matmul(pt[0:n, 0:m], src_ap, I128[0:m, 0:m], start=True, stop=True), reads=[bsrc, bI], writes=[pb])
        P.op('dve', lambda e: e.tensor_copy(out=dst_ap, in_=pt[0:n, 0:m]), reads=[pb], writes=[bdst])

    tiles = [(i * 512, 512) for i in range(8)] + [(4096, 256)]

    for d in range(2):
        P.dma('sp', cw, pr['cw'][l][d], writes=[bcw])
        if d == 0:
            for t, nm in enumerate(('dn_q', 'dn_k', 'dn_v')):
                gather_fm(k, cm, G, X[t], bX[t], nm, lat_off=256, ctx_off=0)
            gather_ct(k, cm, G, ct[0], bct[0], 'dn_b0')
            gather_ct(k, cm, G, ct[1], bct[1], 'dn_a0')
        else:
            for t, nm in enumerate(('dn_q', 'dn_k', 'dn_v')):
                gather_fm(k, cm, G, acc, bacc, nm, lat_off=256, ctx_off=0)
                for blk in range(NB):
                    bp = (1 - blk) if blk < 2 else (35 - blk)
                    s_ = blk % 2
                    mm_evac(k, 128, 128, acc[:, blk * 128:(blk + 1) * 128], I128, [bacc, bI], ttok[s_], bttok[s_], eng_i=blk)
                    mm_evac(k, 128, 128, ttok[s_], cm.J, [bttok[s_], bI], X[t][:, bp * 128:(bp + 1) * 128], bX[t], eng_i=blk + 1)
            gather_ct(k, cm, G, ct[5], bct[5], 'dn_b1')
            flip_ct(k, cm, ct[5], bct[5], ct[0], bct[0], ftmp, bftmp)
            gather_ct(k, cm, G, ct[5], bct[5], 'dn_a1')
            flip_ct(k, cm, ct[5], bct[5], ct[1], bct[1], ftmp, bftmp)
        P.op('pool', lambda e: e.memset(S[:], 0.0), writes=[bS])
        for t in range(3):
            for (s0, sn) in SEGS:
                P.op('dve', lambda e, t=t, s0=s0, sn=sn: e.tensor_scalar(
                    out=acc[:, s0:s0 + sn], in0=X[t][:, s0:s0 + sn], scalar1=cw[:, t, 2:3], scalar2=None, op0=ALU.mult),
                    reads=[bX[t], bcw], writes=[bacc])
                for tap in (0, 1, 3, 4):
                    sh = tap - 2
                    a0 = s0 + max(0, -sh)
                    a1 = s0 + sn - max(0, sh)
                    P.op('dve', lambda e, t=t, tap=tap, sh=sh, a0=a0, a1=a1: e.scalar_tensor_tensor(
                        out=acc[:, a0:a1], in0=X[t][:, a0 + sh:a1 + sh], scalar=cw[:, t, tap:tap + 1], in1=acc[:, a0:a1],
                        op0=ALU.mult, op1=ALU.add), reads=[bX[t], bcw, bacc], writes=[bacc])
            P.op('act', lambda e, t=t: e.activation(out=X[t], in_=acc[:], func=AF.Silu), reads=[bacc], writes=[bX[t]])
        for t in range(2):
            for i, (t0, tn) in enumerate(tiles):
                s0, s1 = scr[(2 * i) % 4], scr[(2 * i + 1) % 4]
                b0, b1 = bscr[(2 * i) % 4], bscr[(2 * i + 1) % 4]
                P.op('act', lambda e, t=t, s0=s0, t0=t0, tn=tn: e.activation(out=s0[:, 0:tn], in_=X[t][:, t0:t0 + tn], func=AF.Square),
                     reads=[bX[t]], writes=[b0])
                pt, pb = k.nextps()
                P.op('pe', lambda e, pt=pt, s0=s0, tn=tn: e.matmul(pt[:, 0:tn], ones[:], s0[:, 0:tn], start=True, stop=True),
                     reads=[bones, b0], writes=[pb])
                P.op('act', lambda e, pt=pt, s1=s1, tn=tn: e.activation(out=s1[:, 0:tn], in_=pt[:, 0:tn], func=AF.Sqrt, bias=epst[:], scale=1.0),
                     reads=[pb, beps], writes=[b1])
                P.op('dve', lambda e, s1=s1, tn=tn: e.reciprocal(out=s1[:, 0:tn], in_=s1[:, 0:tn]), reads=[b1], writes=[b1])
                sc_ = (128 ** -0.5) if t == 0 else 1.0
                P.op('dve', lambda e, t=t, s1=s1, t0=t0, tn=tn, sc_=sc_: e.scalar_tensor_tensor(
                    out=X[t][:, t0:t0 + tn], in0=X[t][:, t0:t0 + tn], scalar=sc_, in1=s1[:, 0:tn], op0=ALU.mult, op1=ALU.mult),
                    reads=[bX[t], b1], writes=[bX[t]])
        P.op('act', lambda e: e.activation(out=ct[0][:], in_=ct[0][:], func=AF.Sigmoid), reads=[bct[0]], writes=[bct[0]])
        P.op('act', lambda e, d=d: e.activation(out=cc[0][:], in_=sc[:, d:d + 1], func=AF.Exp), reads=[bsc], writes=[bcc[0]])
        P.op('dve', lambda e: e.tensor_scalar(out=cc[0][:], in0=cc[0][:], scalar1=-1.0, scalar2=None, op0=ALU.mult), reads=[bcc[0]], writes=[bcc[0]])
        P.op('act', lambda e, d=d: e.activation(out=ct[1][:], in_=ct[1][:], func=AF.Exp, bias=sc[:, 2 + d:3 + d]), reads=[bct[1], bsc], writes=[bct[1]])
        P.op('act', lambda e: e.activation(out=ct[1][:], in_=ct[1][:], func=AF.Ln, bias=onect[:, 0:1]), reads=[bct[1], bconst], writes=[bct[1]])
        P.op('dve', lambda e: e.tensor_scalar(out=ct[1][:], in0=ct[1][:], scalar1=cc[0][:, 0:1], scalar2=None, op0=ALU.mult),
             reads=[bct[1], bcc[0]], writes=[bct[1]])
        P.op('dve', lambda e: e.tensor_tensor_scan(out=ct[2][:], data0=onect[:], data1=ct[1][:], initial=0.0, op0=ALU.mult, op1=ALU.add),
             reads=[bct[1], bconst], writes=[bct[2]])
        P.op('dve', lambda e: e.tensor_copy(out=cc[1][:], in_=ct[2][:, 63:64]), reads=[bct[2]], writes=[bcc[1]])
        P.op('dve', lambda e: e.tensor_scalar(out=ct[3][:], in0=ct[2][:], scalar1=cc[1][:, 0:1], scalar2=None, op0=ALU.subtract),
             reads=[bct[2], bcc[1]], writes=[bct[3]])
        P.op('act', lambda e: e.activation(out=ct[3][:], in_=ct[3][:], func=AF.Exp, scale=-1.0), reads=[bct[3]], writes=[bct[3]])
        P.op('dve', lambda e: e.tensor_scalar(out=ct[4][:], in0=ct[0][:], scalar1=-1.0, scalar2=None, op0=ALU.mult), reads=[bct[0]], writes=[bct[4]])
        for qi, ti in enumerate([2, 0, 4, 3]):
            tr_col(ct[ti][:], bct[ti], cols[:, qi, :], bcols, NCH, 64)
        tr_col(cc[1][:], bcc[1], crow[:], bcrow, NCH, 1)
        P.op('act', lambda e: e.activation(out=crow[:], in_=crow[:], func=AF.Exp), reads=[bcrow], writes=[bcrow])
        pt, pb = k.nextps()
        P.op('pe', lambda e, pt=pt: e.matmul(pt[:, 0:NCH], one1[0:1, :], crow[0:1, :], start=True, stop=True), reads=[bone1, bcrow], writes=[pb])
        P.op('dve', lambda e, pt=pt: e.tensor_copy(out=eglb[:], in_=pt[:, 0:NCH]), reads=[pb], writes=[beglb])
        P.dma('sp', acc[0:1, :].rearrange("o (c i) -> o c i", i=64), ct[2], reads=[bct[2]], writes=[bacc])
        for i, (t0, tn) in enumerate(tiles):
            nck = tn // 64
            c0 = t0 // 64
            s0, s1 = scr[(2 * i) % 4], scr[(2 * i + 1) % 4]
            b0, b1 = bscr[(2 * i) % 4], bscr[(2 * i + 1) % 4]
            pt, pb = k.nextps()
            P.op('pe', lambda e, pt=pt, t0=t0, tn=tn: e.matmul(pt[:, 0:tn], one1[0:1, :], acc[0:1, t0:t0 + tn], start=True, stop=True),
                 reads=[bone1, bacc], writes=[pb])
            P.op('act', lambda e, pt=pt, s0=s0, tn=tn: e.activation(out=s0[:, 0:tn], in_=pt[:, 0:tn], func=AF.Exp), reads=[pb], writes=[b0])
            P.op('dve', lambda e, s0=s0, t0=t0, tn=tn: e.tensor_tensor(out=qd[:, t0:t0 + tn], in0=X[0][:, t0:t0 + tn], in1=s0[:, 0:tn], op=ALU.mult),
                 reads=[bX[0], b0], writes=[bqd])
            P.op('pool', lambda e, s0=s0, t0=t0, tn=tn: e.tensor_tensor(out=kd[:, t0:t0 + tn], in0=X[1][:, t0:t0 + tn], in1=s0[:, 0:tn], op=ALU.mult),
                 reads=[bX[1], b0], writes=[bkd])
            d3 = s1[0:64, 0:tn].rearrange("p (c i) -> p c i", i=64)
            p3 = pt[0:64, 0:tn].rearrange("p (c i) -> p c i", i=64)
            P.op('dve', lambda e, d3=d3, p3=p3, c0=c0, nck=nck: e.tensor_tensor(out=d3, in0=p3, in1=bc_inner(cols[:, 0, c0:c0 + nck], 64), op=ALU.subtract),
                 reads=[pb, bcols], writes=[b1])
            P.op('dve', lambda e, d3=d3, nck=nck: e.tensor_tensor(out=d3, in0=d3, in1=bc_mid(mk[:, 0, :], nck), op=ALU.add),
                 reads=[b1, bmk], writes=[b1])
            P.op('act', lambda e, d3=d3, c0=c0, nck=nck: e.activation(out=DmT[:, c0:c0 + nck, :], in_=d3, func=AF.Exp), reads=[b1], writes=[bDm])
            P.op('dve', lambda e, c0=c0, nck=nck: e.tensor_tensor(out=NB_[:, c0:c0 + nck, :], in0=DmT[:, c0:c0 + nck, :], in1=bc_mid(mk[:, 1, :], nck), op=ALU.mult),
                 reads=[bDm, bmk], writes=[bNB])
            P.op('dve', lambda e, c0=c0, nck=nck: e.tensor_tensor(out=NB_[:, c0:c0 + nck, :], in0=NB_[:, c0:c0 + nck, :], in1=bc_inner(cols[:, 2, c0:c0 + nck], 64), op=ALU.mult),
                 reads=[bNB, bcols], writes=[bNB])

        def prep(c):
            a = c % 3
            sl = slice(c * 64, (c + 1) * 64)
            Q, QT = Qb[a], QTb[a]
            bq, bqt = bQ[a], bQT[a]
            pt, pb = k.nextps()
            P.op('pe', lambda e: e.matmul(pt[0:64, 0:64], X[1][:, sl], X[1][:, sl], start=True, stop=True), reads=[bX[1]], writes=[pb])
            P.op('dve', lambda e: e.tensor_tensor(out=Q[0][:], in0=pt[0:64, 0:64], in1=NB_[:, c, :], op=ALU.mult), reads=[pb, bNB], writes=[bq[0]])
            yield
            pt2, pb2 = k.nextps()
            P.op('pe', lambda e: e.matmul(pt2[0:64, 0:64], Q[0][:], I128[0:64, 0:64], start=True, stop=True), reads=[bq[0], bI], writes=[pb2])
            P.op('act', lambda e: e.activation(out=QT[0][:], in_=pt2[0:64, 0:64], func=AF.Copy), reads=[pb2], writes=[bqt[0]])
            P.op('pool', lambda e: e.tensor_tensor(out=R[a][:], in0=Q[0][:], in1=I128[0:64, 0:64], op=ALU.add), reads=[bq[0], bI], writes=[bR[a]])
            yield
            cur = 0
            for it in range(1, 6):
                nx = 1 - cur
                pq, pqb = k.nextps()
                P.op('pe', lambda e, pq=pq, cur=cur: e.matmul(pq[0:64, 0:64], Q[cur][:], QT[cur][:], start=True, stop=True),
                     reads=[bq[cur], bqt[cur]], writes=[pqb])
                if it < 5:
                    pq2, pq2b = k.nextps()
                    P.op('pe', lambda e, pq2=pq2, cur=cur: e.matmul(pq2[0:64, 0:64], QT[cur][:], Q[cur][:], start=True, stop=True),
                         reads=[bq[cur], bqt[cur]], writes=[pq2b])
                P.op('act', lambda e, pq=pq, nx=nx: e.activation(out=QT[nx][:], in_=pq[0:64, 0:64], func=AF.Copy), reads=[pqb], writes=[bqt[nx]])
                if it < 5:
                    P.op('dve', lambda e, pq2=pq2, nx=nx: e.tensor_copy(out=Q[nx][:], in_=pq2[0:64, 0:64]), reads=[pq2b], writes=[bq[nx]])
                yield
                pr, prb = k.nextps()
                P.op('pe', lambda e, pr=pr, nx=nx: e.matmul(pr[0:64, 0:64], QT[nx][:], R[a][:], start=True, stop=True),
                     reads=[bqt[nx], bR[a]], writes=[prb])
                P.op('dve', lambda e, pr=pr: e.tensor_tensor(out=R[a][:], in0=pr[0:64, 0:64], in1=R[a][:], op=ALU.add), reads=[prb, bR[a]], writes=[bR[a]])
                yield
                cur = nx
            p1, p1b = k.nextps()
            P.op('pe', lambda e: e.matmul(p1[0:64, 0:64], X[1][:, sl], X[0][:, sl], start=True, stop=True), reads=[bX[0], bX[1]], writes=[p1b])
            P.op('dve', lambda e: e.tensor_tensor(out=QKD[a][:], in0=p1[0:64, 0:64], in1=DmT[:, c, :], op=ALU.mult), reads=[p1b, bDm], writes=[bQKD[a]])
            yield
            p2, p2b = k.nextps()
            P.op('pe', lambda e: e.matmul(p2[0:64, 0:128], X[2][:, sl], I128[:], start=True, stop=True), reads=[bX[2], bI], writes=[p2b])
            P.op('act', lambda e: e.activation(out=vtok[a][:], in_=p2[0:64, 0:128], func=AF.Copy), reads=[p2b], writes=[bvtok[a]])
            p3_, p3b = k.nextps()
            P.op('pe', lambda e: e.matmul(p3_[0:64, 0:128], X[1][:, sl], I128[:], start=True, stop=True), reads=[bX[1], bI], writes=[p3b])
            P.op('dve', lambda e: e.tensor_scalar(out=kend[a][:], in0=p3_[0:64, 0:128], scalar1=cols[:, 3, c:c + 1], scalar2=None, op0=ALU.mult),
                 reads=[p3b, bcols], writes=[bkend[a]])
            yield

        def seq(c):
            a = c % 3
            sl = slice(c * 64, (c + 1) * 64)
            pk, pkb = k.nextps()
            P.op('pe', lambda e: e.matmul(pk[0:64, 0:128], kd[:, sl], S[:], start=True, stop=True), reads=[bkd, bS], writes=[pkb])
            po, pob = k.pst[6 + c % 2], k.psb[6 + c % 2]
            P.op('pe', lambda e: e.matmul(po[0:64, 0:128], qd[:, sl], S[:], start=True, stop=False), reads=[bqd, bS], writes=[pob])
            yield
            P.op('dve', lambda e: e.tensor_tensor(out=z[a][:], in0=vtok[a][:], in1=pk[0:64, 0:128], op=ALU.subtract),
                 reads=[bvtok[a], pkb], writes=[bz[a]])
            yield
            ptz, ptzb = k.nextps()
            P.op('pe', lambda e: e.matmul(ptz[0:64, 0:128], R[a][:], z[a][:], start=True, stop=True), reads=[bR[a], bz[a]], writes=[ptzb])
            yield
            P.op('dve', lambda e: e.tensor_scalar(out=vnew[a][:], in0=ptz[0:64, 0:128], scalar1=cols[:, 1, c:c + 1], scalar2=None, op0=ALU.mult),
                 reads=[ptzb, bcols], writes=[bvnew[a]])
            yield
            psu, psub = k.nextps()
            P.op('pe', lambda e: e.matmul(psu[:, 0:128], kend[a][:], vnew[a][:], start=True, stop=True), reads=[bkend[a], bvnew[a]], writes=[psub])
            P.op('pe', lambda e: e.matmul(po[0:64, 0:128], QKD[a][:], vnew[a][:], start=False, stop=True), reads=[bQKD[a], bvnew[a]], writes=[pob],
                 pe_acc=True)
            yield
            P.op('dve', lambda e: e.scalar_tensor_tensor(out=S[:], in0=S[:], scalar=eglb[:, c:c + 1], in1=psu[:, 0:128], op0=ALU.mult, op1=ALU.add),
                 reads=[bS, beglb, psub], writes=[bS])
            if d == 0:
                P.op('act', lambda e: e.activation(out=O[:, c, :], in_=po[0:64, 0:128], func=AF.Copy), reads=[pob], writes=[bO])
            else:
                cn = (3 - c) if c < 4 else (4 + 63 - (c - 4))
                P.op('act', lambda e: e.activation(out=o1c[a][:], in_=po[0:64, 0:128], func=AF.Copy), reads=[pob], writes=[bo1c[a]])
                pj, pjb = k.nextps()
                P.op('pe', lambda e: e.matmul(pj[0:64, 0:128], J64, o1c[a][:], start=True, stop=True), reads=[bJ, bo1c[a]], writes=[pjb])
                P.op('dve', lambda e: e.tensor_tensor(out=O[:, cn, :], in0=pj[0:64, 0:128], in1=O[:, cn, :], op=ALU.add),
                     reads=[pjb, bO], writes=[bO])
            yield

        k.nring = 6
        for _ in prep(0):
            pass
        preps = {}
        if NCH > 1:
            preps[1] = prep(1)
        for c in range(NCH):
            if c + 2 < NCH:
                preps[c + 2] = prep(c + 2)
            sg = seq(c)
            live = [sg] + [preps[i] for i in (c + 1, c + 2) if i in preps]
            must = [sg] + ([preps[c + 1]] if (c + 1) in preps else [])
            while must:
                for g in list(live):
                    try:
                        next(g)
                    except StopIteration:
                        live.remove(g)
                        if g in must:
                            must.remove(g)
            preps.pop(c + 1, None)


    k.nring = 8
    gate = big1[0:64, :].rearrange("p (c d) -> p c d", d=128)
    T = big2[0:64, :].rearrange("p (c d) -> p c d", d=128)
    gather_fm(k, cm, G, acc, bacc, 'dn_g', lat_off=256, ctx_off=0)
    for ch in range(NCH):
        pt, pb = k.nextps()
        P.op('pe', lambda e, pt=pt, ch=ch: e.matmul(pt[0:64, 0:128], acc[:, ch * 64:(ch + 1) * 64], I128, start=True, stop=True),
             reads=[bacc, bI], writes=[pb])
        P.op('act', lambda e, pt=pt, ch=ch: e.activation(out=gate[:, ch, :], in_=pt[0:64, 0:128], func=AF.Silu), reads=[pb], writes=[bqd, bkd])
    P.op('pool', lambda e: e.tensor_tensor(out=T, in0=O, in1=O, op=ALU.mult), reads=[bO], writes=[bX[0], bX[1]])
    P.op('dve', lambda e: e.tensor_reduce(out=ssq, in_=T, axis=AX.X, op=ALU.add), reads=[bX[0], bX[1]], writes=[bssq])
    P.op('act', lambda e: e.activation(out=ssq, in_=ssq, func=AF.Sqrt, bias=epst[0:64, :], scale=1.0 / 128), reads=[bssq, beps], writes=[bssq])
    P.op('dve', lambda e: e.reciprocal(out=ssq, in_=ssq), reads=[bssq], writes=[bssq])
    P.op('dve', lambda e: e.tensor_tensor(out=O, in0=O, in1=bc_inner(ssq, 128), op=ALU.mult), reads=[bO, bssq], writes=[bO])
    P.op('pool', lambda e: e.tensor_tensor(out=O, in0=O, in1=bc_mid(gn, NCH), op=ALU.mult), reads=[bO, bgn], writes=[bO])
    P.op('dve', lambda e: e.tensor_tensor(out=O, in0=O, in1=gate, op=ALU.mult), reads=[bO, bqd, bkd], writes=[bO])
    for ch in range(NCH):
        mm_evac(k, 128, 64, O[:, ch, :], I128[0:64, 0:64], [bO, bI], acc[:, ch * 64:(ch + 1) * 64], bacc, eng_i=ch)
    store_y_fm(k, acc, bacc, ybuf, bybuf, 1, 256, 0)


def emit_mod(k, c):
    P = k.P
    c3_d = k.dram("c3", [128, 16, 3])
    w_d = k.dram("w_mod", [D, 12288])
    b_d = k.dram("b_mod", [128, 96])
    sel_d = k.dram("sel", [128, 2])
    gn_d = k.dram("gnorm", [128, 2 * DEPTH, 16])
    modbuf = k.dram("modbuf", [128, 288], kind="Internal")
    G_mod = k.dram("G_mod", [4 * 128, 288], kind="Internal")
    c.Mlat = k.sb([128, 4 * 96]); c.Mctx = k.sb([128, 4 * 96]); c.gnorm = k.sb([128, 2 * DEPTH, 16])
    c.bM = Buf('M')
    k.persist()
    sc = k.sb([128, 16, 3]); bt = k.sb([128, 96]); sel = k.sb([128, 2]); mt = k.sb([128, 96, 3])
    wt = [k.sb([128, 16, 512]) for _ in range(2)]
    M = k.sb([128, 4, 288])
    bsc, bbt, bsel, bmt, bMM, bmb, bGm = (Buf(n) for n in 'sc bt sel mt MM modbuf Gmod'.split())
    bw = [Buf('w0'), Buf('w1')]
    P.dma('sp', sc, c3_d, writes=[bsc])
    P.dma('sp', bt, b_d, writes=[bbt])
    P.dma('sp', sel, sel_d, writes=[bsel])
    P.dma('sp', c.gnorm, gn_d, writes=[c.bM])
    P.op('act', lambda e: e.activation(out=sc, in_=sc, func=AF.Silu), reads=[bsc], writes=[bsc])
    wv = w_d.rearrange("(kc p) n -> p kc n", p=128)
    for n in range(24):
        s = n % 2
        for hf in range(2):
            P.dma('sp' if hf == 0 else 'act', wt[s][:, hf * 8:(hf + 1) * 8, :], wv[:, hf * 8:(hf + 1) * 8, n * 512:(n + 1) * 512], writes=[bw[s]])
        for cb in range(4):
            cbl = n * 4 + cb
            pt, pb = k.nextps()
            for kc in range(16):
                P.op('pe', lambda e, pt=pt, s=s, kc=kc, cb=cb: e.matmul(pt[:, 0:3], wt[s][:, kc, cb * 128:(cb + 1) * 128], sc[:, kc, :],
                                                                      start=(kc == 0), stop=(kc == 15)),
                     reads=[bsc, bw[s]], writes=[pb], pe_acc=(kc > 0))
            P.op('dve', lambda e, pt=pt, cbl=cbl: e.tensor_scalar(out=mt[:, cbl, :], in0=pt[:, 0:3], scalar1=bt[:, cbl:cbl + 1], scalar2=None, op0=ALU.add),
                 reads=[pb, bbt], writes=[bmt])
    P.dma('sp', modbuf, mt.rearrange("p a b -> p (a b)"), reads=[bmt], writes=[bmb])
    P.op('pool', lambda e: e.collective_compute("AllGather", ALU.bypass, replica_groups=[[0, 1, 2, 3], [4, 5, 6, 7]], dma_qos="P3", ins=[modbuf.opt()], outs=[G_mod.opt()]),
         reads=[bmb], writes=[bGm], dma=True, inc=1)
    P.dma('sp', M, G_mod.rearrange("(r p) x -> p r x", p=128), reads=[bGm], writes=[bMM])
    M3 = M.rearrange("p r (cb x) -> p (r cb) x", x=3)
    P.op('dve', lambda e: e.tensor_scalar(out=c.Mlat, in0=M3[:, :, 0], scalar1=sel[:, 0:1], scalar2=None, op0=ALU.mult), reads=[bMM, bsel], writes=[c.bM])
    P.op('dve', lambda e: e.scalar_tensor_tensor(out=c.Mlat, in0=M3[:, :, 1], scalar=sel[:, 1:2], in1=c.Mlat, op0=ALU.mult, op1=ALU.add),
         reads=[bMM, bsel, c.bM], writes=[c.bM])
    P.op('dve', lambda e: e.tensor_copy(out=c.Mctx, in_=M3[:, :, 2]), reads=[bMM], writes=[c.bM])


def dense_params(k):
    return {'w_in': k.dram("w_in", [DEPTH, D, IN_COLS]), 'w_out': k.dram("w_out", [DEPTH, D, D]),
            'w_fi': k.dram("w_fi", [DEPTH, D, 2 * FFN_H]), 'w_fo': k.dram("w_fo", [DEPTH, FFN_H, D])}


def emit_dense(k, c, l_cur, pr, xsrc, bxsrc, xdst, bxdst, Gy, pbufs):
    P = k.P
    first = l_cur is None
    l_next = 0 if first else l_cur + 1
    last = l_next >= DEPTH
    G_y, bGy = Gy
    p_lat, p_ctx, bp, bpc, G_lat, G_ctx, bG, bGc, groups = pbufs

    def ML(l, v):
        return c.Mlat[:, l * 96 + v * 16:l * 96 + (v + 1) * 16]

    def MC(l, v):
        return c.Mctx[:, l * 96 + v * 16:l * 96 + (v + 1) * 16]

    x = k.sb([128, 16, NTOK]); h = k.sb([128, 16, NTOK], BF16)
    wr = [k.sb([128, 12288], BF16) for _ in range(2)]
    act = [k.sb([128, 2, NTOK], BF16) for _ in range(2)]
    sg = [k.sb([128, 512], BF16) for _ in range(2)]
    stage = [k.sb([128, NTOK]) for _ in range(2)]
    rstd = k.sb([128, NTOK]); coef = k.sb([128, 4, 16])
    ones = c.ones; epst = c.eps
    bx = [Buf('x%d' % i) for i in range(16)]
    bh = Buf('h'); bwr = [Buf('wr0'), Buf('wr1')]; bact = [Buf('a0'), Buf('a1')]; bsg = [Buf('sg0'), Buf('sg1')]
    bstage = [Buf('st0'), Buf('st1')]; brstd = Buf('rstd'); bcoef = Buf('coef')
    bmv = c.bM; bones = c.bconst; beps = c.bconst

    xv = xsrc.rearrange("(kc p) n -> p kc n", p=128)
    for q4 in range(4):
        P.dma('sp' if q4 % 2 == 0 else 'act', x[:, q4 * 4:(q4 + 1) * 4, :], xv[:, q4 * 4:(q4 + 1) * 4, :],
              reads=[bxsrc], writes=bx[q4 * 4:(q4 + 1) * 4])
    if not first:
        for ci, sv in ((0, ML(l_cur, 4)), (1, MC(l_cur, 4))):
            P.op('dve', lambda e, ci=ci, sv=sv: e.scalar_tensor_tensor(out=coef[:, ci, :], in0=sv, scalar=1.0, in1=c.gnorm[:, l_cur, :],
                                                                       op0=ALU.add, op1=ALU.mult), reads=[bmv], writes=[bcoef])
    if not last:
        for ci, sv in ((2, ML(l_next, 1)), (3, MC(l_next, 1))):
            P.op('dve', lambda e, ci=ci, sv=sv: e.scalar_tensor_tensor(out=coef[:, ci, :], in0=sv, scalar=1.0, in1=c.gnorm[:, DEPTH + l_next, :],
                                                                       op0=ALU.add, op1=ALU.mult), reads=[bmv], writes=[bcoef])
    steps = []

    def wview(slot, off, kc, n):
        return wr[slot][:, off:off + kc * n].rearrange("p (kc n) -> p kc n", n=n)

    def norm(a_l, a_c, b_l, b_c):
        for ti, (t0, tn) in enumerate(TT):
            pt, pb = k.nextps()
            for kc in range(16):
                s = kc % 2
                P.op('act', lambda e, s=s, kc=kc, t0=t0, tn=tn: e.activation(
                    out=stage[s][:, 0:tn], in_=x[:, kc, t0:t0 + tn], func=AF.Square), reads=[bx[kc]], writes=[bstage[s]])
                P.op('pe', lambda e, pt=pt, s=s, tn=tn, kc=kc: e.matmul(
                    pt[:, 0:tn], ones, stage[s][:, 0:tn], start=(kc == 0), stop=(kc == 15)),
                    reads=[bones, bstage[s]], writes=[pb], pe_acc=(kc > 0))
            P.op('act', lambda e, pt=pt, t0=t0, tn=tn: e.activation(
                out=rstd[:, t0:t0 + tn], in_=pt[:, 0:tn], func=AF.Sqrt, bias=epst, scale=1.0 / D), reads=[pb, beps], writes=[brstd])
        P.op('dve', lambda e: e.reciprocal(out=rstd, in_=rstd), reads=[brstd], writes=[brstd])
        for kc in range(16):
            s = kc % 2
            P.op('dve', lambda e, s=s, kc=kc: e.tensor_tensor(out=stage[s], in0=x[:, kc, :], in1=rstd, op=ALU.mult),
                 reads=[bx[kc], brstd], writes=[bstage[s]])
            P.op('act', lambda e, s=s, kc=kc: e.activation(
                out=h[:, kc, 0:1024], in_=stage[s][:, 0:1024], func=AF.Identity, bias=b_l[:, kc:kc + 1], scale=a_l[:, kc:kc + 1]),
                reads=[bstage[s], bcoef, bmv], writes=[bh])
            P.op('act', lambda e, s=s, kc=kc: e.activation(
                out=h[:, kc, 1024:NTOK], in_=stage[s][:, 1024:NTOK], func=AF.Identity, bias=b_c[:, kc:kc + 1], scale=a_c[:, kc:kc + 1]),
                reads=[bstage[s], bcoef, bmv], writes=[bh])

    def resid_evac(pt, pb, dc, ti, gl, gc):
        t0, tn = TT[ti]
        g = gc if ti == 2 else gl
        P.op('dve', lambda e: e.scalar_tensor_tensor(
            out=x[:, dc, t0:t0 + tn], in0=pt[:, 0:tn], scalar=g[:, dc:dc + 1], in1=x[:, dc, t0:t0 + tn],
            op0=ALU.mult, op1=ALU.add), reads=[pb, bmv, bx[dc]], writes=[bx[dc]])

    if not first:
        w_out = pr['w_out'][l_cur]; w_fi = pr['w_fi'][l_cur]; w_fo = pr['w_fo'][l_cur]
        Gy2 = G_y

        def load_y():
            for kc in range(16):
                col = IDXC[('y', kc)]
                P.op('pool', lambda e, kc=kc, col=col: e.indirect_dma_start(
                    out=h[:, kc, :], out_offset=None, in_=Gy2, in_offset=bass.IndirectOffsetOnAxis(ap=c.idx[:, col:col + 1], axis=0)),
                    reads=bGy[(kc // 4) * 4:(kc // 4) * 4 + 4] + [c.bconst], writes=[bh], dma=True)
        steps.append(('call', load_y))
        for n4 in range(4):
            def ld(slot, n4=n4):
                P.dma('pool', wview(slot, 0, 16, 512), w_out.rearrange("(kc p) n -> p kc n", p=128)[:, :, n4 * 512:(n4 + 1) * 512],
                      writes=[bwr[slot]])

            def cp(slot, n4=n4):
                wv = wview(slot, 0, 16, 512)
                for m in range(4):
                    dc = n4 * 4 + m
                    for ti, (t0, tn) in enumerate(TT):
                        pt, pb = k.nextps()
                        for kc in range(16):
                            P.op('pe', lambda e, pt=pt, wv=wv, kc=kc, m=m, t0=t0, tn=tn: e.matmul(
                                pt[:, 0:tn], wv[:, kc, m * 128:(m + 1) * 128], h[:, kc, t0:t0 + tn],
                                start=(kc == 0), stop=(kc == 15)), reads=[bwr[slot], bh], writes=[pb], pe_acc=(kc > 0))
                        resid_evac(pt, pb, dc, ti, ML(l_cur, 2), MC(l_cur, 2))
            steps.append(('w', ld, cp))
        steps.append(('call', lambda: norm(coef[:, 0, :], coef[:, 1, :], ML(l_cur, 3), MC(l_cur, 3))))
        for g in range(FFN_H // 256):
            def ld(slot, g=g):
                wfv = w_fi.rearrange("(kc p) n -> p kc n", p=128)
                P.dma('pool', wview(slot, 0, 16, 256), wfv[:, :, g * 256:(g + 1) * 256], writes=[bwr[slot]])
                P.dma('pool', wview(slot, 4096, 16, 256), wfv[:, :, FFN_H + g * 256:FFN_H + (g + 1) * 256], writes=[bwr[slot]])
                P.dma('pool', wview(slot, 8192, 2, 2048), w_fo[g * 256:(g + 1) * 256, :].rearrange("(hc p) n -> p hc n", p=128),
                      writes=[bwr[slot]])

            def cp(slot, g=g):
                wg = wview(slot, 0, 16, 256); wu = wview(slot, 4096, 16, 256); wo = wview(slot, 8192, 2, 2048)
                a = g % 2
                for hc in range(2):
                    for ti, (t0, tn) in enumerate(TT):
                        pg, pgb = k.nextps()
                        pu, pub = k.nextps()
                        for (pt, pb, wv) in ((pg, pgb, wg), (pu, pub, wu)):
                            for kc in range(16):
                                P.op('pe', lambda e, pt=pt, wv=wv, kc=kc, hc=hc, t0=t0, tn=tn: e.matmul(
                                    pt[:, 0:tn], wv[:, kc, hc * 128:(hc + 1) * 128], h[:, kc, t0:t0 + tn],
                                    start=(kc == 0), stop=(kc == 15)), reads=[bwr[slot], bh], writes=[pb], pe_acc=(kc > 0))
                        s = (hc * 3 + ti) % 2
                        P.op('act', lambda e, s=s, pg=pg, tn=tn: e.activation(
                            out=sg[s][:, 0:tn], in_=pg[:, 0:tn], func=AF.Silu), reads=[pgb], writes=[bsg[s]])
                        P.op('dve', lambda e, s=s, pu=pu, a=a, hc=hc, t0=t0, tn=tn: e.tensor_tensor(
                            out=act[a][:, hc, t0:t0 + tn], in0=sg[s][:, 0:tn], in1=pu[:, 0:tn], op=ALU.mult),
                            reads=[bsg[s], pub], writes=[bact[a]])
                for dc in range(16):
                    for ti, (t0, tn) in enumerate(TT):
                        pt, pb = k.nextps()
                        for hc in range(2):
                            P.op('pe', lambda e, pt=pt, hc=hc, dc=dc, t0=t0, tn=tn: e.matmul(
                                pt[:, 0:tn], wo[:, hc, dc * 128:(dc + 1) * 128], act[a][:, hc, t0:t0 + tn],
                                start=(hc == 0), stop=(hc == 1)), reads=[bwr[slot], bact[a]], writes=[pb], pe_acc=(hc > 0))
                        resid_evac(pt, pb, dc, ti, ML(l_cur, 5), MC(l_cur, 5))
            steps.append(('w', ld, cp))

        def store_x():
            xov = xdst.rearrange("(kc p) n -> p kc n", p=128)
            for q4 in range(4):
                P.dma('sp' if q4 % 2 == 0 else 'act', xov[:, q4 * 4:(q4 + 1) * 4, :], x[:, q4 * 4:(q4 + 1) * 4, :],
                      reads=bx[q4 * 4:(q4 + 1) * 4], writes=[bxdst])
        steps.append(('call', store_x))
    if not last:
        w_in = pr['w_in'][l_next]
        steps.append(('call', lambda: norm(coef[:, 2, :], coef[:, 3, :], ML(l_next, 0), MC(l_next, 0))))
        ncol = [(i * 512, 512) for i in (0, 1, 2, 10, 11)] + [(6144, 32)] + [(i * 512, 512) for i in (7, 8, 9, 3, 4, 5, 6)]
        for (c0, cn) in ncol:
            def ld(slot, c0=c0, cn=cn):
                P.dma('pool', wview(slot, 0, 16, cn), w_in.rearrange("(kc p) n -> p kc n", p=128)[:, :, c0:c0 + cn], writes=[bwr[slot]])

            def cp(slot, c0=c0, cn=cn):
                wv = wview(slot, 0, 16, cn)
                for m in range((cn + 127) // 128):
                    mw = min(128, cn - m * 128)
                    s = m % 2
                    for ti, (t0, tn) in enumerate(TT):
                        pt, pb = k.nextps()
                        for kc in range(16):
                            P.op('pe', lambda e, pt=pt, wv=wv, kc=kc, m=m, mw=mw, t0=t0, tn=tn: e.matmul(
                                pt[0:mw, 0:tn], wv[:, kc, m * 128:m * 128 + mw], h[:, kc, t0:t0 + tn],
                                start=(kc == 0), stop=(kc == 15)), reads=[bwr[slot], bh], writes=[pb], pe_acc=(kc > 0))
                        if ti == 1:
                            P.op('dve', lambda e, pt=pt, s=s, mw=mw, t0=t0, tn=tn: e.tensor_copy(
                                out=stage[s][0:mw, t0:t0 + tn], in_=pt[0:mw, 0:tn]), reads=[pb], writes=[bstage[s]])
                        else:
                            P.op('act', lambda e, pt=pt, s=s, mw=mw, t0=t0, tn=tn: e.activation(
                                out=stage[s][0:mw, t0:t0 + tn], in_=pt[0:mw, 0:tn], func=AF.Copy), reads=[pb], writes=[bstage[s]])
                    r0 = c0 + m * 128
                    q = r0 // PCH
                    qc = r0 // PCC
                    P.dma('sp', p_lat[r0:r0 + mw, :], stage[s][0:mw, 0:1024], reads=[bstage[s]], writes=[bp[q]])
                    P.dma('act', p_ctx[r0:r0 + mw, :], stage[s][0:mw, 1024:NTOK], reads=[bstage[s]], writes=[bpc[qc]])
                    if (r0 + mw) % PCH == 0 or (r0 + mw) == IN_COLS:
                        q0 = q * PCH
                        rq = min(PCH, IN_COLS - q0)
                        P.op('pool', lambda e, q0=q0, rq=rq: e.collective_compute(
                            "AllGather", ALU.bypass, replica_groups=groups, dma_qos="P3", ins=[p_lat[q0:q0 + rq, :].opt()],
                            outs=[G_lat[4 * q0:4 * q0 + 4 * rq, :].opt()]), reads=[bp[q]], writes=[bG[q]], dma=True, inc=1, cc=True)
                    if (r0 + mw) % PCC == 0 or (r0 + mw) == IN_COLS:
                        q0 = qc * PCC
                        rq = min(PCC, IN_COLS - q0)
                        P.op('pool', lambda e, q0=q0, rq=rq: e.collective_compute(
                            "AllGather", ALU.bypass, replica_groups=groups, dma_qos="P3", ins=[p_ctx[q0:q0 + rq, :].opt()],
                            outs=[G_ctx[4 * q0:4 * q0 + 4 * rq, :].opt()]), reads=[bpc[qc]], writes=[bGc[qc]], dma=True, inc=1, cc=True)
            steps.append(('w', ld, cp))

    wsteps = [i for i, s in enumerate(steps) if s[0] == 'w']
    slot_of = {si: j % 2 for j, si in enumerate(wsteps)}
    nxt = {wsteps[j]: wsteps[j + 1] for j in range(len(wsteps) - 1)}
    if wsteps:
        steps[wsteps[0]][1](slot_of[wsteps[0]])
    for i, s in enumerate(steps):
        if s[0] == 'call':
            s[1]()
        else:
            if i in nxt:
                steps[nxt[i]][1](slot_of[nxt[i]])
            s[2](slot_of[i])


def build_fused(depth=DEPTH, stop_after=None):
    k = K()
    P = k.P
    c = setup_common(k)
    xT = k.dram("xT", [D, NTOK])
    xo = k.dram("xo", [D, NTOK], kind="ExternalOutput")
    xspill = k.dram("xspill", [D, NTOK], kind="Internal")
    p_lat = k.dram("p_lat", [IN_COLS, 1024], kind="Internal")
    p_ctx = k.dram("p_ctx", [IN_COLS, 64], kind="Internal")
    G_lat = k.dram("G_lat", [4 * IN_COLS, 1024], kind="Internal")
    G_ctx = k.dram("G_ctx", [4 * IN_COLS, 64], kind="Internal")
    ybuf = k.dram("ybuf", [4, 4, 128, NTOK], kind="Internal")
    G_y = k.dram("G_y", [4 * 4 * 4 * 128, NTOK], kind="Internal")
    bxT, bxo, bxs = (Buf(n) for n in 'xT xo xspill'.split())
    NQ = (IN_COLS + PCH - 1) // PCH
    bp = [Buf('p%d' % i) for i in range(NQ)]
    bG = [Buf('G%d' % i) for i in range(NQ)]
    NQC = (IN_COLS + PCC - 1) // PCC
    bpc = [Buf('pc%d' % i) for i in range(NQC)]
    bGc = [Buf('Gc%d' % i) for i in range(NQC)]
    bys = [Buf('y%d' % i) for i in range(16)]
    bGy = [Buf('Gy%d' % i) for i in range(16)]
    prd = dense_params(k)
    pra = {'na': attn_params(k, 'na'), 'wa': attn_params(k, 'wa')}
    prm = ml_params(k)
    prn = dn_params(k)
    groups = [[0, 1, 2, 3], [4, 5, 6, 7]]
    emit_mod(k, c)
    k.phase()

    def stop(tag):
        if stop_after != tag:
            return False
        dbgM = k.dram("dbgM", [128, 2, 384], kind="ExternalOutput")
        dbgG = k.dram("dbgG", [4 * IN_COLS, 64], kind="ExternalOutput")
        dbgY = k.dram("dbgY", [4 * 2048, 64], kind="ExternalOutput")
        P.dma('sp', dbgM[:, 0, :], c.Mlat, reads=[c.bM])
        P.dma('sp', dbgM[:, 1, :], c.Mctx, reads=[c.bM])
        P.dma('sp', dbgG, G_ctx, reads=bGc)
        P.dma('sp', dbgY, G_y[:, 1024:1088], reads=bGy)
        P.dma('act', xo, xT, reads=[bxT], writes=[bxo])
        return True

    bybuf = (bys, G_y, bGy, groups)
    pb_all = (p_lat, p_ctx, bp, bpc, G_lat, G_ctx, bG, bGc, groups)

    if stop('mod'):
        return k.done()
    emit_dense(k, c, None, prd, xT, bxT, None, None, (G_y, bGy), pb_all)
    k.phase()
    if stop('d0'):
        return k.done()
    G = (G_lat, G_ctx, bG, bGc)
    for l in range(depth):
        for (tag, fn) in (('na', lambda: emit_attn(k, c, 'na', l, G, ybuf, bybuf, pra['na'])),
                          ('wa', lambda: emit_attn(k, c, 'wa', l, G, ybuf, bybuf, pra['wa'])),
                          ('ml', lambda: emit_ml(k, c, l, G, ybuf, bybuf, prm)),
                          ('dn', lambda: emit_dn(k, c, l, G, ybuf, bybuf, prn))):
            fn()
            k.phase()
            if stop('%s%d' % (tag, l)):
                return k.done()
        lastl = (l == depth - 1)
        src, bsrc = (xT, bxT) if l == 0 else (xspill, bxs)
        dst, bdst = (xo, bxo) if lastl else (xspill, bxs)
        emit_dense(k, c, l, prd, src, bsrc, dst, bdst, (G_y, bGy), pb_all)
        k.phase()
        if (not lastl) and stop('d%d' % (l + 1)):
            return k.done()
    return k.done()

import numpy as np

NCORES = 8
_PROG = {}


def _rope_tables():
    n = 4096
    t = np.arange(n)
    n_freq = 32
    inv_freq = (np.float32(10000.0) ** (-np.arange(n_freq, dtype=np.float32) / np.float32(n_freq))).astype(np.float32)
    pos = np.stack([t // 64, t % 64], -1).astype(np.float32)
    ang = pos[:, :, None] * inv_freq
    cos = np.cos(ang).astype(np.float32)
    sin = np.sin(ang).astype(np.float32)
    C = np.zeros((128, n), np.float32)
    S = np.zeros((128, n), np.float32)
    RmT = np.zeros((128, 128), np.float32)
    for d in range(128):
        a, tt, f = d // 64, (d // 32) % 2, d % 32
        C[d] = cos[:, a, f]
        S[d] = sin[:, a, f]
        if tt == 0:
            RmT[d + 32, d] = -1.0
        else:
            RmT[d - 32, d] = 1.0
    return C, S, RmT


def _na_bias_tables(rpb):
    NEG = -30000.0
    tab = np.zeros((128, 5, 7 * 128), np.float32)
    classes = [(0, [0, 1, 2, 3]), (1, [-1, 0, 1, 2]), (5, [-2, -1, 0, 1, 2]), (30, [-2, -1, 0, 1]), (31, [-3, -2, -1, 0])]
    kk = np.arange(128)
    qq = np.arange(128)
    for cls, (n, offs) in enumerate(classes):
        r = 2 * n + qq // 64
        qc = qq % 64
        rs = np.clip(r - 4, 0, 56)
        ws = np.clip(qc - 8, 0, 48)
        for ci, off in enumerate(offs):
            ch = n + off
            kr = 2 * ch + kk // 64
            kc = kk % 64
            ok = ((kr[:, None] >= rs[None, :]) & (kr[:, None] < rs[None, :] + 8)
                  & (kc[:, None] >= ws[None, :]) & (kc[:, None] < ws[None, :] + 16))
            dr = np.clip(kr[:, None] - r[None, :] + 7, 0, 14)
            dc = np.clip(kc[:, None] - qc[None, :], -15, 15) + 15
            tab[:, cls, ci * 128:(ci + 1) * 128] = np.where(ok, rpb[dr, dc], NEG)
    return tab


def _core_inputs(I, core, shared):
    b, j = core // 4, core % 4
    m = dict(shared)
    m["idx"] = make_idx(j)
    m["w_mod"] = I['w_ada'][j]
    m["b_mod"] = np.ascontiguousarray(I['b_ada'][j].reshape(96, 128).T)
    sel = np.zeros((128, 2), np.float32)
    sel[:, b] = 1.0
    m["sel"] = sel
    xc = np.concatenate([I['x'][b, j * 1024:(j + 1) * 1024], I['ctx'][b, j * 64:(j + 1) * 64]], 0)
    m["xT"] = np.ascontiguousarray(xc.T)
    m["wa_sinkb"] = np.ascontiguousarray(np.broadcast_to(I['wa_sink'][:, j][:, None, None], (4, 128, 1))).astype(np.float32)
    m["na_bias"] = np.stack([_na_bias_tables(I['na_rpb'][ll][j]) for ll in range(4)], 0)
    gbv = np.stack([I['ml_i_bias'][:, 0, j], I['ml_i_bias'][:, 1, j], I['ml_f_bias'][:, 0, j], I['ml_f_bias'][:, 1, j]], -1)
    m["ml_gb"] = np.ascontiguousarray(np.broadcast_to(gbv[:, None, :], (4, 68, 4))).astype(np.float32)
    m["ml_gn"] = np.ascontiguousarray(np.broadcast_to(I['ml_norm'][:, j][:, None, :], (4, 64, 128))).astype(np.float32)
    cw = np.stack([I['dn_conv'][:, :, t * 512 + j * 128: t * 512 + (j + 1) * 128] for t in range(3)], 1)
    cw2 = np.stack([cw, cw[:, :, ::-1, :]], 1)
    m["dn_cw"] = np.ascontiguousarray(cw2.transpose(0, 1, 4, 2, 3)).astype(np.float32)
    scv = np.stack([I['dn_a_log'][:, 0, j], I['dn_a_log'][:, 1, j], I['dn_dt_bias'][:, 0, j], I['dn_dt_bias'][:, 1, j]], -1)
    m["dn_sc"] = np.ascontiguousarray(np.broadcast_to(scv[:, None, :], (4, 68, 4))).astype(np.float32)
    return m


def kernel(**I):
    I = {k_: np.asarray(v, np.float32) for k_, v in I.items()}
    if 'nc' not in _PROG:
        _PROG['nc'] = build_fused()
    nc = _PROG['nc']
    C, S, RmT = _rope_tables()
    kk = np.arange(128)[:, None]
    qq = np.arange(128)[None, :]
    jj = np.arange(64)
    mk = np.zeros((64, 2, 64), np.float32)
    mk[:, 0, :] = np.where(jj[None, :] >= jj[:, None], 0.0, -30000.0)
    mk[:, 1, :] = (jj[None, :] > jj[:, None]).astype(np.float32)
    c3 = np.stack([I['c'][0], I['c'][1], I['c_ctx']], 0)
    gnorm = np.zeros((128, 8, 16), np.float32)
    for l in range(4):
        gnorm[:, l] = I['norm_ffn'][l].reshape(16, 128).T
        gnorm[:, 4 + l] = I['norm_mix'][l].reshape(16, 128).T
    shared = {
        "I128": np.eye(128, dtype=np.float32), "J128": np.ascontiguousarray(np.eye(128, dtype=np.float32)[::-1]),
        "c3": np.ascontiguousarray(c3.T.reshape(16, 128, 3).transpose(1, 0, 2)), "gnorm": gnorm,
        "w_in": I['w_in'], "w_out": I['w_out'], "w_fi": I['w_ffn_in'], "w_fo": I['w_ffn_out'],
        "na_gains": np.ascontiguousarray(I['na_qk_gain'].transpose(0, 2, 1)),
        "wa_gains": np.ascontiguousarray(I['wa_qk_gain'].transpose(0, 2, 1)),
        "cosT": C, "sinT": S, "rmT": RmT, "wamask": np.concatenate([(kk >= qq), (kk <= qq)], 1).astype(np.float32),
        "ml_tri": (jj[None, :] >= jj[:, None]).astype(np.float32),
        "dn_gn": np.ascontiguousarray(np.broadcast_to(I['dn_norm'][:, None, :], (4, 64, 128))).astype(np.float32),
        "dn_masks": mk,
    }
    ins = [_core_inputs(I, core, shared) for core in range(NCORES)]
    res = run_bass_kernel_spmd(nc, ins, core_ids=list(range(NCORES))).results
    out = np.zeros((2, 4096, 2048), np.float32)
    for core in range(NCORES):
        b, j = core // 4, core % 4
        out[b, j * 1024:(j + 1) * 1024] = res[core]["xo"][:, 0:1024].T
    return out
```
